# Optimizing a Trainium2 kernel written in Bass

```python
import math
import jax
import jax.numpy as jnp
from jax import lax
import numpy as np

D_MODEL = 1024
BATCH = 8
SEQ = 4096
DEPTH = 4

N_EVEN = (DEPTH + 1) // 2
N_ODD = DEPTH // 2
MIX_WIDTH = D_MODEL
NORM_EPS = 1e-6

S5_GROUP = 16
S5_STATE = 64
S5_WIDTH = MIX_WIDTH // 2
S5_GROUPS = S5_WIDTH // S5_GROUP

RWKV_HEAD = 64
RWKV_WIDTH = MIX_WIDTH - S5_WIDTH
RWKV_HEADS = RWKV_WIDTH // RWKV_HEAD
DECAY_LORA = 64
ICLR_LORA = 64
GATE_LORA = 128
RWKV_SHIFTED = 3 * RWKV_WIDTH + DECAY_LORA + ICLR_LORA + GATE_LORA
RWKV_DECAY_SCALE = math.exp(-0.5)
RWKV_LN_EPS = 1e-5 * RWKV_HEAD
EVEN_IN = S5_WIDTH + RWKV_SHIFTED + MIX_WIDTH

GDN_HEAD = 128
GDN_HEADS = MIX_WIDTH // GDN_HEAD
GDN_CONV = 4
GDN_CHUNK = 64
ODD_IN = 3 * MIX_WIDTH + 2 * GDN_HEADS + MIX_WIDTH

kernel_name = 'hybrid_s5_rwkv7_gdn_adaln'


def rms_norm(x, w, eps=NORM_EPS):
    xf = x.astype(jnp.float32)
    y = xf * lax.rsqrt(jnp.mean(xf * xf, axis=-1, keepdims=True) + eps)
    return y.astype(x.dtype) * w


def l2_normalize(t, eps=1e-6):
    return t * lax.rsqrt(jnp.sum(t * t, axis=-1, keepdims=True) + eps)


def token_shift(t):
    return jnp.pad(t, ((0, 0), (1, 0), (0, 0)))[:, :-1]


def causal_depthwise_conv(x, w):
    k_len, ch = w.shape
    xp = jnp.pad(x, ((0, 0), (k_len - 1, 0), (0, 0)))
    return lax.conv_general_dilated(xp, w[:, None, :], window_strides=(1,), padding='VALID',
                                    dimension_numbers=('NWC', 'WIO', 'NWC'), feature_group_count=ch)


def _diag_linear_combine(e1, e2):
    a1, b1 = e1
    a2, b2 = e2
    return a1 * a2, a2 * b1 + b2


def s5_mixer(u, lambda_re, lambda_im, log_step, b_re, b_im, c_re, c_im, d_skip, glu_w, glu_b):
    bsz, seq, _ = u.shape
    f32 = jnp.float32
    uf = u.astype(f32)
    lam = lax.complex(jnp.minimum(lambda_re.astype(f32), -1e-4), lambda_im.astype(f32))
    step = jnp.exp(log_step.astype(f32))[:, None]
    lam_bar = jnp.exp(lam * step)
    b_bar = ((lam_bar - 1.0) / lam)[..., None] * lax.complex(b_re.astype(f32), b_im.astype(f32))
    c_mat = lax.complex(c_re.astype(f32), c_im.astype(f32))
    u_grp = uf.reshape(bsz, seq, S5_GROUPS, S5_GROUP).astype(jnp.complex64)
    bu = jnp.einsum('gpc,blgc->lbgp', b_bar, u_grp)
    a = jnp.broadcast_to(lam_bar[None, None], (seq, 1, S5_GROUPS, S5_STATE))
    _, states = lax.associative_scan(_diag_linear_combine, (a, bu), axis=0)
    y = jnp.real(jnp.einsum('gcp,lbgp->blgc', c_mat, states)).reshape(bsz, seq, S5_WIDTH)
    y = jax.nn.gelu(y + d_skip * uf)
    return y * jax.nn.sigmoid(y @ glu_w + glu_b)


def rwkv7_mixer(feats, mu, w0, w_up, a0, a_up, g_up, k_k, k_a, r_k, ln_w, ln_b):
    f = feats.astype(jnp.float32)
    f = f + mu * (token_shift(f) - f)
    w1 = RWKV_WIDTH
    r, k, v, xw, xa, xg = jnp.split(f, [w1, 2 * w1, 3 * w1, 3 * w1 + DECAY_LORA, 3 * w1 + DECAY_LORA + ICLR_LORA], axis=-1)
    log_decay = -RWKV_DECAY_SCALE * jax.nn.sigmoid(w0 + jnp.tanh(xw) @ w_up)
    a = jax.nn.sigmoid(a0 + xa @ a_up)
    g = jax.nn.sigmoid(xg) @ g_up
    bsz, seq, _ = f.shape
    heads = lambda t: t.reshape(bsz, seq, RWKV_HEADS, RWKV_HEAD)
    r, k, v, a, decay = heads(r), heads(k), heads(v), heads(a), heads(jnp.exp(log_decay))
    kk = l2_normalize(k * k_k.reshape(RWKV_HEADS, RWKV_HEAD))
    k = k * (1.0 + (a - 1.0) * k_a.reshape(RWKV_HEADS, RWKV_HEAD))

    def step(state, inp):
        r_t, w_t, k_t, v_t, kk_t, a_t = inp
        removed = jnp.einsum('bhvk,bhk->bhv', state, kk_t)
        state = (state * w_t[:, :, None, :]
                 - removed[..., None] * (kk_t * a_t)[:, :, None, :]
                 + v_t[..., None] * k_t[:, :, None, :])
        return state, jnp.einsum('bhvk,bhk->bhv', state, r_t)

    time_major = lambda t: jnp.moveaxis(t, 1, 0)
    state0 = jnp.zeros((bsz, RWKV_HEADS, RWKV_HEAD, RWKV_HEAD), jnp.float32)
    _, o = lax.scan(step, state0, (time_major(r), time_major(decay), time_major(k),
                                   time_major(v), time_major(kk), time_major(a)))
    o = jnp.moveaxis(o, 0, 1)
    mean = jnp.mean(o, axis=-1, keepdims=True)
    var = jnp.mean(jnp.square(o - mean), axis=-1, keepdims=True)
    o = ((o - mean) * lax.rsqrt(var + RWKV_LN_EPS)).reshape(bsz, seq, RWKV_WIDTH) * ln_w + ln_b
    bonus = jnp.sum(r * k * r_k.reshape(RWKV_HEADS, RWKV_HEAD), axis=-1, keepdims=True) * v
    return (o + bonus.reshape(bsz, seq, RWKV_WIDTH)) * g


def chunk_gated_delta_rule(q, k, v, log_decay, beta):
    bsz, seq, nh, dk = q.shape
    dv = v.shape[-1]
    n_chunks = seq // GDN_CHUNK
    cs = GDN_CHUNK
    chunks = lambda t: jnp.moveaxis(t.reshape(bsz, n_chunks, cs, nh, -1), 3, 1)
    q, k, v = chunks(q), chunks(k), chunks(v)
    beta = chunks(beta[..., None])[..., 0]
    g = jnp.cumsum(chunks(log_decay[..., None])[..., 0], axis=-1)
    causal = jnp.tril(jnp.ones((cs, cs), bool))
    strict = jnp.tril(jnp.ones((cs, cs), bool), -1)
    decay = jnp.exp(jnp.where(causal, g[..., :, None] - g[..., None, :], -jnp.inf))
    k_beta = k * beta[..., None]
    a_mat = jnp.where(strict, jnp.einsum('bhncd,bhnsd->bhncs', k_beta, k) * decay, 0.0)
    eye = jnp.eye(cs, dtype=q.dtype)
    t_mat = lax.linalg.triangular_solve(eye + a_mat, jnp.broadcast_to(eye, a_mat.shape),
                                        left_side=True, lower=True, unit_diagonal=True)
    u = t_mat @ (v * beta[..., None])
    w = t_mat @ (k_beta * jnp.exp(g)[..., None])
    intra = jnp.where(causal, jnp.einsum('bhncd,bhnsd->bhncs', q, k) * decay, 0.0)
    q_decayed = q * jnp.exp(g)[..., None]
    k_to_end = k * jnp.exp(g[..., -1:] - g)[..., None]
    chunk_decay = jnp.exp(g[..., -1])

    def step(state, inp):
        q_i, k_i, u_i, w_i, intra_i, cd_i = inp
        v_new = u_i - jnp.einsum('bhcd,bhdv->bhcv', w_i, state)
        o_i = jnp.einsum('bhcd,bhdv->bhcv', q_i, state) + jnp.einsum('bhcs,bhsv->bhcv', intra_i, v_new)
        state = state * cd_i[..., None, None] + jnp.einsum('bhcd,bhcv->bhdv', k_i, v_new)
        return state, o_i

    by_chunk = lambda t: jnp.moveaxis(t, 2, 0)
    state0 = jnp.zeros((bsz, nh, dk, dv), q.dtype)
    _, o = lax.scan(step, state0, (by_chunk(q_decayed), by_chunk(k_to_end), by_chunk(u),
                                   by_chunk(w), by_chunk(intra), by_chunk(chunk_decay)))
    return jnp.transpose(o, (1, 0, 3, 2, 4)).reshape(bsz, seq, nh, dv)


def gated_deltanet_mixer(qkv, beta_raw, alpha_raw, conv_w, a_log, dt_bias, norm_w):
    bsz, seq, _ = qkv.shape
    f32 = jnp.float32
    qkv = jax.nn.silu(causal_depthwise_conv(qkv.astype(f32), conv_w.astype(f32)))
    q, k, v = jnp.split(qkv, 3, axis=-1)
    heads = lambda t: t.reshape(bsz, seq, GDN_HEADS, GDN_HEAD)
    q = l2_normalize(heads(q)) * (GDN_HEAD ** -0.5)
    k = l2_normalize(heads(k))
    v = heads(v)
    beta = jax.nn.sigmoid(beta_raw.astype(f32))
    log_decay = -jnp.exp(a_log.astype(f32)) * jax.nn.softplus(alpha_raw.astype(f32) + dt_bias)
    o = chunk_gated_delta_rule(q, k, v, log_decay, beta)
    o = o * lax.rsqrt(jnp.mean(o * o, axis=-1, keepdims=True) + NORM_EPS) * norm_w
    return o.reshape(bsz, seq, MIX_WIDTH)


def setup_inputs(seed: int = 0) -> dict:
    key = jax.random.key(seed)
    ks = list(jax.random.split(key, 40))
    nrm = lambda shape, std: std * jax.random.normal(ks.pop(), shape, jnp.float32)
    uni = lambda shape, lo, hi: jax.random.uniform(ks.pop(), shape, jnp.float32, lo, hi)
    n_idx = jnp.arange(S5_STATE, dtype=jnp.float32)
    dt = jnp.exp(uni((N_ODD, GDN_HEADS), math.log(1e-3), math.log(1e-1)))
    return {
        'x': nrm((BATCH, SEQ, D_MODEL), 1.0),
        'c': nrm((BATCH, D_MODEL), 1.0),
        'norm_w': 1.0 + nrm((DEPTH, D_MODEL), 0.02),
        'ada_w': nrm((DEPTH, D_MODEL, 3 * D_MODEL), 0.5 * D_MODEL ** -0.5),
        'ada_b': nrm((DEPTH, 3 * D_MODEL), 0.02),
        'w_out': nrm((DEPTH, MIX_WIDTH, D_MODEL), MIX_WIDTH ** -0.5),
        'final_norm_w': 1.0 + nrm((D_MODEL,), 0.02),
        'even_w_in': nrm((N_EVEN, D_MODEL, EVEN_IN), D_MODEL ** -0.5),
        's5_lambda_re': -0.5 + nrm((N_EVEN, S5_GROUPS, S5_STATE), 0.01),
        's5_lambda_im': math.pi * n_idx + nrm((N_EVEN, S5_GROUPS, S5_STATE), 0.01),
        's5_log_step': uni((N_EVEN, S5_GROUPS), math.log(1e-3), math.log(1e-1)),
        's5_b_re': nrm((N_EVEN, S5_GROUPS, S5_STATE, S5_GROUP), (2 * S5_GROUP) ** -0.5),
        's5_b_im': nrm((N_EVEN, S5_GROUPS, S5_STATE, S5_GROUP), (2 * S5_GROUP) ** -0.5),
        's5_c_re': nrm((N_EVEN, S5_GROUPS, S5_GROUP, S5_STATE), 0.5),
        's5_c_im': nrm((N_EVEN, S5_GROUPS, S5_GROUP, S5_STATE), 0.5),
        's5_d': nrm((N_EVEN, S5_WIDTH), 0.5),
        's5_glu_w': nrm((N_EVEN, S5_WIDTH, S5_WIDTH), S5_WIDTH ** -0.5),
        's5_glu_b': nrm((N_EVEN, S5_WIDTH), 0.02),
        'rwkv_mu': uni((N_EVEN, RWKV_SHIFTED), 0.0, 1.0),
        'rwkv_w0': uni((N_EVEN, RWKV_WIDTH), -4.0, 2.0),
        'rwkv_w_up': nrm((N_EVEN, DECAY_LORA, RWKV_WIDTH), 0.1),
        'rwkv_a0': nrm((N_EVEN, RWKV_WIDTH), 0.1),
        'rwkv_a_up': nrm((N_EVEN, ICLR_LORA, RWKV_WIDTH), 0.1),
        'rwkv_g_up': nrm((N_EVEN, GATE_LORA, RWKV_WIDTH), GATE_LORA ** -0.5),
        'rwkv_k_k': 0.85 + nrm((N_EVEN, RWKV_WIDTH), 0.02),
        'rwkv_k_a': 1.0 + nrm((N_EVEN, RWKV_WIDTH), 0.02),
        'rwkv_r_k': nrm((N_EVEN, RWKV_WIDTH), 0.1),
        'rwkv_ln_w': 1.0 + nrm((N_EVEN, RWKV_WIDTH), 0.02),
        'rwkv_ln_b': nrm((N_EVEN, RWKV_WIDTH), 0.02),
        'odd_w_in': nrm((N_ODD, D_MODEL, ODD_IN), D_MODEL ** -0.5),
        'gdn_conv_w': nrm((N_ODD, GDN_CONV, 3 * MIX_WIDTH), GDN_CONV ** -0.5),
        'gdn_a_log': jnp.log(uni((N_ODD, GDN_HEADS), 1.0, 16.0)),
        'gdn_dt_bias': dt + jnp.log(-jnp.expm1(-dt)),
        'gdn_norm_w': 1.0 + nrm((N_ODD, GDN_HEAD), 0.02),
    }


def reference(x, c, norm_w, ada_w, ada_b, w_out, final_norm_w, even_w_in,
              s5_lambda_re, s5_lambda_im, s5_log_step, s5_b_re, s5_b_im, s5_c_re, s5_c_im,
              s5_d, s5_glu_w, s5_glu_b, rwkv_mu, rwkv_w0, rwkv_w_up, rwkv_a0, rwkv_a_up,
              rwkv_g_up, rwkv_k_k, rwkv_k_a, rwkv_r_k, rwkv_ln_w, rwkv_ln_b,
              odd_w_in, gdn_conv_w, gdn_a_log, gdn_dt_bias, gdn_norm_w):
    mod = jnp.einsum('bd,lde->lbe', jax.nn.silu(c), ada_w) + ada_b[:, None, :]
    for layer in range(DEPTH):
        shift, scale, gate = jnp.split(mod[layer], 3, axis=-1)
        h = rms_norm(x, norm_w[layer]) * (1.0 + scale[:, None, :]) + shift[:, None, :]
        i = layer // 2
        if layer % 2 == 0:
            proj = h @ even_w_in[i]
            u = proj[..., :S5_WIDTH]
            feats = proj[..., S5_WIDTH:S5_WIDTH + RWKV_SHIFTED]
            z = proj[..., S5_WIDTH + RWKV_SHIFTED:]
            y_a = s5_mixer(u, s5_lambda_re[i], s5_lambda_im[i], s5_log_step[i], s5_b_re[i], s5_b_im[i],
                           s5_c_re[i], s5_c_im[i], s5_d[i], s5_glu_w[i], s5_glu_b[i])
            y_b = rwkv7_mixer(feats, rwkv_mu[i], rwkv_w0[i], rwkv_w_up[i], rwkv_a0[i], rwkv_a_up[i],
                              rwkv_g_up[i], rwkv_k_k[i], rwkv_k_a[i], rwkv_r_k[i], rwkv_ln_w[i], rwkv_ln_b[i])
            y = jnp.concatenate([y_a, y_b], axis=-1)
        else:
            proj = h @ odd_w_in[i]
            qkv = proj[..., :3 * MIX_WIDTH]
            beta_raw = proj[..., 3 * MIX_WIDTH:3 * MIX_WIDTH + GDN_HEADS]
            alpha_raw = proj[..., 3 * MIX_WIDTH + GDN_HEADS:3 * MIX_WIDTH + 2 * GDN_HEADS]
            z = proj[..., 3 * MIX_WIDTH + 2 * GDN_HEADS:]
            y = gated_deltanet_mixer(qkv, beta_raw, alpha_raw, gdn_conv_w[i], gdn_a_log[i],
                                     gdn_dt_bias[i], gdn_norm_w[i])
        y = (y * jax.nn.silu(z.astype(jnp.float32))).astype(x.dtype)
        x = x + gate[:, None, :] * (y @ w_out[layer])
    return rms_norm(x, final_norm_w)
```

```python
import os
import numpy as np
from contextlib import ExitStack
import concourse.bass as bass
import concourse.mybir as mybir
from concourse.bass_utils import run_bass_kernel_spmd

F32 = mybir.dt.float32
BF16 = mybir.dt.bfloat16
AF = mybir.ActivationFunctionType
ALU = mybir.AluOpType
AX = mybir.AxisListType

D = 1024
L = 4096
TB = 512
NB = L // TB
C = 64
NCH = TB // C
EPS = 1e-6


class KB:
    def __init__(self):
        self.nc = bass.Bass("TRN2", target_bir_lowering=False)
        self.es = ExitStack()
        nc = self.nc
        self.eng = {'pe': nc.tensor, 'dve': nc.vector, 'act': nc.scalar, 'pool': nc.gpsimd, 'sp': nc.sync}
        self.sem = {e: self.es.enter_context(nc.semaphore("sem_" + e)) for e in self.eng}
        self.cnt = {e: 0 for e in self.eng}
        self.clock = {e: {} for e in self.eng}
        self.lastw = {}
        self.readers = {}
        self.dsem = {}
        self.nins = 0

    def sb(self, name, shape, dt, stack=None):
        self.minrem = min(getattr(self, 'minrem', 1 << 30), self.nc.sbuf_bytes_remaining)
        if stack is not None:
            self.uid = getattr(self, 'uid', 0) + 1
            name = f"{name}_u{self.uid}"
        return (stack or self.es).enter_context(self.nc.sbuf_tensor(name, shape, dt))

    def ps(self, name, shape, dt, stack=None):
        return (stack or self.es).enter_context(self.nc.psum_tensor(name, shape, dt))

    def _sync(self, e, reads, writes):
        need = {}

        def add(ev):
            if ev is None:
                return
            k, h, v = ev
            if k not in need or need[k][1] < v:
                need[k] = (h, v)

        for r in reads:
            add(self.lastw.get(r))
        for w in writes:
            add(self.lastw.get(w))
            for ev in self.readers.get(w, {}).values():
                add(ev)
        ck = self.clock[e]
        for k, (h, v) in need.items():
            if e == 'pe' and k == 'sem_pe':
                continue
            if ck.get(k, 0) >= v:
                continue
            self.eng[e].wait_ge(h, v)
            ck[k] = v

    def _mark(self, ev, reads, writes):
        for r in reads:
            self.readers.setdefault(r, {})[ev[0]] = ev
        for w in writes:
            self.lastw[w] = ev
            self.readers[w] = {}

    def op(self, e, fn, r=(), w=()):
        reads, writes = r, w
        self._sync(e, reads, writes)
        ins = fn(self.eng[e])
        self.cnt[e] += 1
        self.nins += 1
        ins.then_inc(self.sem[e], 1)
        self._mark(('sem_' + e, self.sem[e], self.cnt[e]), reads, writes)
        return ins

    def dma(self, q, out, in_, slot, r=(), w=()):
        reads, writes = r, w
        self._sync(q, reads, writes)
        if slot not in self.dsem:
            self.dsem[slot] = [self.es.enter_context(self.nc.semaphore("d_" + slot)), 0]
        s = self.dsem[slot]
        s[1] += 16
        self.eng[q].dma_start(out=out, in_=in_).then_inc(s[0], 16)
        self.nins += 1
        self._mark(('d_' + slot, s[0], s[1]), reads, writes)

    def barrier(self):
        for e in self.eng:
            for e2 in self.eng:
                if e2 != e and self.cnt[e2] > self.clock[e].get('sem_' + e2, 0):
                    self.eng[e].wait_ge(self.sem[e2], self.cnt[e2])
                    self.clock[e]['sem_' + e2] = self.cnt[e2]
            for k, (h, v) in self.dsem.items():
                if v > self.clock[e].get('d_' + k, 0):
                    self.eng[e].wait_ge(h, v)
                    self.clock[e]['d_' + k] = v
        self.lastw = {}
        self.readers = {}

    def mm(self, out, lhsT, rhs, start=True, stop=True, r=(), w=()):
        return self.op('pe', lambda E: E.matmul(out, lhsT=lhsT, rhs=rhs, start=start, stop=stop), r, w)

    def tr(self, out, in_, ident, r=(), w=()):
        return self.op('pe', lambda E: E.transpose(out, in_, ident), r, w)

    def act(self, out, in_, func, scale=None, bias=None, r=(), w=(), e='act'):
        kw = {}
        if scale is not None:
            kw['scale'] = scale
        if bias is not None:
            kw['bias'] = bias
        return self.op('act', lambda E: E.activation(out=out, in_=in_, func=func, **kw), r, w)

    def tt(self, e, out, in0, in1, op, r=(), w=()):
        return self.op(e, lambda E: E.tensor_tensor(out=out, in0=in0, in1=in1, op=op), r, w)

    def ts(self, e, out, in0, s1, op0, s2=None, op1=None, r=(), w=()):
        if op1 is None:
            return self.op(e, lambda E: E.tensor_scalar(out=out, in0=in0, scalar1=s1, scalar2=None, op0=op0), r, w)
        return self.op(e, lambda E: E.tensor_scalar(out=out, in0=in0, scalar1=s1, scalar2=s2, op0=op0, op1=op1), r, w)

    def stt(self, out, in0, scalar, in1, op0, op1, r=(), w=()):
        return self.op('dve', lambda E: E.scalar_tensor_tensor(out=out, in0=in0, scalar=scalar, in1=in1, op0=op0, op1=op1), r, w)

    def cp(self, e, out, in_, r=(), w=()):
        if e == 'act':
            return self.op('act', lambda E: E.copy(out=out, in_=in_), r, w)
        return self.op(e, lambda E: E.tensor_copy(out=out, in_=in_), r, w)


def bc(ap, shape):
    return ap.to_broadcast(shape)


CB = {'ident': (0, 128), 'U': (128, 64), 'Lst': (192, 64), 'MsN': (256, 64), 'Mi': (320, 64), 'I64': (384, 64),
      'Ms': (448, 64), 'MiN': (512, 64), 'bones': (576, 128), 'cmask': (704, 512), 'pmask': (1216, 2)}
NCONST = 1219


def make_consts():
    i = np.arange(64)
    blob = np.zeros((128, NCONST), np.float32)
    blob[:, 0:128] = np.eye(128, dtype=np.float32)
    m = {}
    m['U'] = (i[:, None] <= i[None, :]).astype(np.float32)
    m['Lst'] = (i[:, None] > i[None, :]).astype(np.float32)
    m['MsN'] = -(i[:, None] < i[None, :]).astype(np.float32)
    m['Mi'] = (i[:, None] <= i[None, :]).astype(np.float32)
    m['I64'] = np.eye(64, dtype=np.float32)
    m['Ms'] = -m['MsN']
    m['MiN'] = -m['Mi']
    for k, v in m.items():
        blob[0:64, CB[k][0]:CB[k][0] + 64] = v
    bo = np.zeros((128, 128), np.float32)
    bo[0:64, 0:64] = 1.0
    bo[64:128, 64:128] = 1.0
    blob[:, 576:704] = bo
    cm = np.ones(512, np.float32)
    cm[0::64] = 0.0
    blob[:, 704:1216] = cm[None, :]
    p = np.arange(128)
    blob[:, 1216] = ((p % 32) // 16 == 0)
    blob[:, 1217] = ((p % 32) // 16 == 1)
    blob[:, 1218] = (p >= 96)
    return blob


def build(layers, final, nblocks=NB, dbg=None):
    kb = KB()
    nc = kb.nc
    n_odd = 2
    dr = {}

    def din(name, shape, dt=F32):
        dr[name] = nc.dram_tensor(name, list(shape), dt, kind="ExternalInput").ap()
        return dr[name]

    xT = din("xT", [D, L])
    cT = din("cT", [128, 8])
    normw = din("normw", [128, 32])
    adaw = din("adaw", [4, D, 3 * D])
    adab = din("adab", [128, 96])
    woutr = din("woutr", [4, 8, 128, 8, 128])
    fnw_d = din("fnw", [128, 8])
    consts_d = din("consts", [128, NCONST])
    winodd = din("winodd", [2, 32, 128, 8, 128])
    wba_d = din("wba", [2, 128, 8, 16])
    convw_d = din("convw", [2, 128, 24, 4])
    alog_d = din("alog", [2, 128, 8])
    dtb_d = din("dtb", [2, 128, 8])
    gnw_d = din("gnw", [2, 128, 128])
    winev = din("winev", [2, 26, 128, 8, 128])
    rp_d = din("rp", [2, 128, 42])
    wup_d = din("wup", [2, 64, 512])
    aup_d = din("aup", [2, 64, 512])
    gup_d = din("gup", [2, 128, 512])
    lnw_d = din("lnw", [2, 64, 512])
    lnb_d = din("lnb", [2, 64, 512])
    gluw_d = din("gluw", [2, 128, 4, 512])
    s5a_d = din("s5a", [2, 128, 96 + 1024])
    s5b_d = din("s5b", [2, 128, 1028])
    outT = nc.dram_tensor("outT", [D, L], F32, kind="ExternalOutput").ap()
    odd_ids = sorted({l // 2 for l in layers if l % 2 == 1})
    even_ids = sorted({l // 2 for l in layers if l % 2 == 0})

    cst = kb.sb("cst", [128, NCONST], F32)
    identb = kb.sb("identb", [128, 128], BF16)
    onesb = kb.sb("onesb", [128, 128], BF16)
    onesf = kb.sb("onesf", [64, 128], F32)
    xres = kb.sb("xres", [128, 8, TB], F32)
    hT = kb.sb("hT", [128, 8, TB], BF16)
    rstd = kb.sb("rstd", [128, TB], F32)
    sqb = [kb.sb(f"sqb{i}", [128, TB], BF16) for i in range(2)]
    tmpf = [kb.sb(f"tmpf{i}", [128, TB], F32) for i in range(2)]
    wbuf = [kb.sb(f"wbuf{i}", [128, 8, 128], BF16) for i in range(4)]
    ygT = kb.sb("ygT", [128, 8, TB], BF16)
    zs = kb.sb("zs", [128, 8, TB], BF16)
    modT = kb.sb("modT", [128, 4, 24], F32)
    s1 = kb.sb("s1", [128, 4, 8], F32)
    normw_s = kb.sb("normw_s", [128, 32], F32)
    adab_s = kb.sb("adab_s", [128, 96], F32)
    fnw_s = kb.sb("fnw_s", [128, 8], F32)
    sc = kb.sb("sc", [128, 8], F32)
    Sst = [kb.sb(f"Sst{i}", [128, 8, 128], F32) if i in odd_ids else None for i in range(n_odd)]
    Sb = [kb.sb(f"Sb{i}", [128, 8, 128], BF16) if i in odd_ids else None for i in range(n_odd)]
    carry = [kb.sb(f"carry{i}", [128, 24, 3], F32) if i in odd_ids else None for i in range(n_odd)]
    wba_s = [kb.sb(f"wba_s{i}", [128, 8, 16], BF16) if i in odd_ids else None for i in range(n_odd)]
    convw_s = [kb.sb(f"convw_s{i}", [128, 24, 4], F32) if i in odd_ids else None for i in range(n_odd)]
    negA = [kb.sb(f"negA{i}", [128, 8], F32) if i in odd_ids else None for i in range(n_odd)]
    dtb_s = [kb.sb(f"dtb_s{i}", [128, 8], F32) if i in odd_ids else None for i in range(n_odd)]
    gnw_s = [kb.sb(f"gnw_s{i}", [128, 128], F32) if i in odd_ids else None for i in range(n_odd)]

    def per_even(name, shape, dt):
        return [kb.sb(f"{name}{i}", shape, dt) if i in even_ids else None for i in range(2)]
    Hst = per_even("Hst", [64, 8, 64], F32)
    Hb = per_even("Hb", [64, 8, 64], BF16)
    tcarry = per_even("tcarry", [128, 14], F32)
    rp = per_even("rp_s", [128, 42], F32)
    omm = per_even("omm", [128, 14], F32)
    wup_b = per_even("wup_b", [64, 512], BF16)
    aup_b = per_even("aup_b", [128, 512], BF16)
    gup_b = per_even("gup_b", [128, 512], BF16)
    lnw_s = per_even("lnw_s", [64, 512], F32)
    lnb_s = per_even("lnb_s", [64, 512], F32)
    gluw_b = per_even("gluw_b", [128, 4, 512], BF16)
    bonesb = kb.sb("bonesb", [128, 128], BF16)
    def scratch(name, shape, dt):
        return [nc.dram_tensor(f"{name}{i}", list(shape), dt, kind="Internal").ap() if i in even_ids else None
                for i in range(2)]
    BJ_d = scratch("BJ_d", [128, 4, 2, 8, 128], BF16)
    CJlo_d = scratch("CJlo_d", [128, 4, 4, 9, 32], BF16)
    CJhi_d = scratch("CJhi_d", [128, 4, 4, 9, 64], BF16)
    TT_d = [scratch(f"TT{k}_d", [128, 32, 64], F32) for k in range(4)]
    rho8 = per_even("rho8", [128, 32], F32)
    s5carry = per_even("s5carry", [128, 32], F32)

    psI = [kb.ps(f"psI{i}", [128, 512], F32) for i in range(2)]
    psA = kb.ps("psA", [128, 1024], F32)
    psB = kb.ps("psB", [128, 1024], F32)
    psC = kb.ps("psC", [128, 512], F32)
    psT = kb.ps("psT", [128, 1024], BF16)

    def cv(k, rows=64):
        return cst[0:rows, CB[k][0]:CB[k][0] + CB[k][1]]
    U, Lst, MsN, Mi, I64, Ms, MiN = (cv(k) for k in ['U', 'Lst', 'MsN', 'Mi', 'I64', 'Ms', 'MiN'])
    cmask = cv('cmask', 128)

    kb.dma('sp', cst[:], consts_d[:, :], 'par', w=['cst'])
    kb.dma('sp', normw_s[:], normw[:, :], 'par', w=['normw_s'])
    kb.dma('sp', adab_s[:], adab[:, :], 'par', w=['adab_s'])
    kb.dma('sp', fnw_s[:], fnw_d[:, :], 'par', w=['fnw_s'])
    kb.dma('sp', sc[:], cT[:, :], 'par', w=['sc'])
    for i in odd_ids:
        kb.dma('pool', wba_s[i][:], wba_d[i], 'parp', w=[f'wba{i}'])
        kb.dma('sp', convw_s[i][:], convw_d[i], 'par', w=[f'convw{i}'])
        kb.dma('sp', negA[i][:], alog_d[i], 'par', w=[f'negA{i}'])
        kb.dma('sp', dtb_s[i][:], dtb_d[i], 'par', w=[f'dtb{i}'])
        kb.dma('sp', gnw_s[i][:], gnw_d[i], 'par', w=[f'gnw{i}'])
    for i in even_ids:
        kb.dma('sp', rp[i][:], rp_d[i], 'par', w=[f'rp{i}'])
        kb.dma('sp', lnw_s[i][:], lnw_d[i], 'par', w=[f'lnw{i}'])
        kb.dma('sp', lnb_s[i][:], lnb_d[i], 'par', w=[f'lnb{i}'])
        kb.dma('pool', wup_b[i][:], wup_d[i], 'parp', w=[f'wup{i}'])
        kb.dma('pool', aup_b[i][64:128, :], aup_d[i], 'parp', w=[f'aup{i}'])
        kb.dma('pool', gup_b[i][:], gup_d[i], 'parp', w=[f'gup{i}'])
        kb.dma('pool', gluw_b[i][:], gluw_d[i], 'parp', w=[f'gluw{i}'])
    kb.barrier()
    kb.cp('dve', identb[:], cst[:, 0:128], r=['cst'], w=['identb'])
    kb.cp('dve', bonesb[:], cst[:, 576:704], r=['cst'], w=['bonesb'])
    for i in even_ids:
        kb.ts('dve', omm[i][:], rp[i][:, 0:14], -1.0, ALU.mult, 1.0, ALU.add, r=[f'rp{i}'], w=[f'omm{i}'])
        kb.op('pool', lambda E, i=i: E.memset(Hst[i][:], 0.0), w=[f'H{i}'])
        kb.op('pool', lambda E, i=i: E.memset(Hb[i][:], 0.0), w=[f'Hb{i}'])
        kb.op('pool', lambda E, i=i: E.memset(tcarry[i][:], 0.0), w=[f'tcarry{i}'])
    kb.op('dve', lambda E: E.memset(onesb[:], 1.0), w=['onesb'])
    kb.op('dve', lambda E: E.memset(onesf[:], 1.0), w=['onesf'])
    for i in odd_ids:
        kb.act(negA[i][:], negA[i][:], AF.Exp, r=[f'negA{i}'], w=[f'negA{i}'])
        kb.ts('dve', negA[i][:], negA[i][:], -1.0, ALU.mult, r=[f'negA{i}'], w=[f'negA{i}'])
        kb.op('pool', lambda E, i=i: E.memset(Sst[i][:], 0.0), w=[f'S{i}'])
        kb.op('pool', lambda E, i=i: E.memset(Sb[i][:], 0.0), w=[f'Sb{i}'])
        kb.op('pool', lambda E, i=i: E.memset(carry[i][:], 0.0), w=[f'carry{i}'])

    def s5_tables(i):
        with ExitStack() as st:
            K_ = ['s5tab']
            sa = kb.sb("s5a_s", [128, 96 + 1024], F32, st)
            kb.dma('sp', sa[:], s5a_d[i], 's5a', w=K_)
            cnt = [0]
            CJlo = {i: kb.sb("CJlo_t", [128, 4, 4, 9, 32], BF16, st)}
            CJhi = {i: kb.sb("CJhi_t", [128, 4, 4, 9, 64], BF16, st)}
            T1 = {i: kb.sb("T1_t", [128, 32, 64], F32, st)}
            T2 = {i: kb.sb("T2_t", [128, 32, 64], F32, st)}
            T3 = {i: kb.sb("T3_t", [128, 32, 64], F32, st)}
            T4 = {i: kb.sb("T4_t", [128, 32, 64], F32, st)}

            cur = [st]

            def T(shape):
                cnt[0] += 1
                return kb.sb(f"s5t{cnt[0]}", shape, F32, cur[0])

            def tt(o, a, b, op, e='dve'):
                kb.tt(e, o, a, b, op, r=K_, w=K_)

            def ts(o, a, s1, op0, s2=None, op1=None):
                kb.ts('dve', o, a, s1, op0, s2, op1, r=K_, w=K_)

            def cmul(orr, oi, ar, ai, br, bi, t1, t2):
                tt(t1, ar, br, ALU.mult)
                tt(t2, ai, bi, ALU.mult)
                tt(orr, t1, t2, ALU.subtract)
                tt(t1, ar, bi, ALU.mult)
                tt(t2, ai, br, ALU.mult)
                tt(oi, t1, t2, ALU.add)

            def lam_bar(lre_in, lim_in, lst_in, shape):
                lre = T(shape); stp = T(shape); ar = T(shape); ai = T(shape)
                cr = T(shape); si = T(shape); t1 = T(shape); t2 = T(shape); rho = T(shape)
                ts(lre[:], lre_in, -1e-4, ALU.min)
                kb.act(stp[:], lst_in, AF.Exp, r=K_, w=K_)
                tt(ar[:], lre[:], stp[:], ALU.mult)
                tt(ai[:], lim_in, stp[:], ALU.mult)
                kb.act(rho[:], ar[:], AF.Exp, r=K_, w=K_)
                kb.act(si[:], ai[:], AF.Sin, scale=1.0 / 16, r=K_, w=K_)
                ts(t1[:], ai[:], 1.0 / 16, ALU.mult, float(np.pi / 2), ALU.add)
                kb.act(cr[:], t1[:], AF.Sin, r=K_, w=K_)
                for _ in range(4):
                    tt(t1[:], cr[:], cr[:], ALU.mult)
                    tt(t2[:], si[:], si[:], ALU.mult)
                    tt(ar[:], cr[:], si[:], ALU.mult)
                    tt(cr[:], t1[:], t2[:], ALU.subtract)
                    ts(si[:], ar[:], 2.0, ALU.mult)
                lr = T(shape); li = T(shape)
                tt(lr[:], cr[:], rho[:], ALU.mult)
                tt(li[:], si[:], rho[:], ALU.mult)
                return lr, li, rho, cr, si, lre

            sh = [128, 32]
            lr, li, rho, ur, ui, _ = lam_bar(sa[:, 0:32], sa[:, 32:64], sa[:, 64:96], sh)
            t1 = T(sh); t2 = T(sh)
            tt(t1[:], rho[:], rho[:], ALU.mult)
            tt(t2[:], t1[:], t1[:], ALU.mult)
            tt(rho8[i][:], t2[:], t2[:], ALU.mult)
            Lr = T([128, 32, 9]); Li = T([128, 32, 9])
            kb.op('dve', lambda E: E.memset(Lr[:, :, 0], 1.0), r=K_, w=K_)
            kb.op('dve', lambda E: E.memset(Li[:, :, 0], 0.0), r=K_, w=K_)
            for j in range(8):
                cmul(Lr[:, :, j + 1], Li[:, :, j + 1], Lr[:, :, j], Li[:, :, j], lr[:], li[:], t1[:], t2[:])
            kb.op('pool', lambda E: E.memset(CJlo[i][:], 0.0), r=K_, w=K_)
            kb.op('pool', lambda E: E.memset(CJhi[i][:], 0.0), r=K_, w=K_)
            Cr = sa[:, 96:96 + 512].rearrange("p (g c) -> p g c", c=16)
            Ci = sa[:, 96 + 512:96 + 1024].rearrange("p (g c) -> p g c", c=16)
            d1 = T([128, 32, 16]); d2 = T([128, 32, 16]); dr = T([128, 32, 16]); di = T([128, 32, 16])
            for j in range(9):
                lrj = bc(Lr[:, :, j:j + 1], [128, 32, 16])
                lij = bc(Li[:, :, j:j + 1], [128, 32, 16])
                tt(d1[:], Cr, lrj, ALU.mult)
                tt(d2[:], Ci, lij, ALU.mult)
                tt(dr[:], d1[:], d2[:], ALU.subtract)
                tt(d1[:], Cr, lij, ALU.mult)
                tt(d2[:], Ci, lrj, ALU.mult)
                tt(di[:], d1[:], d2[:], ALU.add)
                ts(di[:], di[:], -1.0, ALU.mult)
                drv = dr[:].rearrange("p (q gl) c -> p q gl c", gl=8)
                div = di[:].rearrange("p (q gl) c -> p q gl c", gl=8)
                for gl in range(8):
                    if gl < 4:
                        dst = lambda ps_: CJlo[i][ps_, :, gl, j, (gl % 2) * 16:(gl % 2) * 16 + 16]
                    else:
                        dst = lambda ps_: CJhi[i][ps_, :, gl - 4, j, (gl - 4) * 16:(gl - 4) * 16 + 16]
                    kb.cp('dve', dst(slice(0, 64)), drv[0:64, :, gl, :], r=K_, w=K_)
                    kb.cp('dve', dst(slice(64, 128)), div[64:128, :, gl, :], r=K_, w=K_)
            er = T(sh); ei = T(sh)
            kb.cp('dve', er[:], ur[:], r=K_, w=K_)
            kb.cp('dve', ei[:], ui[:], r=K_, w=K_)
            for _ in range(3):
                tt(t1[:], er[:], er[:], ALU.mult)
                tt(t2[:], ei[:], ei[:], ALU.mult)
                tt(ei[:], er[:], ei[:], ALU.mult)
                tt(er[:], t1[:], t2[:], ALU.subtract)
                ts(ei[:], ei[:], 2.0, ALU.mult)
            Pr = T3[i]
            Pi = T([128, 32, 64])
            kb.cp('dve', Pr[:, :, 0], er[:], r=K_, w=K_)
            kb.cp('dve', Pi[:, :, 0], ei[:], r=K_, w=K_)
            p1 = T([128, 32, 32]); p2 = T([128, 32, 32])
            m = 1
            while m < 64:
                br_ = bc(Pr[:, :, m - 1:m], [128, 32, m])
                bi_ = bc(Pi[:, :, m - 1:m], [128, 32, m])
                cmul(Pr[:, :, m:2 * m], Pi[:, :, m:2 * m], Pr[:, :, 0:m], Pi[:, :, 0:m], br_, bi_,
                     p1[:, :, 0:m], p2[:, :, 0:m])
                m *= 2
            l7r = bc(Lr[:, :, 7:8], [128, 32, 64])
            l7i = bc(Li[:, :, 7:8], [128, 32, 64])
            w1 = T([128, 32, 64]); w2 = T([128, 32, 64])
            tt(w1[:], Pr[:], l7r, ALU.mult)
            tt(w2[:], Pi[:], l7i, ALU.mult)
            tt(T1[i][:], w1[:], w2[:], ALU.add)
            tt(w1[:], Pr[:], l7i, ALU.mult)
            tt(w2[:], Pi[:], l7r, ALU.mult)
            tt(w1[:], w1[:], w2[:], ALU.subtract)
            kb.cp('dve', T2[i][0:64], w1[0:64], r=K_, w=K_)
            ts(T2[i][64:128], w1[64:128], -1.0, ALU.mult)
            kb.cp('dve', T4[i][0:64], Pi[0:64], r=K_, w=K_)
            ts(T4[i][64:128], Pi[64:128], -1.0, ALU.mult)
            kb.dma('sp', CJlo_d[i], CJlo[i][:], 'tabst', r=K_)
            kb.dma('sp', CJhi_d[i], CJhi[i][:], 'tabst', r=K_)
            for k_, t_ in enumerate((T1, T2, T3, T4)):
                kb.dma('sp', TT_d[k_][i], t_[i][:], 'tabst', r=K_)
            kb.barrier()
        with ExitStack() as st:
            cnt[0] += 1000
            cur[0] = st
            BJtab = {i: kb.sb("BJ_t", [128, 4, 2, 8, 128], BF16, st)}
            sbb = kb.sb("s5b_s", [128, 1028], F32, st)
            kb.dma('sp', sbb[:], s5b_d[i], 's5b', w=K_)
            shb = [128, 4, 64]
            v = lambda a: sbb[:, a * 256:(a + 1) * 256].rearrange("p (q n) -> p q n", n=64)
            lstb = bc(sbb[:, 1024:1028].unsqueeze(2), shb)
            blr, bli, brho, _, _, blre = lam_bar(v(0), v(1), lstb, shb)
            bt1 = T(shb); bt2 = T(shb); den = T(shb); kr = T(shb); ki = T(shb); ir = T(shb); ii = T(shb)
            nr = T(shb)
            ts(nr[:], blr[:], -1.0, ALU.add)
            tt(bt1[:], blre[:], blre[:], ALU.mult)
            tt(bt2[:], v(1), v(1), ALU.mult)
            tt(den[:], bt1[:], bt2[:], ALU.add)
            kb.op('dve', lambda E: E.reciprocal(out=den[:], in_=den[:]), r=K_, w=K_)
            tt(bt1[:], nr[:], blre[:], ALU.mult)
            tt(bt2[:], bli[:], v(1), ALU.mult)
            tt(kr[:], bt1[:], bt2[:], ALU.add)
            tt(kr[:], kr[:], den[:], ALU.mult)
            tt(bt1[:], bli[:], blre[:], ALU.mult)
            tt(bt2[:], nr[:], v(1), ALU.mult)
            tt(ki[:], bt1[:], bt2[:], ALU.subtract)
            tt(ki[:], ki[:], den[:], ALU.mult)
            tt(bt1[:], brho[:], brho[:], ALU.mult)
            kb.op('dve', lambda E: E.reciprocal(out=bt1[:], in_=bt1[:]), r=K_, w=K_)
            tt(ir[:], blr[:], bt1[:], ALU.mult)
            tt(ii[:], bli[:], bt1[:], ALU.mult)
            ts(ii[:], ii[:], -1.0, ALU.mult)
            gr = T(shb); gi = T(shb); g2r = T(shb); g2i = T(shb); vr = T(shb); vi = T(shb)
            kb.cp('dve', gr[:], kr[:], r=K_, w=K_)
            kb.cp('dve', gi[:], ki[:], r=K_, w=K_)
            pm = cst[:, 1216:1218]
            for j in range(8):
                cmul(vr[:], vi[:], gr[:], gi[:], v(2), v(3), bt1[:], bt2[:])
                for e in range(2):
                    ts(BJtab[i][:, :, e, j, 0:64], vr[:], pm[:, e:e + 1], ALU.mult)
                    ts(BJtab[i][:, :, e, j, 64:128], vi[:], pm[:, e:e + 1], ALU.mult)
                if j < 7:
                    cmul(g2r[:], g2i[:], gr[:], gi[:], ir[:], ii[:], bt1[:], bt2[:])
                    gr, g2r = g2r, gr
                    gi, g2i = g2i, gi
            kb.op('pool', lambda E: E.memset(s5carry[i][:], 0.0), r=K_, w=K_)
            kb.dma('sp', BJ_d[i], BJtab[i][:], 'tabst', r=K_)
            kb.barrier()

    for i in even_ids:
        s5_tables(i)

    kb.act(sc[:], sc[:], AF.Silu, r=['sc'], w=['sc'])
    with ExitStack() as st:
        abuf = [kb.sb(f"abuf{i}", [128, 3 * D], F32, st) for i in range(2)]
        macc = kb.sb("macc", [128, 24], F32, st)
        n = 0
        for l in layers:
            for k in range(8):
                b = abuf[n % 2]
                kb.dma('sp', b[:], adaw[l, k * 128:(k + 1) * 128, :], f'ab{n % 2}', w=[f'abuf{n % 2}'])
                for j in range(24):
                    kb.mm(psC[:, j:j + 1], b[:, j * 128:(j + 1) * 128], sc[:, k:k + 1],
                          r=[f'abuf{n % 2}', 'sc'], w=['psC'])
                if k == 0:
                    kb.tt('dve', macc[:], psC[:, 0:24], adab_s[:, l * 24:(l + 1) * 24], ALU.add,
                          r=['psC', 'adab_s'], w=['macc'])
                else:
                    kb.tt('dve', macc[:], psC[:, 0:24], macc[:], ALU.add, r=['psC', 'macc'], w=['macc'])
                n += 1
            kb.cp('dve', modT[:, l, :], macc[:], r=['macc'], w=['modT'])
            kb.ts('dve', s1[:, l, :], modT[:, l, 8:16], 1.0, ALU.add, r=['modT'], w=['s1'])
            kb.tt('dve', s1[:, l, :], s1[:, l, :], normw_s[:, l * 8:(l + 1) * 8], ALU.mult,
                  r=['s1', 'normw_s'], w=['s1'])
        kb.barrier()

    wctr = [0]

    def load_w(src_ap):
        i = wctr[0] % 4
        wctr[0] += 1
        kb.dma('pool', wbuf[i][:], src_ap, f'w{i}', w=[f'wbuf{i}'])
        return wbuf[i], f'wbuf{i}'

    ictr = [0]

    def proj_chunk(src_ap, rhs_tile, rhs_key):
        wb, wk = load_w(src_ap)
        i = ictr[0] % 2
        ictr[0] += 1
        for k in range(8):
            kb.mm(psI[i][:], wb[:, k, :], rhs_tile[:, k, :], start=(k == 0), stop=(k == 7),
                  r=[wk, rhs_key], w=[f'psI{i}'])
        return psI[i], f'psI{i}'

    def rms_stats(src, src_key):
        for k in range(8):
            q = sqb[k % 2]
            kb.act(q[:], src[:, k, :], AF.Square, r=[src_key], w=[f'sqb{k % 2}'])
            kb.mm(psC[:, :], onesb[:], q[:], start=(k == 0), stop=(k == 7), r=[f'sqb{k % 2}', 'onesb'], w=['psC'])
        kb.act(rstd[:], psC[:], AF.Sqrt, scale=1.0 / D, bias=EPS, r=['psC'], w=['rstd'])
        kb.op('dve', lambda E: E.reciprocal(out=rstd[:], in_=rstd[:]), r=['rstd'], w=['rstd'])

    def pre_norm(l):
        rms_stats(xres, 'xres')
        for k in range(8):
            t = tmpf[k % 2]
            kb.tt('dve', t[:], xres[:, k, :], rstd[:], ALU.mult, r=['xres', 'rstd'], w=[f'tmpf{k % 2}'])
            kb.act(hT[:, k, :], t[:], AF.Identity, scale=s1[:, l, k:k + 1], bias=modT[:, l, k:k + 1],
                   r=[f'tmpf{k % 2}', 's1', 'modT'], w=['hT'])

    def out_proj(l):
        for dm in range(8):
            ps, pk = proj_chunk(woutr[l, dm], ygT, 'ygT')
            kb.stt(xres[:, dm, :], ps[:], modT[:, l, 16 + dm:17 + dm], xres[:, dm, :], ALU.mult, ALU.add,
                   r=[pk, 'modT', 'xres'], w=['xres'])

    def neumann(Pf, Pb, PTb, Accf, Accb, PTf, Pw):
        f2 = lambda t: t[:].rearrange("p h c -> p (h c)")
        kb.tt('pool', Accf[:], Pf[:], bc(I64.unsqueeze(1), [64, 8, 64]), ALU.add, r=['Pf', 'cst'], w=['Accf'])
        for h in range(8):
            kb.tr(psC[0:64, h * 64:(h + 1) * 64], Pf[:, h, :], cst[0:64, 0:64], r=['Pf', 'cst'], w=['psC'])
        kb.cp('act', f2(PTf), psC[0:64, 0:512], r=['psC'], w=['PTf'])
        cur, curk = Pf, 'Pf'
        oth, othk = Pw, 'Pw'
        for lev in range(5):
            last = (lev == 4)
            if not last:
                for h in range(8):
                    kb.mm(psB[0:64, h * 64:(h + 1) * 64], PTf[:, h, :], cur[:, h, :], r=['PTf', curk], w=['psB0'])
            for h in range(8):
                kb.mm(psB[0:64, 512 + h * 64:512 + (h + 1) * 64], cur[:, h, :], PTf[:, h, :], r=['PTf', curk], w=['psB1'])
            if not last:
                kb.cp('act', f2(oth), psB[0:64, 0:512], r=['psB0'], w=[othk])
            kb.cp('dve', f2(PTf), psB[0:64, 512:1024], r=['psB1'], w=['PTf'])
            cur, curk, oth, othk = oth, othk, cur, curk
            for h in range(8):
                kb.mm(psC[0:64, h * 64:(h + 1) * 64], PTf[:, h, :], Accf[:, h, :], r=['PTf', 'Accf'], w=['psC'])
            kb.tt('dve', f2(Accf), psC[0:64, :], f2(Accf), ALU.add, r=['psC', 'Accf'], w=['Accf'])
        kb.cp('act', Accb[:], Accf[:], r=['Accf'], w=['Accb'])

    def odd_layer(l, bi):
        i = l // 2
        with ExitStack() as st:
            acc = kb.sb("acc", [128, 24, TB], F32, st)
            qkn = kb.sb("qkn", [128, 16, TB], BF16, st)
            vb = kb.sb("vb", [128, 8, TB], BF16, st)
            ba = kb.sb("ba", [64, 8, 16], F32, st)
            beta = kb.sb("beta", [64, 8, 8], F32, st)
            aall = kb.sb("aall", [64, 8, 8], F32, st)
            eg = kb.sb("eg", [64, 8, 8], F32, st)
            eend = kb.sb("eend", [64, 8, 8], F32, st)
            cdb = kb.sb("cdb", [128, 8, 8], F32, st)
            R2 = kb.sb("R2", [64, 8, 64], F32, st)
            decT = kb.sb("decT", [64, 8, 64], F32, st)
            dSb = kb.sb("dSb", [64, 8, 64], F32, st)
            dI = kb.sb("dI", [64, 8, 64], F32, st)
            Pf = kb.sb("Pf", [64, 8, 64], F32, st)
            Pb = kb.sb("Pb", [64, 8, 64], BF16, st)
            PTb = kb.sb("PTb", [64, 8, 64], BF16, st)
            Accf = kb.sb("Accf", [64, 8, 64], F32, st)
            PTf = kb.sb("PTf", [64, 8, 64], F32, st)
            Pw = kb.sb("Pw", [64, 8, 64], F32, st)
            Accb = kb.sb("Accb", [64, 8, 64], BF16, st)
            intraT = kb.sb("intraT", [64, 8, 64], BF16, st)
            vk = kb.sb("vk", [64, 8, 256], BF16, st)
            kend = kb.sb("kend", [64, 8, 128], BF16, st)
            uu = kb.sb("uu", [64, 8, 128], F32, st)
            ww = kb.sb("ww", [64, 8, 128], BF16, st)
            wTb = kb.sb("wTb", [128, 8, 64], BF16, st)
            vnew = kb.sb("vnew", [64, 8, 128], BF16, st)
            o1 = kb.sb("o1", [64, 8, 128], F32, st)
            oo = kb.sb("oo", [64, 8, 128], F32, st)
            osq = kb.sb("osq", [64, 8, 128], F32, st)
            ss = kb.sb("ss", [64, 8], F32, st)
            onb = kb.sb("onb", [64, 8, 128], BF16, st)
            cnew = kb.sb("cnew", [128, 24, 3], F32, st)

            pre_norm(l)
            cw = convw_s[i]
            for m in range(24):
                ps, pk = proj_chunk(winodd[i, m], hT, 'hT')
                am = f'acc{m}'
                kb.act(acc[:, m, :], ps[:], AF.Copy, scale=cw[:, m, 3:4], r=[pk, f'convw{i}'], w=[am])
                kb.cp('act', cnew[:, m, :], ps[:, TB - 3:TB], r=[pk], w=['cnew'])
                for j in range(3):
                    sh = 3 - j
                    kb.stt(acc[:, m, sh:TB], ps[:, 0:TB - sh], cw[:, m, j:j + 1], acc[:, m, sh:TB],
                           ALU.mult, ALU.add, r=[pk, f'convw{i}', am], w=[am])
            accall = [f'acc{m}' for m in range(24)]
            cr = carry[i]
            for j in range(3):
                n = 3 - j
                t3 = tmpf[0]
                tv = t3[:, 0:24 * n].rearrange("p (m n) -> p m n", n=n)
                kb.tt('dve', tv, cr[:, :, j:3], bc(cw[:, :, j:j + 1], [128, 24, n]), ALU.mult,
                      r=[f'carry{i}', f'convw{i}'], w=['tmpf0'])
                kb.tt('dve', acc[:, :, 0:n], acc[:, :, 0:n], tv, ALU.add, r=['tmpf0'] + accall, w=accall)
            kb.cp('dve', cr[:], cnew[:], r=['cnew'], w=[f'carry{i}'])
            kb.act(acc[:], acc[:], AF.Silu, r=accall, w=accall)
            for m in range(8):
                ps, pk = proj_chunk(winodd[i, 24 + m], hT, 'hT')
                kb.act(zs[:, m, :], ps[:], AF.Silu, r=[pk], w=['zs'])
            if dbg:
                kb.op('pool', lambda E: E.memset(zs[:], 1.0), r=['zs'], w=['zs'])
            for m in range(16):
                q = sqb[m % 2]
                kb.act(q[:], acc[:, m, :], AF.Square, r=[f'acc{m}'], w=[f'sqb{m % 2}'])
                kb.mm(psC[:], onesb[:], q[:], r=[f'sqb{m % 2}', 'onesb'], w=['psC'])
                t = tmpf[m % 2]
                kb.act(t[:], psC[:], AF.Sqrt, bias=EPS, r=['psC'], w=[f'tmpf{m % 2}'])
                kb.op('dve', lambda E, t=t: E.reciprocal(out=t[:], in_=t[:]), r=[f'tmpf{m % 2}'], w=[f'tmpf{m % 2}'])
                kb.stt(qkn[:, m, :], acc[:, m, :], (128.0 ** -0.5) if m < 8 else 1.0, t[:], ALU.mult, ALU.mult,
                       r=[f'acc{m}', f'tmpf{m % 2}'], w=['qkn'])
            kb.cp('pool', vb[:], acc[:, 16:24, :], r=accall[16:], w=['vb'])
            for j in range(NCH):
                for k in range(8):
                    kb.mm(psC[0:64, j * 16:(j + 1) * 16], hT[:, k, j * C:(j + 1) * C], wba_s[i][:, k, :],
                          start=(k == 0), stop=(k == 7), r=['hT', f'wba{i}'], w=['psC'])
            kb.cp('dve', ba[:], psC[0:64, 0:128].rearrange("p (j e) -> p j e", e=16), r=['psC'], w=['ba'])
            kb.act(beta[:], ba[:, :, 0:8], AF.Sigmoid, r=['ba'], w=['beta'])
            kb.tt('dve', aall[:], ba[:, :, 8:16], bc(dtb_s[i][0:64, :].unsqueeze(1), [64, 8, 8]), ALU.add,
                  r=['ba', f'dtb{i}'], w=['aall'])
            kb.act(aall[:], aall[:], AF.Exp, r=['aall'], w=['aall'])
            kb.act(aall[:], aall[:], AF.Ln, bias=1.0, r=['aall'], w=['aall'])
            kb.tt('dve', aall[:], aall[:], bc(negA[i][0:64, :].unsqueeze(1), [64, 8, 8]), ALU.mult,
                  r=['aall', f'negA{i}'], w=['aall'])
            a2 = aall[:].rearrange("p j h -> p (j h)")
            kb.mm(psI[0][0:64, 0:64], U, a2, r=['cst', 'aall'], w=['psI0'])
            kb.act(eg[:].rearrange("p j h -> p (j h)"), psI[0][0:64, 0:64], AF.Exp, r=['psI0'], w=['eg'])
            kb.mm(psI[1][0:64, 0:64], Lst, a2, r=['cst', 'aall'], w=['psI1'])
            kb.act(eend[:].rearrange("p j h -> p (j h)"), psI[1][0:64, 0:64], AF.Exp, r=['psI1'], w=['eend'])
            kb.mm(psI[0][:, 64:128], onesf[:], a2, r=['onesf', 'aall'], w=['psI0'])
            kb.act(cdb[:].rearrange("p j h -> p (j h)"), psI[0][:, 64:128], AF.Exp, r=['psI0'], w=['cdb'])

            S = Sst[i]
            Sbf = Sb[i]
            for j in range(NCH):
                cs = slice(j * C, (j + 1) * C)
                be = beta[:, j, :]
                kb.tt('dve', R2[:], bc(U.unsqueeze(1), [64, 8, 64]), bc(aall[:, j, :].unsqueeze(2), [64, 8, 64]), ALU.mult,
                      r=['cst', 'aall'], w=['R2'])
                kb.mm(psC[0:64, :], Lst, R2[:].rearrange("p h c -> p (h c)"), r=['cst', 'R2'], w=['psC'])
                kb.act(decT[:].rearrange("p h c -> p (h c)"), psC[0:64, :], AF.Exp, r=['psC'], w=['decT'])
                kb.tt('pool', dSb[:], decT[:], bc(MsN.unsqueeze(1), [64, 8, 64]), ALU.mult, r=['decT', 'cst'], w=['dSb'])
                kb.tt('pool', dSb[:], dSb[:], bc(be.unsqueeze(2), [64, 8, 64]), ALU.mult, r=['dSb', 'beta'], w=['dSb'])
                kb.tt('pool', dI[:], decT[:], bc(Mi.unsqueeze(1), [64, 8, 64]), ALU.mult, r=['decT', 'cst'], w=['dI'])
                for h in range(8):
                    kb.mm(psA[0:64, h * 64:(h + 1) * 64], qkn[:, 8 + h, cs], qkn[:, 8 + h, cs], r=['qkn'], w=['psA0'])
                for h in range(8):
                    kb.mm(psA[0:64, 512 + h * 64:512 + (h + 1) * 64], qkn[:, 8 + h, cs], qkn[:, h, cs], r=['qkn'], w=['psA1'])
                kb.tt('dve', Pf[:].rearrange("p h c -> p (h c)"), psA[0:64, 0:512], dSb[:].rearrange("p h c -> p (h c)"),
                      ALU.mult, r=['psA0', 'dSb'], w=['Pf'])
                kb.tt('dve', intraT[:].rearrange("p h c -> p (h c)"), psA[0:64, 512:1024],
                      dI[:].rearrange("p h c -> p (h c)"), ALU.mult, r=['psA1', 'dI'], w=['intraT'])
                neumann(Pf, Pb, PTb, Accf, Accb, PTf, Pw)
                for h in range(8):
                    kb.tr(psT[0:64, h * 128:(h + 1) * 128], qkn[:, 8 + h, cs], identb[:], r=['qkn', 'identb'], w=['psT'])
                pT3 = psT[0:64, :].rearrange("p (h d) -> p h d", d=128)
                kb.tt('dve', vk[:, :, 128:256], pT3, bc(eg[:, j, :].unsqueeze(2), [64, 8, 128]), ALU.mult,
                      r=['psT', 'eg'], w=['vk1'])
                kb.tt('dve', kend[:], pT3, bc(eend[:, j, :].unsqueeze(2), [64, 8, 128]), ALU.mult,
                      r=['psT', 'eend'], w=['kend'])
                for h in range(8):
                    kb.tr(psT[0:64, h * 128:(h + 1) * 128], vb[:, h, cs], identb[:], r=['vb', 'identb'], w=['psT'])
                kb.cp('act', vk[:, :, 0:128], pT3, r=['psT'], w=['vk0'])
                for h in range(8):
                    kb.mm(psA[0:64, h * 128:(h + 1) * 128], Accb[:, h, :], vk[:, h, 0:128], r=['Accb', 'vk0'],
                          w=['psA0', 'psA1'])
                for h in range(8):
                    kb.mm(psB[0:64, h * 128:(h + 1) * 128], Accb[:, h, :], vk[:, h, 128:256], r=['Accb', 'vk1'],
                          w=['psB0', 'psB1'])
                pA3 = psA[0:64, :].rearrange("p (h d) -> p h d", d=128)
                pB3 = psB[0:64, :].rearrange("p (h d) -> p h d", d=128)
                bet3 = bc(be.unsqueeze(2), [64, 8, 128])
                kb.tt('dve', uu[:], pA3, bet3, ALU.mult, r=['psA0', 'psA1', 'beta'], w=['uu'])
                kb.tt('dve', ww[:], pB3, bet3, ALU.mult, r=['psB0', 'psB1', 'beta'], w=['ww'])
                for h in range(8):
                    kb.tr(psT[:, h * 64:(h + 1) * 64], ww[:, h, :], identb[0:64, 0:64], r=['ww', 'identb'], w=['psT'])
                kb.cp('act', wTb[:].rearrange("p h c -> p (h c)"), psT[:, 0:512], r=['psT'], w=['wTb'])
                for h in range(8):
                    kb.mm(psA[0:64, h * 128:(h + 1) * 128], wTb[:, h, :], Sbf[:, h, :], r=['wTb', f'Sb{i}'],
                          w=['psA0', 'psA1'])
                kb.tt('dve', vnew[:], uu[:], pA3, ALU.subtract, r=['uu', 'psA0', 'psA1'], w=['vnew'])
                for h in range(8):
                    kb.mm(psB[0:64, h * 128:(h + 1) * 128], qkn[:, h, cs], Sbf[:, h, :], r=['qkn', f'Sb{i}'],
                          w=['psB0', 'psB1'])
                kb.tt('dve', o1[:], pB3, bc(eg[:, j, :].unsqueeze(2), [64, 8, 128]), ALU.mult,
                      r=['psB0', 'psB1', 'eg'], w=['o1'])
                for h in range(8):
                    kb.mm(psA[0:64, h * 128:(h + 1) * 128], intraT[:, h, :], vnew[:, h, :], r=['intraT', 'vnew'],
                          w=['psA0', 'psA1'])
                kb.tt('dve', oo[:], pA3, o1[:], ALU.add, r=['psA0', 'psA1', 'o1'], w=['oo'])
                for h in range(8):
                    kb.mm(psB[:, h * 128:(h + 1) * 128], kend[:, h, :], vnew[:, h, :], r=['kend', 'vnew'],
                          w=['psB0', 'psB1'])
                kb.tt('pool', S[:], S[:], bc(cdb[:, j, :].unsqueeze(2), [128, 8, 128]), ALU.mult, r=[f'S{i}', 'cdb'], w=[f'S{i}'])
                kb.tt('dve', S[:], S[:], psB[:, :].rearrange("p (h d) -> p h d", d=128), ALU.add,
                      r=[f'S{i}', 'psB0', 'psB1'], w=[f'S{i}'])
                kb.cp('act', Sbf[:], S[:], r=[f'S{i}'], w=[f'Sb{i}'])
                kb.tt('pool', osq[:], oo[:], oo[:], ALU.mult, r=['oo'], w=['osq'])
                kb.op('dve', lambda E: E.tensor_reduce(out=ss[:], in_=osq[:], axis=AX.X, op=ALU.add), r=['osq'], w=['ss'])
                kb.act(ss[:], ss[:], AF.Sqrt, scale=1.0 / 128, bias=EPS, r=['ss'], w=['ss'])
                kb.op('dve', lambda E: E.reciprocal(out=ss[:], in_=ss[:]), r=['ss'], w=['ss'])
                kb.tt('pool', osq[:], oo[:], bc(ss[:].unsqueeze(2), [64, 8, 128]), ALU.mult, r=['oo', 'ss'], w=['osq'])
                kb.tt('pool', onb[:], osq[:], bc(gnw_s[i][0:64, :].unsqueeze(1), [64, 8, 128]), ALU.mult,
                      r=['osq', f'gnw{i}'], w=['onb'])
                for h in range(8):
                    kb.tr(psT[:, h * 64:(h + 1) * 64], onb[:, h, :], identb[0:64, 0:64], r=['onb', 'identb'], w=['psT'])
                kb.tt('dve', ygT[:, :, cs], psT[:, 0:512].rearrange("p (h c) -> p h c", c=64), zs[:, :, cs], ALU.mult,
                      r=['psT', 'zs'], w=['ygT'])
            if not dbg:
                out_proj(l)
        kb.barrier()

    RW_DS = float(np.exp(-0.5))
    LN_EPS = 1e-5 * 64

    def rwkv_phase(l, i, bi):
        P = rp[i]
        rk_ = f'rp{i}'
        with ExitStack() as st:
            gbuf = kb.sb("gbuf", [128, TB + 1], F32, st)
            fl = [kb.sb(f"fl{k}", [128, TB], F32, st) for k in range(2)]
            tw = kb.sb("tw", [64, TB], BF16, st)
            xab = kb.sb("xab", [128, TB], BF16, st)
            sg = kb.sb("sg", [128, TB], BF16, st)
            gz = kb.sb("gz", [128, 4, TB], BF16, st)
            bonus = kb.sb("bonus", [128, 4, TB], F32, st)
            vbT = kb.sb("vbT", [128, 4, TB], BF16, st)
            ops6 = {n: kb.sb("op_" + n, [128, 4, TB], BF16, st) for n in ['At', 'Qt', 'Kh', 'Bh', 'Kb', 'Bb']}
            opo = {n: kb.sb("opo_" + n, [64, 4, C], BF16, st) for n in ['At', 'Qt', 'Kh', 'Bh']}
            GC = kb.sb("GC", [64, NCH, 8], F32, st)
            rt = [kb.sb(f"rt{k}", [128, TB], F32, st) for k in range(12)]
            rf, kf, vf, ldm, aam, kk, kp, bb_, lg, lgx, t1, t2 = rt
            rtk = [f'rt{k}' for k in range(12)]
            krf, kkf, kvf, kld, kaa, kkk, kkp, kbb, klg, klgx, kt1, kt2 = rtk
            Pf = kb.sb("rPf", [64, 8, 64], F32, st)
            Pb = kb.sb("rPb", [64, 8, 64], BF16, st)
            PTb = kb.sb("rPTb", [64, 8, 64], BF16, st)
            Accf = kb.sb("rAccf", [64, 8, 64], F32, st)
            PTf = kb.sb("rPTf", [64, 8, 64], F32, st)
            Pw = kb.sb("rPw", [64, 8, 64], F32, st)
            Accb = kb.sb("rAccb", [64, 8, 64], BF16, st)
            Mav = kb.sb("Mav", [64, 8, 64], BF16, st)
            Mqk = kb.sb("Mqk", [64, 8, 64], BF16, st)
            MqbN = kb.sb("MqbN", [64, 8, 64], BF16, st)
            Vt = kb.sb("Vt", [64, 512], BF16, st)
            Kbt = kb.sb("Kbt", [64, 512], BF16, st)
            BbtN = kb.sb("BbtN", [64, 512], BF16, st)
            RHSb = kb.sb("RHSb", [64, 512], BF16, st)
            Pb2 = kb.sb("Pb2", [64, 512], BF16, st)
            tmpR = kb.sb("tmpR", [64, 512], F32, st)
            tmpO = kb.sb("tmpO", [64, 512], F32, st)
            oT = kb.sb("oT", [64, 8, 64], F32, st)
            oc = kb.sb("oc", [64, 8, 64], F32, st)
            osq = kb.sb("rosq", [64, 8, 64], F32, st)
            s8 = kb.sb("s8", [64, 8], F32, st)
            s8b = kb.sb("s8b", [64, 8], F32, st)
            onb = kb.sb("ronb", [64, 512], BF16, st)
            t_o = kb.sb("t_o", [128, 4, C], F32, st)

            def shift(ps, pk, mf, out, outk):
                kb.cp('dve', gbuf[:, 0:1], tcarry[i][:, mf:mf + 1], r=[f'tcarry{i}'], w=['gbuf0'])
                kb.act(gbuf[:, 1:TB + 1], ps[:], AF.Copy, scale=P[:, mf:mf + 1], r=[pk, rk_], w=['gbuf'])
                kb.cp('pool', tcarry[i][:, mf:mf + 1], gbuf[:, TB:TB + 1], r=['gbuf'], w=[f'tcarry{i}'])
                kb.stt(out[:], ps[:], omm[i][:, mf:mf + 1], gbuf[:, 0:TB], ALU.mult, ALU.add,
                       r=[pk, f'omm{i}', 'gbuf', 'gbuf0'], w=[outk])

            for k2 in range(2):
                ps, pk = proj_chunk(winev[i, 16 + k2], hT, 'hT')
                shift(ps, pk, 12 + k2, fl[k2], f'fl{k2}')
            kb.act(tw[:], fl[0][0:64, :], AF.Tanh, r=['fl0'], w=['tw'])
            kb.cp('pool', xab[64:128, :], fl[0][64:128, :], r=['fl0'], w=['xab'])
            kb.act(sg[:], fl[1][:], AF.Sigmoid, r=['fl1'], w=['sg'])
            for m in range(4):
                mc = slice(m * 128, (m + 1) * 128)
                kb.mm(psC[:], wup_b[i][0:64, mc], tw[:], r=[f'wup{i}', 'tw'], w=['psC'])
                kb.act(ldm[:], psC[:], AF.Sigmoid, bias=P[:, 14 + m:15 + m], r=['psC', rk_], w=[kld])
                kb.ts('pool', ldm[:], ldm[:], -RW_DS, ALU.mult, r=[kld], w=[kld])
                kb.mm(psC[:], aup_b[i][64:128, mc], xab[64:128, :], r=[f'aup{i}', 'xab'], w=['psC'])
                kb.act(aam[:], psC[:], AF.Sigmoid, bias=P[:, 18 + m:19 + m], r=['psC', rk_], w=[kaa])
                kb.mm(psC[:], gup_b[i][:, mc], sg[:], r=[f'gup{i}', 'sg'], w=['psC'])
                kb.tt('dve', gz[:, m, :], psC[:], zs[:, 4 + m, :], ALU.mult, r=['psC', 'zs'], w=['gz'])
                ps, pk = proj_chunk(winev[i, 4 + m], hT, 'hT')
                shift(ps, pk, m, rf, krf)
                ps, pk = proj_chunk(winev[i, 8 + m], hT, 'hT')
                shift(ps, pk, 4 + m, kf, kkf)
                ps, pk = proj_chunk(winev[i, 12 + m], hT, 'hT')
                shift(ps, pk, 8 + m, vf, kvf)
                kb.act(sqb[0][:], kf[:], AF.Square, scale=P[:, 22 + m:23 + m], r=[kkf, rk_], w=['sqb0'])
                kb.mm(psC[:], bonesb[:], sqb[0][:], r=['bonesb', 'sqb0'], w=['psC'])
                kb.act(t1[:], psC[:], AF.Sqrt, bias=EPS, r=['psC'], w=[kt1])
                kb.op('dve', lambda E: E.reciprocal(out=t1[:], in_=t1[:]), r=[kt1], w=[kt1])
                kb.stt(kk[:], kf[:], P[:, 22 + m:23 + m], t1[:], ALU.mult, ALU.mult, r=[kkf, rk_, kt1], w=[kkk])
                kb.ts('pool', t2[:], aam[:], -1.0, ALU.add, P[:, 26 + m:27 + m], ALU.mult, r=[kaa, rk_], w=[kt2])
                kb.stt(kp[:], t2[:], 1.0, kf[:], ALU.add, ALU.mult, r=[kt2, kkf], w=[kkp])
                kb.tt('pool', bb_[:], kk[:], aam[:], ALU.mult, r=[kkk, kaa], w=[kbb])
                kb.stt(sqb[1][:], rf[:], P[:, 30 + m:31 + m], kp[:], ALU.mult, ALU.mult, r=[krf, rk_, kkp], w=['sqb1'])
                kb.mm(psC[:], bonesb[:], sqb[1][:], r=['bonesb', 'sqb1'], w=['psC'])
                kb.tt('dve', bonus[:, m, :], psC[:], vf[:], ALU.mult, r=['psC', kvf], w=['bonus'])
                kb.cp('act', vbT[:, m, :], vf[:], r=[kvf], w=['vbT'])
                kb.op('dve', lambda E: E.tensor_tensor_scan(out=lg[:], data0=cmask, data1=ldm[:], initial=0.0,
                                                            op0=ALU.mult, op1=ALU.add), r=['cst', kld], w=[klg])
                kb.tt('pool', lgx[:], lg[:], ldm[:], ALU.subtract, r=[klg, kld], w=[klgx])
                lg3 = lg[:].rearrange("p (j c) -> p j c", c=C)
                kb.act(t1[:], lg[:], AF.Exp, r=[klg], w=[kt1])
                kb.tt('dve', ops6['Qt'][:, m, :], rf[:], t1[:], ALU.mult, r=[krf, kt1], w=['op_Qt'])
                kb.act(t1[:], lgx[:], AF.Exp, r=[klgx], w=[kt1])
                kb.tt('dve', ops6['At'][:, m, :], kk[:], t1[:], ALU.mult, r=[kkk, kt1], w=['op_At'])
                kb.act(t1[:], lg[:], AF.Exp, scale=-1.0, r=[klg], w=[kt1])
                kb.tt('dve', ops6['Kh'][:, m, :], kp[:], t1[:], ALU.mult, r=[kkp, kt1], w=['op_Kh'])
                kb.tt('pool', ops6['Bh'][:, m, :], bb_[:], t1[:], ALU.mult, r=[kbb, kt1], w=['op_Bh'])
                kb.tt('dve', t2[:].rearrange("p (j c) -> p j c", c=C), bc(lg3[:, :, C - 1:C], [128, NCH, C]), lg3,
                      ALU.subtract, r=[klg], w=[kt2])
                kb.act(t2[:], t2[:], AF.Exp, r=[kt2], w=[kt2])
                kb.tt('dve', ops6['Kb'][:, m, :], kp[:], t2[:], ALU.mult, r=[kkp, kt2], w=['op_Kb'])
                kb.tt('pool', ops6['Bb'][:, m, :], bb_[:], t2[:], ALU.mult, r=[kbb, kt2], w=['op_Bb'])
                for par in range(2):
                    kb.act(GC[:, :, 2 * m + par], lg3[par * 64:(par + 1) * 64, :, C - 1], AF.Exp, r=[klg], w=['GC'])

            H = Hst[i]
            Hbf = Hb[i]
            opk = ['op_At', 'op_Qt', 'op_Kh', 'op_Bh']
            for j in range(NCH):
                cs = slice(j * C, (j + 1) * C)
                for n in ['At', 'Qt', 'Kh', 'Bh']:
                    kb.cp('dve', opo[n][:], ops6[n][64:128, :, cs], r=['op_' + n], w=['opo_' + n])

                def X(n, h):
                    return ops6[n][0:64, h // 2, cs] if h % 2 == 0 else opo[n][:, h // 2, :]
                xk = lambda *ns: [k for n in ns for k in ('op_' + n, 'opo_' + n)]
                for h in range(8):
                    kb.mm(psA[0:64, h * 64:(h + 1) * 64], X('Bh', h), X('At', h), r=xk('Bh', 'At'), w=['psA0'])
                for h in range(8):
                    kb.mm(psA[0:64, 512 + h * 64:512 + (h + 1) * 64], X('Kh', h), X('At', h), r=xk('Kh', 'At'), w=['psA1'])
                for h in range(8):
                    kb.mm(psB[0:64, h * 64:(h + 1) * 64], X('Kh', h), X('Qt', h), r=xk('Kh', 'Qt'), w=['psB0'])
                for h in range(8):
                    kb.mm(psB[0:64, 512 + h * 64:512 + (h + 1) * 64], X('Bh', h), X('Qt', h), r=xk('Bh', 'Qt'), w=['psB1'])
                f2 = lambda t: t[:].rearrange("p h c -> p (h c)")
                m3 = lambda mk: bc(mk.unsqueeze(1), [64, 8, 64])
                p3 = lambda ap: ap.rearrange("p (h c) -> p h c", c=64)
                kb.tt('dve', Pf[:], p3(psA[0:64, 0:512]), m3(MsN), ALU.mult, r=['psA0', 'cst'], w=['Pf'])
                kb.tt('dve', Mav[:], p3(psA[0:64, 512:1024]), m3(Ms), ALU.mult, r=['psA1', 'cst'], w=['Mav'])
                kb.tt('dve', Mqk[:], p3(psB[0:64, 0:512]), m3(Mi), ALU.mult, r=['psB0', 'cst'], w=['Mqk'])
                kb.tt('dve', MqbN[:], p3(psB[0:64, 512:1024]), m3(MiN), ALU.mult, r=['psB1', 'cst'], w=['MqbN'])
                neumann(Pf, Pb, PTb, Accf, Accb, PTf, Pw)
                for m in range(4):
                    kb.tr(psT[0:64, m * 128:(m + 1) * 128], vbT[:, m, cs], identb[:], r=['vbT', 'identb'], w=['psT'])
                kb.cp('act', Vt[:], psT[0:64, 0:512], r=['psT'], w=['Vt'])
                for h in range(8):
                    kb.mm(psC[0:64, h * 64:(h + 1) * 64], X('At', h), Hbf[:, h, :], r=xk('At') + [f'Hb{i}'], w=['psC'])
                kb.cp('act', tmpR[:], psC[0:64, :], r=['psC'], w=['tmpR'])
                for h in range(8):
                    kb.mm(psI[0][0:64, h * 64:(h + 1) * 64], Mav[:, h, :], Vt[:, h * 64:(h + 1) * 64], r=['Mav', 'Vt'], w=['psI0'])
                kb.tt('dve', RHSb[:], psI[0][0:64, :], tmpR[:], ALU.add, r=['psI0', 'tmpR'], w=['RHSb'])
                for h in range(8):
                    kb.mm(psC[0:64, h * 64:(h + 1) * 64], Accb[:, h, :], RHSb[:, h * 64:(h + 1) * 64], r=['Accb', 'RHSb'], w=['psC'])
                kb.cp('act', Pb2[:], psC[0:64, :], r=['psC'], w=['Pb2'])
                for h in range(8):
                    kb.mm(psI[1][0:64, h * 64:(h + 1) * 64], X('Qt', h), Hbf[:, h, :], r=xk('Qt') + [f'Hb{i}'], w=['psI1'])
                kb.cp('act', tmpO[:], psI[1][0:64, :], r=['psI1'], w=['tmpO'])
                for h in range(8):
                    hs = slice(h * 64, (h + 1) * 64)
                    kb.mm(psI[0][0:64, hs], Mqk[:, h, :], Vt[:, hs], start=True, stop=False, r=['Mqk', 'Vt'], w=['psI0'])
                    kb.mm(psI[0][0:64, hs], MqbN[:, h, :], Pb2[:, hs], start=False, stop=True, r=['MqbN', 'Pb2'], w=['psI0'])
                kb.tt('dve', f2(oT), psI[0][0:64, :], tmpO[:], ALU.add, r=['psI0', 'tmpO'], w=['oT'])
                for m in range(4):
                    kb.tr(psT[0:64, m * 128:(m + 1) * 128], ops6['Kb'][:, m, cs], identb[:], r=['op_Kb', 'identb'], w=['psT'])
                kb.cp('act', Kbt[:], psT[0:64, 0:512], r=['psT'], w=['Kbt'])
                for m in range(4):
                    kb.tr(psT[0:64, m * 128:(m + 1) * 128], ops6['Bb'][:, m, cs], identb[:], r=['op_Bb', 'identb'], w=['psT'])
                kb.ts('dve', BbtN[:], psT[0:64, 0:512], -1.0, ALU.mult, r=['psT'], w=['BbtN'])
                for h in range(8):
                    hs = slice(h * 64, (h + 1) * 64)
                    kb.mm(psC[0:64, hs], Kbt[:, hs], Vt[:, hs], start=True, stop=False, r=['Kbt', 'Vt'], w=['psC'])
                    kb.mm(psC[0:64, hs], BbtN[:, hs], Pb2[:, hs], start=False, stop=True, r=['BbtN', 'Pb2'], w=['psC'])
                kb.tt('pool', H[:], H[:], bc(GC[:, j, :].unsqueeze(2), [64, 8, 64]), ALU.mult, r=[f'H{i}', 'GC'], w=[f'H{i}'])
                kb.tt('dve', H[:], H[:], p3(psC[0:64, :]), ALU.add, r=[f'H{i}', 'psC'], w=[f'H{i}'])
                kb.cp('act', Hbf[:], H[:], r=[f'H{i}'], w=[f'Hb{i}'])
                kb.op('dve', lambda E: E.tensor_reduce(out=s8[:], in_=oT[:], axis=AX.X, op=ALU.add), r=['oT'], w=['s8'])
                kb.ts('dve', s8[:], s8[:], -1.0 / 64, ALU.mult, r=['s8'], w=['s8'])
                kb.tt('pool', oc[:], oT[:], bc(s8[:].unsqueeze(2), [64, 8, 64]), ALU.add, r=['oT', 's8'], w=['oc'])
                kb.tt('pool', osq[:], oc[:], oc[:], ALU.mult, r=['oc'], w=['osq'])
                kb.op('dve', lambda E: E.tensor_reduce(out=s8b[:], in_=osq[:], axis=AX.X, op=ALU.add), r=['osq'], w=['s8b'])
                kb.act(s8b[:], s8b[:], AF.Sqrt, scale=1.0 / 64, bias=LN_EPS, r=['s8b'], w=['s8b'])
                kb.op('dve', lambda E: E.reciprocal(out=s8b[:], in_=s8b[:]), r=['s8b'], w=['s8b'])
                kb.tt('pool', oc[:], oc[:], bc(s8b[:].unsqueeze(2), [64, 8, 64]), ALU.mult, r=['oc', 's8b'], w=['oc'])
                kb.tt('pool', f2(oc), f2(oc), lnw_s[i][:], ALU.mult, r=['oc', f'lnw{i}'], w=['oc'])
                kb.tt('pool', onb[:], f2(oc), lnb_s[i][:], ALU.add, r=['oc', f'lnb{i}'], w=['onb'])
                for m in range(4):
                    kb.tr(psT[:, m * 64:(m + 1) * 64], onb[:, m * 128:(m + 1) * 128], identb[0:64, 0:64],
                          r=['onb', 'identb'], w=['psT'])
                kb.tt('dve', t_o[:], psT[:, 0:256].rearrange("p (m c) -> p m c", c=C), bonus[:, :, cs], ALU.add,
                      r=['psT', 'bonus'], w=['t_o'])
                kb.tt('dve', ygT[:, 4:8, cs], t_o[:], gz[:, :, cs], ALU.mult, r=['t_o', 'gz'], w=['ygT'])

    def s5_phase(l, i, bi):
        STOP = int(os.environ.get('S5STOP', '99'))
        if STOP <= 0:
            kb.op('pool', lambda E: E.memset(ygT[:, 0:4, :], 0.0), w=['ygT'])
            return
        P = rp[i]
        rk_ = f'rp{i}'
        with ExitStack() as st:
            uT = kb.sb("uT", [128, 4, TB], F32, st)
            uTb = kb.sb("uTb", [128, 4, TB], BF16, st)
            uTb3 = kb.sb("uTb3", [128, 4, TB], BF16, st)
            yT = kb.sb("yT", [128, 4, TB], F32, st)
            ygb = kb.sb("ygb", [128, 4, TB], BF16, st)
            cs1f = kb.sb("cs1f", [128, 4, 8, 64], F32, st)
            cs1b = kb.sb("cs1b", [128, 8, 8, 64], BF16, st)
            cpb = kb.sb("cpb", [128, 8, 64], BF16, st)
            ea = kb.sb("ea", [128, 4, 64], F32, st)
            etm = kb.sb("etm", [128, 4, 64], F32, st)
            et = kb.sb("et", [128, 4, 64], F32, st)
            ch = kb.sb("ch", [128, 4, 64], F32, st)
            cN = kb.sb("cN", [128, 4, 64], F32, st)
            g1 = kb.sb("g1", [128, TB], F32, st)
            g2 = kb.sb("g2", [128, TB], F32, st)
            if STOP < 99 and 'c' not in os.environ.get('S5SKIP', ''):
                kb.op('pool', lambda E: E.memset(yT[:], 0.0), w=['yT'])
                kb.op('pool', lambda E: E.memset(ygb[:], 0.0), w=['ygb'])
                kb.op('pool', lambda E: E.memset(cs1b[:], 0.0), w=['cs1b'])
                kb.op('pool', lambda E: E.memset(cpb[:], 0.0), w=['cpb'])
                kb.op('pool', lambda E: E.memset(cs1f[:], 0.0), w=['cs1f'])
            for q in range(0 if 'd' in os.environ.get('S5SKIP', '') else 4):
                ps, pk = proj_chunk(winev[i, q], hT, 'hT')
                if 'e' not in os.environ.get('S5SKIP', ''):
                    kb.cp('act', uT[:, q, :], ps[:], r=[pk], w=['uT'])
                if 'f' not in os.environ.get('S5SKIP', ''):
                    kb.cp('dve', uTb[:, q, :], uT[:, q, :], r=['uT'], w=['uTb'])
                if 'a' not in os.environ.get('S5SKIP', ''):
                    kb.ts('dve', uTb3[64:128, q, :], uTb[64:128, q, :], cst[64:128, 1218:1219], ALU.mult,
                          r=['uTb', 'cst'], w=['uTb3'])
            banks = [psA[:, 0:512], psA[:, 512:1024], psB[:, 0:512], psB[:, 512:1024]]
            bkeys = ['psA0', 'psA1', 'psB0', 'psB1']
            TTv = [TT_d[k_][i].rearrange("p (q b e) n -> p q b e n", q=4, b=4) for k_ in range(4)]
            BJq = [kb.sb(f"BJq{k_}", [128, 2, 8, 128], BF16, st) for k_ in range(2)]
            CJloq = [kb.sb(f"CJloq{k_}", [128, 4, 9, 32], BF16, st) for k_ in range(2)]
            CJhiq = [kb.sb(f"CJhiq{k_}", [128, 4, 9, 64], BF16, st) for k_ in range(2)]
            TTq = [[kb.sb(f"TTq{k_}_{z_}", [128, 4, 64], F32, st) for k_ in range(4)] for z_ in range(2)]
            carv = s5carry[i][:].rearrange("p (q b e) -> p q b e", q=4, b=4)
            r8v = rho8[i][:].rearrange("p (q b e) -> p q b e", q=4, b=4)
            for q in range(4 if STOP > 1 else 0):
                qb = q % 2
                tk = f'tabq{qb}'
                kb.dma('sp', BJq[qb][:], BJ_d[i][:, q], f'tq{qb}', w=[tk])
                kb.dma('sp', CJloq[qb][:], CJlo_d[i][:, q], f'tq{qb}', w=[tk])
                kb.dma('sp', CJhiq[qb][:], CJhi_d[i][:, q], f'tq{qb}', w=[tk])
                for e in range(2):
                    tke = f'tte{e}'
                    for k_ in range(4):
                        kb.dma('sp', TTq[e][k_][:], TTv[k_][:, q, :, e, :], f'tt{e}', w=[tke])
                    T1v, T2v, T3v, T4v = TTq[e]
                    for b in range(4):
                        pb = slice(32 * b, 32 * b + 32) if b < 3 else slice(64, 128)
                        uv = (uTb if b < 3 else uTb3)[pb, q, :].rearrange("p (n j) -> p j n", j=8)
                        for j in range(8):
                            kb.mm(banks[b][:, j * 64:(j + 1) * 64], BJq[qb][pb, e, j, :], uv[:, j, :],
                                  r=[tk, 'uTb', 'uTb3'], w=[bkeys[b]])
                    if STOP <= 2:
                        continue
                    zA = psA[:, :].rearrange("p (b j n) -> p b j n", b=2, j=8)
                    zB = psB[:, :].rearrange("p (b j n) -> p b j n", b=2, j=8)
                    for (z, zk, bs) in ((zA, ['psA0', 'psA1'], slice(0, 2)), (zB, ['psB0', 'psB1'], slice(2, 4))):
                        kb.cp('dve', cs1f[:, bs, 0, :], z[:, :, 0, :], r=zk, w=['cs1f'])
                        for j in range(1, 8):
                            kb.tt('dve', cs1f[:, bs, j, :], z[:, :, j, :], cs1f[:, bs, j - 1, :], ALU.add,
                                  r=zk + ['cs1f'], w=['cs1f'])
                    cbv = cs1b[:].rearrange("p (b e) j n -> p b e j n", e=2)
                    kb.cp('act', cbv[:, :, e, :, :], cs1f[:], r=['cs1f'], w=['cs1b'])
                    if STOP <= 3:
                        continue
                    x = cs1f[:, :, 7, :]
                    kb.tt('pool', ea[:], x, T1v[:], ALU.mult, r=['cs1f', tke], w=['ea'])
                    kb.tt('dve', etm[0:64], cs1f[64:128, :, 7, :], T2v[64:128], ALU.mult, r=['cs1f', tke], w=['etm'])
                    kb.tt('dve', etm[64:128], cs1f[0:64, :, 7, :], T2v[0:64], ALU.mult, r=['cs1f', tke], w=['etm'])
                    kb.tt('pool', et[:], ea[:], etm[:], ALU.add, r=['ea', 'etm'], w=['et'])
                    for b in range(4):
                        kb.op('dve', lambda E, b=b: E.tensor_tensor_scan(
                            out=ch[:, b, :], data0=r8v[:, q, b, e:e + 1].to_broadcast([128, 64]), data1=et[:, b, :],
                            initial=carv[:, q, b, e:e + 1], op0=ALU.mult, op1=ALU.add), r=['et', f's5c{i}'], w=['ch'])
                    kb.tt('pool', ea[:], ch[:], T3v[:], ALU.mult, r=['ch', tke], w=['ea'])
                    kb.tt('dve', etm[0:64], ch[64:128], T4v[64:128], ALU.mult, r=['ch', tke], w=['etm'])
                    kb.tt('dve', etm[64:128], ch[0:64], T4v[0:64], ALU.mult, r=['ch', tke], w=['etm'])
                    kb.tt('pool', cN[:], ea[:], etm[:], ALU.add, r=['ea', 'etm'], w=['cN'])
                    cpv = cpb[:].rearrange("p (b e) n -> p b e n", e=2)
                    kb.cp('dve', cpv[:, :, e, 0:1], carv[:, q, :, e:e + 1], r=[f's5c{i}'], w=['cpb'])
                    kb.cp('act', cpv[:, :, e, 1:64], cN[:, :, 0:63], r=['cN'], w=['cpb'])
                    kb.cp('dve', carv[:, q, :, e:e + 1], cN[:, :, 63:64], r=['cN'], w=[f's5c{i}'])
                if STOP <= 4:
                    continue
                for j in range(8):
                    jc = slice(j * 64, (j + 1) * 64)
                    for b in range(2):
                        for e in range(2):
                            gl = 2 * b + e
                            o_ = psC[32 * b:32 * b + 32, jc]
                            kb.mm(o_, CJloq[qb][:, gl, j, :], cs1b[:, gl, j, :], start=(e == 0), stop=False,
                                  r=[tk, 'cs1b'], w=['psC'])
                            kb.mm(o_, CJloq[qb][:, gl, j + 1, :], cpb[:, gl, :], start=False, stop=(e == 1),
                                  r=[tk, 'cpb'], w=['psC'])
                    for gl in range(4, 8):
                        o_ = psC[64:128, jc]
                        kb.mm(o_, CJhiq[qb][:, gl - 4, j, :], cs1b[:, gl, j, :], start=(gl == 4), stop=False,
                              r=[tk, 'cs1b'], w=['psC'])
                        kb.mm(o_, CJhiq[qb][:, gl - 4, j + 1, :], cpb[:, gl, :], start=False, stop=(gl == 7),
                              r=[tk, 'cpb'], w=['psC'])
                yv = yT[:, q, :].rearrange("p (n j) -> p j n", j=8)
                uv32 = uT[:, q, :].rearrange("p (n j) -> p j n", j=8)
                kb.stt(yv, uv32, P[:, 34 + q:35 + q], psC[:, :].rearrange("p (j n) -> p j n", j=8), ALU.mult, ALU.add,
                       r=['uT', rk_, 'psC'], w=['yT'])
                xq = yT[:, q, :]
                kb.tt('pool', g1[:], xq, xq, ALU.mult, r=['yT'], w=['g1'])
                kb.ts('pool', g1[:], g1[:], 0.044715, ALU.mult, 1.0, ALU.add, r=['g1'], w=['g1'])
                kb.tt('pool', g1[:], g1[:], xq, ALU.mult, r=['g1', 'yT'], w=['g1'])
                kb.act(g2[:], g1[:], AF.Sigmoid, scale=1.5957691216057308, r=['g1'], w=['g2'])
                kb.tt('dve', yT[:, q, :], xq, g2[:], ALU.mult, r=['yT', 'g2'], w=['yT'])
                kb.cp('act', ygb[:, q, :], yT[:, q, :], r=['yT'], w=['ygb'])
            for qo in range(0 if 'b' in os.environ.get('S5SKIP', '') else 4):
                for k in range(4):
                    kb.mm(psC[:], gluw_b[i][:, k, qo * 128:(qo + 1) * 128], ygb[:, k, :], start=(k == 0), stop=(k == 3),
                          r=[f'gluw{i}', 'ygb'], w=['psC'])
                kb.act(g2[:], psC[:], AF.Sigmoid, bias=P[:, 38 + qo:39 + qo], r=['psC', rk_], w=['g2'])
                kb.tt('dve', g1[:], yT[:, qo, :], g2[:], ALU.mult, r=['yT', 'g2'], w=['g1'])
                kb.tt('dve', ygT[:, qo, :], g1[:], zs[:, qo, :], ALU.mult, r=['g1', 'zs'], w=['ygT'])

    def even_layer(l, bi):
        i = l // 2
        pre_norm(l)
        for m in range(8):
            ps, pk = proj_chunk(winev[i, 18 + m], hT, 'hT')
            kb.act(zs[:, m, :], ps[:], AF.Silu, r=[pk], w=['zs'])
        if dbg:
            kb.op('pool', lambda E: E.memset(zs[:], 1.0), r=['zs'], w=['zs'])
        s5_phase(l, i, bi)
        if not os.environ.get('RWSKIP'):
            rwkv_phase(l, i, bi)
        if not dbg:
            out_proj(l)
        kb.barrier()

    for bi in range(nblocks):
        ts_ = slice(bi * TB, (bi + 1) * TB)
        kb.dma('sp', xres[:], xT[:, ts_].rearrange("(k p) t -> p k t", p=128), 'xin', w=['xres'])
        for l in layers:
            if l % 2 == 1:
                odd_layer(l, bi)
            else:
                even_layer(l, bi)
        with ExitStack() as st:
            obuf = kb.sb("obuf", [128, 8, TB], F32, st)
            if final:
                rms_stats(xres, 'xres')
                for k in range(8):
                    kb.stt(obuf[:, k, :], xres[:, k, :], fnw_s[:, k:k + 1], rstd[:], ALU.mult, ALU.mult,
                           r=['xres', 'fnw_s', 'rstd'], w=['obuf'])
            elif dbg:
                kb.cp('dve', obuf[:], ygT[:], r=['ygT'], w=['obuf'])
            else:
                kb.cp('dve', obuf[:], xres[:], r=['xres'], w=['obuf'])
            kb.dma('sp', outT[:, ts_].rearrange("(k p) t -> p k t", p=128), obuf[:], 'xout', r=['obuf'])
            kb.barrier()
    kb.es.close()
    return nc, kb


def chunkify(W, cols):
    Wc = W[:, cols]
    n_m = Wc.shape[1] // 128
    return np.ascontiguousarray(Wc.reshape(8, 128, n_m, 128).transpose(2, 1, 0, 3))


def prep_shared(inp):
    sh = {}
    sh["normw"] = np.ascontiguousarray(inp["norm_w"].reshape(4, 8, 128).transpose(2, 0, 1).reshape(128, 32))
    sh["adaw"] = np.ascontiguousarray(inp["ada_w"])
    sh["adab"] = np.ascontiguousarray(inp["ada_b"].reshape(4, 24, 128).transpose(2, 0, 1).reshape(128, 96))
    sh["woutr"] = np.stack([chunkify(inp["w_out"][l], np.arange(1024)) for l in range(4)])
    sh["fnw"] = np.ascontiguousarray(inp["final_norm_w"].reshape(8, 128).T)
    sh["consts"] = make_consts()
    cols = np.concatenate([np.arange(0, 3072), np.arange(3088, 4112)])
    sh["winodd"] = np.stack([chunkify(inp["odd_w_in"][i], cols) for i in range(2)])
    sh["wba"] = np.ascontiguousarray(
        np.stack([inp["odd_w_in"][i][:, 3072:3088].reshape(8, 128, 16).transpose(1, 0, 2) for i in range(2)]))
    sh["convw"] = np.ascontiguousarray(
        np.stack([inp["gdn_conv_w"][i].reshape(4, 24, 128).transpose(2, 1, 0) for i in range(2)]))
    sh["alog"] = np.ascontiguousarray(np.broadcast_to(inp["gdn_a_log"][:, None, :], (2, 128, 8)))
    sh["dtb"] = np.ascontiguousarray(np.broadcast_to(inp["gdn_dt_bias"][:, None, :], (2, 128, 8)))
    sh["gnw"] = np.ascontiguousarray(np.broadcast_to(inp["gdn_norm_w"][:, None, :], (2, 128, 128)))
    sh["winev"] = np.stack([chunkify(inp["even_w_in"][i], np.arange(3328)) for i in range(2)])
    pc = lambda v, n: v.reshape(n, 128).T
    sh["rp"] = np.stack([np.concatenate([pc(inp["rwkv_mu"][i], 14), pc(inp["rwkv_w0"][i], 4), pc(inp["rwkv_a0"][i], 4),
                                         pc(inp["rwkv_k_k"][i], 4), pc(inp["rwkv_k_a"][i], 4), pc(inp["rwkv_r_k"][i], 4),
                                         pc(inp["s5_d"][i], 4), pc(inp["s5_glu_b"][i], 4)], axis=1) for i in range(2)])
    sh["wup"] = inp["rwkv_w_up"]
    sh["aup"] = inp["rwkv_a_up"]
    sh["gup"] = inp["rwkv_g_up"]
    sh["lnw"] = np.broadcast_to(inp["rwkv_ln_w"][:, None, :], (2, 64, 512))
    sh["lnb"] = np.broadcast_to(inp["rwkv_ln_b"][:, None, :], (2, 64, 512))
    s5a, s5b = [], []
    p_ = np.arange(128)
    for i in range(2):
        rep = lambda a: np.concatenate([a, a], axis=0)
        lre2 = rep(inp["s5_lambda_re"][i].T)
        lim2 = rep(inp["s5_lambda_im"][i].T)
        lst2 = np.broadcast_to(inp["s5_log_step"][i][None, :], (128, 32))
        crT = rep(inp["s5_c_re"][i].transpose(2, 0, 1)).reshape(128, 512)
        ciT = rep(inp["s5_c_im"][i].transpose(2, 0, 1)).reshape(128, 512)
        s5a.append(np.concatenate([lre2, lim2, lst2, crT, ciT], axis=1))
        g = 8 * np.arange(4)[None, :] + 2 * (p_ // 32)[:, None] + ((p_ % 32) // 16)[:, None]
        cp = (p_ % 16)[:, None]
        lreB = inp["s5_lambda_re"][i][g]
        limB = inp["s5_lambda_im"][i][g]
        breB = inp["s5_b_re"][i][g, :, cp]
        bimB = inp["s5_b_im"][i][g, :, cp]
        lstB = inp["s5_log_step"][i][g]
        s5b.append(np.concatenate([lreB.reshape(128, 256), limB.reshape(128, 256), breB.reshape(128, 256),
                                   bimB.reshape(128, 256), lstB], axis=1))
    sh["s5a"] = np.stack(s5a)
    sh["s5b"] = np.stack(s5b)
    sh["gluw"] = np.stack([inp["s5_glu_w"][i].reshape(4, 128, 512).transpose(1, 0, 2) for i in range(2)])
    return {k: np.ascontiguousarray(np.asarray(v, np.float32)) for k, v in sh.items()}


_PLAN = [([0], False), ([1], False), ([2], False), ([3], True)]


def kernel(**inputs):
    inp = {k: np.asarray(v) for k, v in inputs.items()}
    nb = inp["x"].shape[0]
    sh = prep_shared(inp)
    xT = [np.ascontiguousarray(inp["x"][b].T) for b in range(nb)]
    cTs = [np.ascontiguousarray(inp["c"][b].reshape(8, 128).T) for b in range(nb)]
    for layers, final in _PLAN:
        nc, _ = build(layers, final)
        in_maps = [dict(sh, xT=xT[b], cT=cTs[b]) for b in range(nb)]
        res = run_bass_kernel_spmd(nc, in_maps, core_ids=list(range(nb)))
        xT = [np.asarray(res.results[b]["outT"]) for b in range(nb)]
    return np.stack([x.T for x in xT]).astype(np.float32)
```

```python
import os
import numpy as np
from contextlib import ExitStack
import concourse.bass as bass
import concourse.mybir as mybir
from concourse.bass_utils import run_bass_kernel_spmd

F32 = mybir.dt.float32
BF16 = mybir.dt.bfloat16
AF = mybir.ActivationFunctionType
ALU = mybir.AluOpType
AX = mybir.AxisListType

D = 1024
L = 4096
TB = 512
NB = L // TB
C = 64
NCH = TB // C
EPS = 1e-6


class KB:
    def __init__(self):
        self.nc = bass.Bass("TRN2", target_bir_lowering=False)
        self.es = ExitStack()
        nc = self.nc
        self.eng = {'pe': nc.tensor, 'dve': nc.vector, 'act': nc.scalar, 'pool': nc.gpsimd, 'sp': nc.sync}
        self.sem = {e: self.es.enter_context(nc.semaphore("sem_" + e)) for e in self.eng}
        self.cnt = {e: 0 for e in self.eng}
        self.clock = {e: {} for e in self.eng}
        self.lastw = {}
        self.readers = {}
        self.dsem = {}
        self.nins = 0

    def sb(self, name, shape, dt, stack=None):
        self.minrem = min(getattr(self, 'minrem', 1 << 30), self.nc.sbuf_bytes_remaining)
        if stack is not None:
            self.uid = getattr(self, 'uid', 0) + 1
            name = f"{name}_u{self.uid}"
        return (stack or self.es).enter_context(self.nc.sbuf_tensor(name, shape, dt))

    def ps(self, name, shape, dt, stack=None):
        return (stack or self.es).enter_context(self.nc.psum_tensor(name, shape, dt))

    def _sync(self, e, reads, writes):
        need = {}

        def add(ev):
            if ev is None:
                return
            k, h, v = ev
            if k not in need or need[k][1] < v:
                need[k] = (h, v)

        for r in reads:
            add(self.lastw.get(r))
        for w in writes:
            add(self.lastw.get(w))
            for ev in self.readers.get(w, {}).values():
                add(ev)
        ck = self.clock[e]
        for k, (h, v) in need.items():
            if e == 'pe' and k == 'sem_pe':
                continue
            if ck.get(k, 0) >= v:
                continue
            self.eng[e].wait_ge(h, v)
            ck[k] = v

    def _mark(self, ev, reads, writes):
        for r in reads:
            self.readers.setdefault(r, {})[ev[0]] = ev
        for w in writes:
            self.lastw[w] = ev
            self.readers[w] = {}

    def op(self, e, fn, r=(), w=()):
        reads, writes = r, w
        self._sync(e, reads, writes)
        ins = fn(self.eng[e])
        self.cnt[e] += 1
        self.nins += 1
        ins.then_inc(self.sem[e], 1)
        self._mark(('sem_' + e, self.sem[e], self.cnt[e]), reads, writes)
        return ins

    def dma(self, q, out, in_, slot, r=(), w=()):
        reads, writes = r, w
        self._sync(q, reads, writes)
        if slot not in self.dsem:
            self.dsem[slot] = [self.es.enter_context(self.nc.semaphore("d_" + slot)), 0]
        s = self.dsem[slot]
        s[1] += 16
        self.eng[q].dma_start(out=out, in_=in_).then_inc(s[0], 16)
        self.nins += 1
        self._mark(('d_' + slot, s[0], s[1]), reads, writes)

    def barrier(self):
        for e in self.eng:
            for e2 in self.eng:
                if e2 != e and self.cnt[e2] > self.clock[e].get('sem_' + e2, 0):
                    self.eng[e].wait_ge(self.sem[e2], self.cnt[e2])
                    self.clock[e]['sem_' + e2] = self.cnt[e2]
            for k, (h, v) in self.dsem.items():
                if v > self.clock[e].get('d_' + k, 0):
                    self.eng[e].wait_ge(h, v)
                    self.clock[e]['d_' + k] = v
        self.lastw = {}
        self.readers = {}

    def mm(self, out, lhsT, rhs, start=True, stop=True, r=(), w=()):
        return self.op('pe', lambda E: E.matmul(out, lhsT=lhsT, rhs=rhs, start=start, stop=stop), r, w)

    def tr(self, out, in_, ident, r=(), w=()):
        return self.op('pe', lambda E: E.transpose(out, in_, ident), r, w)

    def act(self, out, in_, func, scale=None, bias=None, r=(), w=(), e='act'):
        kw = {}
        if scale is not None:
            kw['scale'] = scale
        if bias is not None:
            kw['bias'] = bias
        return self.op('act', lambda E: E.activation(out=out, in_=in_, func=func, **kw), r, w)

    def tt(self, e, out, in0, in1, op, r=(), w=()):
        return self.op(e, lambda E: E.tensor_tensor(out=out, in0=in0, in1=in1, op=op), r, w)

    def ts(self, e, out, in0, s1, op0, s2=None, op1=None, r=(), w=()):
        if op1 is None:
            return self.op(e, lambda E: E.tensor_scalar(out=out, in0=in0, scalar1=s1, scalar2=None, op0=op0), r, w)
        return self.op(e, lambda E: E.tensor_scalar(out=out, in0=in0, scalar1=s1, scalar2=s2, op0=op0, op1=op1), r, w)

    def stt(self, out, in0, scalar, in1, op0, op1, r=(), w=()):
        return self.op('dve', lambda E: E.scalar_tensor_tensor(out=out, in0=in0, scalar=scalar, in1=in1, op0=op0, op1=op1), r, w)

    def cp(self, e, out, in_, r=(), w=()):
        if e == 'act':
            return self.op('act', lambda E: E.copy(out=out, in_=in_), r, w)
        return self.op(e, lambda E: E.tensor_copy(out=out, in_=in_), r, w)


def bc(ap, shape):
    return ap.to_broadcast(shape)


CB = {'ident': (0, 128), 'U': (128, 64), 'Lst': (192, 64), 'MsN': (256, 64), 'Mi': (320, 64), 'I64': (384, 64),
      'Ms': (448, 64), 'MiN': (512, 64), 'bones': (576, 128), 'cmask': (704, 512), 'pmask': (1216, 2)}
NCONST = 1219


def make_consts():
    i = np.arange(64)
    blob = np.zeros((128, NCONST), np.float32)
    blob[:, 0:128] = np.eye(128, dtype=np.float32)
    m = {}
    m['U'] = (i[:, None] <= i[None, :]).astype(np.float32)
    m['Lst'] = (i[:, None] > i[None, :]).astype(np.float32)
    m['MsN'] = -(i[:, None] < i[None, :]).astype(np.float32)
    m['Mi'] = (i[:, None] <= i[None, :]).astype(np.float32)
    m['I64'] = np.eye(64, dtype=np.float32)
    m['Ms'] = -m['MsN']
    m['MiN'] = -m['Mi']
    for k, v in m.items():
        blob[0:64, CB[k][0]:CB[k][0] + 64] = v
    bo = np.zeros((128, 128), np.float32)
    bo[0:64, 0:64] = 1.0
    bo[64:128, 64:128] = 1.0
    blob[:, 576:704] = bo
    cm = np.ones(512, np.float32)
    cm[0::64] = 0.0
    blob[:, 704:1216] = cm[None, :]
    p = np.arange(128)
    blob[:, 1216] = ((p % 32) // 16 == 0)
    blob[:, 1217] = ((p % 32) // 16 == 1)
    blob[:, 1218] = (p >= 96)
    return blob


def build(layers, final, nblocks=NB, dbg=None):
    kb = KB()
    nc = kb.nc
    n_odd = 2
    dr = {}

    def din(name, shape, dt=F32):
        dr[name] = nc.dram_tensor(name, list(shape), dt, kind="ExternalInput").ap()
        return dr[name]

    xT = din("xT", [D, L])
    cT = din("cT", [128, 8])
    normw = din("normw", [128, 32])
    adaw = din("adaw", [4, D, 3 * D])
    adab = din("adab", [128, 96])
    woutr = din("woutr", [4, 8, 128, 8, 128])
    fnw_d = din("fnw", [128, 8])
    consts_d = din("consts", [128, NCONST])
    winodd = din("winodd", [2, 32, 128, 8, 128])
    wba_d = din("wba", [2, 128, 8, 16])
    convw_d = din("convw", [2, 128, 24, 4])
    alog_d = din("alog", [2, 128, 8])
    dtb_d = din("dtb", [2, 128, 8])
    gnw_d = din("gnw", [2, 128, 128])
    winev = din("winev", [2, 26, 128, 8, 128])
    rp_d = din("rp", [2, 128, 42])
    wup_d = din("wup", [2, 64, 512])
    aup_d = din("aup", [2, 64, 512])
    gup_d = din("gup", [2, 128, 512])
    lnw_d = din("lnw", [2, 64, 512])
    lnb_d = din("lnb", [2, 64, 512])
    gluw_d = din("gluw", [2, 128, 4, 512])
    s5a_d = din("s5a", [2, 128, 96 + 1024])
    s5b_d = din("s5b", [2, 128, 1028])
    outT = nc.dram_tensor("outT", [D, L], F32, kind="ExternalOutput").ap()
    odd_ids = sorted({l // 2 for l in layers if l % 2 == 1})
    even_ids = sorted({l // 2 for l in layers if l % 2 == 0})

    cst = kb.sb("cst", [128, NCONST], F32)
    identb = kb.sb("identb", [128, 128], BF16)
    onesb = kb.sb("onesb", [128, 128], BF16)
    onesf = kb.sb("onesf", [64, 128], F32)
    xres = kb.sb("xres", [128, 8, TB], F32)
    hT = kb.sb("hT", [128, 8, TB], BF16)
    rstd = kb.sb("rstd", [128, TB], F32)
    sqb = [kb.sb(f"sqb{i}", [128, TB], BF16) for i in range(2)]
    tmpf = [kb.sb(f"tmpf{i}", [128, TB], F32) for i in range(2)]
    wbuf = [kb.sb(f"wbuf{i}", [128, 8, 128], BF16) for i in range(4)]
    ygT = kb.sb("ygT", [128, 8, TB], BF16)
    zs = kb.sb("zs", [128, 8, TB], BF16)
    modT = kb.sb("modT", [128, 4, 24], F32)
    s1 = kb.sb("s1", [128, 4, 8], F32)
    normw_s = kb.sb("normw_s", [128, 32], F32)
    adab_s = kb.sb("adab_s", [128, 96], F32)
    fnw_s = kb.sb("fnw_s", [128, 8], F32)
    sc = kb.sb("sc", [128, 8], F32)
    Sst = [kb.sb(f"Sst{i}", [128, 8, 128], F32) if i in odd_ids else None for i in range(n_odd)]
    Sb = [kb.sb(f"Sb{i}", [128, 8, 128], BF16) if i in odd_ids else None for i in range(n_odd)]
    carry = [kb.sb(f"carry{i}", [128, 24, 3], F32) if i in odd_ids else None for i in range(n_odd)]
    wba_s = [kb.sb(f"wba_s{i}", [128, 8, 16], BF16) if i in odd_ids else None for i in range(n_odd)]
    convw_s = [kb.sb(f"convw_s{i}", [128, 24, 4], F32) if i in odd_ids else None for i in range(n_odd)]
    negA = [kb.sb(f"negA{i}", [128, 8], F32) if i in odd_ids else None for i in range(n_odd)]
    dtb_s = [kb.sb(f"dtb_s{i}", [128, 8], F32) if i in odd_ids else None for i in range(n_odd)]
    gnw_s = [kb.sb(f"gnw_s{i}", [128, 128], F32) if i in odd_ids else None for i in range(n_odd)]

    def per_even(name, shape, dt):
        return [kb.sb(f"{name}{i}", shape, dt) if i in even_ids else None for i in range(2)]
    Hst = per_even("Hst", [64, 8, 64], F32)
    Hb = per_even("Hb", [64, 8, 64], BF16)
    tcarry = per_even("tcarry", [128, 14], F32)
    rp = per_even("rp_s", [128, 42], F32)
    omm = per_even("omm", [128, 14], F32)
    wup_b = per_even("wup_b", [64, 512], BF16)
    aup_b = per_even("aup_b", [128, 512], BF16)
    gup_b = per_even("gup_b", [128, 512], BF16)
    lnw_s = per_even("lnw_s", [64, 512], F32)
    lnb_s = per_even("lnb_s", [64, 512], F32)
    gluw_b = per_even("gluw_b", [128, 4, 512], BF16)
    bonesb = kb.sb("bonesb", [128, 128], BF16)
    def scratch(name, shape, dt):
        return [nc.dram_tensor(f"{name}{i}", list(shape), dt, kind="Internal").ap() if i in even_ids else None
                for i in range(2)]
    BJ_d = scratch("BJ_d", [128, 4, 2, 8, 128], BF16)
    CJlo_d = scratch("CJlo_d", [128, 4, 4, 9, 32], BF16)
    CJhi_d = scratch("CJhi_d", [128, 4, 4, 9, 64], BF16)
    TT_d = [scratch(f"TT{k}_d", [128, 32, 64], F32) for k in range(4)]
    rho8 = per_even("rho8", [128, 32], F32)
    s5carry = per_even("s5carry", [128, 32], F32)

    psI = [kb.ps(f"psI{i}", [128, 512], F32) for i in range(2)]
    psA = kb.ps("psA", [128, 1024], F32)
    psB = kb.ps("psB", [128, 1024], F32)
    psC = kb.ps("psC", [128, 512], F32)
    psT = kb.ps("psT", [128, 1024], BF16)

    def cv(k, rows=64):
        return cst[0:rows, CB[k][0]:CB[k][0] + CB[k][1]]
    U, Lst, MsN, Mi, I64, Ms, MiN = (cv(k) for k in ['U', 'Lst', 'MsN', 'Mi', 'I64', 'Ms', 'MiN'])
    cmask = cv('cmask', 128)

    kb.dma('sp', cst[:], consts_d[:, :], 'par', w=['cst'])
    kb.dma('sp', normw_s[:], normw[:, :], 'par', w=['normw_s'])
    kb.dma('sp', adab_s[:], adab[:, :], 'par', w=['adab_s'])
    kb.dma('sp', fnw_s[:], fnw_d[:, :], 'par', w=['fnw_s'])
    kb.dma('sp', sc[:], cT[:, :], 'par', w=['sc'])
    for i in odd_ids:
        kb.dma('pool', wba_s[i][:], wba_d[i], 'parp', w=[f'wba{i}'])
        kb.dma('sp', convw_s[i][:], convw_d[i], 'par', w=[f'convw{i}'])
        kb.dma('sp', negA[i][:], alog_d[i], 'par', w=[f'negA{i}'])
        kb.dma('sp', dtb_s[i][:], dtb_d[i], 'par', w=[f'dtb{i}'])
        kb.dma('sp', gnw_s[i][:], gnw_d[i], 'par', w=[f'gnw{i}'])
    for i in even_ids:
        kb.dma('sp', rp[i][:], rp_d[i], 'par', w=[f'rp{i}'])
        kb.dma('sp', lnw_s[i][:], lnw_d[i], 'par', w=[f'lnw{i}'])
        kb.dma('sp', lnb_s[i][:], lnb_d[i], 'par', w=[f'lnb{i}'])
        kb.dma('pool', wup_b[i][:], wup_d[i], 'parp', w=[f'wup{i}'])
        kb.dma('pool', aup_b[i][64:128, :], aup_d[i], 'parp', w=[f'aup{i}'])
        kb.dma('pool', gup_b[i][:], gup_d[i], 'parp', w=[f'gup{i}'])
        kb.dma('pool', gluw_b[i][:], gluw_d[i], 'parp', w=[f'gluw{i}'])
    kb.barrier()
    kb.cp('dve', identb[:], cst[:, 0:128], r=['cst'], w=['identb'])
    kb.cp('dve', bonesb[:], cst[:, 576:704], r=['cst'], w=['bonesb'])
    for i in even_ids:
        kb.ts('dve', omm[i][:], rp[i][:, 0:14], -1.0, ALU.mult, 1.0, ALU.add, r=[f'rp{i}'], w=[f'omm{i}'])
        kb.op('pool', lambda E, i=i: E.memset(Hst[i][:], 0.0), w=[f'H{i}'])
        kb.op('pool', lambda E, i=i: E.memset(Hb[i][:], 0.0), w=[f'Hb{i}'])
        kb.op('pool', lambda E, i=i: E.memset(tcarry[i][:], 0.0), w=[f'tcarry{i}'])
    kb.op('dve', lambda E: E.memset(onesb[:], 1.0), w=['onesb'])
    kb.op('dve', lambda E: E.memset(onesf[:], 1.0), w=['onesf'])
    for i in odd_ids:
        kb.act(negA[i][:], negA[i][:], AF.Exp, r=[f'negA{i}'], w=[f'negA{i}'])
        kb.ts('dve', negA[i][:], negA[i][:], -1.0, ALU.mult, r=[f'negA{i}'], w=[f'negA{i}'])
        kb.op('pool', lambda E, i=i: E.memset(Sst[i][:], 0.0), w=[f'S{i}'])
        kb.op('pool', lambda E, i=i: E.memset(Sb[i][:], 0.0), w=[f'Sb{i}'])
        kb.op('pool', lambda E, i=i: E.memset(carry[i][:], 0.0), w=[f'carry{i}'])

    def s5_tables(i):
        with ExitStack() as st:
            K_ = ['s5tab']
            sa = kb.sb("s5a_s", [128, 96 + 1024], F32, st)
            kb.dma('sp', sa[:], s5a_d[i], 's5a', w=K_)
            cnt = [0]
            CJlo = {i: kb.sb("CJlo_t", [128, 4, 4, 9, 32], BF16, st)}
            CJhi = {i: kb.sb("CJhi_t", [128, 4, 4, 9, 64], BF16, st)}
            T1 = {i: kb.sb("T1_t", [128, 32, 64], F32, st)}
            T2 = {i: kb.sb("T2_t", [128, 32, 64], F32, st)}
            T3 = {i: kb.sb("T3_t", [128, 32, 64], F32, st)}
            T4 = {i: kb.sb("T4_t", [128, 32, 64], F32, st)}

            cur = [st]

            def T(shape):
                cnt[0] += 1
                return kb.sb(f"s5t{cnt[0]}", shape, F32, cur[0])

            def tt(o, a, b, op, e='dve'):
                kb.tt(e, o, a, b, op, r=K_, w=K_)

            def ts(o, a, s1, op0, s2=None, op1=None):
                kb.ts('dve', o, a, s1, op0, s2, op1, r=K_, w=K_)

            def cmul(orr, oi, ar, ai, br, bi, t1, t2):
                tt(t1, ar, br, ALU.mult)
                tt(t2, ai, bi, ALU.mult)
                tt(orr, t1, t2, ALU.subtract)
                tt(t1, ar, bi, ALU.mult)
                tt(t2, ai, br, ALU.mult)
                tt(oi, t1, t2, ALU.add)

            def lam_bar(lre_in, lim_in, lst_in, shape):
                lre = T(shape); stp = T(shape); ar = T(shape); ai = T(shape)
                cr = T(shape); si = T(shape); t1 = T(shape); t2 = T(shape); rho = T(shape)
                ts(lre[:], lre_in, -1e-4, ALU.min)
                kb.act(stp[:], lst_in, AF.Exp, r=K_, w=K_)
                tt(ar[:], lre[:], stp[:], ALU.mult)
                tt(ai[:], lim_in, stp[:], ALU.mult)
                kb.act(rho[:], ar[:], AF.Exp, r=K_, w=K_)
                kb.act(si[:], ai[:], AF.Sin, scale=1.0 / 16, r=K_, w=K_)
                ts(t1[:], ai[:], 1.0 / 16, ALU.mult, float(np.pi / 2), ALU.add)
                kb.act(cr[:], t1[:], AF.Sin, r=K_, w=K_)
                for _ in range(4):
                    tt(t1[:], cr[:], cr[:], ALU.mult)
                    tt(t2[:], si[:], si[:], ALU.mult)
                    tt(ar[:], cr[:], si[:], ALU.mult)
                    tt(cr[:], t1[:], t2[:], ALU.subtract)
                    ts(si[:], ar[:], 2.0, ALU.mult)
                lr = T(shape); li = T(shape)
                tt(lr[:], cr[:], rho[:], ALU.mult)
                tt(li[:], si[:], rho[:], ALU.mult)
                return lr, li, rho, cr, si, lre

            sh = [128, 32]
            lr, li, rho, ur, ui, _ = lam_bar(sa[:, 0:32], sa[:, 32:64], sa[:, 64:96], sh)
            t1 = T(sh); t2 = T(sh)
            tt(t1[:], rho[:], rho[:], ALU.mult)
            tt(t2[:], t1[:], t1[:], ALU.mult)
            tt(rho8[i][:], t2[:], t2[:], ALU.mult)
            Lr = T([128, 32, 9]); Li = T([128, 32, 9])
            kb.op('dve', lambda E: E.memset(Lr[:, :, 0], 1.0), r=K_, w=K_)
            kb.op('dve', lambda E: E.memset(Li[:, :, 0], 0.0), r=K_, w=K_)
            for j in range(8):
                cmul(Lr[:, :, j + 1], Li[:, :, j + 1], Lr[:, :, j], Li[:, :, j], lr[:], li[:], t1[:], t2[:])
            kb.op('pool', lambda E: E.memset(CJlo[i][:], 0.0), r=K_, w=K_)
            kb.op('pool', lambda E: E.memset(CJhi[i][:], 0.0), r=K_, w=K_)
            Cr = sa[:, 96:96 + 512].rearrange("p (g c) -> p g c", c=16)
            Ci = sa[:, 96 + 512:96 + 1024].rearrange("p (g c) -> p g c", c=16)
            d1 = T([128, 32, 16]); d2 = T([128, 32, 16]); dr = T([128, 32, 16]); di = T([128, 32, 16])
            for j in range(9):
                lrj = bc(Lr[:, :, j:j + 1], [128, 32, 16])
                lij = bc(Li[:, :, j:j + 1], [128, 32, 16])
                tt(d1[:], Cr, lrj, ALU.mult)
                tt(d2[:], Ci, lij, ALU.mult)
                tt(dr[:], d1[:], d2[:], ALU.subtract)
                tt(d1[:], Cr, lij, ALU.mult)
                tt(d2[:], Ci, lrj, ALU.mult)
                tt(di[:], d1[:], d2[:], ALU.add)
                ts(di[:], di[:], -1.0, ALU.mult)
                drv = dr[:].rearrange("p (q gl) c -> p q gl c", gl=8)
                div = di[:].rearrange("p (q gl) c -> p q gl c", gl=8)
                for gl in range(8):
                    if gl < 4:
                        dst = lambda ps_: CJlo[i][ps_, :, gl, j, (gl % 2) * 16:(gl % 2) * 16 + 16]
                    else:
                        dst = lambda ps_: CJhi[i][ps_, :, gl - 4, j, (gl - 4) * 16:(gl - 4) * 16 + 16]
                    kb.cp('dve', dst(slice(0, 64)), drv[0:64, :, gl, :], r=K_, w=K_)
                    kb.cp('dve', dst(slice(64, 128)), div[64:128, :, gl, :], r=K_, w=K_)
            er = T(sh); ei = T(sh)
            kb.cp('dve', er[:], ur[:], r=K_, w=K_)
            kb.cp('dve', ei[:], ui[:], r=K_, w=K_)
            for _ in range(3):
                tt(t1[:], er[:], er[:], ALU.mult)
                tt(t2[:], ei[:], ei[:], ALU.mult)
                tt(ei[:], er[:], ei[:], ALU.mult)
                tt(er[:], t1[:], t2[:], ALU.subtract)
                ts(ei[:], ei[:], 2.0, ALU.mult)
            Pr = T3[i]
            Pi = T([128, 32, 64])
            kb.cp('dve', Pr[:, :, 0], er[:], r=K_, w=K_)
            kb.cp('dve', Pi[:, :, 0], ei[:], r=K_, w=K_)
            p1 = T([128, 32, 32]); p2 = T([128, 32, 32])
            m = 1
            while m < 64:
                br_ = bc(Pr[:, :, m - 1:m], [128, 32, m])
                bi_ = bc(Pi[:, :, m - 1:m], [128, 32, m])
                cmul(Pr[:, :, m:2 * m], Pi[:, :, m:2 * m], Pr[:, :, 0:m], Pi[:, :, 0:m], br_, bi_,
                     p1[:, :, 0:m], p2[:, :, 0:m])
                m *= 2
            l7r = bc(Lr[:, :, 7:8], [128, 32, 64])
            l7i = bc(Li[:, :, 7:8], [128, 32, 64])
            tt(T1[i][:], Pr[:], l7r, ALU.mult)
            tt(T2[i][:], Pi[:], l7i, ALU.mult)
            tt(T1[i][:], T1[i][:], T2[i][:], ALU.add)
            tt(T2[i][:], Pr[:], l7i, ALU.mult)
            tt(T4[i][:], Pi[:], l7r, ALU.mult)
            tt(T2[i][:], T2[i][:], T4[i][:], ALU.subtract)
            ts(T2[i][64:128], T2[i][64:128], -1.0, ALU.mult)
            kb.cp('dve', T4[i][0:64], Pi[0:64], r=K_, w=K_)
            ts(T4[i][64:128], Pi[64:128], -1.0, ALU.mult)
            kb.dma('sp', CJlo_d[i], CJlo[i][:], 'tabst', r=K_)
            kb.dma('sp', CJhi_d[i], CJhi[i][:], 'tabst', r=K_)
            for k_, t_ in enumerate((T1, T2, T3, T4)):
                kb.dma('sp', TT_d[k_][i], t_[i][:], 'tabst', r=K_)
            kb.barrier()
        with ExitStack() as st:
            cnt[0] += 1000
            cur[0] = st
            BJtab = {i: kb.sb("BJ_t", [128, 4, 2, 8, 128], BF16, st)}
            sbb = kb.sb("s5b_s", [128, 1028], F32, st)
            kb.dma('sp', sbb[:], s5b_d[i], 's5b', w=K_)
            shb = [128, 4, 64]
            v = lambda a: sbb[:, a * 256:(a + 1) * 256].rearrange("p (q n) -> p q n", n=64)
            lstb = bc(sbb[:, 1024:1028].unsqueeze(2), shb)
            blr, bli, brho, _, _, blre = lam_bar(v(0), v(1), lstb, shb)
            bt1 = T(shb); bt2 = T(shb); den = T(shb); kr = T(shb); ki = T(shb); ir = T(shb); ii = T(shb)
            nr = T(shb)
            ts(nr[:], blr[:], -1.0, ALU.add)
            tt(bt1[:], blre[:], blre[:], ALU.mult)
            tt(bt2[:], v(1), v(1), ALU.mult)
            tt(den[:], bt1[:], bt2[:], ALU.add)
            kb.op('dve', lambda E: E.reciprocal(out=den[:], in_=den[:]), r=K_, w=K_)
            tt(bt1[:], nr[:], blre[:], ALU.mult)
            tt(bt2[:], bli[:], v(1), ALU.mult)
            tt(kr[:], bt1[:], bt2[:], ALU.add)
            tt(kr[:], kr[:], den[:], ALU.mult)
            tt(bt1[:], bli[:], blre[:], ALU.mult)
            tt(bt2[:], nr[:], v(1), ALU.mult)
            tt(ki[:], bt1[:], bt2[:], ALU.subtract)
            tt(ki[:], ki[:], den[:], ALU.mult)
            tt(bt1[:], brho[:], brho[:], ALU.mult)
            kb.op('dve', lambda E: E.reciprocal(out=bt1[:], in_=bt1[:]), r=K_, w=K_)
            tt(ir[:], blr[:], bt1[:], ALU.mult)
            tt(ii[:], bli[:], bt1[:], ALU.mult)
            ts(ii[:], ii[:], -1.0, ALU.mult)
            gr = T(shb); gi = T(shb); g2r = T(shb); g2i = T(shb); vr = T(shb); vi = T(shb)
            kb.cp('dve', gr[:], kr[:], r=K_, w=K_)
            kb.cp('dve', gi[:], ki[:], r=K_, w=K_)
            pm = cst[:, 1216:1218]
            for j in range(8):
                cmul(vr[:], vi[:], gr[:], gi[:], v(2), v(3), bt1[:], bt2[:])
                for e in range(2):
                    ts(BJtab[i][:, :, e, j, 0:64], vr[:], pm[:, e:e + 1], ALU.mult)
                    ts(BJtab[i][:, :, e, j, 64:128], vi[:], pm[:, e:e + 1], ALU.mult)
                if j < 7:
                    cmul(g2r[:], g2i[:], gr[:], gi[:], ir[:], ii[:], bt1[:], bt2[:])
                    gr, g2r = g2r, gr
                    gi, g2i = g2i, gi
            kb.op('pool', lambda E: E.memset(s5carry[i][:], 0.0), r=K_, w=K_)
            kb.dma('sp', BJ_d[i], BJtab[i][:], 'tabst', r=K_)
            kb.barrier()

    for i in even_ids:
        s5_tables(i)

    kb.act(sc[:], sc[:], AF.Silu, r=['sc'], w=['sc'])
    with ExitStack() as st:
        abuf = [kb.sb(f"abuf{i}", [128, 3 * D], F32, st) for i in range(2)]
        macc = kb.sb("macc", [128, 24], F32, st)
        n = 0
        for l in layers:
            for k in range(8):
                b = abuf[n % 2]
                kb.dma('sp', b[:], adaw[l, k * 128:(k + 1) * 128, :], f'ab{n % 2}', w=[f'abuf{n % 2}'])
                for j in range(24):
                    kb.mm(psC[:, j:j + 1], b[:, j * 128:(j + 1) * 128], sc[:, k:k + 1],
                          r=[f'abuf{n % 2}', 'sc'], w=['psC'])
                if k == 0:
                    kb.tt('dve', macc[:], psC[:, 0:24], adab_s[:, l * 24:(l + 1) * 24], ALU.add,
                          r=['psC', 'adab_s'], w=['macc'])
                else:
                    kb.tt('dve', macc[:], psC[:, 0:24], macc[:], ALU.add, r=['psC', 'macc'], w=['macc'])
                n += 1
            kb.cp('dve', modT[:, l, :], macc[:], r=['macc'], w=['modT'])
            kb.ts('dve', s1[:, l, :], modT[:, l, 8:16], 1.0, ALU.add, r=['modT'], w=['s1'])
            kb.tt('dve', s1[:, l, :], s1[:, l, :], normw_s[:, l * 8:(l + 1) * 8], ALU.mult,
                  r=['s1', 'normw_s'], w=['s1'])
        kb.barrier()

    wctr = [0]

    def load_w(src_ap):
        i = wctr[0] % 4
        wctr[0] += 1
        kb.dma('pool', wbuf[i][:], src_ap, f'w{i}', w=[f'wbuf{i}'])
        return wbuf[i], f'wbuf{i}'

    ictr = [0]

    def proj_chunk(src_ap, rhs_tile, rhs_key):
        wb, wk = load_w(src_ap)
        i = ictr[0] % 2
        ictr[0] += 1
        for k in range(8):
            kb.mm(psI[i][:], wb[:, k, :], rhs_tile[:, k, :], start=(k == 0), stop=(k == 7),
                  r=[wk, rhs_key], w=[f'psI{i}'])
        return psI[i], f'psI{i}'

    def rms_stats(src, src_key):
        for k in range(8):
            q = sqb[k % 2]
            kb.act(q[:], src[:, k, :], AF.Square, r=[src_key], w=[f'sqb{k % 2}'])
            kb.mm(psC[:, :], onesb[:], q[:], start=(k == 0), stop=(k == 7), r=[f'sqb{k % 2}', 'onesb'], w=['psC'])
        kb.act(rstd[:], psC[:], AF.Sqrt, scale=1.0 / D, bias=EPS, r=['psC'], w=['rstd'])
        kb.op('dve', lambda E: E.reciprocal(out=rstd[:], in_=rstd[:]), r=['rstd'], w=['rstd'])

    def pre_norm(l):
        rms_stats(xres, 'xres')
        for k in range(8):
            t = tmpf[k % 2]
            kb.tt('dve', t[:], xres[:, k, :], rstd[:], ALU.mult, r=['xres', 'rstd'], w=[f'tmpf{k % 2}'])
            kb.act(hT[:, k, :], t[:], AF.Identity, scale=s1[:, l, k:k + 1], bias=modT[:, l, k:k + 1],
                   r=[f'tmpf{k % 2}', 's1', 'modT'], w=['hT'])

    def out_proj(l):
        for dm in range(8):
            ps, pk = proj_chunk(woutr[l, dm], ygT, 'ygT')
            kb.stt(xres[:, dm, :], ps[:], modT[:, l, 16 + dm:17 + dm], xres[:, dm, :], ALU.mult, ALU.add,
                   r=[pk, 'modT', 'xres'], w=['xres'])

    def neumann(Pf, Pb, PTb, Accf, Accb, PTf, Pw):
        f2 = lambda t: t[:].rearrange("p h c -> p (h c)")
        kb.tt('pool', Accf[:], Pf[:], bc(I64.unsqueeze(1), [64, 8, 64]), ALU.add, r=['Pf', 'cst'], w=['Accf'])
        for h in range(8):
            kb.tr(psC[0:64, h * 64:(h + 1) * 64], Pf[:, h, :], cst[0:64, 0:64], r=['Pf', 'cst'], w=['psC'])
        kb.cp('act', f2(PTf), psC[0:64, 0:512], r=['psC'], w=['PTf'])
        cur, curk = Pf, 'Pf'
        oth, othk = Pw, 'Pw'
        for lev in range(5):
            last = (lev == 4)
            if not last:
                for h in range(8):
                    kb.mm(psB[0:64, h * 64:(h + 1) * 64], PTf[:, h, :], cur[:, h, :], r=['PTf', curk], w=['psB0'])
            for h in range(8):
                kb.mm(psB[0:64, 512 + h * 64:512 + (h + 1) * 64], cur[:, h, :], PTf[:, h, :], r=['PTf', curk], w=['psB1'])
            if not last:
                kb.cp('act', f2(oth), psB[0:64, 0:512], r=['psB0'], w=[othk])
            kb.cp('dve', f2(PTf), psB[0:64, 512:1024], r=['psB1'], w=['PTf'])
            cur, curk, oth, othk = oth, othk, cur, curk
            for h in range(8):
                kb.mm(psC[0:64, h * 64:(h + 1) * 64], PTf[:, h, :], Accf[:, h, :], r=['PTf', 'Accf'], w=['psC'])
            kb.tt('dve', f2(Accf), psC[0:64, :], f2(Accf), ALU.add, r=['psC', 'Accf'], w=['Accf'])
        kb.cp('act', Accb[:], Accf[:], r=['Accf'], w=['Accb'])

    def odd_layer(l, bi):
        i = l // 2
        with ExitStack() as st:
            acc = kb.sb("acc", [128, 8, TB], F32, st)
            qkn = kb.sb("qkn", [128, 16, TB], BF16, st)
            vb = kb.sb("vb", [128, 8, TB], BF16, st)
            ba = kb.sb("ba", [64, 8, 16], F32, st)
            beta = kb.sb("beta", [64, 8, 8], F32, st)
            aall = kb.sb("aall", [64, 8, 8], F32, st)
            eg = kb.sb("eg", [64, 8, 8], F32, st)
            eend = kb.sb("eend", [64, 8, 8], F32, st)
            cdb = kb.sb("cdb", [128, 8, 8], F32, st)
            R2 = kb.sb("R2", [64, 8, 64], F32, st)
            decT = kb.sb("decT", [64, 8, 64], F32, st)
            dSb = kb.sb("dSb", [64, 8, 64], F32, st)
            dI = kb.sb("dI", [64, 8, 64], F32, st)
            Pf = kb.sb("Pf", [64, 8, 64], F32, st)
            Pb = PTb = None
            Accf = kb.sb("Accf", [64, 8, 64], F32, st)
            PTf = kb.sb("PTf", [64, 8, 64], F32, st)
            Pw = kb.sb("Pw", [64, 8, 64], F32, st)
            Accb = kb.sb("Accb", [64, 8, 64], BF16, st)
            intraT = kb.sb("intraT", [64, 8, 64], BF16, st)
            vk = kb.sb("vk", [64, 8, 256], BF16, st)
            kend = kb.sb("kend", [64, 8, 128], BF16, st)
            uu = kb.sb("uu", [64, 8, 128], F32, st)
            ww = kb.sb("ww", [64, 8, 128], BF16, st)
            wTb = kb.sb("wTb", [128, 8, 64], BF16, st)
            vnew = kb.sb("vnew", [64, 8, 128], BF16, st)
            o1 = kb.sb("o1", [64, 8, 128], F32, st)
            oo = kb.sb("oo", [64, 8, 128], F32, st)
            osq = kb.sb("osq", [64, 8, 128], F32, st)
            ss = kb.sb("ss", [64, 8], F32, st)
            onb = kb.sb("onb", [64, 8, 128], BF16, st)
            cnew = kb.sb("cnew", [128, 24, 3], F32, st)

            pre_norm(l)
            for m in range(8):
                ps, pk = proj_chunk(winodd[i, 24 + m], hT, 'hT')
                kb.act(zs[:, m, :], ps[:], AF.Silu, r=[pk], w=['zs'])
            if dbg:
                kb.op('pool', lambda E: E.memset(zs[:], 1.0), r=['zs'], w=['zs'])
            cw = convw_s[i]
            cr = carry[i]
            for grp in range(3):
                gsl = slice(grp * 8, grp * 8 + 8)
                accall = [f'acc{mm_}' for mm_ in range(8)]
                for mm_ in range(8):
                    m = grp * 8 + mm_
                    ps, pk = proj_chunk(winodd[i, m], hT, 'hT')
                    am = f'acc{mm_}'
                    kb.act(acc[:, mm_, :], ps[:], AF.Copy, scale=cw[:, m, 3:4], r=[pk, f'convw{i}'], w=[am])
                    kb.cp('act', cnew[:, m, :], ps[:, TB - 3:TB], r=[pk], w=['cnew'])
                    for j in range(3):
                        sh = 3 - j
                        kb.stt(acc[:, mm_, sh:TB], ps[:, 0:TB - sh], cw[:, m, j:j + 1], acc[:, mm_, sh:TB],
                               ALU.mult, ALU.add, r=[pk, f'convw{i}', am], w=[am])
                for j in range(3):
                    n = 3 - j
                    tv = tmpf[0][:, 0:8 * n].rearrange("p (m n) -> p m n", n=n)
                    kb.tt('dve', tv, cr[:, gsl, j:3], bc(cw[:, gsl, j:j + 1], [128, 8, n]), ALU.mult,
                          r=[f'carry{i}', f'convw{i}'], w=['tmpf0'])
                    kb.tt('dve', acc[:, :, 0:n], acc[:, :, 0:n], tv, ALU.add, r=['tmpf0'] + accall, w=accall)
                kb.cp('dve', cr[:, gsl, :], cnew[:, gsl, :], r=['cnew'], w=[f'carry{i}'])
                kb.act(acc[:], acc[:], AF.Silu, r=accall, w=accall)
                if grp == 2:
                    kb.cp('pool', vb[:], acc[:], r=accall, w=['vb'])
                    continue
                for mm_ in range(8):
                    m = grp * 8 + mm_
                    q = sqb[m % 2]
                    kb.act(q[:], acc[:, mm_, :], AF.Square, r=[f'acc{mm_}'], w=[f'sqb{m % 2}'])
                    kb.mm(psC[:], onesb[:], q[:], r=[f'sqb{m % 2}', 'onesb'], w=['psC'])
                    t = tmpf[m % 2]
                    kb.act(t[:], psC[:], AF.Sqrt, bias=EPS, r=['psC'], w=[f'tmpf{m % 2}'])
                    kb.op('dve', lambda E, t=t: E.reciprocal(out=t[:], in_=t[:]), r=[f'tmpf{m % 2}'], w=[f'tmpf{m % 2}'])
                    kb.stt(qkn[:, m, :], acc[:, mm_, :], (128.0 ** -0.5) if m < 8 else 1.0, t[:], ALU.mult, ALU.mult,
                           r=[f'acc{mm_}', f'tmpf{m % 2}'], w=['qkn'])
            for j in range(NCH):
                for k in range(8):
                    kb.mm(psC[0:64, j * 16:(j + 1) * 16], hT[:, k, j * C:(j + 1) * C], wba_s[i][:, k, :],
                          start=(k == 0), stop=(k == 7), r=['hT', f'wba{i}'], w=['psC'])
            kb.cp('dve', ba[:], psC[0:64, 0:128].rearrange("p (j e) -> p j e", e=16), r=['psC'], w=['ba'])
            kb.act(beta[:], ba[:, :, 0:8], AF.Sigmoid, r=['ba'], w=['beta'])
            kb.tt('dve', aall[:], ba[:, :, 8:16], bc(dtb_s[i][0:64, :].unsqueeze(1), [64, 8, 8]), ALU.add,
                  r=['ba', f'dtb{i}'], w=['aall'])
            kb.act(aall[:], aall[:], AF.Exp, r=['aall'], w=['aall'])
            kb.act(aall[:], aall[:], AF.Ln, bias=1.0, r=['aall'], w=['aall'])
            kb.tt('dve', aall[:], aall[:], bc(negA[i][0:64, :].unsqueeze(1), [64, 8, 8]), ALU.mult,
                  r=['aall', f'negA{i}'], w=['aall'])
            a2 = aall[:].rearrange("p j h -> p (j h)")
            kb.mm(psI[0][0:64, 0:64], U, a2, r=['cst', 'aall'], w=['psI0'])
            kb.act(eg[:].rearrange("p j h -> p (j h)"), psI[0][0:64, 0:64], AF.Exp, r=['psI0'], w=['eg'])
            kb.mm(psI[1][0:64, 0:64], Lst, a2, r=['cst', 'aall'], w=['psI1'])
            kb.act(eend[:].rearrange("p j h -> p (j h)"), psI[1][0:64, 0:64], AF.Exp, r=['psI1'], w=['eend'])
            kb.mm(psI[0][:, 64:128], onesf[:], a2, r=['onesf', 'aall'], w=['psI0'])
            kb.act(cdb[:].rearrange("p j h -> p (j h)"), psI[0][:, 64:128], AF.Exp, r=['psI0'], w=['cdb'])

            S = Sst[i]
            Sbf = Sb[i]
            for j in range(NCH):
                cs = slice(j * C, (j + 1) * C)
                be = beta[:, j, :]
                kb.tt('dve', R2[:], bc(U.unsqueeze(1), [64, 8, 64]), bc(aall[:, j, :].unsqueeze(2), [64, 8, 64]), ALU.mult,
                      r=['cst', 'aall'], w=['R2'])
                kb.mm(psC[0:64, :], Lst, R2[:].rearrange("p h c -> p (h c)"), r=['cst', 'R2'], w=['psC'])
                kb.act(decT[:].rearrange("p h c -> p (h c)"), psC[0:64, :], AF.Exp, r=['psC'], w=['decT'])
                kb.tt('pool', dSb[:], decT[:], bc(MsN.unsqueeze(1), [64, 8, 64]), ALU.mult, r=['decT', 'cst'], w=['dSb'])
                kb.tt('pool', dSb[:], dSb[:], bc(be.unsqueeze(2), [64, 8, 64]), ALU.mult, r=['dSb', 'beta'], w=['dSb'])
                kb.tt('pool', dI[:], decT[:], bc(Mi.unsqueeze(1), [64, 8, 64]), ALU.mult, r=['decT', 'cst'], w=['dI'])
                for h in range(8):
                    kb.mm(psA[0:64, h * 64:(h + 1) * 64], qkn[:, 8 + h, cs], qkn[:, 8 + h, cs], r=['qkn'], w=['psA0'])
                for h in range(8):
                    kb.mm(psA[0:64, 512 + h * 64:512 + (h + 1) * 64], qkn[:, 8 + h, cs], qkn[:, h, cs], r=['qkn'], w=['psA1'])
                kb.tt('dve', Pf[:].rearrange("p h c -> p (h c)"), psA[0:64, 0:512], dSb[:].rearrange("p h c -> p (h c)"),
                      ALU.mult, r=['psA0', 'dSb'], w=['Pf'])
                kb.tt('dve', intraT[:].rearrange("p h c -> p (h c)"), psA[0:64, 512:1024],
                      dI[:].rearrange("p h c -> p (h c)"), ALU.mult, r=['psA1', 'dI'], w=['intraT'])
                neumann(Pf, Pb, PTb, Accf, Accb, PTf, Pw)
                for h in range(8):
                    kb.tr(psT[0:64, h * 128:(h + 1) * 128], qkn[:, 8 + h, cs], identb[:], r=['qkn', 'identb'], w=['psT'])
                pT3 = psT[0:64, :].rearrange("p (h d) -> p h d", d=128)
                kb.tt('dve', vk[:, :, 128:256], pT3, bc(eg[:, j, :].unsqueeze(2), [64, 8, 128]), ALU.mult,
                      r=['psT', 'eg'], w=['vk1'])
                kb.tt('dve', kend[:], pT3, bc(eend[:, j, :].unsqueeze(2), [64, 8, 128]), ALU.mult,
                      r=['psT', 'eend'], w=['kend'])
                for h in range(8):
                    kb.tr(psT[0:64, h * 128:(h + 1) * 128], vb[:, h, cs], identb[:], r=['vb', 'identb'], w=['psT'])
                kb.cp('act', vk[:, :, 0:128], pT3, r=['psT'], w=['vk0'])
                for h in range(8):
                    kb.mm(psA[0:64, h * 128:(h + 1) * 128], Accb[:, h, :], vk[:, h, 0:128], r=['Accb', 'vk0'],
                          w=['psA0', 'psA1'])
                for h in range(8):
                    kb.mm(psB[0:64, h * 128:(h + 1) * 128], Accb[:, h, :], vk[:, h, 128:256], r=['Accb', 'vk1'],
                          w=['psB0', 'psB1'])
                pA3 = psA[0:64, :].rearrange("p (h d) -> p h d", d=128)
                pB3 = psB[0:64, :].rearrange("p (h d) -> p h d", d=128)
                bet3 = bc(be.unsqueeze(2), [64, 8, 128])
                kb.tt('dve', uu[:], pA3, bet3, ALU.mult, r=['psA0', 'psA1', 'beta'], w=['uu'])
                kb.tt('dve', ww[:], pB3, bet3, ALU.mult, r=['psB0', 'psB1', 'beta'], w=['ww'])
                for h in range(8):
                    kb.tr(psT[:, h * 64:(h + 1) * 64], ww[:, h, :], identb[0:64, 0:64], r=['ww', 'identb'], w=['psT'])
                kb.cp('act', wTb[:].rearrange("p h c -> p (h c)"), psT[:, 0:512], r=['psT'], w=['wTb'])
                for h in range(8):
                    kb.mm(psA[0:64, h * 128:(h + 1) * 128], wTb[:, h, :], Sbf[:, h, :], r=['wTb', f'Sb{i}'],
                          w=['psA0', 'psA1'])
                kb.tt('dve', vnew[:], uu[:], pA3, ALU.subtract, r=['uu', 'psA0', 'psA1'], w=['vnew'])
                for h in range(8):
                    kb.mm(psB[0:64, h * 128:(h + 1) * 128], qkn[:, h, cs], Sbf[:, h, :], r=['qkn', f'Sb{i}'],
                          w=['psB0', 'psB1'])
                kb.tt('dve', o1[:], pB3, bc(eg[:, j, :].unsqueeze(2), [64, 8, 128]), ALU.mult,
                      r=['psB0', 'psB1', 'eg'], w=['o1'])
                for h in range(8):
                    kb.mm(psA[0:64, h * 128:(h + 1) * 128], intraT[:, h, :], vnew[:, h, :], r=['intraT', 'vnew'],
                          w=['psA0', 'psA1'])
                kb.tt('dve', oo[:], pA3, o1[:], ALU.add, r=['psA0', 'psA1', 'o1'], w=['oo'])
                for h in range(8):
                    kb.mm(psB[:, h * 128:(h + 1) * 128], kend[:, h, :], vnew[:, h, :], r=['kend', 'vnew'],
                          w=['psB0', 'psB1'])
                kb.tt('pool', S[:], S[:], bc(cdb[:, j, :].unsqueeze(2), [128, 8, 128]), ALU.mult, r=[f'S{i}', 'cdb'], w=[f'S{i}'])
                kb.tt('dve', S[:], S[:], psB[:, :].rearrange("p (h d) -> p h d", d=128), ALU.add,
                      r=[f'S{i}', 'psB0', 'psB1'], w=[f'S{i}'])
                kb.cp('act', Sbf[:], S[:], r=[f'S{i}'], w=[f'Sb{i}'])
                kb.tt('pool', osq[:], oo[:], oo[:], ALU.mult, r=['oo'], w=['osq'])
                kb.op('dve', lambda E: E.tensor_reduce(out=ss[:], in_=osq[:], axis=AX.X, op=ALU.add), r=['osq'], w=['ss'])
                kb.act(ss[:], ss[:], AF.Sqrt, scale=1.0 / 128, bias=EPS, r=['ss'], w=['ss'])
                kb.op('dve', lambda E: E.reciprocal(out=ss[:], in_=ss[:]), r=['ss'], w=['ss'])
                kb.tt('pool', osq[:], oo[:], bc(ss[:].unsqueeze(2), [64, 8, 128]), ALU.mult, r=['oo', 'ss'], w=['osq'])
                kb.tt('pool', onb[:], osq[:], bc(gnw_s[i][0:64, :].unsqueeze(1), [64, 8, 128]), ALU.mult,
                      r=['osq', f'gnw{i}'], w=['onb'])
                for h in range(8):
                    kb.tr(psT[:, h * 64:(h + 1) * 64], onb[:, h, :], identb[0:64, 0:64], r=['onb', 'identb'], w=['psT'])
                kb.tt('dve', ygT[:, :, cs], psT[:, 0:512].rearrange("p (h c) -> p h c", c=64), zs[:, :, cs], ALU.mult,
                      r=['psT', 'zs'], w=['ygT'])
            if not dbg:
                out_proj(l)
        kb.barrier()

    RW_DS = float(np.exp(-0.5))
    LN_EPS = 1e-5 * 64

    def rwkv_phase(l, i, bi):
        P = rp[i]
        rk_ = f'rp{i}'
        with ExitStack() as st:
            gbuf = kb.sb("gbuf", [128, TB + 1], F32, st)
            tw = kb.sb("tw", [64, TB], BF16, st)
            xab = kb.sb("xab", [128, TB], BF16, st)
            sg = kb.sb("sg", [128, TB], BF16, st)
            gz = kb.sb("gz", [128, 4, TB], BF16, st)
            bonus = kb.sb("bonus", [128, 4, TB], BF16, st)
            vbT = kb.sb("vbT", [128, 4, TB], BF16, st)
            ops6 = {n: kb.sb("op_" + n, [128, 4, TB], BF16, st) for n in ['At', 'Qt', 'Kh', 'Bh', 'Kb', 'Bb']}
            opo = {n: kb.sb("opo_" + n, [64, 4, C], BF16, st) for n in ['At', 'Qt', 'Kh', 'Bh']}
            GC = kb.sb("GC", [64, NCH, 8], F32, st)
            rt = [kb.sb(f"rt{k}", [128, TB], F32, st) for k in range(12)]
            rf, kf, vf, ldm, aam, kk, kp, bb_, lg, lgx, t1, t2 = rt
            rtk = [f'rt{k}' for k in range(12)]
            krf, kkf, kvf, kld, kaa, kkk, kkp, kbb, klg, klgx, kt1, kt2 = rtk
            fl = [t1, t2]
            flk = [kt1, kt2]
            Pf = kb.sb("rPf", [64, 8, 64], F32, st)
            Pb = PTb = None
            Accf = kb.sb("rAccf", [64, 8, 64], F32, st)
            PTf = kb.sb("rPTf", [64, 8, 64], F32, st)
            Pw = kb.sb("rPw", [64, 8, 64], F32, st)
            Accb = kb.sb("rAccb", [64, 8, 64], BF16, st)
            Mav = kb.sb("Mav", [64, 8, 64], BF16, st)
            Mqk = kb.sb("Mqk", [64, 8, 64], BF16, st)
            MqbN = kb.sb("MqbN", [64, 8, 64], BF16, st)
            Vt = kb.sb("Vt", [64, 512], BF16, st)
            Kbt = kb.sb("Kbt", [64, 512], BF16, st)
            BbtN = kb.sb("BbtN", [64, 512], BF16, st)
            RHSb = kb.sb("RHSb", [64, 512], BF16, st)
            Pb2 = kb.sb("Pb2", [64, 512], BF16, st)
            tmpR = kb.sb("tmpR", [64, 512], F32, st)
            tmpO = kb.sb("tmpO", [64, 512], F32, st)
            oT = kb.sb("oT", [64, 8, 64], F32, st)
            oc = kb.sb("oc", [64, 8, 64], F32, st)
            osq = kb.sb("rosq", [64, 8, 64], F32, st)
            s8 = kb.sb("s8", [64, 8], F32, st)
            s8b = kb.sb("s8b", [64, 8], F32, st)
            onb = kb.sb("ronb", [64, 512], BF16, st)
            t_o = kb.sb("t_o", [128, 4, C], F32, st)

            def shift(ps, pk, mf, out, outk):
                kb.cp('dve', gbuf[:, 0:1], tcarry[i][:, mf:mf + 1], r=[f'tcarry{i}'], w=['gbuf0'])
                kb.act(gbuf[:, 1:TB + 1], ps[:], AF.Copy, scale=P[:, mf:mf + 1], r=[pk, rk_], w=['gbuf'])
                kb.cp('pool', tcarry[i][:, mf:mf + 1], gbuf[:, TB:TB + 1], r=['gbuf'], w=[f'tcarry{i}'])
                kb.stt(out[:], ps[:], omm[i][:, mf:mf + 1], gbuf[:, 0:TB], ALU.mult, ALU.add,
                       r=[pk, f'omm{i}', 'gbuf', 'gbuf0'], w=[outk])

            for k2 in range(2):
                ps, pk = proj_chunk(winev[i, 16 + k2], hT, 'hT')
                shift(ps, pk, 12 + k2, fl[k2], flk[k2])
            kb.act(tw[:], fl[0][0:64, :], AF.Tanh, r=[flk[0]], w=['tw'])
            kb.cp('pool', xab[64:128, :], fl[0][64:128, :], r=[flk[0]], w=['xab'])
            kb.act(sg[:], fl[1][:], AF.Sigmoid, r=[flk[1]], w=['sg'])
            for m in range(4):
                mc = slice(m * 128, (m + 1) * 128)
                kb.mm(psC[:], wup_b[i][0:64, mc], tw[:], r=[f'wup{i}', 'tw'], w=['psC'])
                kb.act(ldm[:], psC[:], AF.Sigmoid, bias=P[:, 14 + m:15 + m], r=['psC', rk_], w=[kld])
                kb.ts('pool', ldm[:], ldm[:], -RW_DS, ALU.mult, r=[kld], w=[kld])
                kb.mm(psC[:], aup_b[i][64:128, mc], xab[64:128, :], r=[f'aup{i}', 'xab'], w=['psC'])
                kb.act(aam[:], psC[:], AF.Sigmoid, bias=P[:, 18 + m:19 + m], r=['psC', rk_], w=[kaa])
                kb.mm(psC[:], gup_b[i][:, mc], sg[:], r=[f'gup{i}', 'sg'], w=['psC'])
                kb.tt('dve', gz[:, m, :], psC[:], zs[:, 4 + m, :], ALU.mult, r=['psC', 'zs'], w=['gz'])
                ps, pk = proj_chunk(winev[i, 4 + m], hT, 'hT')
                shift(ps, pk, m, rf, krf)
                ps, pk = proj_chunk(winev[i, 8 + m], hT, 'hT')
                shift(ps, pk, 4 + m, kf, kkf)
                ps, pk = proj_chunk(winev[i, 12 + m], hT, 'hT')
                shift(ps, pk, 8 + m, vf, kvf)
                kb.act(sqb[0][:], kf[:], AF.Square, scale=P[:, 22 + m:23 + m], r=[kkf, rk_], w=['sqb0'])
                kb.mm(psC[:], bonesb[:], sqb[0][:], r=['bonesb', 'sqb0'], w=['psC'])
                kb.act(t1[:], psC[:], AF.Sqrt, bias=EPS, r=['psC'], w=[kt1])
                kb.op('dve', lambda E: E.reciprocal(out=t1[:], in_=t1[:]), r=[kt1], w=[kt1])
                kb.stt(kk[:], kf[:], P[:, 22 + m:23 + m], t1[:], ALU.mult, ALU.mult, r=[kkf, rk_, kt1], w=[kkk])
                kb.ts('pool', t2[:], aam[:], -1.0, ALU.add, P[:, 26 + m:27 + m], ALU.mult, r=[kaa, rk_], w=[kt2])
                kb.stt(kp[:], t2[:], 1.0, kf[:], ALU.add, ALU.mult, r=[kt2, kkf], w=[kkp])
                kb.tt('pool', bb_[:], kk[:], aam[:], ALU.mult, r=[kkk, kaa], w=[kbb])
                kb.stt(sqb[1][:], rf[:], P[:, 30 + m:31 + m], kp[:], ALU.mult, ALU.mult, r=[krf, rk_, kkp], w=['sqb1'])
                kb.mm(psC[:], bonesb[:], sqb[1][:], r=['bonesb', 'sqb1'], w=['psC'])
                kb.tt('dve', bonus[:, m, :], psC[:], vf[:], ALU.mult, r=['psC', kvf], w=['bonus'])
                kb.cp('act', vbT[:, m, :], vf[:], r=[kvf], w=['vbT'])
                kb.op('dve', lambda E: E.tensor_tensor_scan(out=lg[:], data0=cmask, data1=ldm[:], initial=0.0,
                                                            op0=ALU.mult, op1=ALU.add), r=['cst', kld], w=[klg])
                kb.tt('pool', lgx[:], lg[:], ldm[:], ALU.subtract, r=[klg, kld], w=[klgx])
                lg3 = lg[:].rearrange("p (j c) -> p j c", c=C)
                kb.act(t1[:], lg[:], AF.Exp, r=[klg], w=[kt1])
                kb.tt('dve', ops6['Qt'][:, m, :], rf[:], t1[:], ALU.mult, r=[krf, kt1], w=['op_Qt'])
                kb.act(t1[:], lgx[:], AF.Exp, r=[klgx], w=[kt1])
                kb.tt('dve', ops6['At'][:, m, :], kk[:], t1[:], ALU.mult, r=[kkk, kt1], w=['op_At'])
                kb.act(t1[:], lg[:], AF.Exp, scale=-1.0, r=[klg], w=[kt1])
                kb.tt('dve', ops6['Kh'][:, m, :], kp[:], t1[:], ALU.mult, r=[kkp, kt1], w=['op_Kh'])
                kb.tt('pool', ops6['Bh'][:, m, :], bb_[:], t1[:], ALU.mult, r=[kbb, kt1], w=['op_Bh'])
                kb.tt('dve', t2[:].rearrange("p (j c) -> p j c", c=C), bc(lg3[:, :, C - 1:C], [128, NCH, C]), lg3,
                      ALU.subtract, r=[klg], w=[kt2])
                kb.act(t2[:], t2[:], AF.Exp, r=[kt2], w=[kt2])
                kb.tt('dve', ops6['Kb'][:, m, :], kp[:], t2[:], ALU.mult, r=[kkp, kt2], w=['op_Kb'])
                kb.tt('pool', ops6['Bb'][:, m, :], bb_[:], t2[:], ALU.mult, r=[kbb, kt2], w=['op_Bb'])
                for par in range(2):
                    kb.act(GC[:, :, 2 * m + par], lg3[par * 64:(par + 1) * 64, :, C - 1], AF.Exp, r=[klg], w=['GC'])

            H = Hst[i]
            Hbf = Hb[i]
            opk = ['op_At', 'op_Qt', 'op_Kh', 'op_Bh']
            for j in range(NCH):
                cs = slice(j * C, (j + 1) * C)
                for n in ['At', 'Qt', 'Kh', 'Bh']:
                    kb.cp('dve', opo[n][:], ops6[n][64:128, :, cs], r=['op_' + n], w=['opo_' + n])

                def X(n, h):
                    return ops6[n][0:64, h // 2, cs] if h % 2 == 0 else opo[n][:, h // 2, :]
                xk = lambda *ns: [k for n in ns for k in ('op_' + n, 'opo_' + n)]
                for h in range(8):
                    kb.mm(psA[0:64, h * 64:(h + 1) * 64], X('Bh', h), X('At', h), r=xk('Bh', 'At'), w=['psA0'])
                for h in range(8):
                    kb.mm(psA[0:64, 512 + h * 64:512 + (h + 1) * 64], X('Kh', h), X('At', h), r=xk('Kh', 'At'), w=['psA1'])
                for h in range(8):
                    kb.mm(psB[0:64, h * 64:(h + 1) * 64], X('Kh', h), X('Qt', h), r=xk('Kh', 'Qt'), w=['psB0'])
                for h in range(8):
                    kb.mm(psB[0:64, 512 + h * 64:512 + (h + 1) * 64], X('Bh', h), X('Qt', h), r=xk('Bh', 'Qt'), w=['psB1'])
                f2 = lambda t: t[:].rearrange("p h c -> p (h c)")
                m3 = lambda mk: bc(mk.unsqueeze(1), [64, 8, 64])
                p3 = lambda ap: ap.rearrange("p (h c) -> p h c", c=64)
                kb.tt('dve', Pf[:], p3(psA[0:64, 0:512]), m3(MsN), ALU.mult, r=['psA0', 'cst'], w=['Pf'])
                kb.tt('dve', Mav[:], p3(psA[0:64, 512:1024]), m3(Ms), ALU.mult, r=['psA1', 'cst'], w=['Mav'])
                kb.tt('dve', Mqk[:], p3(psB[0:64, 0:512]), m3(Mi), ALU.mult, r=['psB0', 'cst'], w=['Mqk'])
                kb.tt('dve', MqbN[:], p3(psB[0:64, 512:1024]), m3(MiN), ALU.mult, r=['psB1', 'cst'], w=['MqbN'])
                neumann(Pf, Pb, PTb, Accf, Accb, PTf, Pw)
                for m in range(4):
                    kb.tr(psT[0:64, m * 128:(m + 1) * 128], vbT[:, m, cs], identb[:], r=['vbT', 'identb'], w=['psT'])
                kb.cp('act', Vt[:], psT[0:64, 0:512], r=['psT'], w=['Vt'])
                for h in range(8):
                    kb.mm(psC[0:64, h * 64:(h + 1) * 64], X('At', h), Hbf[:, h, :], r=xk('At') + [f'Hb{i}'], w=['psC'])
                kb.cp('act', tmpR[:], psC[0:64, :], r=['psC'], w=['tmpR'])
                for h in range(8):
                    kb.mm(psI[0][0:64, h * 64:(h + 1) * 64], Mav[:, h, :], Vt[:, h * 64:(h + 1) * 64], r=['Mav', 'Vt'], w=['psI0'])
                kb.tt('dve', RHSb[:], psI[0][0:64, :], tmpR[:], ALU.add, r=['psI0', 'tmpR'], w=['RHSb'])
                for h in range(8):
                    kb.mm(psC[0:64, h * 64:(h + 1) * 64], Accb[:, h, :], RHSb[:, h * 64:(h + 1) * 64], r=['Accb', 'RHSb'], w=['psC'])
                kb.cp('act', Pb2[:], psC[0:64, :], r=['psC'], w=['Pb2'])
                for h in range(8):
                    kb.mm(psI[1][0:64, h * 64:(h + 1) * 64], X('Qt', h), Hbf[:, h, :], r=xk('Qt') + [f'Hb{i}'], w=['psI1'])
                kb.cp('act', tmpO[:], psI[1][0:64, :], r=['psI1'], w=['tmpO'])
                for h in range(8):
                    hs = slice(h * 64, (h + 1) * 64)
                    kb.mm(psI[0][0:64, hs], Mqk[:, h, :], Vt[:, hs], start=True, stop=False, r=['Mqk', 'Vt'], w=['psI0'])
                    kb.mm(psI[0][0:64, hs], MqbN[:, h, :], Pb2[:, hs], start=False, stop=True, r=['MqbN', 'Pb2'], w=['psI0'])
                kb.tt('dve', f2(oT), psI[0][0:64, :], tmpO[:], ALU.add, r=['psI0', 'tmpO'], w=['oT'])
                for m in range(4):
                    kb.tr(psT[0:64, m * 128:(m + 1) * 128], ops6['Kb'][:, m, cs], identb[:], r=['op_Kb', 'identb'], w=['psT'])
                kb.cp('act', Kbt[:], psT[0:64, 0:512], r=['psT'], w=['Kbt'])
                for m in range(4):
                    kb.tr(psT[0:64, m * 128:(m + 1) * 128], ops6['Bb'][:, m, cs], identb[:], r=['op_Bb', 'identb'], w=['psT'])
                kb.ts('dve', BbtN[:], psT[0:64, 0:512], -1.0, ALU.mult, r=['psT'], w=['BbtN'])
                for h in range(8):
                    hs = slice(h * 64, (h + 1) * 64)
                    kb.mm(psC[0:64, hs], Kbt[:, hs], Vt[:, hs], start=True, stop=False, r=['Kbt', 'Vt'], w=['psC'])
                    kb.mm(psC[0:64, hs], BbtN[:, hs], Pb2[:, hs], start=False, stop=True, r=['BbtN', 'Pb2'], w=['psC'])
                kb.tt('pool', H[:], H[:], bc(GC[:, j, :].unsqueeze(2), [64, 8, 64]), ALU.mult, r=[f'H{i}', 'GC'], w=[f'H{i}'])
                kb.tt('dve', H[:], H[:], p3(psC[0:64, :]), ALU.add, r=[f'H{i}', 'psC'], w=[f'H{i}'])
                kb.cp('act', Hbf[:], H[:], r=[f'H{i}'], w=[f'Hb{i}'])
                kb.op('dve', lambda E: E.tensor_reduce(out=s8[:], in_=oT[:], axis=AX.X, op=ALU.add), r=['oT'], w=['s8'])
                kb.ts('dve', s8[:], s8[:], -1.0 / 64, ALU.mult, r=['s8'], w=['s8'])
                kb.tt('pool', oc[:], oT[:], bc(s8[:].unsqueeze(2), [64, 8, 64]), ALU.add, r=['oT', 's8'], w=['oc'])
                kb.tt('pool', osq[:], oc[:], oc[:], ALU.mult, r=['oc'], w=['osq'])
                kb.op('dve', lambda E: E.tensor_reduce(out=s8b[:], in_=osq[:], axis=AX.X, op=ALU.add), r=['osq'], w=['s8b'])
                kb.act(s8b[:], s8b[:], AF.Sqrt, scale=1.0 / 64, bias=LN_EPS, r=['s8b'], w=['s8b'])
                kb.op('dve', lambda E: E.reciprocal(out=s8b[:], in_=s8b[:]), r=['s8b'], w=['s8b'])
                kb.tt('pool', oc[:], oc[:], bc(s8b[:].unsqueeze(2), [64, 8, 64]), ALU.mult, r=['oc', 's8b'], w=['oc'])
                kb.tt('pool', f2(oc), f2(oc), lnw_s[i][:], ALU.mult, r=['oc', f'lnw{i}'], w=['oc'])
                kb.tt('pool', onb[:], f2(oc), lnb_s[i][:], ALU.add, r=['oc', f'lnb{i}'], w=['onb'])
                for m in range(4):
                    kb.tr(psT[:, m * 64:(m + 1) * 64], onb[:, m * 128:(m + 1) * 128], identb[0:64, 0:64],
                          r=['onb', 'identb'], w=['psT'])
                kb.tt('dve', t_o[:], psT[:, 0:256].rearrange("p (m c) -> p m c", c=C), bonus[:, :, cs], ALU.add,
                      r=['psT', 'bonus'], w=['t_o'])
                kb.tt('dve', ygT[:, 4:8, cs], t_o[:], gz[:, :, cs], ALU.mult, r=['t_o', 'gz'], w=['ygT'])

    def s5_phase(l, i, bi):
        STOP = int(os.environ.get('S5STOP', '99'))
        if STOP <= 0:
            kb.op('pool', lambda E: E.memset(ygT[:, 0:4, :], 0.0), w=['ygT'])
            return
        P = rp[i]
        rk_ = f'rp{i}'
        with ExitStack() as st:
            uT = kb.sb("uT", [128, 4, TB], F32, st)
            uTb = kb.sb("uTb", [128, 4, TB], BF16, st)
            uTb3 = kb.sb("uTb3", [128, 4, TB], BF16, st)
            yT = kb.sb("yT", [128, 4, TB], F32, st)
            ygb = kb.sb("ygb", [128, 4, TB], BF16, st)
            cs1f = kb.sb("cs1f", [128, 4, 8, 64], F32, st)
            cs1b = kb.sb("cs1b", [128, 8, 8, 64], BF16, st)
            cpb = kb.sb("cpb", [128, 8, 64], BF16, st)
            ea = kb.sb("ea", [128, 4, 64], F32, st)
            etm = kb.sb("etm", [128, 4, 64], F32, st)
            et = kb.sb("et", [128, 4, 64], F32, st)
            ch = kb.sb("ch", [128, 4, 64], F32, st)
            cN = kb.sb("cN", [128, 4, 64], F32, st)
            g1 = kb.sb("g1", [128, TB], F32, st)
            g2 = kb.sb("g2", [128, TB], F32, st)
            if STOP < 99 and 'c' not in os.environ.get('S5SKIP', ''):
                kb.op('pool', lambda E: E.memset(yT[:], 0.0), w=['yT'])
                kb.op('pool', lambda E: E.memset(ygb[:], 0.0), w=['ygb'])
                kb.op('pool', lambda E: E.memset(cs1b[:], 0.0), w=['cs1b'])
                kb.op('pool', lambda E: E.memset(cpb[:], 0.0), w=['cpb'])
                kb.op('pool', lambda E: E.memset(cs1f[:], 0.0), w=['cs1f'])
            for q in range(0 if 'd' in os.environ.get('S5SKIP', '') else 4):
                ps, pk = proj_chunk(winev[i, q], hT, 'hT')
                if 'e' not in os.environ.get('S5SKIP', ''):
                    kb.cp('act', uT[:, q, :], ps[:], r=[pk], w=['uT'])
                if 'f' not in os.environ.get('S5SKIP', ''):
                    kb.cp('dve', uTb[:, q, :], uT[:, q, :], r=['uT'], w=['uTb'])
                if 'a' not in os.environ.get('S5SKIP', ''):
                    kb.ts('dve', uTb3[64:128, q, :], uTb[64:128, q, :], cst[64:128, 1218:1219], ALU.mult,
                          r=['uTb', 'cst'], w=['uTb3'])
            banks = [psA[:, 0:512], psA[:, 512:1024], psB[:, 0:512], psB[:, 512:1024]]
            bkeys = ['psA0', 'psA1', 'psB0', 'psB1']
            TTv = [TT_d[k_][i].rearrange("p (q b e) n -> p q b e n", q=4, b=4) for k_ in range(4)]
            BJq = [kb.sb(f"BJq{k_}", [128, 2, 8, 128], BF16, st) for k_ in range(2)]
            CJloq = [kb.sb(f"CJloq{k_}", [128, 4, 9, 32], BF16, st) for k_ in range(2)]
            CJhiq = [kb.sb(f"CJhiq{k_}", [128, 4, 9, 64], BF16, st) for k_ in range(2)]
            TTq = [[kb.sb(f"TTq{k_}_{z_}", [128, 4, 64], F32, st) for k_ in range(4)] for z_ in range(2)]
            carv = s5carry[i][:].rearrange("p (q b e) -> p q b e", q=4, b=4)
            r8v = rho8[i][:].rearrange("p (q b e) -> p q b e", q=4, b=4)
            for q in range(4 if STOP > 1 else 0):
                qb = q % 2
                tk = f'tabq{qb}'
                kb.dma('sp', BJq[qb][:], BJ_d[i][:, q], f'tq{qb}', w=[tk])
                kb.dma('sp', CJloq[qb][:], CJlo_d[i][:, q], f'tq{qb}', w=[tk])
                kb.dma('sp', CJhiq[qb][:], CJhi_d[i][:, q], f'tq{qb}', w=[tk])
                for e in range(2):
                    tke = f'tte{e}'
                    for k_ in range(4):
                        kb.dma('sp', TTq[e][k_][:], TTv[k_][:, q, :, e, :], f'tt{e}', w=[tke])
                    T1v, T2v, T3v, T4v = TTq[e]
                    for b in range(4):
                        pb = slice(32 * b, 32 * b + 32) if b < 3 else slice(64, 128)
                        uv = (uTb if b < 3 else uTb3)[pb, q, :].rearrange("p (n j) -> p j n", j=8)
                        for j in range(8):
                            kb.mm(banks[b][:, j * 64:(j + 1) * 64], BJq[qb][pb, e, j, :], uv[:, j, :],
                                  r=[tk, 'uTb', 'uTb3'], w=[bkeys[b]])
                    if STOP <= 2:
                        continue
                    zA = psA[:, :].rearrange("p (b j n) -> p b j n", b=2, j=8)
                    zB = psB[:, :].rearrange("p (b j n) -> p b j n", b=2, j=8)
                    for (z, zk, bs) in ((zA, ['psA0', 'psA1'], slice(0, 2)), (zB, ['psB0', 'psB1'], slice(2, 4))):
                        kb.cp('dve', cs1f[:, bs, 0, :], z[:, :, 0, :], r=zk, w=['cs1f'])
                        for j in range(1, 8):
                            kb.tt('dve', cs1f[:, bs, j, :], z[:, :, j, :], cs1f[:, bs, j - 1, :], ALU.add,
                                  r=zk + ['cs1f'], w=['cs1f'])
                    cbv = cs1b[:].rearrange("p (b e) j n -> p b e j n", e=2)
                    kb.cp('act', cbv[:, :, e, :, :], cs1f[:], r=['cs1f'], w=['cs1b'])
                    if STOP <= 3:
                        continue
                    x = cs1f[:, :, 7, :]
                    kb.tt('pool', ea[:], x, T1v[:], ALU.mult, r=['cs1f', tke], w=['ea'])
                    kb.tt('dve', etm[0:64], cs1f[64:128, :, 7, :], T2v[64:128], ALU.mult, r=['cs1f', tke], w=['etm'])
                    kb.tt('dve', etm[64:128], cs1f[0:64, :, 7, :], T2v[0:64], ALU.mult, r=['cs1f', tke], w=['etm'])
                    kb.tt('pool', et[:], ea[:], etm[:], ALU.add, r=['ea', 'etm'], w=['et'])
                    for b in range(4):
                        kb.op('dve', lambda E, b=b: E.tensor_tensor_scan(
                            out=ch[:, b, :], data0=r8v[:, q, b, e:e + 1].to_broadcast([128, 64]), data1=et[:, b, :],
                            initial=carv[:, q, b, e:e + 1], op0=ALU.mult, op1=ALU.add), r=['et', f's5c{i}'], w=['ch'])
                    kb.tt('pool', ea[:], ch[:], T3v[:], ALU.mult, r=['ch', tke], w=['ea'])
                    kb.tt('dve', etm[0:64], ch[64:128], T4v[64:128], ALU.mult, r=['ch', tke], w=['etm'])
                    kb.tt('dve', etm[64:128], ch[0:64], T4v[0:64], ALU.mult, r=['ch', tke], w=['etm'])
                    kb.tt('pool', cN[:], ea[:], etm[:], ALU.add, r=['ea', 'etm'], w=['cN'])
                    cpv = cpb[:].rearrange("p (b e) n -> p b e n", e=2)
                    kb.cp('dve', cpv[:, :, e, 0:1], carv[:, q, :, e:e + 1], r=[f's5c{i}'], w=['cpb'])
                    kb.cp('act', cpv[:, :, e, 1:64], cN[:, :, 0:63], r=['cN'], w=['cpb'])
                    kb.cp('dve', carv[:, q, :, e:e + 1], cN[:, :, 63:64], r=['cN'], w=[f's5c{i}'])
                if STOP <= 4:
                    continue
                for j in range(8):
                    jc = slice(j * 64, (j + 1) * 64)
                    for b in range(2):
                        for e in range(2):
                            gl = 2 * b + e
                            o_ = psC[32 * b:32 * b + 32, jc]
                            kb.mm(o_, CJloq[qb][:, gl, j, :], cs1b[:, gl, j, :], start=(e == 0), stop=False,
                                  r=[tk, 'cs1b'], w=['psC'])
                            kb.mm(o_, CJloq[qb][:, gl, j + 1, :], cpb[:, gl, :], start=False, stop=(e == 1),
                                  r=[tk, 'cpb'], w=['psC'])
                    for gl in range(4, 8):
                        o_ = psC[64:128, jc]
                        kb.mm(o_, CJhiq[qb][:, gl - 4, j, :], cs1b[:, gl, j, :], start=(gl == 4), stop=False,
                              r=[tk, 'cs1b'], w=['psC'])
                        kb.mm(o_, CJhiq[qb][:, gl - 4, j + 1, :], cpb[:, gl, :], start=False, stop=(gl == 7),
                              r=[tk, 'cpb'], w=['psC'])
                yv = yT[:, q, :].rearrange("p (n j) -> p j n", j=8)
                uv32 = uT[:, q, :].rearrange("p (n j) -> p j n", j=8)
                kb.stt(yv, uv32, P[:, 34 + q:35 + q], psC[:, :].rearrange("p (j n) -> p j n", j=8), ALU.mult, ALU.add,
                       r=['uT', rk_, 'psC'], w=['yT'])
                xq = yT[:, q, :]
                kb.tt('pool', g1[:], xq, xq, ALU.mult, r=['yT'], w=['g1'])
                kb.ts('pool', g1[:], g1[:], 0.044715, ALU.mult, 1.0, ALU.add, r=['g1'], w=['g1'])
                kb.tt('pool', g1[:], g1[:], xq, ALU.mult, r=['g1', 'yT'], w=['g1'])
                kb.act(g2[:], g1[:], AF.Sigmoid, scale=1.5957691216057308, r=['g1'], w=['g2'])
                kb.tt('dve', yT[:, q, :], xq, g2[:], ALU.mult, r=['yT', 'g2'], w=['yT'])
                kb.cp('act', ygb[:, q, :], yT[:, q, :], r=['yT'], w=['ygb'])
            for qo in range(0 if 'b' in os.environ.get('S5SKIP', '') else 4):
                for k in range(4):
                    kb.mm(psC[:], gluw_b[i][:, k, qo * 128:(qo + 1) * 128], ygb[:, k, :], start=(k == 0), stop=(k == 3),
                          r=[f'gluw{i}', 'ygb'], w=['psC'])
                kb.act(g2[:], psC[:], AF.Sigmoid, bias=P[:, 38 + qo:39 + qo], r=['psC', rk_], w=['g2'])
                kb.tt('dve', g1[:], yT[:, qo, :], g2[:], ALU.mult, r=['yT', 'g2'], w=['g1'])
                kb.tt('dve', ygT[:, qo, :], g1[:], zs[:, qo, :], ALU.mult, r=['g1', 'zs'], w=['ygT'])

    def even_layer(l, bi):
        i = l // 2
        pre_norm(l)
        for m in range(8):
            ps, pk = proj_chunk(winev[i, 18 + m], hT, 'hT')
            kb.act(zs[:, m, :], ps[:], AF.Silu, r=[pk], w=['zs'])
        if dbg:
            kb.op('pool', lambda E: E.memset(zs[:], 1.0), r=['zs'], w=['zs'])
        s5_phase(l, i, bi)
        if not os.environ.get('RWSKIP'):
            rwkv_phase(l, i, bi)
        if not dbg:
            out_proj(l)
        kb.barrier()

    for bi in range(nblocks):
        ts_ = slice(bi * TB, (bi + 1) * TB)
        kb.dma('sp', xres[:], xT[:, ts_].rearrange("(k p) t -> p k t", p=128), 'xin', w=['xres'])
        for l in layers:
            if l % 2 == 1:
                odd_layer(l, bi)
            else:
                even_layer(l, bi)
        with ExitStack() as st:
            obuf = kb.sb("obuf", [128, 8, TB], F32, st)
            if final:
                rms_stats(xres, 'xres')
                for k in range(8):
                    kb.stt(obuf[:, k, :], xres[:, k, :], fnw_s[:, k:k + 1], rstd[:], ALU.mult, ALU.mult,
                           r=['xres', 'fnw_s', 'rstd'], w=['obuf'])
            elif dbg:
                kb.cp('dve', obuf[:], ygT[:], r=['ygT'], w=['obuf'])
            else:
                kb.cp('dve', obuf[:], xres[:], r=['xres'], w=['obuf'])
            kb.dma('sp', outT[:, ts_].rearrange("(k p) t -> p k t", p=128), obuf[:], 'xout', r=['obuf'])
            kb.barrier()
    kb.es.close()
    return nc, kb


def chunkify(W, cols):
    Wc = W[:, cols]
    n_m = Wc.shape[1] // 128
    return np.ascontiguousarray(Wc.reshape(8, 128, n_m, 128).transpose(2, 1, 0, 3))


def prep_shared(inp):
    sh = {}
    sh["normw"] = np.ascontiguousarray(inp["norm_w"].reshape(4, 8, 128).transpose(2, 0, 1).reshape(128, 32))
    sh["adaw"] = np.ascontiguousarray(inp["ada_w"])
    sh["adab"] = np.ascontiguousarray(inp["ada_b"].reshape(4, 24, 128).transpose(2, 0, 1).reshape(128, 96))
    sh["woutr"] = np.stack([chunkify(inp["w_out"][l], np.arange(1024)) for l in range(4)])
    sh["fnw"] = np.ascontiguousarray(inp["final_norm_w"].reshape(8, 128).T)
    sh["consts"] = make_consts()
    cols = np.concatenate([np.arange(0, 3072), np.arange(3088, 4112)])
    sh["winodd"] = np.stack([chunkify(inp["odd_w_in"][i], cols) for i in range(2)])
    sh["wba"] = np.ascontiguousarray(
        np.stack([inp["odd_w_in"][i][:, 3072:3088].reshape(8, 128, 16).transpose(1, 0, 2) for i in range(2)]))
    sh["convw"] = np.ascontiguousarray(
        np.stack([inp["gdn_conv_w"][i].reshape(4, 24, 128).transpose(2, 1, 0) for i in range(2)]))
    sh["alog"] = np.ascontiguousarray(np.broadcast_to(inp["gdn_a_log"][:, None, :], (2, 128, 8)))
    sh["dtb"] = np.ascontiguousarray(np.broadcast_to(inp["gdn_dt_bias"][:, None, :], (2, 128, 8)))
    sh["gnw"] = np.ascontiguousarray(np.broadcast_to(inp["gdn_norm_w"][:, None, :], (2, 128, 128)))
    sh["winev"] = np.stack([chunkify(inp["even_w_in"][i], np.arange(3328)) for i in range(2)])
    pc = lambda v, n: v.reshape(n, 128).T
    sh["rp"] = np.stack([np.concatenate([pc(inp["rwkv_mu"][i], 14), pc(inp["rwkv_w0"][i], 4), pc(inp["rwkv_a0"][i], 4),
                                         pc(inp["rwkv_k_k"][i], 4), pc(inp["rwkv_k_a"][i], 4), pc(inp["rwkv_r_k"][i], 4),
                                         pc(inp["s5_d"][i], 4), pc(inp["s5_glu_b"][i], 4)], axis=1) for i in range(2)])
    sh["wup"] = inp["rwkv_w_up"]
    sh["aup"] = inp["rwkv_a_up"]
    sh["gup"] = inp["rwkv_g_up"]
    sh["lnw"] = np.broadcast_to(inp["rwkv_ln_w"][:, None, :], (2, 64, 512))
    sh["lnb"] = np.broadcast_to(inp["rwkv_ln_b"][:, None, :], (2, 64, 512))
    s5a, s5b = [], []
    p_ = np.arange(128)
    for i in range(2):
        rep = lambda a: np.concatenate([a, a], axis=0)
        lre2 = rep(inp["s5_lambda_re"][i].T)
        lim2 = rep(inp["s5_lambda_im"][i].T)
        lst2 = np.broadcast_to(inp["s5_log_step"][i][None, :], (128, 32))
        crT = rep(inp["s5_c_re"][i].transpose(2, 0, 1)).reshape(128, 512)
        ciT = rep(inp["s5_c_im"][i].transpose(2, 0, 1)).reshape(128, 512)
        s5a.append(np.concatenate([lre2, lim2, lst2, crT, ciT], axis=1))
        g = 8 * np.arange(4)[None, :] + 2 * (p_ // 32)[:, None] + ((p_ % 32) // 16)[:, None]
        cp = (p_ % 16)[:, None]
        lreB = inp["s5_lambda_re"][i][g]
        limB = inp["s5_lambda_im"][i][g]
        breB = inp["s5_b_re"][i][g, :, cp]
        bimB = inp["s5_b_im"][i][g, :, cp]
        lstB = inp["s5_log_step"][i][g]
        s5b.append(np.concatenate([lreB.reshape(128, 256), limB.reshape(128, 256), breB.reshape(128, 256),
                                   bimB.reshape(128, 256), lstB], axis=1))
    sh["s5a"] = np.stack(s5a)
    sh["s5b"] = np.stack(s5b)
    sh["gluw"] = np.stack([inp["s5_glu_w"][i].reshape(4, 128, 512).transpose(1, 0, 2) for i in range(2)])
    return {k: np.ascontiguousarray(np.asarray(v, np.float32)) for k, v in sh.items()}


_PLAN = [([0, 1, 2, 3], True)]


def kernel(**inputs):
    inp = {k: np.asarray(v) for k, v in inputs.items()}
    nb = inp["x"].shape[0]
    sh = prep_shared(inp)
    xT = [np.ascontiguousarray(inp["x"][b].T) for b in range(nb)]
    cTs = [np.ascontiguousarray(inp["c"][b].reshape(8, 128).T) for b in range(nb)]
    for layers, final in _PLAN:
        nc, _ = build(layers, final)
        in_maps = [dict(sh, xT=xT[b], cT=cTs[b]) for b in range(nb)]
        res = run_bass_kernel_spmd(nc, in_maps, core_ids=list(range(nb)))
        xT = [np.asarray(res.results[b]["outT"]) for b in range(nb)]
    return np.stack([x.T for x in xT]).astype(np.float32)
```

```python
import os
import numpy as np
from contextlib import ExitStack
import concourse.bass as bass
import concourse.mybir as mybir
from concourse.bass_utils import run_bass_kernel_spmd

F32 = mybir.dt.float32
BF16 = mybir.dt.bfloat16
AF = mybir.ActivationFunctionType
ALU = mybir.AluOpType
AX = mybir.AxisListType

D = 1024
L = 4096
TB = 512
NB = L // TB
C = 64
NCH = TB // C
EPS = 1e-6


class KB:
    def __init__(self):
        self.nc = bass.Bass("TRN2", target_bir_lowering=False)
        self.es = ExitStack()
        nc = self.nc
        self.eng = {'pe': nc.tensor, 'dve': nc.vector, 'act': nc.scalar, 'pool': nc.gpsimd, 'sp': nc.sync}
        self.sem = {e: self.es.enter_context(nc.semaphore("sem_" + e)) for e in self.eng}
        self.cnt = {e: 0 for e in self.eng}
        self.clock = {e: {} for e in self.eng}
        self.lastw = {}
        self.readers = {}
        self.dsem = {}
        self.nins = 0

    def sb(self, name, shape, dt, stack=None):
        self.minrem = min(getattr(self, 'minrem', 1 << 30), self.nc.sbuf_bytes_remaining)
        if stack is not None:
            self.uid = getattr(self, 'uid', 0) + 1
            name = f"{name}_u{self.uid}"
        return (stack or self.es).enter_context(self.nc.sbuf_tensor(name, shape, dt))

    def ps(self, name, shape, dt, stack=None):
        return (stack or self.es).enter_context(self.nc.psum_tensor(name, shape, dt))

    def _sync(self, e, reads, writes):
        need = {}

        def add(ev):
            if ev is None:
                return
            k, h, v = ev
            if k not in need or need[k][1] < v:
                need[k] = (h, v)

        for r in reads:
            add(self.lastw.get(r))
        for w in writes:
            add(self.lastw.get(w))
            for ev in self.readers.get(w, {}).values():
                add(ev)
        ck = self.clock[e]
        for k, (h, v) in need.items():
            if e == 'pe' and k == 'sem_pe':
                continue
            if ck.get(k, 0) >= v:
                continue
            self.eng[e].wait_ge(h, v)
            ck[k] = v

    def _mark(self, ev, reads, writes):
        for r in reads:
            self.readers.setdefault(r, {})[ev[0]] = ev
        for w in writes:
            self.lastw[w] = ev
            self.readers[w] = {}

    def op(self, e, fn, r=(), w=()):
        reads, writes = r, list(w) + [k for k in r if k.startswith('ps')]
        if getattr(self, '_yp', None) is not None:
            tl = self._tl
            last = getattr(tl, 'last', None)
            if e == 'pe' and last is not None and last != 'pe' and not getattr(self, '_hold', False):
                self._yp()
            tl.last = e
        self._sync(e, reads, writes)
        ins = fn(self.eng[e])
        self.cnt[e] += 1
        self.nins += 1
        ins.then_inc(self.sem[e], 1)
        self._mark(('sem_' + e, self.sem[e], self.cnt[e]), reads, writes)
        return ins

    def dma(self, q, out, in_, slot, r=(), w=()):
        reads, writes = r, w
        self._sync(q, reads, writes)
        if slot not in self.dsem:
            self.dsem[slot] = [self.es.enter_context(self.nc.semaphore("d_" + slot)), 0]
        s = self.dsem[slot]
        s[1] += 16
        self.eng[q].dma_start(out=out, in_=in_).then_inc(s[0], 16)
        self.nins += 1
        self._mark(('d_' + slot, s[0], s[1]), reads, writes)

    def run_streams(self, bodies):
        import threading
        n = len(bodies)
        sems = [threading.Semaphore(0) for _ in range(n)]
        alive = [True] * n
        done = threading.Semaphore(0)
        errs = []
        tl = threading.local()

        def nxt(i):
            for d in range(1, n + 1):
                k = (i + d) % n
                if alive[k]:
                    return k
            return None

        def yp():
            i = tl.idx
            k = nxt(i)
            if k is None or k == i:
                return
            sems[k].release()
            sems[i].acquire()

        def runner(i):
            tl.idx = i
            sems[i].acquire()
            try:
                bodies[i]()
            except BaseException as e:
                errs.append(e)
            alive[i] = False
            k = nxt(i)
            if k is None:
                done.release()
            else:
                sems[k].release()

        self._yp = yp
        self._tl = tl
        ths = [threading.Thread(target=runner, args=(i,)) for i in range(n)]
        for t in ths:
            t.start()
        sems[0].release()
        done.acquire()
        for t in ths:
            t.join()
        self._yp = None
        if errs:
            raise errs[0]

    def barrier(self):
        for e in self.eng:
            for e2 in self.eng:
                if e2 != e and self.cnt[e2] > self.clock[e].get('sem_' + e2, 0):
                    self.eng[e].wait_ge(self.sem[e2], self.cnt[e2])
                    self.clock[e]['sem_' + e2] = self.cnt[e2]
            for k, (h, v) in self.dsem.items():
                if v > self.clock[e].get('d_' + k, 0):
                    self.eng[e].wait_ge(h, v)
                    self.clock[e]['d_' + k] = v
        self.lastw = {}
        self.readers = {}

    def mm(self, out, lhsT, rhs, start=True, stop=True, r=(), w=()):
        self._hold = not stop
        return self.op('pe', lambda E: E.matmul(out, lhsT=lhsT, rhs=rhs, start=start, stop=stop), r, w)

    def tr(self, out, in_, ident, r=(), w=()):
        return self.op('pe', lambda E: E.transpose(out, in_, ident), r, w)

    def act(self, out, in_, func, scale=None, bias=None, r=(), w=(), e='act'):
        kw = {}
        if scale is not None:
            kw['scale'] = scale
        if bias is not None:
            kw['bias'] = bias
        return self.op('act', lambda E: E.activation(out=out, in_=in_, func=func, **kw), r, w)

    def tt(self, e, out, in0, in1, op, r=(), w=()):
        return self.op(e, lambda E: E.tensor_tensor(out=out, in0=in0, in1=in1, op=op), r, w)

    def ts(self, e, out, in0, s1, op0, s2=None, op1=None, r=(), w=()):
        if op1 is None:
            return self.op(e, lambda E: E.tensor_scalar(out=out, in0=in0, scalar1=s1, scalar2=None, op0=op0), r, w)
        return self.op(e, lambda E: E.tensor_scalar(out=out, in0=in0, scalar1=s1, scalar2=s2, op0=op0, op1=op1), r, w)

    def stt(self, out, in0, scalar, in1, op0, op1, r=(), w=()):
        return self.op('dve', lambda E: E.scalar_tensor_tensor(out=out, in0=in0, scalar=scalar, in1=in1, op0=op0, op1=op1), r, w)

    def cp(self, e, out, in_, r=(), w=()):
        if e == 'act':
            return self.op('act', lambda E: E.copy(out=out, in_=in_), r, w)
        return self.op(e, lambda E: E.tensor_copy(out=out, in_=in_), r, w)


def bc(ap, shape):
    return ap.to_broadcast(shape)


CB = {'ident': (0, 128), 'U': (128, 64), 'Lst': (192, 64), 'MsN': (256, 64), 'Mi': (320, 64), 'I64': (384, 64),
      'Ms': (448, 64), 'MiN': (512, 64), 'bones': (576, 128), 'cmask': (704, 512), 'pmask': (1216, 2)}
NCONST = 1219


def make_consts():
    i = np.arange(64)
    blob = np.zeros((128, NCONST), np.float32)
    blob[:, 0:128] = np.eye(128, dtype=np.float32)
    m = {}
    m['U'] = (i[:, None] <= i[None, :]).astype(np.float32)
    m['Lst'] = (i[:, None] > i[None, :]).astype(np.float32)
    m['MsN'] = -(i[:, None] < i[None, :]).astype(np.float32)
    m['Mi'] = (i[:, None] <= i[None, :]).astype(np.float32)
    m['I64'] = np.eye(64, dtype=np.float32)
    m['Ms'] = -m['MsN']
    m['MiN'] = -m['Mi']
    for k, v in m.items():
        blob[0:64, CB[k][0]:CB[k][0] + 64] = v
    bo = np.zeros((128, 128), np.float32)
    bo[0:64, 0:64] = 1.0
    bo[64:128, 64:128] = 1.0
    blob[:, 576:704] = bo
    cm = np.ones(512, np.float32)
    cm[0::64] = 0.0
    blob[:, 704:1216] = cm[None, :]
    p = np.arange(128)
    blob[:, 1216] = ((p % 32) // 16 == 0)
    blob[:, 1217] = ((p % 32) // 16 == 1)
    blob[:, 1218] = (p >= 96)
    return blob


def build(layers, final, nblocks=NB, dbg=None):
    kb = KB()
    nc = kb.nc
    n_odd = 2
    dr = {}

    def din(name, shape, dt=F32):
        dr[name] = nc.dram_tensor(name, list(shape), dt, kind="ExternalInput").ap()
        return dr[name]

    xT = din("xT", [D, L])
    cT = din("cT", [128, 8])
    normw = din("normw", [128, 32])
    adaw = din("adaw", [4, D, 3 * D])
    adab = din("adab", [128, 96])
    woutr = din("woutr", [4, 8, 128, 8, 128])
    fnw_d = din("fnw", [128, 8])
    consts_d = din("consts", [128, NCONST])
    winodd = din("winodd", [2, 32, 128, 8, 128])
    wba_d = din("wba", [2, 128, 8, 16])
    convw_d = din("convw", [2, 128, 24, 4])
    alog_d = din("alog", [2, 128, 8])
    dtb_d = din("dtb", [2, 128, 8])
    gnw_d = din("gnw", [2, 128, 128])
    winev = din("winev", [2, 26, 128, 8, 128])
    rp_d = din("rp", [2, 128, 42])
    wup_d = din("wup", [2, 64, 512])
    aup_d = din("aup", [2, 64, 512])
    gup_d = din("gup", [2, 128, 512])
    lnw_d = din("lnw", [2, 64, 512])
    lnb_d = din("lnb", [2, 64, 512])
    gluw_d = din("gluw", [2, 128, 4, 512])
    s5a_d = din("s5a", [2, 128, 96 + 1024])
    s5b_d = din("s5b", [2, 128, 1028])
    outT = nc.dram_tensor("outT", [D, L], F32, kind="ExternalOutput").ap()
    odd_ids = sorted({l // 2 for l in layers if l % 2 == 1})
    even_ids = sorted({l // 2 for l in layers if l % 2 == 0})

    cst = kb.sb("cst", [128, NCONST], F32)
    identb = kb.sb("identb", [128, 128], BF16)
    onesb = kb.sb("onesb", [128, 128], BF16)
    onesf = kb.sb("onesf", [64, 128], F32)
    xres = kb.sb("xres", [128, 8, TB], F32)
    hT = kb.sb("hT", [128, 8, TB], BF16)
    rstd = kb.sb("rstd", [128, TB], F32)
    sqb = [kb.sb(f"sqb{i}", [128, TB], BF16) for i in range(2)]
    tmpf = [kb.sb(f"tmpf{i}", [128, TB], F32) for i in range(2)]
    wbuf = [kb.sb(f"wbuf{i}", [128, 8, 128], BF16) for i in range(4)]
    ygT = kb.sb("ygT", [128, 8, TB], BF16)
    zs = kb.sb("zs", [128, 8, TB], BF16)
    modT = kb.sb("modT", [128, 4, 24], F32)
    s1 = kb.sb("s1", [128, 4, 8], F32)
    normw_s = kb.sb("normw_s", [128, 32], F32)
    adab_s = kb.sb("adab_s", [128, 96], F32)
    fnw_s = kb.sb("fnw_s", [128, 8], F32)
    sc = kb.sb("sc", [128, 8], F32)
    Sst = [kb.sb(f"Sst{i}", [128, 8, 128], F32) if i in odd_ids else None for i in range(n_odd)]
    Sb = [kb.sb(f"Sb{i}", [128, 8, 128], BF16) if i in odd_ids else None for i in range(n_odd)]
    carry = [kb.sb(f"carry{i}", [128, 24, 3], F32) if i in odd_ids else None for i in range(n_odd)]
    wba_s = [kb.sb(f"wba_s{i}", [128, 8, 16], BF16) if i in odd_ids else None for i in range(n_odd)]
    convw_s = [kb.sb(f"convw_s{i}", [128, 24, 4], F32) if i in odd_ids else None for i in range(n_odd)]
    negA = [kb.sb(f"negA{i}", [128, 8], F32) if i in odd_ids else None for i in range(n_odd)]
    dtb_s = [kb.sb(f"dtb_s{i}", [128, 8], F32) if i in odd_ids else None for i in range(n_odd)]
    gnw_s = [kb.sb(f"gnw_s{i}", [128, 128], F32) if i in odd_ids else None for i in range(n_odd)]

    def per_even(name, shape, dt):
        return [kb.sb(f"{name}{i}", shape, dt) if i in even_ids else None for i in range(2)]
    Hst = per_even("Hst", [64, 8, 64], F32)
    Hb = per_even("Hb", [64, 8, 64], BF16)
    tcarry = per_even("tcarry", [128, 14], F32)
    rp = per_even("rp_s", [128, 42], F32)
    omm = per_even("omm", [128, 14], F32)
    wup_b = per_even("wup_b", [64, 512], BF16)
    aup_b = per_even("aup_b", [128, 512], BF16)
    gup_b = per_even("gup_b", [128, 512], BF16)
    lnw_s = per_even("lnw_s", [64, 512], F32)
    lnb_s = per_even("lnb_s", [64, 512], F32)
    gluw_b = per_even("gluw_b", [128, 4, 512], BF16)
    bonesb = kb.sb("bonesb", [128, 128], BF16)
    def scratch(name, shape, dt):
        return [nc.dram_tensor(f"{name}{i}", list(shape), dt, kind="Internal").ap() if i in even_ids else None
                for i in range(2)]
    BJ_d = scratch("BJ_d", [128, 4, 2, 8, 128], BF16)
    CJlo_d = scratch("CJlo_d", [128, 4, 4, 9, 32], BF16)
    CJhi_d = scratch("CJhi_d", [128, 4, 4, 9, 64], BF16)
    TT_d = [scratch(f"TT{k}_d", [128, 32, 64], F32) for k in range(4)]
    rho8 = per_even("rho8", [128, 32], F32)
    s5carry = per_even("s5carry", [128, 32], F32)

    psI = [kb.ps(f"psI{i}", [128, 512], F32) for i in range(2)]
    psA = kb.ps("psA", [128, 1024], F32)
    psB = kb.ps("psB", [128, 1024], F32)
    psC = kb.ps("psC", [128, 512], F32)
    psT = kb.ps("psT", [128, 1024], BF16)

    def cv(k, rows=64):
        return cst[0:rows, CB[k][0]:CB[k][0] + CB[k][1]]
    U, Lst, MsN, Mi, I64, Ms, MiN = (cv(k) for k in ['U', 'Lst', 'MsN', 'Mi', 'I64', 'Ms', 'MiN'])
    cmask = cv('cmask', 128)

    kb.dma('sp', cst[:], consts_d[:, :], 'par', w=['cst'])
    kb.dma('sp', normw_s[:], normw[:, :], 'par', w=['normw_s'])
    kb.dma('sp', adab_s[:], adab[:, :], 'par', w=['adab_s'])
    kb.dma('sp', fnw_s[:], fnw_d[:, :], 'par', w=['fnw_s'])
    kb.dma('sp', sc[:], cT[:, :], 'par', w=['sc'])
    for i in odd_ids:
        kb.dma('pool', wba_s[i][:], wba_d[i], 'parp', w=[f'wba{i}'])
        kb.dma('sp', convw_s[i][:], convw_d[i], 'par', w=[f'convw{i}'])
        kb.dma('sp', negA[i][:], alog_d[i], 'par', w=[f'negA{i}'])
        kb.dma('sp', dtb_s[i][:], dtb_d[i], 'par', w=[f'dtb{i}'])
        kb.dma('sp', gnw_s[i][:], gnw_d[i], 'par', w=[f'gnw{i}'])
    for i in even_ids:
        kb.dma('sp', rp[i][:], rp_d[i], 'par', w=[f'rp{i}'])
        kb.dma('sp', lnw_s[i][:], lnw_d[i], 'par', w=[f'lnw{i}'])
        kb.dma('sp', lnb_s[i][:], lnb_d[i], 'par', w=[f'lnb{i}'])
        kb.dma('pool', wup_b[i][:], wup_d[i], 'parp', w=[f'wup{i}'])
        kb.dma('pool', aup_b[i][64:128, :], aup_d[i], 'parp', w=[f'aup{i}'])
        kb.dma('pool', gup_b[i][:], gup_d[i], 'parp', w=[f'gup{i}'])
        kb.dma('pool', gluw_b[i][:], gluw_d[i], 'parp', w=[f'gluw{i}'])
    kb.barrier()
    kb.cp('dve', identb[:], cst[:, 0:128], r=['cst'], w=['identb'])
    kb.cp('dve', bonesb[:], cst[:, 576:704], r=['cst'], w=['bonesb'])
    for i in even_ids:
        kb.ts('dve', omm[i][:], rp[i][:, 0:14], -1.0, ALU.mult, 1.0, ALU.add, r=[f'rp{i}'], w=[f'omm{i}'])
        kb.op('pool', lambda E, i=i: E.memset(Hst[i][:], 0.0), w=[f'H{i}_0', f'H{i}_1'])
        kb.op('pool', lambda E, i=i: E.memset(Hb[i][:], 0.0), w=[f'Hb{i}_0', f'Hb{i}_1'])
        kb.op('pool', lambda E, i=i: E.memset(tcarry[i][:], 0.0), w=[f'tcarry{i}'])
    kb.op('dve', lambda E: E.memset(onesb[:], 1.0), w=['onesb'])
    kb.op('dve', lambda E: E.memset(onesf[:], 1.0), w=['onesf'])
    for i in odd_ids:
        kb.act(negA[i][:], negA[i][:], AF.Exp, r=[f'negA{i}'], w=[f'negA{i}'])
        kb.ts('dve', negA[i][:], negA[i][:], -1.0, ALU.mult, r=[f'negA{i}'], w=[f'negA{i}'])
        kb.op('pool', lambda E, i=i: E.memset(Sst[i][:], 0.0), w=[f'S{i}_0', f'S{i}_1'])
        kb.op('pool', lambda E, i=i: E.memset(Sb[i][:], 0.0), w=[f'Sb{i}_0', f'Sb{i}_1'])
        kb.op('pool', lambda E, i=i: E.memset(carry[i][:], 0.0), w=[f'carry{i}'])

    def s5_tables(i):
        with ExitStack() as st:
            K_ = ['s5tab']
            sa = kb.sb("s5a_s", [128, 96 + 1024], F32, st)
            kb.dma('sp', sa[:], s5a_d[i], 's5a', w=K_)
            cnt = [0]
            CJlo = {i: kb.sb("CJlo_t", [128, 4, 4, 9, 32], BF16, st)}
            CJhi = {i: kb.sb("CJhi_t", [128, 4, 4, 9, 64], BF16, st)}
            T1 = {i: kb.sb("T1_t", [128, 32, 64], F32, st)}
            T2 = {i: kb.sb("T2_t", [128, 32, 64], F32, st)}
            T3 = {i: kb.sb("T3_t", [128, 32, 64], F32, st)}
            T4 = {i: kb.sb("T4_t", [128, 32, 64], F32, st)}

            cur = [st]

            def T(shape):
                cnt[0] += 1
                return kb.sb(f"s5t{cnt[0]}", shape, F32, cur[0])

            def tt(o, a, b, op, e='dve'):
                kb.tt(e, o, a, b, op, r=K_, w=K_)

            def ts(o, a, s1, op0, s2=None, op1=None):
                kb.ts('dve', o, a, s1, op0, s2, op1, r=K_, w=K_)

            def cmul(orr, oi, ar, ai, br, bi, t1, t2):
                tt(t1, ar, br, ALU.mult)
                tt(t2, ai, bi, ALU.mult)
                tt(orr, t1, t2, ALU.subtract)
                tt(t1, ar, bi, ALU.mult)
                tt(t2, ai, br, ALU.mult)
                tt(oi, t1, t2, ALU.add)

            def lam_bar(lre_in, lim_in, lst_in, shape):
                lre = T(shape); stp = T(shape); ar = T(shape); ai = T(shape)
                cr = T(shape); si = T(shape); t1 = T(shape); t2 = T(shape); rho = T(shape)
                ts(lre[:], lre_in, -1e-4, ALU.min)
                kb.act(stp[:], lst_in, AF.Exp, r=K_, w=K_)
                tt(ar[:], lre[:], stp[:], ALU.mult)
                tt(ai[:], lim_in, stp[:], ALU.mult)
                kb.act(rho[:], ar[:], AF.Exp, r=K_, w=K_)
                kb.act(si[:], ai[:], AF.Sin, scale=1.0 / 16, r=K_, w=K_)
                ts(t1[:], ai[:], 1.0 / 16, ALU.mult, float(np.pi / 2), ALU.add)
                kb.act(cr[:], t1[:], AF.Sin, r=K_, w=K_)
                for _ in range(4):
                    tt(t1[:], cr[:], cr[:], ALU.mult)
                    tt(t2[:], si[:], si[:], ALU.mult)
                    tt(ar[:], cr[:], si[:], ALU.mult)
                    tt(cr[:], t1[:], t2[:], ALU.subtract)
                    ts(si[:], ar[:], 2.0, ALU.mult)
                lr = T(shape); li = T(shape)
                tt(lr[:], cr[:], rho[:], ALU.mult)
                tt(li[:], si[:], rho[:], ALU.mult)
                return lr, li, rho, cr, si, lre

            sh = [128, 32]
            lr, li, rho, ur, ui, _ = lam_bar(sa[:, 0:32], sa[:, 32:64], sa[:, 64:96], sh)
            t1 = T(sh); t2 = T(sh)
            tt(t1[:], rho[:], rho[:], ALU.mult)
            tt(t2[:], t1[:], t1[:], ALU.mult)
            tt(rho8[i][:], t2[:], t2[:], ALU.mult)
            Lr = T([128, 32, 9]); Li = T([128, 32, 9])
            kb.op('dve', lambda E: E.memset(Lr[:, :, 0], 1.0), r=K_, w=K_)
            kb.op('dve', lambda E: E.memset(Li[:, :, 0], 0.0), r=K_, w=K_)
            for j in range(8):
                cmul(Lr[:, :, j + 1], Li[:, :, j + 1], Lr[:, :, j], Li[:, :, j], lr[:], li[:], t1[:], t2[:])
            kb.op('pool', lambda E: E.memset(CJlo[i][:], 0.0), r=K_, w=K_)
            kb.op('pool', lambda E: E.memset(CJhi[i][:], 0.0), r=K_, w=K_)
            Cr = sa[:, 96:96 + 512].rearrange("p (g c) -> p g c", c=16)
            Ci = sa[:, 96 + 512:96 + 1024].rearrange("p (g c) -> p g c", c=16)
            d1 = T([128, 32, 16]); d2 = T([128, 32, 16]); dr = T([128, 32, 16]); di = T([128, 32, 16])
            for j in range(9):
                lrj = bc(Lr[:, :, j:j + 1], [128, 32, 16])
                lij = bc(Li[:, :, j:j + 1], [128, 32, 16])
                tt(d1[:], Cr, lrj, ALU.mult)
                tt(d2[:], Ci, lij, ALU.mult)
                tt(dr[:], d1[:], d2[:], ALU.subtract)
                tt(d1[:], Cr, lij, ALU.mult)
                tt(d2[:], Ci, lrj, ALU.mult)
                tt(di[:], d1[:], d2[:], ALU.add)
                ts(di[:], di[:], -1.0, ALU.mult)
                drv = dr[:].rearrange("p (q gl) c -> p q gl c", gl=8)
                div = di[:].rearrange("p (q gl) c -> p q gl c", gl=8)
                for gl in range(8):
                    if gl < 4:
                        dst = lambda ps_: CJlo[i][ps_, :, gl, j, (gl % 2) * 16:(gl % 2) * 16 + 16]
                    else:
                        dst = lambda ps_: CJhi[i][ps_, :, gl - 4, j, (gl - 4) * 16:(gl - 4) * 16 + 16]
                    kb.cp('dve', dst(slice(0, 64)), drv[0:64, :, gl, :], r=K_, w=K_)
                    kb.cp('dve', dst(slice(64, 128)), div[64:128, :, gl, :], r=K_, w=K_)
            er = T(sh); ei = T(sh)
            kb.cp('dve', er[:], ur[:], r=K_, w=K_)
            kb.cp('dve', ei[:], ui[:], r=K_, w=K_)
            for _ in range(3):
                tt(t1[:], er[:], er[:], ALU.mult)
                tt(t2[:], ei[:], ei[:], ALU.mult)
                tt(ei[:], er[:], ei[:], ALU.mult)
                tt(er[:], t1[:], t2[:], ALU.subtract)
                ts(ei[:], ei[:], 2.0, ALU.mult)
            Pr = T3[i]
            Pi = T([128, 32, 64])
            kb.cp('dve', Pr[:, :, 0], er[:], r=K_, w=K_)
            kb.cp('dve', Pi[:, :, 0], ei[:], r=K_, w=K_)
            p1 = T([128, 32, 32]); p2 = T([128, 32, 32])
            m = 1
            while m < 64:
                br_ = bc(Pr[:, :, m - 1:m], [128, 32, m])
                bi_ = bc(Pi[:, :, m - 1:m], [128, 32, m])
                cmul(Pr[:, :, m:2 * m], Pi[:, :, m:2 * m], Pr[:, :, 0:m], Pi[:, :, 0:m], br_, bi_,
                     p1[:, :, 0:m], p2[:, :, 0:m])
                m *= 2
            l7r = bc(Lr[:, :, 7:8], [128, 32, 64])
            l7i = bc(Li[:, :, 7:8], [128, 32, 64])
            tt(T1[i][:], Pr[:], l7r, ALU.mult)
            tt(T2[i][:], Pi[:], l7i, ALU.mult)
            tt(T1[i][:], T1[i][:], T2[i][:], ALU.add)
            tt(T2[i][:], Pr[:], l7i, ALU.mult)
            tt(T4[i][:], Pi[:], l7r, ALU.mult)
            tt(T2[i][:], T2[i][:], T4[i][:], ALU.subtract)
            ts(T2[i][64:128], T2[i][64:128], -1.0, ALU.mult)
            kb.cp('dve', T4[i][0:64], Pi[0:64], r=K_, w=K_)
            ts(T4[i][64:128], Pi[64:128], -1.0, ALU.mult)
            kb.dma('sp', CJlo_d[i], CJlo[i][:], 'tabst', r=K_)
            kb.dma('sp', CJhi_d[i], CJhi[i][:], 'tabst', r=K_)
            for k_, t_ in enumerate((T1, T2, T3, T4)):
                kb.dma('sp', TT_d[k_][i], t_[i][:], 'tabst', r=K_)
            kb.barrier()
        with ExitStack() as st:
            cnt[0] += 1000
            cur[0] = st
            BJtab = {i: kb.sb("BJ_t", [128, 4, 2, 8, 128], BF16, st)}
            sbb = kb.sb("s5b_s", [128, 1028], F32, st)
            kb.dma('sp', sbb[:], s5b_d[i], 's5b', w=K_)
            shb = [128, 4, 64]
            v = lambda a: sbb[:, a * 256:(a + 1) * 256].rearrange("p (q n) -> p q n", n=64)
            lstb = bc(sbb[:, 1024:1028].unsqueeze(2), shb)
            blr, bli, brho, _, _, blre = lam_bar(v(0), v(1), lstb, shb)
            bt1 = T(shb); bt2 = T(shb); den = T(shb); kr = T(shb); ki = T(shb); ir = T(shb); ii = T(shb)
            nr = T(shb)
            ts(nr[:], blr[:], -1.0, ALU.add)
            tt(bt1[:], blre[:], blre[:], ALU.mult)
            tt(bt2[:], v(1), v(1), ALU.mult)
            tt(den[:], bt1[:], bt2[:], ALU.add)
            kb.op('dve', lambda E: E.reciprocal(out=den[:], in_=den[:]), r=K_, w=K_)
            tt(bt1[:], nr[:], blre[:], ALU.mult)
            tt(bt2[:], bli[:], v(1), ALU.mult)
            tt(kr[:], bt1[:], bt2[:], ALU.add)
            tt(kr[:], kr[:], den[:], ALU.mult)
            tt(bt1[:], bli[:], blre[:], ALU.mult)
            tt(bt2[:], nr[:], v(1), ALU.mult)
            tt(ki[:], bt1[:], bt2[:], ALU.subtract)
            tt(ki[:], ki[:], den[:], ALU.mult)
            tt(bt1[:], brho[:], brho[:], ALU.mult)
            kb.op('dve', lambda E: E.reciprocal(out=bt1[:], in_=bt1[:]), r=K_, w=K_)
            tt(ir[:], blr[:], bt1[:], ALU.mult)
            tt(ii[:], bli[:], bt1[:], ALU.mult)
            ts(ii[:], ii[:], -1.0, ALU.mult)
            gr = T(shb); gi = T(shb); g2r = T(shb); g2i = T(shb); vr = T(shb); vi = T(shb)
            kb.cp('dve', gr[:], kr[:], r=K_, w=K_)
            kb.cp('dve', gi[:], ki[:], r=K_, w=K_)
            pm = cst[:, 1216:1218]
            for j in range(8):
                cmul(vr[:], vi[:], gr[:], gi[:], v(2), v(3), bt1[:], bt2[:])
                for e in range(2):
                    ts(BJtab[i][:, :, e, j, 0:64], vr[:], pm[:, e:e + 1], ALU.mult)
                    ts(BJtab[i][:, :, e, j, 64:128], vi[:], pm[:, e:e + 1], ALU.mult)
                if j < 7:
                    cmul(g2r[:], g2i[:], gr[:], gi[:], ir[:], ii[:], bt1[:], bt2[:])
                    gr, g2r = g2r, gr
                    gi, g2i = g2i, gi
            kb.op('pool', lambda E: E.memset(s5carry[i][:], 0.0), r=K_, w=K_)
            kb.dma('sp', BJ_d[i], BJtab[i][:], 'tabst', r=K_)
            kb.barrier()

    for i in even_ids:
        s5_tables(i)

    kb.act(sc[:], sc[:], AF.Silu, r=['sc'], w=['sc'])
    with ExitStack() as st:
        abuf = [kb.sb(f"abuf{i}", [128, 3 * D], F32, st) for i in range(2)]
        macc = kb.sb("macc", [128, 24], F32, st)
        n = 0
        for l in layers:
            for k in range(8):
                b = abuf[n % 2]
                kb.dma('sp', b[:], adaw[l, k * 128:(k + 1) * 128, :], f'ab{n % 2}', w=[f'abuf{n % 2}'])
                for j in range(24):
                    kb.mm(psC[:, j:j + 1], b[:, j * 128:(j + 1) * 128], sc[:, k:k + 1],
                          r=[f'abuf{n % 2}', 'sc'], w=['psC'])
                if k == 0:
                    kb.tt('dve', macc[:], psC[:, 0:24], adab_s[:, l * 24:(l + 1) * 24], ALU.add,
                          r=['psC', 'adab_s'], w=['macc'])
                else:
                    kb.tt('dve', macc[:], psC[:, 0:24], macc[:], ALU.add, r=['psC', 'macc'], w=['macc'])
                n += 1
            kb.cp('dve', modT[:, l, :], macc[:], r=['macc'], w=['modT'])
            kb.ts('dve', s1[:, l, :], modT[:, l, 8:16], 1.0, ALU.add, r=['modT'], w=['s1'])
            kb.tt('dve', s1[:, l, :], s1[:, l, :], normw_s[:, l * 8:(l + 1) * 8], ALU.mult,
                  r=['s1', 'normw_s'], w=['s1'])
        kb.barrier()

    wctr = [0]

    def load_w(src_ap):
        i = wctr[0] % 4
        wctr[0] += 1
        kb.dma('pool', wbuf[i][:], src_ap, f'w{i}', w=[f'wbuf{i}'])
        return wbuf[i], f'wbuf{i}'

    ictr = [0]

    def proj_chunk(src_ap, rhs_tile, rhs_key):
        wb, wk = load_w(src_ap)
        i = ictr[0] % 2
        ictr[0] += 1
        for k in range(8):
            rk_l = rhs_key if isinstance(rhs_key, list) else [rhs_key]
            kb.mm(psI[i][:], wb[:, k, :], rhs_tile[:, k, :], start=(k == 0), stop=(k == 7),
                  r=[wk] + rk_l, w=[f'psI{i}'])
        return psI[i], f'psI{i}'

    def rms_stats(src, src_key):
        for k in range(8):
            q = sqb[k % 2]
            kb.act(q[:], src[:, k, :], AF.Square, r=[src_key], w=[f'sqb{k % 2}'])
            kb.mm(psC[:, :], onesb[:], q[:], start=(k == 0), stop=(k == 7), r=[f'sqb{k % 2}', 'onesb'], w=['psC'])
        kb.act(rstd[:], psC[:], AF.Ln, scale=1.0 / D, bias=EPS, r=['psC'], w=['rstd'])
        kb.act(rstd[:], rstd[:], AF.Exp, scale=-0.5, r=['rstd'], w=['rstd'])

    def pre_norm(l):
        rms_stats(xres, 'xres')
        for k in range(8):
            t = tmpf[k % 2]
            kb.tt('dve', t[:], xres[:, k, :], rstd[:], ALU.mult, r=['xres', 'rstd'], w=[f'tmpf{k % 2}'])
            kb.act(hT[:, k, :], t[:], AF.Identity, scale=s1[:, l, k:k + 1], bias=modT[:, l, k:k + 1],
                   r=[f'tmpf{k % 2}', 's1', 'modT'], w=['hT'])

    def out_proj(l):
        for dm in range(8):
            ps, pk = proj_chunk(woutr[l, dm], ygT, ['ygT', 'ygT0', 'ygT1'])
            kb.stt(xres[:, dm, :], ps[:], modT[:, l, 16 + dm:17 + dm], xres[:, dm, :], ALU.mult, ALU.add,
                   r=[pk, 'modT', 'xres'], w=['xres'])


    def neumann(T, g):
        HN = T['HN']
        Pf, Accf, Accb, PTf, Pw = T['Pf'], T['Accf'], T['Accb'], T['PTf'], T['Pw']
        pA, pB, pC = T['pA'], T['pB'], T['pC']
        kA, kB, kC = T['kA'], T['kB'], T['kC']
        K = lambda n: f'{n}_{g}'
        W = HN * 64
        f2 = lambda t: t[:].rearrange("p h c -> p (h c)")
        kb.tt('pool', Accf[:], Pf[:], bc(I64.unsqueeze(1), [64, HN, 64]), ALU.add, r=[K('Pf'), 'cst'], w=[K('Accf')])
        for h in range(HN):
            kb.tr(pC[0:64, h * 64:(h + 1) * 64], Pf[:, h, :], cst[0:64, 0:64], r=[K('Pf'), 'cst'], w=kC)
        kb.cp('act', f2(PTf), pC[0:64, 0:W], r=kC, w=[K('PTf')])
        cur, curk = Pf, K('Pf')
        oth, othk = Pw, K('Pw')
        for lev in range(5):
            last = (lev == 4)
            if not last:
                for h in range(HN):
                    kb.mm(pA[0:64, h * 64:(h + 1) * 64], PTf[:, h, :], cur[:, h, :], r=[K('PTf'), curk], w=kA)
            for h in range(HN):
                kb.mm(pB[0:64, h * 64:(h + 1) * 64], cur[:, h, :], PTf[:, h, :], r=[K('PTf'), curk], w=kB)
            if not last:
                kb.cp('act', f2(oth), pA[0:64, 0:W], r=kA, w=[othk])
            kb.cp('dve', f2(PTf), pB[0:64, 0:W], r=kB, w=[K('PTf')])
            cur, curk, oth, othk = oth, othk, cur, curk
            for h in range(HN):
                kb.mm(pC[0:64, h * 64:(h + 1) * 64], PTf[:, h, :], Accf[:, h, :], r=[K('PTf'), K('Accf')], w=kC)
            kb.tt('dve', f2(Accf), pC[0:64, 0:W], f2(Accf), ALU.add, r=kC + [K('Accf')], w=[K('Accf')])
        kb.cp('act', Accb[:], Accf[:], r=[K('Accf')], w=[K('Accb')])

    def psum_group(g, G=2):
        if G == 1:
            return dict(pA=psA[:, :], pB=psB[:, :], pC=psC[:, :], pT=psT[:, :], HN=8,
                        kA=['psA0', 'psA1'], kB=['psB0', 'psB1'], kC=['psC'], kT=['psT'])
        if g == 0:
            return dict(pA=psA[:, 0:512], pB=psA[:, 512:1024], pC=psC[:, :], pT=psT[:, :], HN=4,
                        kA=['psA0'], kB=['psA1'], kC=['psC'], kT=['psT'])
        return dict(pA=psB[:, 0:512], pB=psB[:, 512:1024], pC=psI[0][:, :], pT=psI[1][:, :].bitcast(BF16), HN=4,
                    kA=['psB0'], kB=['psB1'], kC=['psI0'], kT=['psI1'])

    def odd_layer(l, bi):
        i = l // 2
        with ExitStack() as st:
            acc = kb.sb("acc", [128, 8, TB], F32, st)
            qkn = kb.sb("qkn", [128, 16, TB], BF16, st)
            vb = kb.sb("vb", [128, 8, TB], BF16, st)
            ba = kb.sb("ba", [64, 8, 16], F32, st)
            beta = kb.sb("beta", [64, 8, 8], F32, st)
            aall = kb.sb("aall", [64, 8, 8], F32, st)
            eg = kb.sb("eg", [64, 8, 8], F32, st)
            eend = kb.sb("eend", [64, 8, 8], F32, st)
            cdb = kb.sb("cdb", [128, 8, 8], F32, st)
            TG = []
            GG = int(os.environ.get('GDN_G', '2'))
            HN = 8 // GG
            for g_ in range(GG):
                T_ = psum_group(g_, GG)
                for n_, shp, dt_ in [('R2', [64, HN, 64], F32), ('decT', [64, HN, 64], F32), ('dSb', [64, HN, 64], F32),
                                     ('dI', [64, HN, 64], F32), ('Pf', [64, HN, 64], F32), ('Accf', [64, HN, 64], F32),
                                     ('PTf', [64, HN, 64], F32), ('Pw', [64, HN, 64], F32), ('Accb', [64, HN, 64], BF16),
                                     ('intraT', [64, HN, 64], BF16), ('vk', [64, HN, 256], BF16), ('kend', [64, HN, 128], BF16),
                                     ('uu', [64, HN, 128], F32), ('ww', [64, HN, 128], BF16), ('wTb', [128, HN, 64], BF16),
                                     ('vnew', [64, HN, 128], BF16), ('o1', [64, HN, 128], F32), ('oo', [64, HN, 128], F32),
                                     ('osq', [64, HN, 128], F32), ('ss', [64, HN], F32), ('onb', [64, HN, 128], BF16)]:
                    T_[n_] = kb.sb(f"g{g_}{n_}", shp, dt_, st)
                TG.append(T_)
            cnew = kb.sb("cnew", [128, 24, 3], F32, st)

            pre_norm(l)
            for m in range(8):
                ps, pk = proj_chunk(winodd[i, 24 + m], hT, 'hT')
                kb.act(zs[:, m, :], ps[:], AF.Silu, r=[pk], w=['zs'])
            if dbg:
                kb.op('pool', lambda E: E.memset(zs[:], 1.0), r=['zs'], w=['zs'])
            cw = convw_s[i]
            cr = carry[i]
            for grp in range(3):
                gsl = slice(grp * 8, grp * 8 + 8)
                accall = [f'acc{mm_}' for mm_ in range(8)]
                for mm_ in range(8):
                    m = grp * 8 + mm_
                    ps, pk = proj_chunk(winodd[i, m], hT, 'hT')
                    am = f'acc{mm_}'
                    kb.act(acc[:, mm_, :], ps[:], AF.Copy, scale=cw[:, m, 3:4], r=[pk, f'convw{i}'], w=[am])
                    kb.cp('act', cnew[:, m, :], ps[:, TB - 3:TB], r=[pk], w=['cnew'])
                    for j in range(3):
                        sh = 3 - j
                        kb.stt(acc[:, mm_, sh:TB], ps[:, 0:TB - sh], cw[:, m, j:j + 1], acc[:, mm_, sh:TB],
                               ALU.mult, ALU.add, r=[pk, f'convw{i}', am], w=[am])
                for j in range(3):
                    n = 3 - j
                    tv = tmpf[0][:, 0:8 * n].rearrange("p (m n) -> p m n", n=n)
                    kb.tt('dve', tv, cr[:, gsl, j:3], bc(cw[:, gsl, j:j + 1], [128, 8, n]), ALU.mult,
                          r=[f'carry{i}', f'convw{i}'], w=['tmpf0'])
                    kb.tt('dve', acc[:, :, 0:n], acc[:, :, 0:n], tv, ALU.add, r=['tmpf0'] + accall, w=accall)
                kb.cp('dve', cr[:, gsl, :], cnew[:, gsl, :], r=['cnew'], w=[f'carry{i}'])
                kb.act(acc[:], acc[:], AF.Silu, r=accall, w=accall)
                if grp == 2:
                    kb.cp('pool', vb[:], acc[:], r=accall, w=['vb'])
                    continue
                for mm_ in range(8):
                    m = grp * 8 + mm_
                    q = sqb[m % 2]
                    kb.act(q[:], acc[:, mm_, :], AF.Square, r=[f'acc{mm_}'], w=[f'sqb{m % 2}'])
                    kb.mm(psC[:], onesb[:], q[:], r=[f'sqb{m % 2}', 'onesb'], w=['psC'])
                    t = tmpf[m % 2]
                    kb.act(t[:], psC[:], AF.Ln, bias=EPS, r=['psC'], w=[f'tmpf{m % 2}'])
                    kb.act(t[:], t[:], AF.Exp, scale=-0.5, r=[f'tmpf{m % 2}'], w=[f'tmpf{m % 2}'])
                    kb.stt(qkn[:, m, :], acc[:, mm_, :], (128.0 ** -0.5) if m < 8 else 1.0, t[:], ALU.mult, ALU.mult,
                           r=[f'acc{mm_}', f'tmpf{m % 2}'], w=['qkn'])
            for j in range(NCH):
                for k in range(8):
                    kb.mm(psC[0:64, j * 16:(j + 1) * 16], hT[:, k, j * C:(j + 1) * C], wba_s[i][:, k, :],
                          start=(k == 0), stop=(k == 7), r=['hT', f'wba{i}'], w=['psC'])
            kb.cp('dve', ba[:], psC[0:64, 0:128].rearrange("p (j e) -> p j e", e=16), r=['psC'], w=['ba'])
            kb.act(beta[:], ba[:, :, 0:8], AF.Sigmoid, r=['ba'], w=['beta'])
            kb.tt('dve', aall[:], ba[:, :, 8:16], bc(dtb_s[i][0:64, :].unsqueeze(1), [64, 8, 8]), ALU.add,
                  r=['ba', f'dtb{i}'], w=['aall'])
            kb.act(aall[:], aall[:], AF.Exp, r=['aall'], w=['aall'])
            kb.act(aall[:], aall[:], AF.Ln, bias=1.0, r=['aall'], w=['aall'])
            kb.tt('dve', aall[:], aall[:], bc(negA[i][0:64, :].unsqueeze(1), [64, 8, 8]), ALU.mult,
                  r=['aall', f'negA{i}'], w=['aall'])
            a2 = aall[:].rearrange("p j h -> p (j h)")
            kb.mm(psI[0][0:64, 0:64], U, a2, r=['cst', 'aall'], w=['psI0'])
            kb.act(eg[:].rearrange("p j h -> p (j h)"), psI[0][0:64, 0:64], AF.Exp, r=['psI0'], w=['eg'])
            kb.mm(psI[1][0:64, 0:64], Lst, a2, r=['cst', 'aall'], w=['psI1'])
            kb.act(eend[:].rearrange("p j h -> p (j h)"), psI[1][0:64, 0:64], AF.Exp, r=['psI1'], w=['eend'])
            kb.mm(psI[0][:, 64:128], onesf[:], a2, r=['onesf', 'aall'], w=['psI0'])
            kb.act(cdb[:].rearrange("p j h -> p (j h)"), psI[0][:, 64:128], AF.Exp, r=['psI0'], w=['cdb'])

            S = Sst[i]
            Sbf = Sb[i]

            def gdn_stream(g):
                T = TG[g]
                HN = T['HN']
                K = lambda n: f'{n}_{g}'
                pA, pB, pC, pT = T['pA'], T['pB'], T['pC'], T['pT']
                kA, kC, kT = T['kA'], T['kC'], T['kT']
                kBs = T['kB']
                W = HN * 64
                hs = slice(g * HN, (g + 1) * HN)
                f2 = lambda t: t[:].rearrange("p h c -> p (h c)")
                R2, decT, dSb, dI, Pf, intraT, Accb = (T[n] for n in ['R2', 'decT', 'dSb', 'dI', 'Pf', 'intraT', 'Accb'])
                vk, kend, uu, ww, wTb, vnew, o1, oo, osq, ss, onb = (T[n] for n in
                    ['vk', 'kend', 'uu', 'ww', 'wTb', 'vnew', 'o1', 'oo', 'osq', 'ss', 'onb'])
                for j in range(NCH):
                    cs = slice(j * C, (j + 1) * C)
                    be = beta[:, j, hs]
                    kb.tt('dve', R2[:], bc(U.unsqueeze(1), [64, HN, 64]), bc(aall[:, j, hs].unsqueeze(2), [64, HN, 64]),
                          ALU.mult, r=['cst', 'aall'], w=[K('R2')])
                    kb.mm(pC[0:64, 0:W], Lst, f2(R2), r=['cst', K('R2')], w=kC)
                    kb.act(f2(decT), pC[0:64, 0:W], AF.Exp, r=kC, w=[K('decT')])
                    kb.tt('pool', dSb[:], decT[:], bc(MsN.unsqueeze(1), [64, HN, 64]), ALU.mult, r=[K('decT'), 'cst'], w=[K('dSb')])
                    kb.tt('pool', dSb[:], dSb[:], bc(be.unsqueeze(2), [64, HN, 64]), ALU.mult, r=[K('dSb'), 'beta'], w=[K('dSb')])
                    kb.tt('pool', dI[:], decT[:], bc(Mi.unsqueeze(1), [64, HN, 64]), ALU.mult, r=[K('decT'), 'cst'], w=[K('dI')])
                    for h in range(HN):
                        H_ = g * HN + h
                        kb.mm(pA[0:64, h * 64:(h + 1) * 64], qkn[:, 8 + H_, cs], qkn[:, 8 + H_, cs], r=['qkn'], w=kA)
                    for h in range(HN):
                        H_ = g * HN + h
                        kb.mm(pA[0:64, W + h * 64:W + (h + 1) * 64], qkn[:, 8 + H_, cs], qkn[:, H_, cs], r=['qkn'], w=kA)
                    kb.tt('dve', f2(Pf), pA[0:64, 0:W], f2(dSb), ALU.mult, r=kA + [K('dSb')], w=[K('Pf')])
                    kb.tt('dve', f2(intraT), pA[0:64, W:2 * W], f2(dI), ALU.mult, r=kA + [K('dI')], w=[K('intraT')])
                    neumann(T, g)
                    for h in range(HN):
                        kb.tr(pT[0:64, h * 128:(h + 1) * 128], qkn[:, 8 + g * HN + h, cs], identb[:], r=['qkn', 'identb'], w=kT)
                    pT3 = pT[0:64, 0:HN * 128].rearrange("p (h d) -> p h d", d=128)
                    kb.tt('dve', vk[:, :, 128:256], pT3, bc(eg[:, j, hs].unsqueeze(2), [64, HN, 128]), ALU.mult,
                          r=kT + ['eg'], w=[K('vk1')])
                    kb.tt('dve', kend[:], pT3, bc(eend[:, j, hs].unsqueeze(2), [64, HN, 128]), ALU.mult,
                          r=kT + ['eend'], w=[K('kend')])
                    for h in range(HN):
                        kb.tr(pT[0:64, h * 128:(h + 1) * 128], vb[:, g * HN + h, cs], identb[:], r=['vb', 'identb'], w=kT)
                    kb.cp('act', vk[:, :, 0:128], pT3, r=kT, w=[K('vk0')])
                    for h in range(HN):
                        kb.mm(pA[0:64, h * 128:(h + 1) * 128], Accb[:, h, :], vk[:, h, 0:128], r=[K('Accb'), K('vk0')], w=kA)
                    for h in range(HN):
                        kb.mm(pB[0:64, h * 128:(h + 1) * 128], Accb[:, h, :], vk[:, h, 128:256], r=[K('Accb'), K('vk1')], w=kBs)
                    pA3 = pA[0:64, :].rearrange("p (h d) -> p h d", d=128)
                    pB3 = pB[0:64, :].rearrange("p (h d) -> p h d", d=128)
                    bet3 = bc(be.unsqueeze(2), [64, HN, 128])
                    kb.tt('dve', uu[:], pA3, bet3, ALU.mult, r=kA + ['beta'], w=[K('uu')])
                    kb.tt('dve', ww[:], pB3, bet3, ALU.mult, r=kBs + ['beta'], w=[K('ww')])
                    for h in range(HN):
                        kb.tr(pT[:, h * 64:(h + 1) * 64], ww[:, h, :], identb[0:64, 0:64], r=[K('ww'), 'identb'], w=kT)
                    kb.cp('act', f2(wTb), pT[:, 0:W], r=kT, w=[K('wTb')])
                    kS, kSb = f'S{i}_{g}', f'Sb{i}_{g}'
                    for h in range(HN):
                        kb.mm(pA[0:64, h * 128:(h + 1) * 128], wTb[:, h, :], Sbf[:, g * HN + h, :], r=[K('wTb'), kSb], w=kA)
                    kb.tt('dve', vnew[:], uu[:], pA3, ALU.subtract, r=[K('uu')] + kA, w=[K('vnew')])
                    for h in range(HN):
                        kb.mm(pB[0:64, h * 128:(h + 1) * 128], qkn[:, g * HN + h, cs], Sbf[:, g * HN + h, :], r=['qkn', kSb], w=kBs)
                    kb.tt('dve', o1[:], pB3, bc(eg[:, j, hs].unsqueeze(2), [64, HN, 128]), ALU.mult, r=kBs + ['eg'], w=[K('o1')])
                    for h in range(HN):
                        kb.mm(pA[0:64, h * 128:(h + 1) * 128], intraT[:, h, :], vnew[:, h, :], r=[K('intraT'), K('vnew')], w=kA)
                    kb.tt('dve', oo[:], pA3, o1[:], ALU.add, r=kA + [K('o1')], w=[K('oo')])
                    for h in range(HN):
                        kb.mm(pB[:, h * 128:(h + 1) * 128], kend[:, h, :], vnew[:, h, :], r=[K('kend'), K('vnew')], w=kBs)
                    kb.tt('pool', S[:, hs, :], S[:, hs, :], bc(cdb[:, j, hs].unsqueeze(2), [128, HN, 128]), ALU.mult,
                          r=[kS, 'cdb'], w=[kS])
                    kb.tt('dve', S[:, hs, :], S[:, hs, :], pB[:, :].rearrange("p (h d) -> p h d", d=128), ALU.add,
                          r=[kS] + kBs, w=[kS])
                    kb.cp('act', Sbf[:, hs, :], S[:, hs, :], r=[kS], w=[kSb])
                    kb.tt('pool', osq[:], oo[:], oo[:], ALU.mult, r=[K('oo')], w=[K('osq')])
                    kb.op('dve', lambda E: E.tensor_reduce(out=ss[:], in_=osq[:], axis=AX.X, op=ALU.add), r=[K('osq')], w=[K('ss')])
                    kb.act(ss[:], ss[:], AF.Ln, scale=1.0 / 128, bias=EPS, r=[K('ss')], w=[K('ss')])
                    kb.act(ss[:], ss[:], AF.Exp, scale=-0.5, r=[K('ss')], w=[K('ss')])
                    kb.tt('pool', osq[:], oo[:], bc(ss[:].unsqueeze(2), [64, HN, 128]), ALU.mult, r=[K('oo'), K('ss')], w=[K('osq')])
                    kb.tt('pool', onb[:], osq[:], bc(gnw_s[i][0:64, :].unsqueeze(1), [64, HN, 128]), ALU.mult,
                          r=[K('osq'), f'gnw{i}'], w=[K('onb')])
                    for h in range(HN):
                        kb.tr(pT[:, h * 64:(h + 1) * 64], onb[:, h, :], identb[0:64, 0:64], r=[K('onb'), 'identb'], w=kT)
                    kb.tt('dve', ygT[:, hs, cs], pT[:, 0:W].rearrange("p (h c) -> p h c", c=64), zs[:, hs, cs], ALU.mult,
                          r=kT + ['zs'], w=[f'ygT{g}'])

            kb.run_streams([(lambda g_=g_: gdn_stream(g_)) for g_ in range(GG)])
            if not dbg:
                out_proj(l)
        kb.barrier()

    RW_DS = float(np.exp(-0.5))
    LN_EPS = 1e-5 * 64

    def rwkv_phase(l, i, bi):
        P = rp[i]
        rk_ = f'rp{i}'
        with ExitStack() as st:
            gbuf = kb.sb("gbuf", [128, TB + 1], F32, st)
            tw = kb.sb("tw", [64, TB], BF16, st)
            xab = kb.sb("xab", [128, TB], BF16, st)
            sg = kb.sb("sg", [128, TB], BF16, st)
            gz = kb.sb("gz", [128, 4, TB], BF16, st)
            bonus = kb.sb("bonus", [128, 4, TB], BF16, st)
            vbT = kb.sb("vbT", [128, 4, TB], BF16, st)
            ops6 = {n: kb.sb("op_" + n, [128, 4, TB], BF16, st) for n in ['At', 'Qt', 'Kh', 'Bh', 'Kb', 'Bb']}
            GC = kb.sb("GC", [64, NCH, 8], F32, st)
            rt = [kb.sb(f"rt{k}", [128, TB], F32, st) for k in range(12)]
            rf, kf, vf, ldm, aam, kk, kp, bb_, lg, lgx, t1, t2 = rt
            rtk = [f'rt{k}' for k in range(12)]
            krf, kkf, kvf, kld, kaa, kkk, kkp, kbb, klg, klgx, kt1, kt2 = rtk
            fl = [t1, t2]
            flk = [kt1, kt2]
            RG = []
            RGN = int(os.environ.get('RWKV_G', '2'))
            HN = 8 // RGN
            for g_ in range(RGN):
                T_ = psum_group(g_, RGN)
                for n_, shp, dt_ in [('Pf', [64, HN, 64], F32), ('Accf', [64, HN, 64], F32), ('PTf', [64, HN, 64], F32),
                                     ('Pw', [64, HN, 64], F32), ('Accb', [64, HN, 64], BF16), ('Mav', [64, HN, 64], BF16),
                                     ('Mqk', [64, HN, 64], BF16), ('MqbN', [64, HN, 64], BF16), ('Vt', [64, HN * 64], BF16),
                                     ('Kbt', [64, HN * 64], BF16), ('BbtN', [64, HN * 64], BF16), ('RHSb', [64, HN * 64], BF16),
                                     ('Pb2', [64, HN * 64], BF16), ('tmpR', [64, HN * 64], F32), ('tmpO', [64, HN * 64], F32),
                                     ('oT', [64, HN, 64], F32), ('oc', [64, HN, 64], F32), ('osq', [64, HN, 64], F32),
                                     ('s8', [64, HN], F32), ('s8b', [64, HN], F32), ('onb', [64, HN * 64], BF16),
                                     ('t_o', [128, HN // 2, C], F32)]:
                    T_[n_] = kb.sb(f"r{g_}{n_}", shp, dt_, st)
                T_['opo'] = {n_: kb.sb(f"r{g_}opo_{n_}", [64, HN // 2, C], BF16, st) for n_ in ['At', 'Qt', 'Kh', 'Bh']}
                RG.append(T_)

            def shift(ps, pk, mf, out, outk):
                kb.cp('dve', gbuf[:, 0:1], tcarry[i][:, mf:mf + 1], r=[f'tcarry{i}'], w=['gbuf0'])
                kb.act(gbuf[:, 1:TB + 1], ps[:], AF.Copy, scale=P[:, mf:mf + 1], r=[pk, rk_], w=['gbuf'])
                kb.cp('pool', tcarry[i][:, mf:mf + 1], gbuf[:, TB:TB + 1], r=['gbuf'], w=[f'tcarry{i}'])
                kb.stt(out[:], ps[:], omm[i][:, mf:mf + 1], gbuf[:, 0:TB], ALU.mult, ALU.add,
                       r=[pk, f'omm{i}', 'gbuf', 'gbuf0'], w=[outk])

            for k2 in range(2):
                ps, pk = proj_chunk(winev[i, 16 + k2], hT, 'hT')
                shift(ps, pk, 12 + k2, fl[k2], flk[k2])
            kb.act(tw[:], fl[0][0:64, :], AF.Tanh, r=[flk[0]], w=['tw'])
            kb.cp('pool', xab[64:128, :], fl[0][64:128, :], r=[flk[0]], w=['xab'])
            kb.act(sg[:], fl[1][:], AF.Sigmoid, r=[flk[1]], w=['sg'])
            for m in range(4):
                mc = slice(m * 128, (m + 1) * 128)
                kb.mm(psC[:], wup_b[i][0:64, mc], tw[:], r=[f'wup{i}', 'tw'], w=['psC'])
                kb.act(ldm[:], psC[:], AF.Sigmoid, bias=P[:, 14 + m:15 + m], r=['psC', rk_], w=[kld])
                kb.ts('pool', ldm[:], ldm[:], -RW_DS, ALU.mult, r=[kld], w=[kld])
                kb.mm(psC[:], aup_b[i][64:128, mc], xab[64:128, :], r=[f'aup{i}', 'xab'], w=['psC'])
                kb.act(aam[:], psC[:], AF.Sigmoid, bias=P[:, 18 + m:19 + m], r=['psC', rk_], w=[kaa])
                kb.mm(psC[:], gup_b[i][:, mc], sg[:], r=[f'gup{i}', 'sg'], w=['psC'])
                kb.tt('dve', gz[:, m, :], psC[:], zs[:, 4 + m, :], ALU.mult, r=['psC', 'zs'], w=['gz'])
                ps, pk = proj_chunk(winev[i, 4 + m], hT, 'hT')
                shift(ps, pk, m, rf, krf)
                ps, pk = proj_chunk(winev[i, 8 + m], hT, 'hT')
                shift(ps, pk, 4 + m, kf, kkf)
                ps, pk = proj_chunk(winev[i, 12 + m], hT, 'hT')
                shift(ps, pk, 8 + m, vf, kvf)
                kb.act(sqb[0][:], kf[:], AF.Square, scale=P[:, 22 + m:23 + m], r=[kkf, rk_], w=['sqb0'])
                kb.mm(psC[:], bonesb[:], sqb[0][:], r=['bonesb', 'sqb0'], w=['psC'])
                kb.act(t1[:], psC[:], AF.Ln, bias=EPS, r=['psC'], w=[kt1])
                kb.act(t1[:], t1[:], AF.Exp, scale=-0.5, r=[kt1], w=[kt1])
                kb.stt(kk[:], kf[:], P[:, 22 + m:23 + m], t1[:], ALU.mult, ALU.mult, r=[kkf, rk_, kt1], w=[kkk])
                kb.ts('pool', t2[:], aam[:], -1.0, ALU.add, P[:, 26 + m:27 + m], ALU.mult, r=[kaa, rk_], w=[kt2])
                kb.stt(kp[:], t2[:], 1.0, kf[:], ALU.add, ALU.mult, r=[kt2, kkf], w=[kkp])
                kb.tt('pool', bb_[:], kk[:], aam[:], ALU.mult, r=[kkk, kaa], w=[kbb])
                kb.stt(sqb[1][:], rf[:], P[:, 30 + m:31 + m], kp[:], ALU.mult, ALU.mult, r=[krf, rk_, kkp], w=['sqb1'])
                kb.mm(psC[:], bonesb[:], sqb[1][:], r=['bonesb', 'sqb1'], w=['psC'])
                kb.tt('dve', bonus[:, m, :], psC[:], vf[:], ALU.mult, r=['psC', kvf], w=['bonus'])
                kb.cp('act', vbT[:, m, :], vf[:], r=[kvf], w=['vbT'])
                kb.op('dve', lambda E: E.tensor_tensor_scan(out=lg[:], data0=cmask, data1=ldm[:], initial=0.0,
                                                            op0=ALU.mult, op1=ALU.add), r=['cst', kld], w=[klg])
                kb.tt('pool', lgx[:], lg[:], ldm[:], ALU.subtract, r=[klg, kld], w=[klgx])
                lg3 = lg[:].rearrange("p (j c) -> p j c", c=C)
                kb.act(t1[:], lg[:], AF.Exp, r=[klg], w=[kt1])
                kb.tt('dve', ops6['Qt'][:, m, :], rf[:], t1[:], ALU.mult, r=[krf, kt1], w=['op_Qt'])
                kb.act(t1[:], lgx[:], AF.Exp, r=[klgx], w=[kt1])
                kb.tt('dve', ops6['At'][:, m, :], kk[:], t1[:], ALU.mult, r=[kkk, kt1], w=['op_At'])
                kb.act(t1[:], lg[:], AF.Exp, scale=-1.0, r=[klg], w=[kt1])
                kb.tt('dve', ops6['Kh'][:, m, :], kp[:], t1[:], ALU.mult, r=[kkp, kt1], w=['op_Kh'])
                kb.tt('pool', ops6['Bh'][:, m, :], bb_[:], t1[:], ALU.mult, r=[kbb, kt1], w=['op_Bh'])
                kb.tt('dve', t2[:].rearrange("p (j c) -> p j c", c=C), bc(lg3[:, :, C - 1:C], [128, NCH, C]), lg3,
                      ALU.subtract, r=[klg], w=[kt2])
                kb.act(t2[:], t2[:], AF.Exp, r=[kt2], w=[kt2])
                kb.tt('dve', ops6['Kb'][:, m, :], kp[:], t2[:], ALU.mult, r=[kkp, kt2], w=['op_Kb'])
                kb.tt('pool', ops6['Bb'][:, m, :], bb_[:], t2[:], ALU.mult, r=[kbb, kt2], w=['op_Bb'])
                for par in range(2):
                    kb.act(GC[:, :, 2 * m + par], lg3[par * 64:(par + 1) * 64, :, C - 1], AF.Exp, r=[klg], w=['GC'])

            H = Hst[i]
            Hbf = Hb[i]

            def rwkv_stream(g):
                T = RG[g]
                HN = T['HN']
                K = lambda n: f'r{n}_{g}'
                pA, pB, pC, pT = T['pA'], T['pB'], T['pC'], T['pT']
                kA, kB, kC, kT = T['kA'], T['kB'], T['kC'], T['kT']
                W = HN * 64
                hs = slice(g * HN, (g + 1) * HN)
                NM = HN // 2
                ms = slice(NM * g, NM * g + NM)
                f2 = lambda t: t[:].rearrange("p h c -> p (h c)")
                m3 = lambda mk: bc(mk.unsqueeze(1), [64, HN, 64])
                p3 = lambda ap: ap.rearrange("p (h c) -> p h c", c=64)
                opo = T['opo']
                Pf, Accb, Mav, Mqk, MqbN = T['Pf'], T['Accb'], T['Mav'], T['Mqk'], T['MqbN']
                Vt, Kbt, BbtN, RHSb, Pb2, tmpR, tmpO = (T[n] for n in ['Vt', 'Kbt', 'BbtN', 'RHSb', 'Pb2', 'tmpR', 'tmpO'])
                oT, oc, osq, s8, s8b, onb, t_o = (T[n] for n in ['oT', 'oc', 'osq', 's8', 's8b', 'onb', 't_o'])
                kH, kHb = f'H{i}_{g}', f'Hb{i}_{g}'
                for j in range(NCH):
                    cs = slice(j * C, (j + 1) * C)
                    for n in ['At', 'Qt', 'Kh', 'Bh']:
                        kb.cp('dve', opo[n][:], ops6[n][64:128, ms, cs], r=['op_' + n], w=[K('opo_' + n)])

                    def X(n, hl):
                        h = g * HN + hl
                        return ops6[n][0:64, h // 2, cs] if h % 2 == 0 else opo[n][:, hl // 2, :]
                    xk = lambda *ns: [k for n in ns for k in ('op_' + n, K('opo_' + n))]
                    for h in range(HN):
                        kb.mm(pA[0:64, h * 64:(h + 1) * 64], X('Bh', h), X('At', h), r=xk('Bh', 'At'), w=kA)
                    for h in range(HN):
                        kb.mm(pA[0:64, W + h * 64:W + (h + 1) * 64], X('Kh', h), X('At', h), r=xk('Kh', 'At'), w=kA)
                    for h in range(HN):
                        kb.mm(pB[0:64, h * 64:(h + 1) * 64], X('Kh', h), X('Qt', h), r=xk('Kh', 'Qt'), w=kB)
                    for h in range(HN):
                        kb.mm(pB[0:64, W + h * 64:W + (h + 1) * 64], X('Bh', h), X('Qt', h), r=xk('Bh', 'Qt'), w=kB)
                    kb.tt('dve', Pf[:], p3(pA[0:64, 0:W]), m3(MsN), ALU.mult, r=kA + ['cst'], w=[f'Pf_{g}'])
                    kb.tt('dve', Mav[:], p3(pA[0:64, W:2 * W]), m3(Ms), ALU.mult, r=kA + ['cst'], w=[K('Mav')])
                    kb.tt('dve', Mqk[:], p3(pB[0:64, 0:W]), m3(Mi), ALU.mult, r=kB + ['cst'], w=[K('Mqk')])
                    kb.tt('dve', MqbN[:], p3(pB[0:64, W:2 * W]), m3(MiN), ALU.mult, r=kB + ['cst'], w=[K('MqbN')])
                    neumann(T, g)
                    kAcc = f'Accb_{g}'
                    for ml in range(NM):
                        kb.tr(pT[0:64, ml * 128:(ml + 1) * 128], vbT[:, NM * g + ml, cs], identb[:], r=['vbT', 'identb'], w=kT)
                    kb.cp('act', Vt[:], pT[0:64, 0:W], r=kT, w=[K('Vt')])
                    for h in range(HN):
                        kb.mm(pC[0:64, h * 64:(h + 1) * 64], X('At', h), Hbf[:, g * HN + h, :], r=xk('At') + [kHb], w=kC)
                    kb.cp('act', tmpR[:], pC[0:64, 0:W], r=kC, w=[K('tmpR')])
                    for h in range(HN):
                        kb.mm(pA[0:64, h * 64:(h + 1) * 64], Mav[:, h, :], Vt[:, h * 64:(h + 1) * 64], r=[K('Mav'), K('Vt')], w=kA)
                    kb.tt('dve', RHSb[:], pA[0:64, 0:W], tmpR[:], ALU.add, r=kA + [K('tmpR')], w=[K('RHSb')])
                    for h in range(HN):
                        kb.mm(pC[0:64, h * 64:(h + 1) * 64], Accb[:, h, :], RHSb[:, h * 64:(h + 1) * 64], r=[kAcc, K('RHSb')], w=kC)
                    kb.cp('act', Pb2[:], pC[0:64, 0:W], r=kC, w=[K('Pb2')])
                    for h in range(HN):
                        kb.mm(pB[0:64, h * 64:(h + 1) * 64], X('Qt', h), Hbf[:, g * HN + h, :], r=xk('Qt') + [kHb], w=kB)
                    kb.cp('act', tmpO[:], pB[0:64, 0:W], r=kB, w=[K('tmpO')])
                    for h in range(HN):
                        hsl = slice(h * 64, (h + 1) * 64)
                        kb.mm(pA[0:64, hsl], Mqk[:, h, :], Vt[:, hsl], start=True, stop=False, r=[K('Mqk'), K('Vt')], w=kA)
                        kb.mm(pA[0:64, hsl], MqbN[:, h, :], Pb2[:, hsl], start=False, stop=True, r=[K('MqbN'), K('Pb2')], w=kA)
                    kb.tt('dve', f2(oT), pA[0:64, 0:W], tmpO[:], ALU.add, r=kA + [K('tmpO')], w=[K('oT')])
                    for ml in range(NM):
                        kb.tr(pT[0:64, ml * 128:(ml + 1) * 128], ops6['Kb'][:, NM * g + ml, cs], identb[:], r=['op_Kb', 'identb'], w=kT)
                    kb.cp('act', Kbt[:], pT[0:64, 0:W], r=kT, w=[K('Kbt')])
                    for ml in range(NM):
                        kb.tr(pT[0:64, ml * 128:(ml + 1) * 128], ops6['Bb'][:, NM * g + ml, cs], identb[:], r=['op_Bb', 'identb'], w=kT)
                    kb.ts('dve', BbtN[:], pT[0:64, 0:W], -1.0, ALU.mult, r=kT, w=[K('BbtN')])
                    for h in range(HN):
                        hsl = slice(h * 64, (h + 1) * 64)
                        kb.mm(pC[0:64, hsl], Kbt[:, hsl], Vt[:, hsl], start=True, stop=False, r=[K('Kbt'), K('Vt')], w=kC)
                        kb.mm(pC[0:64, hsl], BbtN[:, hsl], Pb2[:, hsl], start=False, stop=True, r=[K('BbtN'), K('Pb2')], w=kC)
                    kb.tt('pool', H[:, hs, :], H[:, hs, :], bc(GC[:, j, hs].unsqueeze(2), [64, HN, 64]), ALU.mult, r=[kH, 'GC'], w=[kH])
                    kb.tt('dve', H[:, hs, :], H[:, hs, :], p3(pC[0:64, 0:W]), ALU.add, r=[kH] + kC, w=[kH])
                    kb.cp('act', Hbf[:, hs, :], H[:, hs, :], r=[kH], w=[kHb])
                    kb.op('dve', lambda E: E.tensor_reduce(out=s8[:], in_=oT[:], axis=AX.X, op=ALU.add), r=[K('oT')], w=[K('s8')])
                    kb.ts('dve', s8[:], s8[:], -1.0 / 64, ALU.mult, r=[K('s8')], w=[K('s8')])
                    kb.tt('pool', oc[:], oT[:], bc(s8[:].unsqueeze(2), [64, HN, 64]), ALU.add, r=[K('oT'), K('s8')], w=[K('oc')])
                    kb.tt('pool', osq[:], oc[:], oc[:], ALU.mult, r=[K('oc')], w=[K('osq')])
                    kb.op('dve', lambda E: E.tensor_reduce(out=s8b[:], in_=osq[:], axis=AX.X, op=ALU.add), r=[K('osq')], w=[K('s8b')])
                    kb.act(s8b[:], s8b[:], AF.Ln, scale=1.0 / 64, bias=LN_EPS, r=[K('s8b')], w=[K('s8b')])
                    kb.act(s8b[:], s8b[:], AF.Exp, scale=-0.5, r=[K('s8b')], w=[K('s8b')])
                    kb.tt('pool', oc[:], oc[:], bc(s8b[:].unsqueeze(2), [64, HN, 64]), ALU.mult, r=[K('oc'), K('s8b')], w=[K('oc')])
                    kb.tt('pool', f2(oc), f2(oc), lnw_s[i][:, g * W:(g + 1) * W], ALU.mult, r=[K('oc'), f'lnw{i}'], w=[K('oc')])
                    kb.tt('pool', onb[:], f2(oc), lnb_s[i][:, g * W:(g + 1) * W], ALU.add, r=[K('oc'), f'lnb{i}'], w=[K('onb')])
                    for ml in range(NM):
                        kb.tr(pT[:, ml * 64:(ml + 1) * 64], onb[:, ml * 128:(ml + 1) * 128], identb[0:64, 0:64],
                              r=[K('onb'), 'identb'], w=kT)
                    kb.tt('dve', t_o[:], pT[:, 0:NM * 64].rearrange("p (m c) -> p m c", c=C), bonus[:, ms, cs], ALU.add,
                          r=kT + ['bonus'], w=[K('t_o')])
                    kb.tt('dve', ygT[:, 4 + NM * g:4 + NM * g + NM, cs], t_o[:], gz[:, ms, cs], ALU.mult, r=[K('t_o'), 'gz'], w=[f'ygT{g}'])

            kb.run_streams([(lambda g_=g_: rwkv_stream(g_)) for g_ in range(RGN)])

    def s5_phase(l, i, bi):
        STOP = int(os.environ.get('S5STOP', '99'))
        if STOP <= 0:
            kb.op('pool', lambda E: E.memset(ygT[:, 0:4, :], 0.0), w=['ygT'])
            return
        P = rp[i]
        rk_ = f'rp{i}'
        with ExitStack() as st:
            uT = kb.sb("uT", [128, 4, TB], F32, st)
            uTb = kb.sb("uTb", [128, 4, TB], BF16, st)
            uTb3 = kb.sb("uTb3", [128, 4, TB], BF16, st)
            yT = kb.sb("yT", [128, 4, TB], F32, st)
            ygb = kb.sb("ygb", [128, 4, TB], BF16, st)
            cs1f = kb.sb("cs1f", [128, 4, 8, 64], F32, st)
            cs1b = kb.sb("cs1b", [128, 8, 8, 64], BF16, st)
            cpb = kb.sb("cpb", [128, 8, 64], BF16, st)
            ea = kb.sb("ea", [128, 4, 64], F32, st)
            etm = kb.sb("etm", [128, 4, 64], F32, st)
            et = kb.sb("et", [128, 4, 64], F32, st)
            ch = kb.sb("ch", [128, 4, 64], F32, st)
            cN = kb.sb("cN", [128, 4, 64], F32, st)
            g1 = kb.sb("g1", [128, TB], F32, st)
            g2 = kb.sb("g2", [128, TB], F32, st)
            if STOP < 99 and 'c' not in os.environ.get('S5SKIP', ''):
                kb.op('pool', lambda E: E.memset(yT[:], 0.0), w=['yT'])
                kb.op('pool', lambda E: E.memset(ygb[:], 0.0), w=['ygb'])
                kb.op('pool', lambda E: E.memset(cs1b[:], 0.0), w=['cs1b'])
                kb.op('pool', lambda E: E.memset(cpb[:], 0.0), w=['cpb'])
                kb.op('pool', lambda E: E.memset(cs1f[:], 0.0), w=['cs1f'])
            for q in range(0 if 'd' in os.environ.get('S5SKIP', '') else 4):
                ps, pk = proj_chunk(winev[i, q], hT, 'hT')
                if 'e' not in os.environ.get('S5SKIP', ''):
                    kb.cp('act', uT[:, q, :], ps[:], r=[pk], w=['uT'])
                if 'f' not in os.environ.get('S5SKIP', ''):
                    kb.cp('dve', uTb[:, q, :], uT[:, q, :], r=['uT'], w=['uTb'])
                if 'a' not in os.environ.get('S5SKIP', ''):
                    kb.ts('dve', uTb3[64:128, q, :], uTb[64:128, q, :], cst[64:128, 1218:1219], ALU.mult,
                          r=['uTb', 'cst'], w=['uTb3'])
            banks = [psA[:, 0:512], psA[:, 512:1024], psB[:, 0:512], psB[:, 512:1024]]
            bkeys = ['psA0', 'psA1', 'psB0', 'psB1']
            TTv = [TT_d[k_][i].rearrange("p (q b e) n -> p q b e n", q=4, b=4) for k_ in range(4)]
            BJq = [kb.sb(f"BJq{k_}", [128, 2, 8, 128], BF16, st) for k_ in range(2)]
            CJloq = [kb.sb(f"CJloq{k_}", [128, 4, 9, 32], BF16, st) for k_ in range(2)]
            CJhiq = [kb.sb(f"CJhiq{k_}", [128, 4, 9, 64], BF16, st) for k_ in range(2)]
            TTq = [[kb.sb(f"TTq{k_}_{z_}", [128, 4, 64], F32, st) for k_ in range(4)] for z_ in range(2)]
            carv = s5carry[i][:].rearrange("p (q b e) -> p q b e", q=4, b=4)
            r8v = rho8[i][:].rearrange("p (q b e) -> p q b e", q=4, b=4)
            for q in range(4 if STOP > 1 else 0):
                qb = q % 2
                tk = f'tabq{qb}'
                kb.dma('sp', BJq[qb][:], BJ_d[i][:, q], f'tq{qb}', w=[tk])
                kb.dma('sp', CJloq[qb][:], CJlo_d[i][:, q], f'tq{qb}', w=[tk])
                kb.dma('sp', CJhiq[qb][:], CJhi_d[i][:, q], f'tq{qb}', w=[tk])
                for e in range(2):
                    tke = f'tte{e}'
                    for k_ in range(4):
                        kb.dma('sp', TTq[e][k_][:], TTv[k_][:, q, :, e, :], f'tt{e}', w=[tke])
                    T1v, T2v, T3v, T4v = TTq[e]
                    for b in range(4):
                        pb = slice(32 * b, 32 * b + 32) if b < 3 else slice(64, 128)
                        uv = (uTb if b < 3 else uTb3)[pb, q, :].rearrange("p (n j) -> p j n", j=8)
                        for j in range(8):
                            kb.mm(banks[b][:, j * 64:(j + 1) * 64], BJq[qb][pb, e, j, :], uv[:, j, :],
                                  r=[tk, 'uTb', 'uTb3'], w=[bkeys[b]])
                    if STOP <= 2:
                        continue
                    zA = psA[:, :].rearrange("p (b j n) -> p b j n", b=2, j=8)
                    zB = psB[:, :].rearrange("p (b j n) -> p b j n", b=2, j=8)
                    for (z, zk, bs) in ((zA, ['psA0', 'psA1'], slice(0, 2)), (zB, ['psB0', 'psB1'], slice(2, 4))):
                        kb.cp('dve', cs1f[:, bs, 0, :], z[:, :, 0, :], r=zk, w=['cs1f'])
                        for j in range(1, 8):
                            kb.tt('dve', cs1f[:, bs, j, :], z[:, :, j, :], cs1f[:, bs, j - 1, :], ALU.add,
                                  r=zk + ['cs1f'], w=['cs1f'])
                    cbv = cs1b[:].rearrange("p (b e) j n -> p b e j n", e=2)
                    kb.cp('act', cbv[:, :, e, :, :], cs1f[:], r=['cs1f'], w=['cs1b'])
                    if STOP <= 3:
                        continue
                    x = cs1f[:, :, 7, :]
                    kb.tt('pool', ea[:], x, T1v[:], ALU.mult, r=['cs1f', tke], w=['ea'])
                    kb.tt('dve', etm[0:64], cs1f[64:128, :, 7, :], T2v[64:128], ALU.mult, r=['cs1f', tke], w=['etm'])
                    kb.tt('dve', etm[64:128], cs1f[0:64, :, 7, :], T2v[0:64], ALU.mult, r=['cs1f', tke], w=['etm'])
                    kb.tt('pool', et[:], ea[:], etm[:], ALU.add, r=['ea', 'etm'], w=['et'])
                    for b in range(4):
                        kb.op('dve', lambda E, b=b: E.tensor_tensor_scan(
                            out=ch[:, b, :], data0=r8v[:, q, b, e:e + 1].to_broadcast([128, 64]), data1=et[:, b, :],
                            initial=carv[:, q, b, e:e + 1], op0=ALU.mult, op1=ALU.add), r=['et', f's5c{i}'], w=['ch'])
                    kb.tt('pool', ea[:], ch[:], T3v[:], ALU.mult, r=['ch', tke], w=['ea'])
                    kb.tt('dve', etm[0:64], ch[64:128], T4v[64:128], ALU.mult, r=['ch', tke], w=['etm'])
                    kb.tt('dve', etm[64:128], ch[0:64], T4v[0:64], ALU.mult, r=['ch', tke], w=['etm'])
                    kb.tt('pool', cN[:], ea[:], etm[:], ALU.add, r=['ea', 'etm'], w=['cN'])
                    cpv = cpb[:].rearrange("p (b e) n -> p b e n", e=2)
                    kb.cp('dve', cpv[:, :, e, 0:1], carv[:, q, :, e:e + 1], r=[f's5c{i}'], w=['cpb'])
                    kb.cp('act', cpv[:, :, e, 1:64], cN[:, :, 0:63], r=['cN'], w=['cpb'])
                    kb.cp('dve', carv[:, q, :, e:e + 1], cN[:, :, 63:64], r=['cN'], w=[f's5c{i}'])
                if STOP <= 4:
                    continue
                for j in range(8):
                    jc = slice(j * 64, (j + 1) * 64)
                    for b in range(2):
                        for e in range(2):
                            gl = 2 * b + e
                            o_ = psC[32 * b:32 * b + 32, jc]
                            kb.mm(o_, CJloq[qb][:, gl, j, :], cs1b[:, gl, j, :], start=(e == 0), stop=False,
                                  r=[tk, 'cs1b'], w=['psC'])
                            kb.mm(o_, CJloq[qb][:, gl, j + 1, :], cpb[:, gl, :], start=False, stop=(e == 1),
                                  r=[tk, 'cpb'], w=['psC'])
                    for gl in range(4, 8):
                        o_ = psC[64:128, jc]
                        kb.mm(o_, CJhiq[qb][:, gl - 4, j, :], cs1b[:, gl, j, :], start=(gl == 4), stop=False,
                              r=[tk, 'cs1b'], w=['psC'])
                        kb.mm(o_, CJhiq[qb][:, gl - 4, j + 1, :], cpb[:, gl, :], start=False, stop=(gl == 7),
                              r=[tk, 'cpb'], w=['psC'])
                yv = yT[:, q, :].rearrange("p (n j) -> p j n", j=8)
                uv32 = uT[:, q, :].rearrange("p (n j) -> p j n", j=8)
                kb.stt(yv, uv32, P[:, 34 + q:35 + q], psC[:, :].rearrange("p (j n) -> p j n", j=8), ALU.mult, ALU.add,
                       r=['uT', rk_, 'psC'], w=['yT'])
                xq = yT[:, q, :]
                kb.tt('pool', g1[:], xq, xq, ALU.mult, r=['yT'], w=['g1'])
                kb.ts('pool', g1[:], g1[:], 0.044715, ALU.mult, 1.0, ALU.add, r=['g1'], w=['g1'])
                kb.tt('pool', g1[:], g1[:], xq, ALU.mult, r=['g1', 'yT'], w=['g1'])
                kb.act(g2[:], g1[:], AF.Sigmoid, scale=1.5957691216057308, r=['g1'], w=['g2'])
                kb.tt('dve', yT[:, q, :], xq, g2[:], ALU.mult, r=['yT', 'g2'], w=['yT'])
                kb.cp('act', ygb[:, q, :], yT[:, q, :], r=['yT'], w=['ygb'])
            for qo in range(0 if 'b' in os.environ.get('S5SKIP', '') else 4):
                for k in range(4):
                    kb.mm(psC[:], gluw_b[i][:, k, qo * 128:(qo + 1) * 128], ygb[:, k, :], start=(k == 0), stop=(k == 3),
                          r=[f'gluw{i}', 'ygb'], w=['psC'])
                kb.act(g2[:], psC[:], AF.Sigmoid, bias=P[:, 38 + qo:39 + qo], r=['psC', rk_], w=['g2'])
                kb.tt('dve', g1[:], yT[:, qo, :], g2[:], ALU.mult, r=['yT', 'g2'], w=['g1'])
                kb.tt('dve', ygT[:, qo, :], g1[:], zs[:, qo, :], ALU.mult, r=['g1', 'zs'], w=['ygT'])

    def even_layer(l, bi):
        i = l // 2
        pre_norm(l)
        for m in range(8):
            ps, pk = proj_chunk(winev[i, 18 + m], hT, 'hT')
            kb.act(zs[:, m, :], ps[:], AF.Silu, r=[pk], w=['zs'])
        if dbg:
            kb.op('pool', lambda E: E.memset(zs[:], 1.0), r=['zs'], w=['zs'])
        s5_phase(l, i, bi)
        if not os.environ.get('RWSKIP'):
            rwkv_phase(l, i, bi)
        if not dbg:
            out_proj(l)
        kb.barrier()

    for bi in range(nblocks):
        ts_ = slice(bi * TB, (bi + 1) * TB)
        kb.dma('sp', xres[:], xT[:, ts_].rearrange("(k p) t -> p k t", p=128), 'xin', w=['xres'])
        for l in layers:
            if l % 2 == 1:
                odd_layer(l, bi)
            else:
                even_layer(l, bi)
        with ExitStack() as st:
            obuf = kb.sb("obuf", [128, 8, TB], F32, st)
            if final:
                rms_stats(xres, 'xres')
                for k in range(8):
                    kb.stt(obuf[:, k, :], xres[:, k, :], fnw_s[:, k:k + 1], rstd[:], ALU.mult, ALU.mult,
                           r=['xres', 'fnw_s', 'rstd'], w=['obuf'])
            elif dbg:
                kb.cp('dve', obuf[:], ygT[:], r=['ygT', 'ygT0', 'ygT1'], w=['obuf'])
            else:
                kb.cp('dve', obuf[:], xres[:], r=['xres'], w=['obuf'])
            kb.dma('sp', outT[:, ts_].rearrange("(k p) t -> p k t", p=128), obuf[:], 'xout', r=['obuf'])
            kb.barrier()
    kb.es.close()
    return nc, kb


def chunkify(W, cols):
    Wc = W[:, cols]
    n_m = Wc.shape[1] // 128
    return np.ascontiguousarray(Wc.reshape(8, 128, n_m, 128).transpose(2, 1, 0, 3))


def prep_shared(inp):
    sh = {}
    sh["normw"] = np.ascontiguousarray(inp["norm_w"].reshape(4, 8, 128).transpose(2, 0, 1).reshape(128, 32))
    sh["adaw"] = np.ascontiguousarray(inp["ada_w"])
    sh["adab"] = np.ascontiguousarray(inp["ada_b"].reshape(4, 24, 128).transpose(2, 0, 1).reshape(128, 96))
    sh["woutr"] = np.stack([chunkify(inp["w_out"][l], np.arange(1024)) for l in range(4)])
    sh["fnw"] = np.ascontiguousarray(inp["final_norm_w"].reshape(8, 128).T)
    sh["consts"] = make_consts()
    cols = np.concatenate([np.arange(0, 3072), np.arange(3088, 4112)])
    sh["winodd"] = np.stack([chunkify(inp["odd_w_in"][i], cols) for i in range(2)])
    sh["wba"] = np.ascontiguousarray(
        np.stack([inp["odd_w_in"][i][:, 3072:3088].reshape(8, 128, 16).transpose(1, 0, 2) for i in range(2)]))
    sh["convw"] = np.ascontiguousarray(
        np.stack([inp["gdn_conv_w"][i].reshape(4, 24, 128).transpose(2, 1, 0) for i in range(2)]))
    sh["alog"] = np.ascontiguousarray(np.broadcast_to(inp["gdn_a_log"][:, None, :], (2, 128, 8)))
    sh["dtb"] = np.ascontiguousarray(np.broadcast_to(inp["gdn_dt_bias"][:, None, :], (2, 128, 8)))
    sh["gnw"] = np.ascontiguousarray(np.broadcast_to(inp["gdn_norm_w"][:, None, :], (2, 128, 128)))
    sh["winev"] = np.stack([chunkify(inp["even_w_in"][i], np.arange(3328)) for i in range(2)])
    pc = lambda v, n: v.reshape(n, 128).T
    sh["rp"] = np.stack([np.concatenate([pc(inp["rwkv_mu"][i], 14), pc(inp["rwkv_w0"][i], 4), pc(inp["rwkv_a0"][i], 4),
                                         pc(inp["rwkv_k_k"][i], 4), pc(inp["rwkv_k_a"][i], 4), pc(inp["rwkv_r_k"][i], 4),
                                         pc(inp["s5_d"][i], 4), pc(inp["s5_glu_b"][i], 4)], axis=1) for i in range(2)])
    sh["wup"] = inp["rwkv_w_up"]
    sh["aup"] = inp["rwkv_a_up"]
    sh["gup"] = inp["rwkv_g_up"]
    sh["lnw"] = np.broadcast_to(inp["rwkv_ln_w"][:, None, :], (2, 64, 512))
    sh["lnb"] = np.broadcast_to(inp["rwkv_ln_b"][:, None, :], (2, 64, 512))
    s5a, s5b = [], []
    p_ = np.arange(128)
    for i in range(2):
        rep = lambda a: np.concatenate([a, a], axis=0)
        lre2 = rep(inp["s5_lambda_re"][i].T)
        lim2 = rep(inp["s5_lambda_im"][i].T)
        lst2 = np.broadcast_to(inp["s5_log_step"][i][None, :], (128, 32))
        crT = rep(inp["s5_c_re"][i].transpose(2, 0, 1)).reshape(128, 512)
        ciT = rep(inp["s5_c_im"][i].transpose(2, 0, 1)).reshape(128, 512)
        s5a.append(np.concatenate([lre2, lim2, lst2, crT, ciT], axis=1))
        g = 8 * np.arange(4)[None, :] + 2 * (p_ // 32)[:, None] + ((p_ % 32) // 16)[:, None]
        cp = (p_ % 16)[:, None]
        lreB = inp["s5_lambda_re"][i][g]
        limB = inp["s5_lambda_im"][i][g]
        breB = inp["s5_b_re"][i][g, :, cp]
        bimB = inp["s5_b_im"][i][g, :, cp]
        lstB = inp["s5_log_step"][i][g]
        s5b.append(np.concatenate([lreB.reshape(128, 256), limB.reshape(128, 256), breB.reshape(128, 256),
                                   bimB.reshape(128, 256), lstB], axis=1))
    sh["s5a"] = np.stack(s5a)
    sh["s5b"] = np.stack(s5b)
    sh["gluw"] = np.stack([inp["s5_glu_w"][i].reshape(4, 128, 512).transpose(1, 0, 2) for i in range(2)])
    return {k: np.ascontiguousarray(np.asarray(v, np.float32)) for k, v in sh.items()}


_PLAN = [([0, 1, 2, 3], True)]


def kernel(**inputs):
    inp = {k: np.asarray(v) for k, v in inputs.items()}
    nb = inp["x"].shape[0]
    sh = prep_shared(inp)
    xT = [np.ascontiguousarray(inp["x"][b].T) for b in range(nb)]
    cTs = [np.ascontiguousarray(inp["c"][b].reshape(8, 128).T) for b in range(nb)]
    for layers, final in _PLAN:
        nc, _ = build(layers, final)
        in_maps = [dict(sh, xT=xT[b], cT=cTs[b]) for b in range(nb)]
        res = run_bass_kernel_spmd(nc, in_maps, core_ids=list(range(nb)))
        xT = [np.asarray(res.results[b]["outT"]) for b in range(nb)]
    return np.stack([x.T for x in xT]).astype(np.float32)
```

```python
import os
import numpy as np
from contextlib import ExitStack
import concourse.bass as bass
import concourse.mybir as mybir
from concourse.bass_utils import run_bass_kernel_spmd

F32 = mybir.dt.float32
BF16 = mybir.dt.bfloat16
AF = mybir.ActivationFunctionType
ALU = mybir.AluOpType
AX = mybir.AxisListType

D = 1024
L = 4096
TB = 512
NB = L // TB
C = 64
NCH = TB // C
EPS = 1e-6


class KB:
    def __init__(self):
        self.nc = bass.Bass("TRN2", target_bir_lowering=False)
        self.es = ExitStack()
        nc = self.nc
        self.eng = {'pe': nc.tensor, 'dve': nc.vector, 'act': nc.scalar, 'pool': nc.gpsimd, 'sp': nc.sync}
        self.sem = {e: self.es.enter_context(nc.semaphore("sem_" + e)) for e in self.eng}
        self.cnt = {e: 0 for e in self.eng}
        self.clock = {e: {} for e in self.eng}
        self.lastw = {}
        self.readers = {}
        self.dsem = {}
        self.nins = 0

    def sb(self, name, shape, dt, stack=None):
        self.minrem = min(getattr(self, 'minrem', 1 << 30), self.nc.sbuf_bytes_remaining)
        if stack is not None:
            self.uid = getattr(self, 'uid', 0) + 1
            name = f"{name}_u{self.uid}"
        return (stack or self.es).enter_context(self.nc.sbuf_tensor(name, shape, dt))

    def ps(self, name, shape, dt, stack=None):
        return (stack or self.es).enter_context(self.nc.psum_tensor(name, shape, dt))

    def _sync(self, e, reads, writes):
        need = {}

        def add(ev):
            if ev is None:
                return
            k, h, v = ev
            if k not in need or need[k][1] < v:
                need[k] = (h, v)

        for r in reads:
            add(self.lastw.get(r))
        for w in writes:
            add(self.lastw.get(w))
            for ev in self.readers.get(w, {}).values():
                add(ev)
        ck = self.clock[e]
        for k, (h, v) in need.items():
            if e == 'pe' and k == 'sem_pe':
                continue
            if ck.get(k, 0) >= v:
                continue
            self.eng[e].wait_ge(h, v)
            ck[k] = v

    def _mark(self, ev, reads, writes):
        for r in reads:
            self.readers.setdefault(r, {})[ev[0]] = ev
        for w in writes:
            self.lastw[w] = ev
            self.readers[w] = {}

    def op(self, e, fn, r=(), w=()):
        reads, writes = r, list(w) + [k for k in r if k.startswith('ps')]
        if getattr(self, '_yp', None) is not None:
            tl = self._tl
            last = getattr(tl, 'last', None)
            if e == 'pe' and last is not None and last != 'pe' and not getattr(self, '_hold', False):
                self._yp()
            tl.last = e
        self._sync(e, reads, writes)
        ins = fn(self.eng[e])
        self.cnt[e] += 1
        self.nins += 1
        ins.then_inc(self.sem[e], 1)
        self._mark(('sem_' + e, self.sem[e], self.cnt[e]), reads, writes)
        return ins

    def dma(self, q, out, in_, slot, r=(), w=()):
        reads, writes = r, w
        self._sync(q, reads, writes)
        if slot not in self.dsem:
            self.dsem[slot] = [self.es.enter_context(self.nc.semaphore("d_" + slot)), 0]
        s = self.dsem[slot]
        s[1] += 16
        self.eng[q].dma_start(out=out, in_=in_).then_inc(s[0], 16)
        self.nins += 1
        self._mark(('d_' + slot, s[0], s[1]), reads, writes)

    def run_streams(self, bodies):
        import threading
        n = len(bodies)
        sems = [threading.Semaphore(0) for _ in range(n)]
        alive = [True] * n
        done = threading.Semaphore(0)
        errs = []
        tl = threading.local()

        def nxt(i):
            for d in range(1, n + 1):
                k = (i + d) % n
                if alive[k]:
                    return k
            return None

        def yp():
            i = tl.idx
            k = nxt(i)
            if k is None or k == i:
                return
            sems[k].release()
            sems[i].acquire()

        def runner(i):
            tl.idx = i
            sems[i].acquire()
            try:
                bodies[i]()
            except BaseException as e:
                errs.append(e)
            alive[i] = False
            k = nxt(i)
            if k is None:
                done.release()
            else:
                sems[k].release()

        self._yp = yp
        self._tl = tl
        ths = [threading.Thread(target=runner, args=(i,)) for i in range(n)]
        for t in ths:
            t.start()
        sems[0].release()
        done.acquire()
        for t in ths:
            t.join()
        self._yp = None
        if errs:
            raise errs[0]

    def barrier(self):
        for e in self.eng:
            for e2 in self.eng:
                if e2 != e and self.cnt[e2] > self.clock[e].get('sem_' + e2, 0):
                    self.eng[e].wait_ge(self.sem[e2], self.cnt[e2])
                    self.clock[e]['sem_' + e2] = self.cnt[e2]
            for k, (h, v) in self.dsem.items():
                if v > self.clock[e].get('d_' + k, 0):
                    self.eng[e].wait_ge(h, v)
                    self.clock[e]['d_' + k] = v
        self.lastw = {}
        self.readers = {}

    def mm(self, out, lhsT, rhs, start=True, stop=True, r=(), w=()):
        self._hold = not stop
        return self.op('pe', lambda E: E.matmul(out, lhsT=lhsT, rhs=rhs, start=start, stop=stop), r, w)

    def tr(self, out, in_, ident, r=(), w=()):
        return self.op('pe', lambda E: E.transpose(out, in_, ident), r, w)

    def act(self, out, in_, func, scale=None, bias=None, r=(), w=(), e='act'):
        kw = {}
        if scale is not None:
            kw['scale'] = scale
        if bias is not None:
            kw['bias'] = bias
        return self.op('act', lambda E: E.activation(out=out, in_=in_, func=func, **kw), r, w)

    def tt(self, e, out, in0, in1, op, r=(), w=()):
        return self.op(e, lambda E: E.tensor_tensor(out=out, in0=in0, in1=in1, op=op), r, w)

    def ts(self, e, out, in0, s1, op0, s2=None, op1=None, r=(), w=()):
        if op1 is None:
            return self.op(e, lambda E: E.tensor_scalar(out=out, in0=in0, scalar1=s1, scalar2=None, op0=op0), r, w)
        return self.op(e, lambda E: E.tensor_scalar(out=out, in0=in0, scalar1=s1, scalar2=s2, op0=op0, op1=op1), r, w)

    def stt(self, out, in0, scalar, in1, op0, op1, r=(), w=()):
        return self.op('dve', lambda E: E.scalar_tensor_tensor(out=out, in0=in0, scalar=scalar, in1=in1, op0=op0, op1=op1), r, w)

    def cp(self, e, out, in_, r=(), w=()):
        if e == 'act':
            return self.op('act', lambda E: E.copy(out=out, in_=in_), r, w)
        return self.op(e, lambda E: E.tensor_copy(out=out, in_=in_), r, w)


def bc(ap, shape):
    return ap.to_broadcast(shape)


CB = {'ident': (0, 128), 'U': (128, 64), 'Lst': (192, 64), 'MsN': (256, 64), 'Mi': (320, 64), 'I64': (384, 64),
      'Ms': (448, 64), 'MiN': (512, 64), 'bones': (576, 128), 'cmask': (704, 512), 'pmask': (1216, 2)}
NCONST = 1219


def make_consts():
    i = np.arange(64)
    blob = np.zeros((128, NCONST), np.float32)
    blob[:, 0:128] = np.eye(128, dtype=np.float32)
    m = {}
    m['U'] = (i[:, None] <= i[None, :]).astype(np.float32)
    m['Lst'] = (i[:, None] > i[None, :]).astype(np.float32)
    m['MsN'] = -(i[:, None] < i[None, :]).astype(np.float32)
    m['Mi'] = (i[:, None] <= i[None, :]).astype(np.float32)
    m['I64'] = np.eye(64, dtype=np.float32)
    m['Ms'] = -m['MsN']
    m['MiN'] = -m['Mi']
    for k, v in m.items():
        blob[0:64, CB[k][0]:CB[k][0] + 64] = v
    bo = np.zeros((128, 128), np.float32)
    bo[0:64, 0:64] = 1.0
    bo[64:128, 64:128] = 1.0
    blob[:, 576:704] = bo
    cm = np.ones(512, np.float32)
    cm[0::64] = 0.0
    blob[:, 704:1216] = cm[None, :]
    p = np.arange(128)
    blob[:, 1216] = ((p % 32) // 16 == 0)
    blob[:, 1217] = ((p % 32) // 16 == 1)
    blob[:, 1218] = (p >= 96)
    return blob


def build(layers, final, nblocks=NB, dbg=None):
    kb = KB()
    nc = kb.nc
    n_odd = 2
    dr = {}

    def din(name, shape, dt=F32):
        dr[name] = nc.dram_tensor(name, list(shape), dt, kind="ExternalInput").ap()
        return dr[name]

    xT = din("xT", [D, L])
    cT = din("cT", [128, 8])
    normw = din("normw", [128, 32])
    adaw = din("adaw", [4, D, 3 * D])
    adab = din("adab", [128, 96])
    woutr = din("woutr", [4, 8, 128, 8, 128])
    fnw_d = din("fnw", [128, 8])
    consts_d = din("consts", [128, NCONST])
    winodd = din("winodd", [2, 32, 128, 8, 128])
    wba_d = din("wba", [2, 128, 8, 16])
    convw_d = din("convw", [2, 128, 24, 4])
    alog_d = din("alog", [2, 128, 8])
    dtb_d = din("dtb", [2, 128, 8])
    gnw_d = din("gnw", [2, 128, 128])
    winev = din("winev", [2, 26, 128, 8, 128])
    rp_d = din("rp", [2, 128, 42])
    wup_d = din("wup", [2, 64, 512])
    aup_d = din("aup", [2, 64, 512])
    gup_d = din("gup", [2, 128, 512])
    lnw_d = din("lnw", [2, 64, 512])
    lnb_d = din("lnb", [2, 64, 512])
    gluw_d = din("gluw", [2, 128, 4, 512])
    s5a_d = din("s5a", [2, 128, 96 + 1024])
    s5b_d = din("s5b", [2, 128, 1028])
    outT = nc.dram_tensor("outT", [D, L], F32, kind="ExternalOutput").ap()
    odd_ids = sorted({l // 2 for l in layers if l % 2 == 1})
    even_ids = sorted({l // 2 for l in layers if l % 2 == 0})

    cst = kb.sb("cst", [128, NCONST], F32)
    identb = kb.sb("identb", [128, 128], BF16)
    onesb = kb.sb("onesb", [128, 128], BF16)
    onesf = kb.sb("onesf", [64, 128], F32)
    xres = kb.sb("xres", [128, 8, TB], F32)
    hT = kb.sb("hT", [128, 8, TB], BF16)
    rstd = kb.sb("rstd", [128, TB], F32)
    sqb = [kb.sb(f"sqb{i}", [128, TB], BF16) for i in range(2)]
    tmpf = [kb.sb(f"tmpf{i}", [128, TB], F32) for i in range(2)]
    wbuf = [kb.sb(f"wbuf{i}", [128, 8, 128], BF16) for i in range(4)]
    ygT = kb.sb("ygT", [128, 8, TB], BF16)
    zs = kb.sb("zs", [128, 8, TB], BF16)
    modT = kb.sb("modT", [128, 4, 24], F32)
    s1 = kb.sb("s1", [128, 4, 8], F32)
    normw_s = kb.sb("normw_s", [128, 32], F32)
    adab_s = kb.sb("adab_s", [128, 96], F32)
    fnw_s = kb.sb("fnw_s", [128, 8], F32)
    sc = kb.sb("sc", [128, 8], F32)
    Sst = [kb.sb(f"Sst{i}", [128, 8, 128], F32) if i in odd_ids else None for i in range(n_odd)]
    Sb = [kb.sb(f"Sb{i}", [128, 8, 128], BF16) if i in odd_ids else None for i in range(n_odd)]
    carry = [kb.sb(f"carry{i}", [128, 24, 3], F32) if i in odd_ids else None for i in range(n_odd)]
    wba_s = [kb.sb(f"wba_s{i}", [128, 8, 16], BF16) if i in odd_ids else None for i in range(n_odd)]
    convw_s = [kb.sb(f"convw_s{i}", [128, 24, 4], F32) if i in odd_ids else None for i in range(n_odd)]
    negA = [kb.sb(f"negA{i}", [128, 8], F32) if i in odd_ids else None for i in range(n_odd)]
    dtb_s = [kb.sb(f"dtb_s{i}", [128, 8], F32) if i in odd_ids else None for i in range(n_odd)]
    gnw_s = [kb.sb(f"gnw_s{i}", [128, 128], F32) if i in odd_ids else None for i in range(n_odd)]

    def per_even(name, shape, dt):
        return [kb.sb(f"{name}{i}", shape, dt) if i in even_ids else None for i in range(2)]
    Hst = per_even("Hst", [64, 8, 64], F32)
    Hb = per_even("Hb", [64, 8, 64], BF16)
    tcarry = per_even("tcarry", [128, 14], F32)
    rp = per_even("rp_s", [128, 42], F32)
    omm = per_even("omm", [128, 14], F32)
    wup_b = per_even("wup_b", [64, 512], BF16)
    aup_b = per_even("aup_b", [128, 512], BF16)
    gup_b = per_even("gup_b", [128, 512], BF16)
    lnw_s = per_even("lnw_s", [64, 512], F32)
    lnb_s = per_even("lnb_s", [64, 512], F32)
    gluw_b = per_even("gluw_b", [128, 4, 512], BF16)
    bonesb = kb.sb("bonesb", [128, 128], BF16)
    def scratch(name, shape, dt):
        return [nc.dram_tensor(f"{name}{i}", list(shape), dt, kind="Internal").ap() if i in even_ids else None
                for i in range(2)]
    BJ_d = scratch("BJ_d", [128, 4, 2, 8, 128], BF16)
    CJlo_d = scratch("CJlo_d", [128, 4, 4, 9, 32], BF16)
    CJhi_d = scratch("CJhi_d", [128, 4, 4, 9, 64], BF16)
    TT_d = [scratch(f"TT{k}_d", [128, 32, 64], F32) for k in range(4)]
    rho8 = per_even("rho8", [128, 32], F32)
    s5carry = per_even("s5carry", [128, 32], F32)

    psI = [kb.ps(f"psI{i}", [128, 512], F32) for i in range(2)]
    psA = kb.ps("psA", [128, 1024], F32)
    psB = kb.ps("psB", [128, 1024], F32)
    psC = kb.ps("psC", [128, 512], F32)
    psT = kb.ps("psT", [128, 1024], BF16)

    def cv(k, rows=64):
        return cst[0:rows, CB[k][0]:CB[k][0] + CB[k][1]]
    U, Lst, MsN, Mi, I64, Ms, MiN = (cv(k) for k in ['U', 'Lst', 'MsN', 'Mi', 'I64', 'Ms', 'MiN'])
    cmask = cv('cmask', 128)

    kb.dma('sp', cst[:], consts_d[:, :], 'par', w=['cst'])
    kb.dma('sp', normw_s[:], normw[:, :], 'par', w=['normw_s'])
    kb.dma('sp', adab_s[:], adab[:, :], 'par', w=['adab_s'])
    kb.dma('sp', fnw_s[:], fnw_d[:, :], 'par', w=['fnw_s'])
    kb.dma('sp', sc[:], cT[:, :], 'par', w=['sc'])
    for i in odd_ids:
        kb.dma('pool', wba_s[i][:], wba_d[i], 'parp', w=[f'wba{i}'])
        kb.dma('sp', convw_s[i][:], convw_d[i], 'par', w=[f'convw{i}'])
        kb.dma('sp', negA[i][:], alog_d[i], 'par', w=[f'negA{i}'])
        kb.dma('sp', dtb_s[i][:], dtb_d[i], 'par', w=[f'dtb{i}'])
        kb.dma('sp', gnw_s[i][:], gnw_d[i], 'par', w=[f'gnw{i}'])
    for i in even_ids:
        kb.dma('sp', rp[i][:], rp_d[i], 'par', w=[f'rp{i}'])
        kb.dma('sp', lnw_s[i][:], lnw_d[i], 'par', w=[f'lnw{i}'])
        kb.dma('sp', lnb_s[i][:], lnb_d[i], 'par', w=[f'lnb{i}'])
        kb.dma('pool', wup_b[i][:], wup_d[i], 'parp', w=[f'wup{i}'])
        kb.dma('pool', aup_b[i][64:128, :], aup_d[i], 'parp', w=[f'aup{i}'])
        kb.dma('pool', gup_b[i][:], gup_d[i], 'parp', w=[f'gup{i}'])
        kb.dma('pool', gluw_b[i][:], gluw_d[i], 'parp', w=[f'gluw{i}'])
    kb.barrier()
    kb.cp('dve', identb[:], cst[:, 0:128], r=['cst'], w=['identb'])
    kb.cp('dve', bonesb[:], cst[:, 576:704], r=['cst'], w=['bonesb'])
    for i in even_ids:
        kb.ts('dve', omm[i][:], rp[i][:, 0:14], -1.0, ALU.mult, 1.0, ALU.add, r=[f'rp{i}'], w=[f'omm{i}'])
        kb.op('pool', lambda E, i=i: E.memset(Hst[i][:], 0.0), w=[f'H{i}_0', f'H{i}_1'])
        kb.op('pool', lambda E, i=i: E.memset(Hb[i][:], 0.0), w=[f'Hb{i}_0', f'Hb{i}_1'])
        kb.op('pool', lambda E, i=i: E.memset(tcarry[i][:], 0.0), w=[f'tcarry{i}'])
    kb.op('dve', lambda E: E.memset(onesb[:], 1.0), w=['onesb'])
    kb.op('dve', lambda E: E.memset(onesf[:], 1.0), w=['onesf'])
    for i in odd_ids:
        kb.act(negA[i][:], negA[i][:], AF.Exp, r=[f'negA{i}'], w=[f'negA{i}'])
        kb.ts('dve', negA[i][:], negA[i][:], -1.0, ALU.mult, r=[f'negA{i}'], w=[f'negA{i}'])
        kb.op('pool', lambda E, i=i: E.memset(Sst[i][:], 0.0), w=[f'S{i}_0', f'S{i}_1'])
        kb.op('pool', lambda E, i=i: E.memset(Sb[i][:], 0.0), w=[f'Sb{i}_0', f'Sb{i}_1'])
        kb.op('pool', lambda E, i=i: E.memset(carry[i][:], 0.0), w=[f'carry{i}'])

    def s5_tables(i):
        with ExitStack() as st:
            K_ = ['s5tab']
            sa = kb.sb("s5a_s", [128, 96 + 1024], F32, st)
            kb.dma('sp', sa[:], s5a_d[i], 's5a', w=K_)
            cnt = [0]
            CJlo = {i: kb.sb("CJlo_t", [128, 4, 4, 9, 32], BF16, st)}
            CJhi = {i: kb.sb("CJhi_t", [128, 4, 4, 9, 64], BF16, st)}
            T1 = {i: kb.sb("T1_t", [128, 32, 64], F32, st)}
            T2 = {i: kb.sb("T2_t", [128, 32, 64], F32, st)}
            T3 = {i: kb.sb("T3_t", [128, 32, 64], F32, st)}
            T4 = {i: kb.sb("T4_t", [128, 32, 64], F32, st)}

            cur = [st]

            def T(shape):
                cnt[0] += 1
                return kb.sb(f"s5t{cnt[0]}", shape, F32, cur[0])

            def tt(o, a, b, op, e='dve'):
                kb.tt(e, o, a, b, op, r=K_, w=K_)

            def ts(o, a, s1, op0, s2=None, op1=None):
                kb.ts('dve', o, a, s1, op0, s2, op1, r=K_, w=K_)

            def cmul(orr, oi, ar, ai, br, bi, t1, t2):
                tt(t1, ar, br, ALU.mult)
                tt(t2, ai, bi, ALU.mult)
                tt(orr, t1, t2, ALU.subtract)
                tt(t1, ar, bi, ALU.mult)
                tt(t2, ai, br, ALU.mult)
                tt(oi, t1, t2, ALU.add)

            def lam_bar(lre_in, lim_in, lst_in, shape):
                lre = T(shape); stp = T(shape); ar = T(shape); ai = T(shape)
                cr = T(shape); si = T(shape); t1 = T(shape); t2 = T(shape); rho = T(shape)
                ts(lre[:], lre_in, -1e-4, ALU.min)
                kb.act(stp[:], lst_in, AF.Exp, r=K_, w=K_)
                tt(ar[:], lre[:], stp[:], ALU.mult)
                tt(ai[:], lim_in, stp[:], ALU.mult)
                kb.act(rho[:], ar[:], AF.Exp, r=K_, w=K_)
                kb.act(si[:], ai[:], AF.Sin, scale=1.0 / 16, r=K_, w=K_)
                ts(t1[:], ai[:], 1.0 / 16, ALU.mult, float(np.pi / 2), ALU.add)
                kb.act(cr[:], t1[:], AF.Sin, r=K_, w=K_)
                for _ in range(4):
                    tt(t1[:], cr[:], cr[:], ALU.mult)
                    tt(t2[:], si[:], si[:], ALU.mult)
                    tt(ar[:], cr[:], si[:], ALU.mult)
                    tt(cr[:], t1[:], t2[:], ALU.subtract)
                    ts(si[:], ar[:], 2.0, ALU.mult)
                lr = T(shape); li = T(shape)
                tt(lr[:], cr[:], rho[:], ALU.mult)
                tt(li[:], si[:], rho[:], ALU.mult)
                return lr, li, rho, cr, si, lre

            sh = [128, 32]
            lr, li, rho, ur, ui, _ = lam_bar(sa[:, 0:32], sa[:, 32:64], sa[:, 64:96], sh)
            t1 = T(sh); t2 = T(sh)
            tt(t1[:], rho[:], rho[:], ALU.mult)
            tt(t2[:], t1[:], t1[:], ALU.mult)
            tt(rho8[i][:], t2[:], t2[:], ALU.mult)
            Lr = T([128, 32, 9]); Li = T([128, 32, 9])
            kb.op('dve', lambda E: E.memset(Lr[:, :, 0], 1.0), r=K_, w=K_)
            kb.op('dve', lambda E: E.memset(Li[:, :, 0], 0.0), r=K_, w=K_)
            for j in range(8):
                cmul(Lr[:, :, j + 1], Li[:, :, j + 1], Lr[:, :, j], Li[:, :, j], lr[:], li[:], t1[:], t2[:])
            kb.op('pool', lambda E: E.memset(CJlo[i][:], 0.0), r=K_, w=K_)
            kb.op('pool', lambda E: E.memset(CJhi[i][:], 0.0), r=K_, w=K_)
            Cr = sa[:, 96:96 + 512].rearrange("p (g c) -> p g c", c=16)
            Ci = sa[:, 96 + 512:96 + 1024].rearrange("p (g c) -> p g c", c=16)
            d1 = T([128, 32, 16]); d2 = T([128, 32, 16]); dr = T([128, 32, 16]); di = T([128, 32, 16])
            for j in range(9):
                lrj = bc(Lr[:, :, j:j + 1], [128, 32, 16])
                lij = bc(Li[:, :, j:j + 1], [128, 32, 16])
                tt(d1[:], Cr, lrj, ALU.mult)
                tt(d2[:], Ci, lij, ALU.mult)
                tt(dr[:], d1[:], d2[:], ALU.subtract)
                tt(d1[:], Cr, lij, ALU.mult)
                tt(d2[:], Ci, lrj, ALU.mult)
                tt(di[:], d1[:], d2[:], ALU.add)
                ts(di[:], di[:], -1.0, ALU.mult)
                drv = dr[:].rearrange("p (q gl) c -> p q gl c", gl=8)
                div = di[:].rearrange("p (q gl) c -> p q gl c", gl=8)
                for gl in range(8):
                    if gl < 4:
                        dst = lambda ps_: CJlo[i][ps_, :, gl, j, (gl % 2) * 16:(gl % 2) * 16 + 16]
                    else:
                        dst = lambda ps_: CJhi[i][ps_, :, gl - 4, j, (gl - 4) * 16:(gl - 4) * 16 + 16]
                    kb.cp('dve', dst(slice(0, 64)), drv[0:64, :, gl, :], r=K_, w=K_)
                    kb.cp('dve', dst(slice(64, 128)), div[64:128, :, gl, :], r=K_, w=K_)
            er = T(sh); ei = T(sh)
            kb.cp('dve', er[:], ur[:], r=K_, w=K_)
            kb.cp('dve', ei[:], ui[:], r=K_, w=K_)
            for _ in range(3):
                tt(t1[:], er[:], er[:], ALU.mult)
                tt(t2[:], ei[:], ei[:], ALU.mult)
                tt(ei[:], er[:], ei[:], ALU.mult)
                tt(er[:], t1[:], t2[:], ALU.subtract)
                ts(ei[:], ei[:], 2.0, ALU.mult)
            Pr = T3[i]
            Pi = T([128, 32, 64])
            kb.cp('dve', Pr[:, :, 0], er[:], r=K_, w=K_)
            kb.cp('dve', Pi[:, :, 0], ei[:], r=K_, w=K_)
            p1 = T([128, 32, 32]); p2 = T([128, 32, 32])
            m = 1
            while m < 64:
                br_ = bc(Pr[:, :, m - 1:m], [128, 32, m])
                bi_ = bc(Pi[:, :, m - 1:m], [128, 32, m])
                cmul(Pr[:, :, m:2 * m], Pi[:, :, m:2 * m], Pr[:, :, 0:m], Pi[:, :, 0:m], br_, bi_,
                     p1[:, :, 0:m], p2[:, :, 0:m])
                m *= 2
            l7r = bc(Lr[:, :, 7:8], [128, 32, 64])
            l7i = bc(Li[:, :, 7:8], [128, 32, 64])
            tt(T1[i][:], Pr[:], l7r, ALU.mult)
            tt(T2[i][:], Pi[:], l7i, ALU.mult)
            tt(T1[i][:], T1[i][:], T2[i][:], ALU.add)
            tt(T2[i][:], Pr[:], l7i, ALU.mult)
            tt(T4[i][:], Pi[:], l7r, ALU.mult)
            tt(T2[i][:], T2[i][:], T4[i][:], ALU.subtract)
            ts(T2[i][64:128], T2[i][64:128], -1.0, ALU.mult)
            kb.cp('dve', T4[i][0:64], Pi[0:64], r=K_, w=K_)
            ts(T4[i][64:128], Pi[64:128], -1.0, ALU.mult)
            kb.dma('sp', CJlo_d[i], CJlo[i][:], 'tabst', r=K_)
            kb.dma('sp', CJhi_d[i], CJhi[i][:], 'tabst', r=K_)
            for k_, t_ in enumerate((T1, T2, T3, T4)):
                kb.dma('sp', TT_d[k_][i], t_[i][:], 'tabst', r=K_)
            kb.barrier()
        with ExitStack() as st:
            cnt[0] += 1000
            cur[0] = st
            BJtab = {i: kb.sb("BJ_t", [128, 4, 2, 8, 128], BF16, st)}
            sbb = kb.sb("s5b_s", [128, 1028], F32, st)
            kb.dma('sp', sbb[:], s5b_d[i], 's5b', w=K_)
            shb = [128, 4, 64]
            v = lambda a: sbb[:, a * 256:(a + 1) * 256].rearrange("p (q n) -> p q n", n=64)
            lstb = bc(sbb[:, 1024:1028].unsqueeze(2), shb)
            blr, bli, brho, _, _, blre = lam_bar(v(0), v(1), lstb, shb)
            bt1 = T(shb); bt2 = T(shb); den = T(shb); kr = T(shb); ki = T(shb); ir = T(shb); ii = T(shb)
            nr = T(shb)
            ts(nr[:], blr[:], -1.0, ALU.add)
            tt(bt1[:], blre[:], blre[:], ALU.mult)
            tt(bt2[:], v(1), v(1), ALU.mult)
            tt(den[:], bt1[:], bt2[:], ALU.add)
            kb.op('dve', lambda E: E.reciprocal(out=den[:], in_=den[:]), r=K_, w=K_)
            tt(bt1[:], nr[:], blre[:], ALU.mult)
            tt(bt2[:], bli[:], v(1), ALU.mult)
            tt(kr[:], bt1[:], bt2[:], ALU.add)
            tt(kr[:], kr[:], den[:], ALU.mult)
            tt(bt1[:], bli[:], blre[:], ALU.mult)
            tt(bt2[:], nr[:], v(1), ALU.mult)
            tt(ki[:], bt1[:], bt2[:], ALU.subtract)
            tt(ki[:], ki[:], den[:], ALU.mult)
            tt(bt1[:], brho[:], brho[:], ALU.mult)
            kb.op('dve', lambda E: E.reciprocal(out=bt1[:], in_=bt1[:]), r=K_, w=K_)
            tt(ir[:], blr[:], bt1[:], ALU.mult)
            tt(ii[:], bli[:], bt1[:], ALU.mult)
            ts(ii[:], ii[:], -1.0, ALU.mult)
            gr = T(shb); gi = T(shb); g2r = T(shb); g2i = T(shb); vr = T(shb); vi = T(shb)
            kb.cp('dve', gr[:], kr[:], r=K_, w=K_)
            kb.cp('dve', gi[:], ki[:], r=K_, w=K_)
            pm = cst[:, 1216:1218]
            for j in range(8):
                cmul(vr[:], vi[:], gr[:], gi[:], v(2), v(3), bt1[:], bt2[:])
                for e in range(2):
                    ts(BJtab[i][:, :, e, j, 0:64], vr[:], pm[:, e:e + 1], ALU.mult)
                    ts(BJtab[i][:, :, e, j, 64:128], vi[:], pm[:, e:e + 1], ALU.mult)
                if j < 7:
                    cmul(g2r[:], g2i[:], gr[:], gi[:], ir[:], ii[:], bt1[:], bt2[:])
                    gr, g2r = g2r, gr
                    gi, g2i = g2i, gi
            kb.op('pool', lambda E: E.memset(s5carry[i][:], 0.0), r=K_, w=K_)
            kb.dma('sp', BJ_d[i], BJtab[i][:], 'tabst', r=K_)
            kb.barrier()

    for i in even_ids:
        s5_tables(i)

    kb.act(sc[:], sc[:], AF.Silu, r=['sc'], w=['sc'])
    with ExitStack() as st:
        abuf = [kb.sb(f"abuf{i}", [128, 3 * D], F32, st) for i in range(2)]
        macc = kb.sb("macc", [128, 24], F32, st)
        n = 0
        for l in layers:
            for k in range(8):
                b = abuf[n % 2]
                kb.dma('sp', b[:], adaw[l, k * 128:(k + 1) * 128, :], f'ab{n % 2}', w=[f'abuf{n % 2}'])
                for j in range(24):
                    kb.mm(psC[:, j:j + 1], b[:, j * 128:(j + 1) * 128], sc[:, k:k + 1],
                          r=[f'abuf{n % 2}', 'sc'], w=['psC'])
                if k == 0:
                    kb.tt('dve', macc[:], psC[:, 0:24], adab_s[:, l * 24:(l + 1) * 24], ALU.add,
                          r=['psC', 'adab_s'], w=['macc'])
                else:
                    kb.tt('dve', macc[:], psC[:, 0:24], macc[:], ALU.add, r=['psC', 'macc'], w=['macc'])
                n += 1
            kb.cp('dve', modT[:, l, :], macc[:], r=['macc'], w=['modT'])
            kb.ts('dve', s1[:, l, :], modT[:, l, 8:16], 1.0, ALU.add, r=['modT'], w=['s1'])
            kb.tt('dve', s1[:, l, :], s1[:, l, :], normw_s[:, l * 8:(l + 1) * 8], ALU.mult,
                  r=['s1', 'normw_s'], w=['s1'])
        kb.barrier()

    wctr = [0]

    def load_w(src_ap):
        i = wctr[0] % 4
        wctr[0] += 1
        kb.dma('pool', wbuf[i][:], src_ap, f'w{i}', w=[f'wbuf{i}'])
        return wbuf[i], f'wbuf{i}'

    ictr = [0]

    def proj_chunk(src_ap, rhs_tile, rhs_key):
        wb, wk = load_w(src_ap)
        i = ictr[0] % 2
        ictr[0] += 1
        for k in range(8):
            rk_l = rhs_key if isinstance(rhs_key, list) else [rhs_key]
            kb.mm(psI[i][:], wb[:, k, :], rhs_tile[:, k, :], start=(k == 0), stop=(k == 7),
                  r=[wk] + rk_l, w=[f'psI{i}'])
        return psI[i], f'psI{i}'

    def rms_stats(src, src_key):
        for k in range(8):
            q = sqb[k % 2]
            kb.act(q[:], src[:, k, :], AF.Square, r=[src_key], w=[f'sqb{k % 2}'])
            kb.mm(psC[:, :], onesb[:], q[:], start=(k == 0), stop=(k == 7), r=[f'sqb{k % 2}', 'onesb'], w=['psC'])
        kb.act(rstd[:], psC[:], AF.Ln, scale=1.0 / D, bias=EPS, r=['psC'], w=['rstd'])
        kb.act(rstd[:], rstd[:], AF.Exp, scale=-0.5, r=['rstd'], w=['rstd'])

    def pre_norm(l):
        rms_stats(xres, 'xres')
        for k in range(8):
            t = tmpf[k % 2]
            kb.tt('dve', t[:], xres[:, k, :], rstd[:], ALU.mult, r=['xres', 'rstd'], w=[f'tmpf{k % 2}'])
            kb.act(hT[:, k, :], t[:], AF.Identity, scale=s1[:, l, k:k + 1], bias=modT[:, l, k:k + 1],
                   r=[f'tmpf{k % 2}', 's1', 'modT'], w=['hT'])

    def out_proj(l):
        for dm in range(8):
            ps, pk = proj_chunk(woutr[l, dm], ygT, ['ygT', 'ygT0', 'ygT1'])
            kb.stt(xres[:, dm, :], ps[:], modT[:, l, 16 + dm:17 + dm], xres[:, dm, :], ALU.mult, ALU.add,
                   r=[pk, 'modT', 'xres'], w=['xres'])


    POOLE = os.environ.get('POOLTO', 'dve')

    def neumann(T, g):
        HN = T['HN']
        Pf, Accf, Accb, PTf, Pw = T['Pf'], T['Accf'], T['Accb'], T['PTf'], T['Pw']
        pA, pB, pC = T['pA'], T['pB'], T['pC']
        kA, kB, kC = T['kA'], T['kB'], T['kC']
        K = lambda n: f'{n}_{g}'
        W = HN * 64
        f2 = lambda t: t[:].rearrange("p h c -> p (h c)")
        kb.tt(POOLE, Accf[:], Pf[:], bc(I64.unsqueeze(1), [64, HN, 64]), ALU.add, r=[K('Pf'), 'cst'], w=[K('Accf')])
        for h in range(HN):
            kb.tr(pC[0:64, h * 64:(h + 1) * 64], Pf[:, h, :], cst[0:64, 0:64], r=[K('Pf'), 'cst'], w=kC)
        kb.cp('act', f2(PTf), pC[0:64, 0:W], r=kC, w=[K('PTf')])
        cur, curk = Pf, K('Pf')
        oth, othk = Pw, K('Pw')
        for lev in range(5):
            last = (lev == 4)
            if not last:
                for h in range(HN):
                    kb.mm(pA[0:64, h * 64:(h + 1) * 64], PTf[:, h, :], cur[:, h, :], r=[K('PTf'), curk], w=kA)
            for h in range(HN):
                kb.mm(pB[0:64, h * 64:(h + 1) * 64], cur[:, h, :], PTf[:, h, :], r=[K('PTf'), curk], w=kB)
            if not last:
                kb.cp('act', f2(oth), pA[0:64, 0:W], r=kA, w=[othk])
            kb.cp('dve', f2(PTf), pB[0:64, 0:W], r=kB, w=[K('PTf')])
            cur, curk, oth, othk = oth, othk, cur, curk
            for h in range(HN):
                kb.mm(pC[0:64, h * 64:(h + 1) * 64], PTf[:, h, :], Accf[:, h, :], r=[K('PTf'), K('Accf')], w=kC)
            kb.tt('dve', f2(Accf), pC[0:64, 0:W], f2(Accf), ALU.add, r=kC + [K('Accf')], w=[K('Accf')])
        kb.cp('act', Accb[:], Accf[:], r=[K('Accf')], w=[K('Accb')])

    def psum_group(g, G=2):
        if G == 1:
            return dict(pA=psA[:, :], pB=psB[:, :], pC=psC[:, :], pT=psT[:, :], HN=8,
                        kA=['psA0', 'psA1'], kB=['psB0', 'psB1'], kC=['psC'], kT=['psT'])
        if g == 0:
            return dict(pA=psA[:, 0:512], pB=psA[:, 512:1024], pC=psC[:, :], pT=psT[:, :], HN=4,
                        kA=['psA0'], kB=['psA1'], kC=['psC'], kT=['psT'])
        return dict(pA=psB[:, 0:512], pB=psB[:, 512:1024], pC=psI[0][:, :], pT=psI[1][:, :].bitcast(BF16), HN=4,
                    kA=['psB0'], kB=['psB1'], kC=['psI0'], kT=['psI1'])

    def odd_layer(l, bi):
        i = l // 2
        with ExitStack() as st:
            acc = kb.sb("acc", [128, 8, TB], F32, st)
            qkn = kb.sb("qkn", [128, 16, TB], BF16, st)
            vb = kb.sb("vb", [128, 8, TB], BF16, st)
            ba = kb.sb("ba", [64, 8, 16], F32, st)
            beta = kb.sb("beta", [64, 8, 8], F32, st)
            aall = kb.sb("aall", [64, 8, 8], F32, st)
            eg = kb.sb("eg", [64, 8, 8], F32, st)
            eend = kb.sb("eend", [64, 8, 8], F32, st)
            cdb = kb.sb("cdb", [128, 8, 8], F32, st)
            TG = []
            GG = int(os.environ.get('GDN_G', '2'))
            HN = 8 // GG
            for g_ in range(GG):
                T_ = psum_group(g_, GG)
                for n_, shp, dt_ in [('R2', [64, HN, 64], F32), ('decT', [64, HN, 64], F32), ('dSb', [64, HN, 64], F32),
                                     ('dI', [64, HN, 64], F32), ('Pf', [64, HN, 64], F32), ('Accf', [64, HN, 64], F32),
                                     ('PTf', [64, HN, 64], F32), ('Pw', [64, HN, 64], F32), ('Accb', [64, HN, 64], BF16),
                                     ('intraT', [64, HN, 64], BF16), ('vk', [64, HN, 256], BF16), ('kend', [64, HN, 128], BF16),
                                     ('uu', [64, HN, 128], F32), ('ww', [64, HN, 128], BF16), ('wTb', [128, HN, 64], BF16),
                                     ('vnew', [64, HN, 128], BF16), ('o1', [64, HN, 128], F32), ('oo', [64, HN, 128], F32),
                                     ('osq', [64, HN, 128], F32), ('ss', [64, HN], F32), ('onb', [64, HN, 128], BF16)]:
                    T_[n_] = kb.sb(f"g{g_}{n_}", shp, dt_, st)
                TG.append(T_)
            cnew = kb.sb("cnew", [128, 24, 3], F32, st)

            pre_norm(l)
            for m in range(8):
                ps, pk = proj_chunk(winodd[i, 24 + m], hT, 'hT')
                kb.act(zs[:, m, :], ps[:], AF.Silu, r=[pk], w=['zs'])
            if dbg:
                kb.op('pool', lambda E: E.memset(zs[:], 1.0), r=['zs'], w=['zs'])
            cw = convw_s[i]
            cr = carry[i]
            for grp in range(3):
                gsl = slice(grp * 8, grp * 8 + 8)
                accall = [f'acc{mm_}' for mm_ in range(8)]
                for mm_ in range(8):
                    m = grp * 8 + mm_
                    ps, pk = proj_chunk(winodd[i, m], hT, 'hT')
                    am = f'acc{mm_}'
                    kb.act(acc[:, mm_, :], ps[:], AF.Copy, scale=cw[:, m, 3:4], r=[pk, f'convw{i}'], w=[am])
                    kb.cp('act', cnew[:, m, :], ps[:, TB - 3:TB], r=[pk], w=['cnew'])
                    for j in range(3):
                        sh = 3 - j
                        kb.stt(acc[:, mm_, sh:TB], ps[:, 0:TB - sh], cw[:, m, j:j + 1], acc[:, mm_, sh:TB],
                               ALU.mult, ALU.add, r=[pk, f'convw{i}', am], w=[am])
                for j in range(3):
                    n = 3 - j
                    tv = tmpf[0][:, 0:8 * n].rearrange("p (m n) -> p m n", n=n)
                    kb.tt('dve', tv, cr[:, gsl, j:3], bc(cw[:, gsl, j:j + 1], [128, 8, n]), ALU.mult,
                          r=[f'carry{i}', f'convw{i}'], w=['tmpf0'])
                    kb.tt('dve', acc[:, :, 0:n], acc[:, :, 0:n], tv, ALU.add, r=['tmpf0'] + accall, w=accall)
                kb.cp('dve', cr[:, gsl, :], cnew[:, gsl, :], r=['cnew'], w=[f'carry{i}'])
                kb.act(acc[:], acc[:], AF.Silu, r=accall, w=accall)
                if grp == 2:
                    kb.cp('pool', vb[:], acc[:], r=accall, w=['vb'])
                    continue
                for mm_ in range(8):
                    m = grp * 8 + mm_
                    q = sqb[m % 2]
                    kb.act(q[:], acc[:, mm_, :], AF.Square, r=[f'acc{mm_}'], w=[f'sqb{m % 2}'])
                    kb.mm(psC[:], onesb[:], q[:], r=[f'sqb{m % 2}', 'onesb'], w=['psC'])
                    t = tmpf[m % 2]
                    kb.act(t[:], psC[:], AF.Ln, bias=EPS, r=['psC'], w=[f'tmpf{m % 2}'])
                    kb.act(t[:], t[:], AF.Exp, scale=-0.5, r=[f'tmpf{m % 2}'], w=[f'tmpf{m % 2}'])
                    kb.stt(qkn[:, m, :], acc[:, mm_, :], (128.0 ** -0.5) if m < 8 else 1.0, t[:], ALU.mult, ALU.mult,
                           r=[f'acc{mm_}', f'tmpf{m % 2}'], w=['qkn'])
            for j in range(NCH):
                for k in range(8):
                    kb.mm(psC[0:64, j * 16:(j + 1) * 16], hT[:, k, j * C:(j + 1) * C], wba_s[i][:, k, :],
                          start=(k == 0), stop=(k == 7), r=['hT', f'wba{i}'], w=['psC'])
            kb.cp('dve', ba[:], psC[0:64, 0:128].rearrange("p (j e) -> p j e", e=16), r=['psC'], w=['ba'])
            kb.act(beta[:], ba[:, :, 0:8], AF.Sigmoid, r=['ba'], w=['beta'])
            kb.tt('dve', aall[:], ba[:, :, 8:16], bc(dtb_s[i][0:64, :].unsqueeze(1), [64, 8, 8]), ALU.add,
                  r=['ba', f'dtb{i}'], w=['aall'])
            kb.act(aall[:], aall[:], AF.Exp, r=['aall'], w=['aall'])
            kb.act(aall[:], aall[:], AF.Ln, bias=1.0, r=['aall'], w=['aall'])
            kb.tt('dve', aall[:], aall[:], bc(negA[i][0:64, :].unsqueeze(1), [64, 8, 8]), ALU.mult,
                  r=['aall', f'negA{i}'], w=['aall'])
            a2 = aall[:].rearrange("p j h -> p (j h)")
            kb.mm(psI[0][0:64, 0:64], U, a2, r=['cst', 'aall'], w=['psI0'])
            kb.act(eg[:].rearrange("p j h -> p (j h)"), psI[0][0:64, 0:64], AF.Exp, r=['psI0'], w=['eg'])
            kb.mm(psI[1][0:64, 0:64], Lst, a2, r=['cst', 'aall'], w=['psI1'])
            kb.act(eend[:].rearrange("p j h -> p (j h)"), psI[1][0:64, 0:64], AF.Exp, r=['psI1'], w=['eend'])
            kb.mm(psI[0][:, 64:128], onesf[:], a2, r=['onesf', 'aall'], w=['psI0'])
            kb.act(cdb[:].rearrange("p j h -> p (j h)"), psI[0][:, 64:128], AF.Exp, r=['psI0'], w=['cdb'])

            S = Sst[i]
            Sbf = Sb[i]

            def gdn_stream(g):
                T = TG[g]
                HN = T['HN']
                K = lambda n: f'{n}_{g}'
                pA, pB, pC, pT = T['pA'], T['pB'], T['pC'], T['pT']
                kA, kC, kT = T['kA'], T['kC'], T['kT']
                kBs = T['kB']
                W = HN * 64
                hs = slice(g * HN, (g + 1) * HN)
                f2 = lambda t: t[:].rearrange("p h c -> p (h c)")
                R2, decT, dSb, dI, Pf, intraT, Accb = (T[n] for n in ['R2', 'decT', 'dSb', 'dI', 'Pf', 'intraT', 'Accb'])
                vk, kend, uu, ww, wTb, vnew, o1, oo, osq, ss, onb = (T[n] for n in
                    ['vk', 'kend', 'uu', 'ww', 'wTb', 'vnew', 'o1', 'oo', 'osq', 'ss', 'onb'])
                for j in range(NCH):
                    cs = slice(j * C, (j + 1) * C)
                    be = beta[:, j, hs]
                    kb.tt('dve', R2[:], bc(U.unsqueeze(1), [64, HN, 64]), bc(aall[:, j, hs].unsqueeze(2), [64, HN, 64]),
                          ALU.mult, r=['cst', 'aall'], w=[K('R2')])
                    kb.mm(pC[0:64, 0:W], Lst, f2(R2), r=['cst', K('R2')], w=kC)
                    kb.act(f2(decT), pC[0:64, 0:W], AF.Exp, r=kC, w=[K('decT')])
                    kb.tt(POOLE, dSb[:], decT[:], bc(MsN.unsqueeze(1), [64, HN, 64]), ALU.mult, r=[K('decT'), 'cst'], w=[K('dSb')])
                    kb.tt(POOLE, dSb[:], dSb[:], bc(be.unsqueeze(2), [64, HN, 64]), ALU.mult, r=[K('dSb'), 'beta'], w=[K('dSb')])
                    kb.tt(POOLE, dI[:], decT[:], bc(Mi.unsqueeze(1), [64, HN, 64]), ALU.mult, r=[K('decT'), 'cst'], w=[K('dI')])
                    for h in range(HN):
                        H_ = g * HN + h
                        kb.mm(pA[0:64, h * 64:(h + 1) * 64], qkn[:, 8 + H_, cs], qkn[:, 8 + H_, cs], r=['qkn'], w=kA)
                    for h in range(HN):
                        H_ = g * HN + h
                        kb.mm(pA[0:64, W + h * 64:W + (h + 1) * 64], qkn[:, 8 + H_, cs], qkn[:, H_, cs], r=['qkn'], w=kA)
                    kb.tt('dve', f2(Pf), pA[0:64, 0:W], f2(dSb), ALU.mult, r=kA + [K('dSb')], w=[K('Pf')])
                    kb.tt('dve', f2(intraT), pA[0:64, W:2 * W], f2(dI), ALU.mult, r=kA + [K('dI')], w=[K('intraT')])
                    neumann(T, g)
                    for h in range(HN):
                        kb.tr(pT[0:64, h * 128:(h + 1) * 128], qkn[:, 8 + g * HN + h, cs], identb[:], r=['qkn', 'identb'], w=kT)
                    pT3 = pT[0:64, 0:HN * 128].rearrange("p (h d) -> p h d", d=128)
                    kb.tt('dve', vk[:, :, 128:256], pT3, bc(eg[:, j, hs].unsqueeze(2), [64, HN, 128]), ALU.mult,
                          r=kT + ['eg'], w=[K('vk1')])
                    kb.tt('dve', kend[:], pT3, bc(eend[:, j, hs].unsqueeze(2), [64, HN, 128]), ALU.mult,
                          r=kT + ['eend'], w=[K('kend')])
                    for h in range(HN):
                        kb.tr(pT[0:64, h * 128:(h + 1) * 128], vb[:, g * HN + h, cs], identb[:], r=['vb', 'identb'], w=kT)
                    kb.cp('act', vk[:, :, 0:128], pT3, r=kT, w=[K('vk0')])
                    for h in range(HN):
                        kb.mm(pA[0:64, h * 128:(h + 1) * 128], Accb[:, h, :], vk[:, h, 0:128], r=[K('Accb'), K('vk0')], w=kA)
                    for h in range(HN):
                        kb.mm(pB[0:64, h * 128:(h + 1) * 128], Accb[:, h, :], vk[:, h, 128:256], r=[K('Accb'), K('vk1')], w=kBs)
                    pA3 = pA[0:64, :].rearrange("p (h d) -> p h d", d=128)
                    pB3 = pB[0:64, :].rearrange("p (h d) -> p h d", d=128)
                    bet3 = bc(be.unsqueeze(2), [64, HN, 128])
                    kb.tt('dve', uu[:], pA3, bet3, ALU.mult, r=kA + ['beta'], w=[K('uu')])
                    kb.tt('dve', ww[:], pB3, bet3, ALU.mult, r=kBs + ['beta'], w=[K('ww')])
                    for h in range(HN):
                        kb.tr(pT[:, h * 64:(h + 1) * 64], ww[:, h, :], identb[0:64, 0:64], r=[K('ww'), 'identb'], w=kT)
                    kb.cp('act', f2(wTb), pT[:, 0:W], r=kT, w=[K('wTb')])
                    kS, kSb = f'S{i}_{g}', f'Sb{i}_{g}'
                    for h in range(HN):
                        kb.mm(pA[0:64, h * 128:(h + 1) * 128], wTb[:, h, :], Sbf[:, g * HN + h, :], r=[K('wTb'), kSb], w=kA)
                    kb.tt('dve', vnew[:], uu[:], pA3, ALU.subtract, r=[K('uu')] + kA, w=[K('vnew')])
                    for h in range(HN):
                        kb.mm(pB[0:64, h * 128:(h + 1) * 128], qkn[:, g * HN + h, cs], Sbf[:, g * HN + h, :], r=['qkn', kSb], w=kBs)
                    kb.tt('dve', o1[:], pB3, bc(eg[:, j, hs].unsqueeze(2), [64, HN, 128]), ALU.mult, r=kBs + ['eg'], w=[K('o1')])
                    for h in range(HN):
                        kb.mm(pA[0:64, h * 128:(h + 1) * 128], intraT[:, h, :], vnew[:, h, :], r=[K('intraT'), K('vnew')], w=kA)
                    kb.tt('dve', oo[:], pA3, o1[:], ALU.add, r=kA + [K('o1')], w=[K('oo')])
                    for h in range(HN):
                        kb.mm(pB[:, h * 128:(h + 1) * 128], kend[:, h, :], vnew[:, h, :], r=[K('kend'), K('vnew')], w=kBs)
                    kb.tt(POOLE, S[:, hs, :], S[:, hs, :], bc(cdb[:, j, hs].unsqueeze(2), [128, HN, 128]), ALU.mult,
                          r=[kS, 'cdb'], w=[kS])
                    kb.tt('dve', S[:, hs, :], S[:, hs, :], pB[:, :].rearrange("p (h d) -> p h d", d=128), ALU.add,
                          r=[kS] + kBs, w=[kS])
                    kb.cp('act', Sbf[:, hs, :], S[:, hs, :], r=[kS], w=[kSb])
                    kb.tt(POOLE, osq[:], oo[:], oo[:], ALU.mult, r=[K('oo')], w=[K('osq')])
                    kb.op('dve', lambda E: E.tensor_reduce(out=ss[:], in_=osq[:], axis=AX.X, op=ALU.add), r=[K('osq')], w=[K('ss')])
                    kb.act(ss[:], ss[:], AF.Ln, scale=1.0 / 128, bias=EPS, r=[K('ss')], w=[K('ss')])
                    kb.act(ss[:], ss[:], AF.Exp, scale=-0.5, r=[K('ss')], w=[K('ss')])
                    kb.tt(POOLE, osq[:], oo[:], bc(ss[:].unsqueeze(2), [64, HN, 128]), ALU.mult, r=[K('oo'), K('ss')], w=[K('osq')])
                    kb.tt(POOLE, onb[:], osq[:], bc(gnw_s[i][0:64, :].unsqueeze(1), [64, HN, 128]), ALU.mult,
                          r=[K('osq'), f'gnw{i}'], w=[K('onb')])
                    for h in range(HN):
                        kb.tr(pT[:, h * 64:(h + 1) * 64], onb[:, h, :], identb[0:64, 0:64], r=[K('onb'), 'identb'], w=kT)
                    kb.tt('dve', ygT[:, hs, cs], pT[:, 0:W].rearrange("p (h c) -> p h c", c=64), zs[:, hs, cs], ALU.mult,
                          r=kT + ['zs'], w=[f'ygT{g}'])

            kb.run_streams([(lambda g_=g_: gdn_stream(g_)) for g_ in range(GG)])
            if not dbg:
                out_proj(l)
        kb.barrier()

    RW_DS = float(np.exp(-0.5))
    LN_EPS = 1e-5 * 64

    def rwkv_phase(l, i, bi):
        P = rp[i]
        rk_ = f'rp{i}'
        with ExitStack() as st:
            gbuf = kb.sb("gbuf", [128, TB + 1], F32, st)
            tw = kb.sb("tw", [64, TB], BF16, st)
            xab = kb.sb("xab", [128, TB], BF16, st)
            sg = kb.sb("sg", [128, TB], BF16, st)
            gz = kb.sb("gz", [128, 4, TB], BF16, st)
            bonus = kb.sb("bonus", [128, 4, TB], BF16, st)
            vbT = kb.sb("vbT", [128, 4, TB], BF16, st)
            ops6 = {n: kb.sb("op_" + n, [128, 4, TB], BF16, st) for n in ['At', 'Qt', 'Kh', 'Bh', 'Kb', 'Bb']}
            GC = kb.sb("GC", [64, NCH, 8], F32, st)
            rt = [kb.sb(f"rt{k}", [128, TB], F32, st) for k in range(12)]
            rf, kf, vf, ldm, aam, kk, kp, bb_, lg, lgx, t1, t2 = rt
            rtk = [f'rt{k}' for k in range(12)]
            krf, kkf, kvf, kld, kaa, kkk, kkp, kbb, klg, klgx, kt1, kt2 = rtk
            fl = [t1, t2]
            flk = [kt1, kt2]
            RG = []
            RGN = int(os.environ.get('RWKV_G', '2'))
            HN = 8 // RGN
            for g_ in range(RGN):
                T_ = psum_group(g_, RGN)
                for n_, shp, dt_ in [('Pf', [64, HN, 64], F32), ('Accf', [64, HN, 64], F32), ('PTf', [64, HN, 64], F32),
                                     ('Pw', [64, HN, 64], F32), ('Accb', [64, HN, 64], BF16), ('Mav', [64, HN, 64], BF16),
                                     ('Mqk', [64, HN, 64], BF16), ('MqbN', [64, HN, 64], BF16), ('Vt', [64, HN * 64], BF16),
                                     ('Kbt', [64, HN * 64], BF16), ('BbtN', [64, HN * 64], BF16), ('RHSb', [64, HN * 64], BF16),
                                     ('Pb2', [64, HN * 64], BF16), ('tmpR', [64, HN * 64], F32), ('tmpO', [64, HN * 64], F32),
                                     ('oT', [64, HN, 64], F32), ('oc', [64, HN, 64], F32), ('osq', [64, HN, 64], F32),
                                     ('s8', [64, HN], F32), ('s8b', [64, HN], F32), ('onb', [64, HN * 64], BF16),
                                     ('t_o', [128, HN // 2, C], F32)]:
                    T_[n_] = kb.sb(f"r{g_}{n_}", shp, dt_, st)
                T_['opo'] = {n_: kb.sb(f"r{g_}opo_{n_}", [64, HN // 2, C], BF16, st) for n_ in ['At', 'Qt', 'Kh', 'Bh']}
                RG.append(T_)

            def shift(ps, pk, mf, out, outk):
                kb.cp('dve', gbuf[:, 0:1], tcarry[i][:, mf:mf + 1], r=[f'tcarry{i}'], w=['gbuf0'])
                kb.act(gbuf[:, 1:TB + 1], ps[:], AF.Copy, scale=P[:, mf:mf + 1], r=[pk, rk_], w=['gbuf'])
                kb.cp('pool', tcarry[i][:, mf:mf + 1], gbuf[:, TB:TB + 1], r=['gbuf'], w=[f'tcarry{i}'])
                kb.stt(out[:], ps[:], omm[i][:, mf:mf + 1], gbuf[:, 0:TB], ALU.mult, ALU.add,
                       r=[pk, f'omm{i}', 'gbuf', 'gbuf0'], w=[outk])

            for k2 in range(2):
                ps, pk = proj_chunk(winev[i, 16 + k2], hT, 'hT')
                shift(ps, pk, 12 + k2, fl[k2], flk[k2])
            kb.act(tw[:], fl[0][0:64, :], AF.Tanh, r=[flk[0]], w=['tw'])
            kb.cp('pool', xab[64:128, :], fl[0][64:128, :], r=[flk[0]], w=['xab'])
            kb.act(sg[:], fl[1][:], AF.Sigmoid, r=[flk[1]], w=['sg'])
            for m in range(4):
                mc = slice(m * 128, (m + 1) * 128)
                kb.mm(psC[:], wup_b[i][0:64, mc], tw[:], r=[f'wup{i}', 'tw'], w=['psC'])
                kb.act(ldm[:], psC[:], AF.Sigmoid, bias=P[:, 14 + m:15 + m], r=['psC', rk_], w=[kld])
                kb.ts('pool', ldm[:], ldm[:], -RW_DS, ALU.mult, r=[kld], w=[kld])
                kb.mm(psC[:], aup_b[i][64:128, mc], xab[64:128, :], r=[f'aup{i}', 'xab'], w=['psC'])
                kb.act(aam[:], psC[:], AF.Sigmoid, bias=P[:, 18 + m:19 + m], r=['psC', rk_], w=[kaa])
                kb.mm(psC[:], gup_b[i][:, mc], sg[:], r=[f'gup{i}', 'sg'], w=['psC'])
                kb.tt('dve', gz[:, m, :], psC[:], zs[:, 4 + m, :], ALU.mult, r=['psC', 'zs'], w=['gz'])
                ps, pk = proj_chunk(winev[i, 4 + m], hT, 'hT')
                shift(ps, pk, m, rf, krf)
                ps, pk = proj_chunk(winev[i, 8 + m], hT, 'hT')
                shift(ps, pk, 4 + m, kf, kkf)
                ps, pk = proj_chunk(winev[i, 12 + m], hT, 'hT')
                shift(ps, pk, 8 + m, vf, kvf)
                kb.act(sqb[0][:], kf[:], AF.Square, scale=P[:, 22 + m:23 + m], r=[kkf, rk_], w=['sqb0'])
                kb.mm(psC[:], bonesb[:], sqb[0][:], r=['bonesb', 'sqb0'], w=['psC'])
                kb.act(t1[:], psC[:], AF.Ln, bias=EPS, r=['psC'], w=[kt1])
                kb.act(t1[:], t1[:], AF.Exp, scale=-0.5, r=[kt1], w=[kt1])
                kb.stt(kk[:], kf[:], P[:, 22 + m:23 + m], t1[:], ALU.mult, ALU.mult, r=[kkf, rk_, kt1], w=[kkk])
                kb.ts('pool', t2[:], aam[:], -1.0, ALU.add, P[:, 26 + m:27 + m], ALU.mult, r=[kaa, rk_], w=[kt2])
                kb.stt(kp[:], t2[:], 1.0, kf[:], ALU.add, ALU.mult, r=[kt2, kkf], w=[kkp])
                kb.tt(POOLE, bb_[:], kk[:], aam[:], ALU.mult, r=[kkk, kaa], w=[kbb])
                kb.stt(sqb[1][:], rf[:], P[:, 30 + m:31 + m], kp[:], ALU.mult, ALU.mult, r=[krf, rk_, kkp], w=['sqb1'])
                kb.mm(psC[:], bonesb[:], sqb[1][:], r=['bonesb', 'sqb1'], w=['psC'])
                kb.tt('dve', bonus[:, m, :], psC[:], vf[:], ALU.mult, r=['psC', kvf], w=['bonus'])
                kb.cp('act', vbT[:, m, :], vf[:], r=[kvf], w=['vbT'])
                kb.op('dve', lambda E: E.tensor_tensor_scan(out=lg[:], data0=cmask, data1=ldm[:], initial=0.0,
                                                            op0=ALU.mult, op1=ALU.add), r=['cst', kld], w=[klg])
                kb.tt(POOLE, lgx[:], lg[:], ldm[:], ALU.subtract, r=[klg, kld], w=[klgx])
                lg3 = lg[:].rearrange("p (j c) -> p j c", c=C)
                kb.act(t1[:], lg[:], AF.Exp, r=[klg], w=[kt1])
                kb.tt('dve', ops6['Qt'][:, m, :], rf[:], t1[:], ALU.mult, r=[krf, kt1], w=['op_Qt'])
                kb.act(t1[:], lgx[:], AF.Exp, r=[klgx], w=[kt1])
                kb.tt('dve', ops6['At'][:, m, :], kk[:], t1[:], ALU.mult, r=[kkk, kt1], w=['op_At'])
                kb.act(t1[:], lg[:], AF.Exp, scale=-1.0, r=[klg], w=[kt1])
                kb.tt('dve', ops6['Kh'][:, m, :], kp[:], t1[:], ALU.mult, r=[kkp, kt1], w=['op_Kh'])
                kb.tt(POOLE, ops6['Bh'][:, m, :], bb_[:], t1[:], ALU.mult, r=[kbb, kt1], w=['op_Bh'])
                kb.tt('dve', t2[:].rearrange("p (j c) -> p j c", c=C), bc(lg3[:, :, C - 1:C], [128, NCH, C]), lg3,
                      ALU.subtract, r=[klg], w=[kt2])
                kb.act(t2[:], t2[:], AF.Exp, r=[kt2], w=[kt2])
                kb.tt('dve', ops6['Kb'][:, m, :], kp[:], t2[:], ALU.mult, r=[kkp, kt2], w=['op_Kb'])
                kb.tt(POOLE, ops6['Bb'][:, m, :], bb_[:], t2[:], ALU.mult, r=[kbb, kt2], w=['op_Bb'])
                for par in range(2):
                    kb.act(GC[:, :, 2 * m + par], lg3[par * 64:(par + 1) * 64, :, C - 1], AF.Exp, r=[klg], w=['GC'])

            H = Hst[i]
            Hbf = Hb[i]

            def rwkv_stream(g):
                T = RG[g]
                HN = T['HN']
                K = lambda n: f'r{n}_{g}'
                pA, pB, pC, pT = T['pA'], T['pB'], T['pC'], T['pT']
                kA, kB, kC, kT = T['kA'], T['kB'], T['kC'], T['kT']
                W = HN * 64
                hs = slice(g * HN, (g + 1) * HN)
                NM = HN // 2
                ms = slice(NM * g, NM * g + NM)
                f2 = lambda t: t[:].rearrange("p h c -> p (h c)")
                m3 = lambda mk: bc(mk.unsqueeze(1), [64, HN, 64])
                p3 = lambda ap: ap.rearrange("p (h c) -> p h c", c=64)
                opo = T['opo']
                Pf, Accb, Mav, Mqk, MqbN = T['Pf'], T['Accb'], T['Mav'], T['Mqk'], T['MqbN']
                Vt, Kbt, BbtN, RHSb, Pb2, tmpR, tmpO = (T[n] for n in ['Vt', 'Kbt', 'BbtN', 'RHSb', 'Pb2', 'tmpR', 'tmpO'])
                oT, oc, osq, s8, s8b, onb, t_o = (T[n] for n in ['oT', 'oc', 'osq', 's8', 's8b', 'onb', 't_o'])
                kH, kHb = f'H{i}_{g}', f'Hb{i}_{g}'
                for j in range(NCH):
                    cs = slice(j * C, (j + 1) * C)
                    for n in ['At', 'Qt', 'Kh', 'Bh']:
                        kb.cp('dve', opo[n][:], ops6[n][64:128, ms, cs], r=['op_' + n], w=[K('opo_' + n)])

                    def X(n, hl):
                        h = g * HN + hl
                        return ops6[n][0:64, h // 2, cs] if h % 2 == 0 else opo[n][:, hl // 2, :]
                    xk = lambda *ns: [k for n in ns for k in ('op_' + n, K('opo_' + n))]
                    for h in range(HN):
                        kb.mm(pA[0:64, h * 64:(h + 1) * 64], X('Bh', h), X('At', h), r=xk('Bh', 'At'), w=kA)
                    for h in range(HN):
                        kb.mm(pA[0:64, W + h * 64:W + (h + 1) * 64], X('Kh', h), X('At', h), r=xk('Kh', 'At'), w=kA)
                    for h in range(HN):
                        kb.mm(pB[0:64, h * 64:(h + 1) * 64], X('Kh', h), X('Qt', h), r=xk('Kh', 'Qt'), w=kB)
                    for h in range(HN):
                        kb.mm(pB[0:64, W + h * 64:W + (h + 1) * 64], X('Bh', h), X('Qt', h), r=xk('Bh', 'Qt'), w=kB)
                    kb.tt('dve', Pf[:], p3(pA[0:64, 0:W]), m3(MsN), ALU.mult, r=kA + ['cst'], w=[f'Pf_{g}'])
                    kb.tt('dve', Mav[:], p3(pA[0:64, W:2 * W]), m3(Ms), ALU.mult, r=kA + ['cst'], w=[K('Mav')])
                    kb.tt('dve', Mqk[:], p3(pB[0:64, 0:W]), m3(Mi), ALU.mult, r=kB + ['cst'], w=[K('Mqk')])
                    kb.tt('dve', MqbN[:], p3(pB[0:64, W:2 * W]), m3(MiN), ALU.mult, r=kB + ['cst'], w=[K('MqbN')])
                    neumann(T, g)
                    kAcc = f'Accb_{g}'
                    for ml in range(NM):
                        kb.tr(pT[0:64, ml * 128:(ml + 1) * 128], vbT[:, NM * g + ml, cs], identb[:], r=['vbT', 'identb'], w=kT)
                    kb.cp('act', Vt[:], pT[0:64, 0:W], r=kT, w=[K('Vt')])
                    for h in range(HN):
                        kb.mm(pC[0:64, h * 64:(h + 1) * 64], X('At', h), Hbf[:, g * HN + h, :], r=xk('At') + [kHb], w=kC)
                    kb.cp('act', tmpR[:], pC[0:64, 0:W], r=kC, w=[K('tmpR')])
                    for h in range(HN):
                        kb.mm(pA[0:64, h * 64:(h + 1) * 64], Mav[:, h, :], Vt[:, h * 64:(h + 1) * 64], r=[K('Mav'), K('Vt')], w=kA)
                    kb.tt('dve', RHSb[:], pA[0:64, 0:W], tmpR[:], ALU.add, r=kA + [K('tmpR')], w=[K('RHSb')])
                    for h in range(HN):
                        kb.mm(pC[0:64, h * 64:(h + 1) * 64], Accb[:, h, :], RHSb[:, h * 64:(h + 1) * 64], r=[kAcc, K('RHSb')], w=kC)
                    kb.cp('act', Pb2[:], pC[0:64, 0:W], r=kC, w=[K('Pb2')])
                    for h in range(HN):
                        kb.mm(pB[0:64, h * 64:(h + 1) * 64], X('Qt', h), Hbf[:, g * HN + h, :], r=xk('Qt') + [kHb], w=kB)
                    kb.cp('act', tmpO[:], pB[0:64, 0:W], r=kB, w=[K('tmpO')])
                    for h in range(HN):
                        hsl = slice(h * 64, (h + 1) * 64)
                        kb.mm(pA[0:64, hsl], Mqk[:, h, :], Vt[:, hsl], start=True, stop=False, r=[K('Mqk'), K('Vt')], w=kA)
                        kb.mm(pA[0:64, hsl], MqbN[:, h, :], Pb2[:, hsl], start=False, stop=True, r=[K('MqbN'), K('Pb2')], w=kA)
                    kb.tt('dve', f2(oT), pA[0:64, 0:W], tmpO[:], ALU.add, r=kA + [K('tmpO')], w=[K('oT')])
                    for ml in range(NM):
                        kb.tr(pT[0:64, ml * 128:(ml + 1) * 128], ops6['Kb'][:, NM * g + ml, cs], identb[:], r=['op_Kb', 'identb'], w=kT)
                    kb.cp('act', Kbt[:], pT[0:64, 0:W], r=kT, w=[K('Kbt')])
                    for ml in range(NM):
                        kb.tr(pT[0:64, ml * 128:(ml + 1) * 128], ops6['Bb'][:, NM * g + ml, cs], identb[:], r=['op_Bb', 'identb'], w=kT)
                    kb.ts('dve', BbtN[:], pT[0:64, 0:W], -1.0, ALU.mult, r=kT, w=[K('BbtN')])
                    for h in range(HN):
                        hsl = slice(h * 64, (h + 1) * 64)
                        kb.mm(pC[0:64, hsl], Kbt[:, hsl], Vt[:, hsl], start=True, stop=False, r=[K('Kbt'), K('Vt')], w=kC)
                        kb.mm(pC[0:64, hsl], BbtN[:, hsl], Pb2[:, hsl], start=False, stop=True, r=[K('BbtN'), K('Pb2')], w=kC)
                    kb.tt(POOLE, H[:, hs, :], H[:, hs, :], bc(GC[:, j, hs].unsqueeze(2), [64, HN, 64]), ALU.mult, r=[kH, 'GC'], w=[kH])
                    kb.tt('dve', H[:, hs, :], H[:, hs, :], p3(pC[0:64, 0:W]), ALU.add, r=[kH] + kC, w=[kH])
                    kb.cp('act', Hbf[:, hs, :], H[:, hs, :], r=[kH], w=[kHb])
                    kb.op('dve', lambda E: E.tensor_reduce(out=s8[:], in_=oT[:], axis=AX.X, op=ALU.add), r=[K('oT')], w=[K('s8')])
                    kb.ts('dve', s8[:], s8[:], -1.0 / 64, ALU.mult, r=[K('s8')], w=[K('s8')])
                    kb.tt(POOLE, oc[:], oT[:], bc(s8[:].unsqueeze(2), [64, HN, 64]), ALU.add, r=[K('oT'), K('s8')], w=[K('oc')])
                    kb.tt(POOLE, osq[:], oc[:], oc[:], ALU.mult, r=[K('oc')], w=[K('osq')])
                    kb.op('dve', lambda E: E.tensor_reduce(out=s8b[:], in_=osq[:], axis=AX.X, op=ALU.add), r=[K('osq')], w=[K('s8b')])
                    kb.act(s8b[:], s8b[:], AF.Ln, scale=1.0 / 64, bias=LN_EPS, r=[K('s8b')], w=[K('s8b')])
                    kb.act(s8b[:], s8b[:], AF.Exp, scale=-0.5, r=[K('s8b')], w=[K('s8b')])
                    kb.tt(POOLE, oc[:], oc[:], bc(s8b[:].unsqueeze(2), [64, HN, 64]), ALU.mult, r=[K('oc'), K('s8b')], w=[K('oc')])
                    kb.tt(POOLE, f2(oc), f2(oc), lnw_s[i][:, g * W:(g + 1) * W], ALU.mult, r=[K('oc'), f'lnw{i}'], w=[K('oc')])
                    kb.tt(POOLE, onb[:], f2(oc), lnb_s[i][:, g * W:(g + 1) * W], ALU.add, r=[K('oc'), f'lnb{i}'], w=[K('onb')])
                    for ml in range(NM):
                        kb.tr(pT[:, ml * 64:(ml + 1) * 64], onb[:, ml * 128:(ml + 1) * 128], identb[0:64, 0:64],
                              r=[K('onb'), 'identb'], w=kT)
                    kb.tt('dve', t_o[:], pT[:, 0:NM * 64].rearrange("p (m c) -> p m c", c=C), bonus[:, ms, cs], ALU.add,
                          r=kT + ['bonus'], w=[K('t_o')])
                    kb.tt('dve', ygT[:, 4 + NM * g:4 + NM * g + NM, cs], t_o[:], gz[:, ms, cs], ALU.mult, r=[K('t_o'), 'gz'], w=[f'ygT{g}'])

            kb.run_streams([(lambda g_=g_: rwkv_stream(g_)) for g_ in range(RGN)])

    def s5_phase(l, i, bi):
        STOP = int(os.environ.get('S5STOP', '99'))
        if STOP <= 0:
            kb.op('pool', lambda E: E.memset(ygT[:, 0:4, :], 0.0), w=['ygT'])
            return
        P = rp[i]
        rk_ = f'rp{i}'
        with ExitStack() as st:
            uT = kb.sb("uT", [128, 4, TB], F32, st)
            uTb = kb.sb("uTb", [128, 4, TB], BF16, st)
            uTb3 = kb.sb("uTb3", [128, 4, TB], BF16, st)
            yT = kb.sb("yT", [128, 4, TB], F32, st)
            ygb = kb.sb("ygb", [128, 4, TB], BF16, st)
            cs1f = kb.sb("cs1f", [128, 4, 8, 64], F32, st)
            cs1b = kb.sb("cs1b", [128, 8, 8, 64], BF16, st)
            cpb = kb.sb("cpb", [128, 8, 64], BF16, st)
            ea = kb.sb("ea", [128, 4, 64], F32, st)
            etm = kb.sb("etm", [128, 4, 64], F32, st)
            et = kb.sb("et", [128, 4, 64], F32, st)
            ch = kb.sb("ch", [128, 4, 64], F32, st)
            cN = kb.sb("cN", [128, 4, 64], F32, st)
            g1 = kb.sb("g1", [128, TB], F32, st)
            g2 = kb.sb("g2", [128, TB], F32, st)
            if STOP < 99 and 'c' not in os.environ.get('S5SKIP', ''):
                kb.op('pool', lambda E: E.memset(yT[:], 0.0), w=['yT'])
                kb.op('pool', lambda E: E.memset(ygb[:], 0.0), w=['ygb'])
                kb.op('pool', lambda E: E.memset(cs1b[:], 0.0), w=['cs1b'])
                kb.op('pool', lambda E: E.memset(cpb[:], 0.0), w=['cpb'])
                kb.op('pool', lambda E: E.memset(cs1f[:], 0.0), w=['cs1f'])
            for q in range(0 if 'd' in os.environ.get('S5SKIP', '') else 4):
                ps, pk = proj_chunk(winev[i, q], hT, 'hT')
                if 'e' not in os.environ.get('S5SKIP', ''):
                    kb.cp('act', uT[:, q, :], ps[:], r=[pk], w=['uT'])
                if 'f' not in os.environ.get('S5SKIP', ''):
                    kb.cp('dve', uTb[:, q, :], uT[:, q, :], r=['uT'], w=['uTb'])
                if 'a' not in os.environ.get('S5SKIP', ''):
                    kb.ts('dve', uTb3[64:128, q, :], uTb[64:128, q, :], cst[64:128, 1218:1219], ALU.mult,
                          r=['uTb', 'cst'], w=['uTb3'])
            banks = [psA[:, 0:512], psA[:, 512:1024], psB[:, 0:512], psB[:, 512:1024]]
            bkeys = ['psA0', 'psA1', 'psB0', 'psB1']
            TTv = [TT_d[k_][i].rearrange("p (q b e) n -> p q b e n", q=4, b=4) for k_ in range(4)]
            BJq = [kb.sb(f"BJq{k_}", [128, 2, 8, 128], BF16, st) for k_ in range(2)]
            CJloq = [kb.sb(f"CJloq{k_}", [128, 4, 9, 32], BF16, st) for k_ in range(2)]
            CJhiq = [kb.sb(f"CJhiq{k_}", [128, 4, 9, 64], BF16, st) for k_ in range(2)]
            TTq = [[kb.sb(f"TTq{k_}_{z_}", [128, 4, 64], F32, st) for k_ in range(4)] for z_ in range(2)]
            carv = s5carry[i][:].rearrange("p (q b e) -> p q b e", q=4, b=4)
            r8v = rho8[i][:].rearrange("p (q b e) -> p q b e", q=4, b=4)
            for q in range(4 if STOP > 1 else 0):
                qb = q % 2
                tk = f'tabq{qb}'
                kb.dma('sp', BJq[qb][:], BJ_d[i][:, q], f'tq{qb}', w=[tk])
                kb.dma('sp', CJloq[qb][:], CJlo_d[i][:, q], f'tq{qb}', w=[tk])
                kb.dma('sp', CJhiq[qb][:], CJhi_d[i][:, q], f'tq{qb}', w=[tk])
                for e in range(2):
                    tke = f'tte{e}'
                    for k_ in range(4):
                        kb.dma('sp', TTq[e][k_][:], TTv[k_][:, q, :, e, :], f'tt{e}', w=[tke])
                    T1v, T2v, T3v, T4v = TTq[e]
                    for b in range(4):
                        pb = slice(32 * b, 32 * b + 32) if b < 3 else slice(64, 128)
                        uv = (uTb if b < 3 else uTb3)[pb, q, :].rearrange("p (n j) -> p j n", j=8)
                        for j in range(8):
                            kb.mm(banks[b][:, j * 64:(j + 1) * 64], BJq[qb][pb, e, j, :], uv[:, j, :],
                                  r=[tk, 'uTb', 'uTb3'], w=[bkeys[b]])
                    if STOP <= 2:
                        continue
                    zA = psA[:, :].rearrange("p (b j n) -> p b j n", b=2, j=8)
                    zB = psB[:, :].rearrange("p (b j n) -> p b j n", b=2, j=8)
                    for (z, zk, bs) in ((zA, ['psA0', 'psA1'], slice(0, 2)), (zB, ['psB0', 'psB1'], slice(2, 4))):
                        kb.cp('dve', cs1f[:, bs, 0, :], z[:, :, 0, :], r=zk, w=['cs1f'])
                        for j in range(1, 8):
                            kb.tt('dve', cs1f[:, bs, j, :], z[:, :, j, :], cs1f[:, bs, j - 1, :], ALU.add,
                                  r=zk + ['cs1f'], w=['cs1f'])
                    cbv = cs1b[:].rearrange("p (b e) j n -> p b e j n", e=2)
                    kb.cp('act', cbv[:, :, e, :, :], cs1f[:], r=['cs1f'], w=['cs1b'])
                    if STOP <= 3:
                        continue
                    x = cs1f[:, :, 7, :]
                    kb.tt('pool', ea[:], x, T1v[:], ALU.mult, r=['cs1f', tke], w=['ea'])
                    kb.tt('dve', etm[0:64], cs1f[64:128, :, 7, :], T2v[64:128], ALU.mult, r=['cs1f', tke], w=['etm'])
                    kb.tt('dve', etm[64:128], cs1f[0:64, :, 7, :], T2v[0:64], ALU.mult, r=['cs1f', tke], w=['etm'])
                    kb.tt('pool', et[:], ea[:], etm[:], ALU.add, r=['ea', 'etm'], w=['et'])
                    for b in range(4):
                        kb.op('dve', lambda E, b=b: E.tensor_tensor_scan(
                            out=ch[:, b, :], data0=r8v[:, q, b, e:e + 1].to_broadcast([128, 64]), data1=et[:, b, :],
                            initial=carv[:, q, b, e:e + 1], op0=ALU.mult, op1=ALU.add), r=['et', f's5c{i}'], w=['ch'])
                    kb.tt('pool', ea[:], ch[:], T3v[:], ALU.mult, r=['ch', tke], w=['ea'])
                    kb.tt('dve', etm[0:64], ch[64:128], T4v[64:128], ALU.mult, r=['ch', tke], w=['etm'])
                    kb.tt('dve', etm[64:128], ch[0:64], T4v[0:64], ALU.mult, r=['ch', tke], w=['etm'])
                    kb.tt('pool', cN[:], ea[:], etm[:], ALU.add, r=['ea', 'etm'], w=['cN'])
                    cpv = cpb[:].rearrange("p (b e) n -> p b e n", e=2)
                    kb.cp('dve', cpv[:, :, e, 0:1], carv[:, q, :, e:e + 1], r=[f's5c{i}'], w=['cpb'])
                    kb.cp('act', cpv[:, :, e, 1:64], cN[:, :, 0:63], r=['cN'], w=['cpb'])
                    kb.cp('dve', carv[:, q, :, e:e + 1], cN[:, :, 63:64], r=['cN'], w=[f's5c{i}'])
                if STOP <= 4:
                    continue
                for j in range(8):
                    jc = slice(j * 64, (j + 1) * 64)
                    for b in range(2):
                        for e in range(2):
                            gl = 2 * b + e
                            o_ = psC[32 * b:32 * b + 32, jc]
                            kb.mm(o_, CJloq[qb][:, gl, j, :], cs1b[:, gl, j, :], start=(e == 0), stop=False,
                                  r=[tk, 'cs1b'], w=['psC'])
                            kb.mm(o_, CJloq[qb][:, gl, j + 1, :], cpb[:, gl, :], start=False, stop=(e == 1),
                                  r=[tk, 'cpb'], w=['psC'])
                    for gl in range(4, 8):
                        o_ = psC[64:128, jc]
                        kb.mm(o_, CJhiq[qb][:, gl - 4, j, :], cs1b[:, gl, j, :], start=(gl == 4), stop=False,
                              r=[tk, 'cs1b'], w=['psC'])
                        kb.mm(o_, CJhiq[qb][:, gl - 4, j + 1, :], cpb[:, gl, :], start=False, stop=(gl == 7),
                              r=[tk, 'cpb'], w=['psC'])
                yv = yT[:, q, :].rearrange("p (n j) -> p j n", j=8)
                uv32 = uT[:, q, :].rearrange("p (n j) -> p j n", j=8)
                kb.stt(yv, uv32, P[:, 34 + q:35 + q], psC[:, :].rearrange("p (j n) -> p j n", j=8), ALU.mult, ALU.add,
                       r=['uT', rk_, 'psC'], w=['yT'])
                xq = yT[:, q, :]
                kb.tt('pool', g1[:], xq, xq, ALU.mult, r=['yT'], w=['g1'])
                kb.ts('pool', g1[:], g1[:], 0.044715, ALU.mult, 1.0, ALU.add, r=['g1'], w=['g1'])
                kb.tt('pool', g1[:], g1[:], xq, ALU.mult, r=['g1', 'yT'], w=['g1'])
                kb.act(g2[:], g1[:], AF.Sigmoid, scale=1.5957691216057308, r=['g1'], w=['g2'])
                kb.tt('dve', yT[:, q, :], xq, g2[:], ALU.mult, r=['yT', 'g2'], w=['yT'])
                kb.cp('act', ygb[:, q, :], yT[:, q, :], r=['yT'], w=['ygb'])
            for qo in range(0 if 'b' in os.environ.get('S5SKIP', '') else 4):
                for k in range(4):
                    kb.mm(psC[:], gluw_b[i][:, k, qo * 128:(qo + 1) * 128], ygb[:, k, :], start=(k == 0), stop=(k == 3),
                          r=[f'gluw{i}', 'ygb'], w=['psC'])
                kb.act(g2[:], psC[:], AF.Sigmoid, bias=P[:, 38 + qo:39 + qo], r=['psC', rk_], w=['g2'])
                kb.tt('dve', g1[:], yT[:, qo, :], g2[:], ALU.mult, r=['yT', 'g2'], w=['g1'])
                kb.tt('dve', ygT[:, qo, :], g1[:], zs[:, qo, :], ALU.mult, r=['g1', 'zs'], w=['ygT'])

    def even_layer(l, bi):
        i = l // 2
        pre_norm(l)
        for m in range(8):
            ps, pk = proj_chunk(winev[i, 18 + m], hT, 'hT')
            kb.act(zs[:, m, :], ps[:], AF.Silu, r=[pk], w=['zs'])
        if dbg:
            kb.op('pool', lambda E: E.memset(zs[:], 1.0), r=['zs'], w=['zs'])
        s5_phase(l, i, bi)
        if not os.environ.get('RWSKIP'):
            rwkv_phase(l, i, bi)
        if not dbg:
            out_proj(l)
        kb.barrier()

    for bi in range(nblocks):
        ts_ = slice(bi * TB, (bi + 1) * TB)
        kb.dma('sp', xres[:], xT[:, ts_].rearrange("(k p) t -> p k t", p=128), 'xin', w=['xres'])
        for l in layers:
            if l % 2 == 1:
                odd_layer(l, bi)
            else:
                even_layer(l, bi)
        with ExitStack() as st:
            obuf = kb.sb("obuf", [128, 8, TB], F32, st)
            if final:
                rms_stats(xres, 'xres')
                for k in range(8):
                    kb.stt(obuf[:, k, :], xres[:, k, :], fnw_s[:, k:k + 1], rstd[:], ALU.mult, ALU.mult,
                           r=['xres', 'fnw_s', 'rstd'], w=['obuf'])
            elif dbg:
                kb.cp('dve', obuf[:], ygT[:], r=['ygT', 'ygT0', 'ygT1'], w=['obuf'])
            else:
                kb.cp('dve', obuf[:], xres[:], r=['xres'], w=['obuf'])
            kb.dma('sp', outT[:, ts_].rearrange("(k p) t -> p k t", p=128), obuf[:], 'xout', r=['obuf'])
            kb.barrier()
    kb.es.close()
    return nc, kb


def chunkify(W, cols):
    Wc = W[:, cols]
    n_m = Wc.shape[1] // 128
    return np.ascontiguousarray(Wc.reshape(8, 128, n_m, 128).transpose(2, 1, 0, 3))


def prep_shared(inp):
    sh = {}
    sh["normw"] = np.ascontiguousarray(inp["norm_w"].reshape(4, 8, 128).transpose(2, 0, 1).reshape(128, 32))
    sh["adaw"] = np.ascontiguousarray(inp["ada_w"])
    sh["adab"] = np.ascontiguousarray(inp["ada_b"].reshape(4, 24, 128).transpose(2, 0, 1).reshape(128, 96))
    sh["woutr"] = np.stack([chunkify(inp["w_out"][l], np.arange(1024)) for l in range(4)])
    sh["fnw"] = np.ascontiguousarray(inp["final_norm_w"].reshape(8, 128).T)
    sh["consts"] = make_consts()
    cols = np.concatenate([np.arange(0, 3072), np.arange(3088, 4112)])
    sh["winodd"] = np.stack([chunkify(inp["odd_w_in"][i], cols) for i in range(2)])
    sh["wba"] = np.ascontiguousarray(
        np.stack([inp["odd_w_in"][i][:, 3072:3088].reshape(8, 128, 16).transpose(1, 0, 2) for i in range(2)]))
    sh["convw"] = np.ascontiguousarray(
        np.stack([inp["gdn_conv_w"][i].reshape(4, 24, 128).transpose(2, 1, 0) for i in range(2)]))
    sh["alog"] = np.ascontiguousarray(np.broadcast_to(inp["gdn_a_log"][:, None, :], (2, 128, 8)))
    sh["dtb"] = np.ascontiguousarray(np.broadcast_to(inp["gdn_dt_bias"][:, None, :], (2, 128, 8)))
    sh["gnw"] = np.ascontiguousarray(np.broadcast_to(inp["gdn_norm_w"][:, None, :], (2, 128, 128)))
    sh["winev"] = np.stack([chunkify(inp["even_w_in"][i], np.arange(3328)) for i in range(2)])
    pc = lambda v, n: v.reshape(n, 128).T
    sh["rp"] = np.stack([np.concatenate([pc(inp["rwkv_mu"][i], 14), pc(inp["rwkv_w0"][i], 4), pc(inp["rwkv_a0"][i], 4),
                                         pc(inp["rwkv_k_k"][i], 4), pc(inp["rwkv_k_a"][i], 4), pc(inp["rwkv_r_k"][i], 4),
                                         pc(inp["s5_d"][i], 4), pc(inp["s5_glu_b"][i], 4)], axis=1) for i in range(2)])
    sh["wup"] = inp["rwkv_w_up"]
    sh["aup"] = inp["rwkv_a_up"]
    sh["gup"] = inp["rwkv_g_up"]
    sh["lnw"] = np.broadcast_to(inp["rwkv_ln_w"][:, None, :], (2, 64, 512))
    sh["lnb"] = np.broadcast_to(inp["rwkv_ln_b"][:, None, :], (2, 64, 512))
    s5a, s5b = [], []
    p_ = np.arange(128)
    for i in range(2):
        rep = lambda a: np.concatenate([a, a], axis=0)
        lre2 = rep(inp["s5_lambda_re"][i].T)
        lim2 = rep(inp["s5_lambda_im"][i].T)
        lst2 = np.broadcast_to(inp["s5_log_step"][i][None, :], (128, 32))
        crT = rep(inp["s5_c_re"][i].transpose(2, 0, 1)).reshape(128, 512)
        ciT = rep(inp["s5_c_im"][i].transpose(2, 0, 1)).reshape(128, 512)
        s5a.append(np.concatenate([lre2, lim2, lst2, crT, ciT], axis=1))
        g = 8 * np.arange(4)[None, :] + 2 * (p_ // 32)[:, None] + ((p_ % 32) // 16)[:, None]
        cp = (p_ % 16)[:, None]
        lreB = inp["s5_lambda_re"][i][g]
        limB = inp["s5_lambda_im"][i][g]
        breB = inp["s5_b_re"][i][g, :, cp]
        bimB = inp["s5_b_im"][i][g, :, cp]
        lstB = inp["s5_log_step"][i][g]
        s5b.append(np.concatenate([lreB.reshape(128, 256), limB.reshape(128, 256), breB.reshape(128, 256),
                                   bimB.reshape(128, 256), lstB], axis=1))
    sh["s5a"] = np.stack(s5a)
    sh["s5b"] = np.stack(s5b)
    sh["gluw"] = np.stack([inp["s5_glu_w"][i].reshape(4, 128, 512).transpose(1, 0, 2) for i in range(2)])
    return {k: np.ascontiguousarray(np.asarray(v, np.float32)) for k, v in sh.items()}


_PLAN = [([0, 1, 2, 3], True)]


def kernel(**inputs):
    inp = {k: np.asarray(v) for k, v in inputs.items()}
    nb = inp["x"].shape[0]
    sh = prep_shared(inp)
    xT = [np.ascontiguousarray(inp["x"][b].T) for b in range(nb)]
    cTs = [np.ascontiguousarray(inp["c"][b].reshape(8, 128).T) for b in range(nb)]
    for layers, final in _PLAN:
        nc, _ = build(layers, final)
        in_maps = [dict(sh, xT=xT[b], cT=cTs[b]) for b in range(nb)]
        res = run_bass_kernel_spmd(nc, in_maps, core_ids=list(range(nb)))
        xT = [np.asarray(res.results[b]["outT"]) for b in range(nb)]
    return np.stack([x.T for x in xT]).astype(np.float32)
```

```python
import os
import numpy as np
from contextlib import ExitStack
import concourse.bass as bass
import concourse.mybir as mybir
from concourse.bass_utils import run_bass_kernel_spmd

F32 = mybir.dt.float32
BF16 = mybir.dt.bfloat16
AF = mybir.ActivationFunctionType
ALU = mybir.AluOpType
AX = mybir.AxisListType

D = 1024
L = 4096
TB = 512
NB = L // TB
C = 64
NCH = TB // C
EPS = 1e-6


class KB:
    def __init__(self):
        self.nc = bass.Bass("TRN2", target_bir_lowering=False)
        self.es = ExitStack()
        nc = self.nc
        self.eng = {'pe': nc.tensor, 'dve': nc.vector, 'act': nc.scalar, 'pool': nc.gpsimd, 'sp': nc.sync}
        self.sem = {e: self.es.enter_context(nc.semaphore("sem_" + e)) for e in self.eng}
        self.cnt = {e: 0 for e in self.eng}
        self.clock = {e: {} for e in self.eng}
        self.lastw = {}
        self.readers = {}
        self.dsem = {}
        self.nins = 0

    def sb(self, name, shape, dt, stack=None):
        self.minrem = min(getattr(self, 'minrem', 1 << 30), self.nc.sbuf_bytes_remaining)
        if stack is not None:
            self.uid = getattr(self, 'uid', 0) + 1
            name = f"{name}_u{self.uid}"
        return (stack or self.es).enter_context(self.nc.sbuf_tensor(name, shape, dt))

    def ps(self, name, shape, dt, stack=None):
        return (stack or self.es).enter_context(self.nc.psum_tensor(name, shape, dt))

    def _sync(self, e, reads, writes):
        need = {}

        def add(ev):
            if ev is None:
                return
            k, h, v = ev
            if k not in need or need[k][1] < v:
                need[k] = (h, v)

        for r in reads:
            add(self.lastw.get(r))
        for w in writes:
            add(self.lastw.get(w))
            for ev in self.readers.get(w, {}).values():
                add(ev)
        ck = self.clock[e]
        for k, (h, v) in need.items():
            if e == 'pe' and k == 'sem_pe':
                continue
            if ck.get(k, 0) >= v:
                continue
            self.eng[e].wait_ge(h, v)
            ck[k] = v

    def _mark(self, ev, reads, writes):
        for r in reads:
            self.readers.setdefault(r, {})[ev[0]] = ev
        for w in writes:
            self.lastw[w] = ev
            self.readers[w] = {}

    def op(self, e, fn, r=(), w=()):
        reads, writes = r, list(w) + [k for k in r if k.startswith('ps')]
        if getattr(self, '_yp', None) is not None:
            tl = self._tl
            last = getattr(tl, 'last', None)
            if e == 'pe' and last is not None and last != 'pe' and not getattr(self, '_hold', False):
                self._yp()
            tl.last = e
        self._sync(e, reads, writes)
        ins = fn(self.eng[e])
        self.cnt[e] += 1
        self.nins += 1
        ins.then_inc(self.sem[e], 1)
        self._mark(('sem_' + e, self.sem[e], self.cnt[e]), reads, writes)
        return ins

    def dma(self, q, out, in_, slot, r=(), w=()):
        reads, writes = r, w
        self._sync(q, reads, writes)
        if slot not in self.dsem:
            self.dsem[slot] = [self.es.enter_context(self.nc.semaphore("d_" + slot)), 0]
        s = self.dsem[slot]
        s[1] += 16
        self.eng[q].dma_start(out=out, in_=in_).then_inc(s[0], 16)
        self.nins += 1
        self._mark(('d_' + slot, s[0], s[1]), reads, writes)

    def run_streams(self, bodies):
        import threading
        n = len(bodies)
        sems = [threading.Semaphore(0) for _ in range(n)]
        alive = [True] * n
        done = threading.Semaphore(0)
        errs = []
        tl = threading.local()

        def nxt(i):
            for d in range(1, n + 1):
                k = (i + d) % n
                if alive[k]:
                    return k
            return None

        def yp():
            i = tl.idx
            k = nxt(i)
            if k is None or k == i:
                return
            sems[k].release()
            sems[i].acquire()

        def runner(i):
            tl.idx = i
            sems[i].acquire()
            try:
                bodies[i]()
            except BaseException as e:
                errs.append(e)
            alive[i] = False
            k = nxt(i)
            if k is None:
                done.release()
            else:
                sems[k].release()

        self._yp = yp
        self._tl = tl
        ths = [threading.Thread(target=runner, args=(i,)) for i in range(n)]
        for t in ths:
            t.start()
        sems[0].release()
        done.acquire()
        for t in ths:
            t.join()
        self._yp = None
        if errs:
            raise errs[0]

    def barrier(self):
        for e in self.eng:
            for e2 in self.eng:
                if e2 != e and self.cnt[e2] > self.clock[e].get('sem_' + e2, 0):
                    self.eng[e].wait_ge(self.sem[e2], self.cnt[e2])
                    self.clock[e]['sem_' + e2] = self.cnt[e2]
            for k, (h, v) in self.dsem.items():
                if v > self.clock[e].get('d_' + k, 0):
                    self.eng[e].wait_ge(h, v)
                    self.clock[e]['d_' + k] = v
        self.lastw = {}
        self.readers = {}

    def mm(self, out, lhsT, rhs, start=True, stop=True, r=(), w=()):
        self._hold = not stop
        return self.op('pe', lambda E: E.matmul(out, lhsT=lhsT, rhs=rhs, start=start, stop=stop), r, w)

    def tr(self, out, in_, ident, r=(), w=()):
        return self.op('pe', lambda E: E.transpose(out, in_, ident), r, w)

    def act(self, out, in_, func, scale=None, bias=None, r=(), w=(), e='act'):
        kw = {}
        if scale is not None:
            kw['scale'] = scale
        if bias is not None:
            kw['bias'] = bias
        return self.op('act', lambda E: E.activation(out=out, in_=in_, func=func, **kw), r, w)

    def tt(self, e, out, in0, in1, op, r=(), w=()):
        return self.op(e, lambda E: E.tensor_tensor(out=out, in0=in0, in1=in1, op=op), r, w)

    def ts(self, e, out, in0, s1, op0, s2=None, op1=None, r=(), w=()):
        if op1 is None:
            return self.op(e, lambda E: E.tensor_scalar(out=out, in0=in0, scalar1=s1, scalar2=None, op0=op0), r, w)
        return self.op(e, lambda E: E.tensor_scalar(out=out, in0=in0, scalar1=s1, scalar2=s2, op0=op0, op1=op1), r, w)

    def stt(self, out, in0, scalar, in1, op0, op1, r=(), w=()):
        return self.op('dve', lambda E: E.scalar_tensor_tensor(out=out, in0=in0, scalar=scalar, in1=in1, op0=op0, op1=op1), r, w)

    def cp(self, e, out, in_, r=(), w=()):
        if e == 'act':
            return self.op('act', lambda E: E.copy(out=out, in_=in_), r, w)
        return self.op(e, lambda E: E.tensor_copy(out=out, in_=in_), r, w)


def bc(ap, shape):
    return ap.to_broadcast(shape)


CB = {'ident': (0, 128), 'U': (128, 64), 'Lst': (192, 64), 'MsN': (256, 64), 'Mi': (320, 64), 'I64': (384, 64),
      'Ms': (448, 64), 'MiN': (512, 64), 'bones': (576, 128), 'cmask': (704, 512), 'pmask': (1216, 2)}
NCONST = 1219


def make_consts():
    i = np.arange(64)
    blob = np.zeros((128, NCONST), np.float32)
    blob[:, 0:128] = np.eye(128, dtype=np.float32)
    m = {}
    m['U'] = (i[:, None] <= i[None, :]).astype(np.float32)
    m['Lst'] = (i[:, None] > i[None, :]).astype(np.float32)
    m['MsN'] = -(i[:, None] < i[None, :]).astype(np.float32)
    m['Mi'] = (i[:, None] <= i[None, :]).astype(np.float32)
    m['I64'] = np.eye(64, dtype=np.float32)
    m['Ms'] = -m['MsN']
    m['MiN'] = -m['Mi']
    for k, v in m.items():
        blob[0:64, CB[k][0]:CB[k][0] + 64] = v
    bo = np.zeros((128, 128), np.float32)
    bo[0:64, 0:64] = 1.0
    bo[64:128, 64:128] = 1.0
    blob[:, 576:704] = bo
    cm = np.ones(512, np.float32)
    cm[0::64] = 0.0
    blob[:, 704:1216] = cm[None, :]
    p = np.arange(128)
    blob[:, 1216] = ((p % 32) // 16 == 0)
    blob[:, 1217] = ((p % 32) // 16 == 1)
    blob[:, 1218] = (p >= 96)
    return blob


def build(layers, final, nblocks=NB, dbg=None):
    kb = KB()
    nc = kb.nc
    n_odd = 2
    dr = {}

    def din(name, shape, dt=F32):
        dr[name] = nc.dram_tensor(name, list(shape), dt, kind="ExternalInput").ap()
        return dr[name]

    xT = din("xT", [D, L])
    cT = din("cT", [128, 8])
    normw = din("normw", [128, 32])
    adaw = din("adaw", [4, D, 3 * D])
    adab = din("adab", [128, 96])
    woutr = din("woutr", [4, 8, 128, 8, 128])
    fnw_d = din("fnw", [128, 8])
    consts_d = din("consts", [128, NCONST])
    winodd = din("winodd", [2, 32, 128, 8, 128])
    wba_d = din("wba", [2, 128, 8, 16])
    convw_d = din("convw", [2, 128, 24, 4])
    alog_d = din("alog", [2, 128, 8])
    dtb_d = din("dtb", [2, 128, 8])
    gnw_d = din("gnw", [2, 128, 128])
    winev = din("winev", [2, 26, 128, 8, 128])
    rp_d = din("rp", [2, 128, 42])
    wup_d = din("wup", [2, 64, 512])
    aup_d = din("aup", [2, 64, 512])
    gup_d = din("gup", [2, 128, 512])
    lnw_d = din("lnw", [2, 64, 512])
    lnb_d = din("lnb", [2, 64, 512])
    gluw_d = din("gluw", [2, 128, 4, 512])
    s5a_d = din("s5a", [2, 128, 96 + 1024])
    s5b_d = din("s5b", [2, 128, 1028])
    outT = nc.dram_tensor("outT", [D, L], F32, kind="ExternalOutput").ap()
    odd_ids = sorted({l // 2 for l in layers if l % 2 == 1})
    even_ids = sorted({l // 2 for l in layers if l % 2 == 0})

    cst = kb.sb("cst", [128, NCONST], F32)
    identb = kb.sb("identb", [128, 128], BF16)
    onesb = kb.sb("onesb", [128, 128], BF16)
    onesf = kb.sb("onesf", [64, 128], F32)
    xres = kb.sb("xres", [128, 8, TB], F32)
    hT = kb.sb("hT", [128, 8, TB], BF16)
    rstd = kb.sb("rstd", [128, TB], F32)
    sqb = [kb.sb(f"sqb{i}", [128, TB], BF16) for i in range(2)]
    tmpf = [kb.sb(f"tmpf{i}", [128, TB], F32) for i in range(2)]
    wbuf = [kb.sb(f"wbuf{i}", [128, 8, 128], BF16) for i in range(4)]
    ygT = kb.sb("ygT", [128, 8, TB], BF16)
    zs = kb.sb("zs", [128, 8, TB], BF16)
    modT = kb.sb("modT", [128, 4, 24], F32)
    s1 = kb.sb("s1", [128, 4, 8], F32)
    normw_s = kb.sb("normw_s", [128, 32], F32)
    adab_s = kb.sb("adab_s", [128, 96], F32)
    fnw_s = kb.sb("fnw_s", [128, 8], F32)
    sc = kb.sb("sc", [128, 8], F32)
    Sst = [kb.sb(f"Sst{i}", [128, 8, 128], F32) if i in odd_ids else None for i in range(n_odd)]
    Sb = [kb.sb(f"Sb{i}", [128, 8, 128], BF16) if i in odd_ids else None for i in range(n_odd)]
    carry = [kb.sb(f"carry{i}", [128, 24, 3], F32) if i in odd_ids else None for i in range(n_odd)]
    wba_s = [kb.sb(f"wba_s{i}", [128, 8, 16], BF16) if i in odd_ids else None for i in range(n_odd)]
    convw_s = [kb.sb(f"convw_s{i}", [128, 24, 4], F32) if i in odd_ids else None for i in range(n_odd)]
    negA = [kb.sb(f"negA{i}", [128, 8], F32) if i in odd_ids else None for i in range(n_odd)]
    dtb_s = [kb.sb(f"dtb_s{i}", [128, 8], F32) if i in odd_ids else None for i in range(n_odd)]
    gnw_s = [kb.sb(f"gnw_s{i}", [128, 128], F32) if i in odd_ids else None for i in range(n_odd)]

    def per_even(name, shape, dt):
        return [kb.sb(f"{name}{i}", shape, dt) if i in even_ids else None for i in range(2)]
    Hst = per_even("Hst", [64, 8, 64], F32)
    Hb = per_even("Hb", [64, 8, 64], BF16)
    tcarry = per_even("tcarry", [128, 14], F32)
    rp = per_even("rp_s", [128, 42], F32)
    omm = per_even("omm", [128, 14], F32)
    wup_b = per_even("wup_b", [64, 512], BF16)
    aup_b = per_even("aup_b", [128, 512], BF16)
    gup_b = per_even("gup_b", [128, 512], BF16)
    lnw_s = per_even("lnw_s", [64, 512], F32)
    lnb_s = per_even("lnb_s", [64, 512], F32)
    gluw_b = per_even("gluw_b", [128, 4, 512], BF16)
    bonesb = kb.sb("bonesb", [128, 128], BF16)
    def scratch(name, shape, dt):
        return [nc.dram_tensor(f"{name}{i}", list(shape), dt, kind="Internal").ap() if i in even_ids else None
                for i in range(2)]
    BJ_d = scratch("BJ_d", [128, 4, 2, 8, 128], BF16)
    CJlo_d = scratch("CJlo_d", [128, 4, 4, 9, 32], BF16)
    CJhi_d = scratch("CJhi_d", [128, 4, 4, 9, 64], BF16)
    TT_d = [scratch(f"TT{k}_d", [128, 32, 64], F32) for k in range(4)]
    rho8 = per_even("rho8", [128, 32], F32)
    s5carry = per_even("s5carry", [128, 32], F32)

    psI = [kb.ps(f"psI{i}", [128, 512], F32) for i in range(2)]
    psA = kb.ps("psA", [128, 1024], F32)
    psB = kb.ps("psB", [128, 1024], F32)
    psC = kb.ps("psC", [128, 512], F32)
    psT = kb.ps("psT", [128, 1024], BF16)

    def cv(k, rows=64):
        return cst[0:rows, CB[k][0]:CB[k][0] + CB[k][1]]
    U, Lst, MsN, Mi, I64, Ms, MiN = (cv(k) for k in ['U', 'Lst', 'MsN', 'Mi', 'I64', 'Ms', 'MiN'])
    cmask = cv('cmask', 128)

    kb.dma('sp', cst[:], consts_d[:, :], 'par', w=['cst'])
    kb.dma('sp', normw_s[:], normw[:, :], 'par', w=['normw_s'])
    kb.dma('sp', adab_s[:], adab[:, :], 'par', w=['adab_s'])
    kb.dma('sp', fnw_s[:], fnw_d[:, :], 'par', w=['fnw_s'])
    kb.dma('sp', sc[:], cT[:, :], 'par', w=['sc'])
    for i in odd_ids:
        kb.dma('pool', wba_s[i][:], wba_d[i], 'parp', w=[f'wba{i}'])
        kb.dma('sp', convw_s[i][:], convw_d[i], 'par', w=[f'convw{i}'])
        kb.dma('sp', negA[i][:], alog_d[i], 'par', w=[f'negA{i}'])
        kb.dma('sp', dtb_s[i][:], dtb_d[i], 'par', w=[f'dtb{i}'])
        kb.dma('sp', gnw_s[i][:], gnw_d[i], 'par', w=[f'gnw{i}'])
    for i in even_ids:
        kb.dma('sp', rp[i][:], rp_d[i], 'par', w=[f'rp{i}'])
        kb.dma('sp', lnw_s[i][:], lnw_d[i], 'par', w=[f'lnw{i}'])
        kb.dma('sp', lnb_s[i][:], lnb_d[i], 'par', w=[f'lnb{i}'])
        kb.dma('pool', wup_b[i][:], wup_d[i], 'parp', w=[f'wup{i}'])
        kb.dma('pool', aup_b[i][64:128, :], aup_d[i], 'parp', w=[f'aup{i}'])
        kb.dma('pool', gup_b[i][:], gup_d[i], 'parp', w=[f'gup{i}'])
        kb.dma('pool', gluw_b[i][:], gluw_d[i], 'parp', w=[f'gluw{i}'])
    kb.barrier()
    kb.cp('dve', identb[:], cst[:, 0:128], r=['cst'], w=['identb'])
    kb.cp('dve', bonesb[:], cst[:, 576:704], r=['cst'], w=['bonesb'])
    for i in even_ids:
        kb.ts('dve', omm[i][:], rp[i][:, 0:14], -1.0, ALU.mult, 1.0, ALU.add, r=[f'rp{i}'], w=[f'omm{i}'])
        kb.op('pool', lambda E, i=i: E.memset(Hst[i][:], 0.0), w=[f'H{i}_0', f'H{i}_1'])
        kb.op('pool', lambda E, i=i: E.memset(Hb[i][:], 0.0), w=[f'Hb{i}_0', f'Hb{i}_1'])
        kb.op('pool', lambda E, i=i: E.memset(tcarry[i][:], 0.0), w=[f'tcarry{i}'])
    kb.op('dve', lambda E: E.memset(onesb[:], 1.0), w=['onesb'])
    kb.op('dve', lambda E: E.memset(onesf[:], 1.0), w=['onesf'])
    for i in odd_ids:
        kb.act(negA[i][:], negA[i][:], AF.Exp, r=[f'negA{i}'], w=[f'negA{i}'])
        kb.ts('dve', negA[i][:], negA[i][:], -1.0, ALU.mult, r=[f'negA{i}'], w=[f'negA{i}'])
        kb.op('pool', lambda E, i=i: E.memset(Sst[i][:], 0.0), w=[f'S{i}_0', f'S{i}_1'])
        kb.op('pool', lambda E, i=i: E.memset(Sb[i][:], 0.0), w=[f'Sb{i}_0', f'Sb{i}_1'])
        kb.op('pool', lambda E, i=i: E.memset(carry[i][:], 0.0), w=[f'carry{i}'])

    def s5_tables(i):
        with ExitStack() as st:
            K_ = ['s5tab']
            sa = kb.sb("s5a_s", [128, 96 + 1024], F32, st)
            kb.dma('sp', sa[:], s5a_d[i], 's5a', w=K_)
            cnt = [0]
            CJlo = {i: kb.sb("CJlo_t", [128, 4, 4, 9, 32], BF16, st)}
            CJhi = {i: kb.sb("CJhi_t", [128, 4, 4, 9, 64], BF16, st)}
            T1 = {i: kb.sb("T1_t", [128, 32, 64], F32, st)}
            T2 = {i: kb.sb("T2_t", [128, 32, 64], F32, st)}
            T3 = {i: kb.sb("T3_t", [128, 32, 64], F32, st)}
            T4 = {i: kb.sb("T4_t", [128, 32, 64], F32, st)}

            cur = [st]

            def T(shape):
                cnt[0] += 1
                return kb.sb(f"s5t{cnt[0]}", shape, F32, cur[0])

            def tt(o, a, b, op, e='dve'):
                kb.tt(e, o, a, b, op, r=K_, w=K_)

            def ts(o, a, s1, op0, s2=None, op1=None):
                kb.ts('dve', o, a, s1, op0, s2, op1, r=K_, w=K_)

            def cmul(orr, oi, ar, ai, br, bi, t1, t2):
                tt(t1, ar, br, ALU.mult)
                tt(t2, ai, bi, ALU.mult)
                tt(orr, t1, t2, ALU.subtract)
                tt(t1, ar, bi, ALU.mult)
                tt(t2, ai, br, ALU.mult)
                tt(oi, t1, t2, ALU.add)

            def lam_bar(lre_in, lim_in, lst_in, shape):
                lre = T(shape); stp = T(shape); ar = T(shape); ai = T(shape)
                cr = T(shape); si = T(shape); t1 = T(shape); t2 = T(shape); rho = T(shape)
                ts(lre[:], lre_in, -1e-4, ALU.min)
                kb.act(stp[:], lst_in, AF.Exp, r=K_, w=K_)
                tt(ar[:], lre[:], stp[:], ALU.mult)
                tt(ai[:], lim_in, stp[:], ALU.mult)
                kb.act(rho[:], ar[:], AF.Exp, r=K_, w=K_)
                kb.act(si[:], ai[:], AF.Sin, scale=1.0 / 16, r=K_, w=K_)
                ts(t1[:], ai[:], 1.0 / 16, ALU.mult, float(np.pi / 2), ALU.add)
                kb.act(cr[:], t1[:], AF.Sin, r=K_, w=K_)
                for _ in range(4):
                    tt(t1[:], cr[:], cr[:], ALU.mult)
                    tt(t2[:], si[:], si[:], ALU.mult)
                    tt(ar[:], cr[:], si[:], ALU.mult)
                    tt(cr[:], t1[:], t2[:], ALU.subtract)
                    ts(si[:], ar[:], 2.0, ALU.mult)
                lr = T(shape); li = T(shape)
                tt(lr[:], cr[:], rho[:], ALU.mult)
                tt(li[:], si[:], rho[:], ALU.mult)
                return lr, li, rho, cr, si, lre

            sh = [128, 32]
            lr, li, rho, ur, ui, _ = lam_bar(sa[:, 0:32], sa[:, 32:64], sa[:, 64:96], sh)
            t1 = T(sh); t2 = T(sh)
            tt(t1[:], rho[:], rho[:], ALU.mult)
            tt(t2[:], t1[:], t1[:], ALU.mult)
            tt(rho8[i][:], t2[:], t2[:], ALU.mult)
            Lr = T([128, 32, 9]); Li = T([128, 32, 9])
            kb.op('dve', lambda E: E.memset(Lr[:, :, 0], 1.0), r=K_, w=K_)
            kb.op('dve', lambda E: E.memset(Li[:, :, 0], 0.0), r=K_, w=K_)
            for j in range(8):
                cmul(Lr[:, :, j + 1], Li[:, :, j + 1], Lr[:, :, j], Li[:, :, j], lr[:], li[:], t1[:], t2[:])
            kb.op('pool', lambda E: E.memset(CJlo[i][:], 0.0), r=K_, w=K_)
            kb.op('pool', lambda E: E.memset(CJhi[i][:], 0.0), r=K_, w=K_)
            Cr = sa[:, 96:96 + 512].rearrange("p (g c) -> p g c", c=16)
            Ci = sa[:, 96 + 512:96 + 1024].rearrange("p (g c) -> p g c", c=16)
            d1 = T([128, 32, 16]); d2 = T([128, 32, 16]); dr = T([128, 32, 16]); di = T([128, 32, 16])
            for j in range(9):
                lrj = bc(Lr[:, :, j:j + 1], [128, 32, 16])
                lij = bc(Li[:, :, j:j + 1], [128, 32, 16])
                tt(d1[:], Cr, lrj, ALU.mult)
                tt(d2[:], Ci, lij, ALU.mult)
                tt(dr[:], d1[:], d2[:], ALU.subtract)
                tt(d1[:], Cr, lij, ALU.mult)
                tt(d2[:], Ci, lrj, ALU.mult)
                tt(di[:], d1[:], d2[:], ALU.add)
                ts(di[:], di[:], -1.0, ALU.mult)
                drv = dr[:].rearrange("p (q gl) c -> p q gl c", gl=8)
                div = di[:].rearrange("p (q gl) c -> p q gl c", gl=8)
                for gl in range(8):
                    if gl < 4:
                        dst = lambda ps_: CJlo[i][ps_, :, gl, j, (gl % 2) * 16:(gl % 2) * 16 + 16]
                    else:
                        dst = lambda ps_: CJhi[i][ps_, :, gl - 4, j, (gl - 4) * 16:(gl - 4) * 16 + 16]
                    kb.cp('dve', dst(slice(0, 64)), drv[0:64, :, gl, :], r=K_, w=K_)
                    kb.cp('dve', dst(slice(64, 128)), div[64:128, :, gl, :], r=K_, w=K_)
            er = T(sh); ei = T(sh)
            kb.cp('dve', er[:], ur[:], r=K_, w=K_)
            kb.cp('dve', ei[:], ui[:], r=K_, w=K_)
            for _ in range(3):
                tt(t1[:], er[:], er[:], ALU.mult)
                tt(t2[:], ei[:], ei[:], ALU.mult)
                tt(ei[:], er[:], ei[:], ALU.mult)
                tt(er[:], t1[:], t2[:], ALU.subtract)
                ts(ei[:], ei[:], 2.0, ALU.mult)
            Pr = T3[i]
            Pi = T([128, 32, 64])
            kb.cp('dve', Pr[:, :, 0], er[:], r=K_, w=K_)
            kb.cp('dve', Pi[:, :, 0], ei[:], r=K_, w=K_)
            p1 = T([128, 32, 32]); p2 = T([128, 32, 32])
            m = 1
            while m < 64:
                br_ = bc(Pr[:, :, m - 1:m], [128, 32, m])
                bi_ = bc(Pi[:, :, m - 1:m], [128, 32, m])
                cmul(Pr[:, :, m:2 * m], Pi[:, :, m:2 * m], Pr[:, :, 0:m], Pi[:, :, 0:m], br_, bi_,
                     p1[:, :, 0:m], p2[:, :, 0:m])
                m *= 2
            l7r = bc(Lr[:, :, 7:8], [128, 32, 64])
            l7i = bc(Li[:, :, 7:8], [128, 32, 64])
            tt(T1[i][:], Pr[:], l7r, ALU.mult)
            tt(T2[i][:], Pi[:], l7i, ALU.mult)
            tt(T1[i][:], T1[i][:], T2[i][:], ALU.add)
            tt(T2[i][:], Pr[:], l7i, ALU.mult)
            tt(T4[i][:], Pi[:], l7r, ALU.mult)
            tt(T2[i][:], T2[i][:], T4[i][:], ALU.subtract)
            ts(T2[i][64:128], T2[i][64:128], -1.0, ALU.mult)
            kb.cp('dve', T4[i][0:64], Pi[0:64], r=K_, w=K_)
            ts(T4[i][64:128], Pi[64:128], -1.0, ALU.mult)
            kb.dma('sp', CJlo_d[i], CJlo[i][:], 'tabst', r=K_)
            kb.dma('sp', CJhi_d[i], CJhi[i][:], 'tabst', r=K_)
            for k_, t_ in enumerate((T1, T2, T3, T4)):
                kb.dma('sp', TT_d[k_][i], t_[i][:], 'tabst', r=K_)
            kb.barrier()
        with ExitStack() as st:
            cnt[0] += 1000
            cur[0] = st
            BJtab = {i: kb.sb("BJ_t", [128, 4, 2, 8, 128], BF16, st)}
            sbb = kb.sb("s5b_s", [128, 1028], F32, st)
            kb.dma('sp', sbb[:], s5b_d[i], 's5b', w=K_)
            shb = [128, 4, 64]
            v = lambda a: sbb[:, a * 256:(a + 1) * 256].rearrange("p (q n) -> p q n", n=64)
            lstb = bc(sbb[:, 1024:1028].unsqueeze(2), shb)
            blr, bli, brho, _, _, blre = lam_bar(v(0), v(1), lstb, shb)
            bt1 = T(shb); bt2 = T(shb); den = T(shb); kr = T(shb); ki = T(shb); ir = T(shb); ii = T(shb)
            nr = T(shb)
            ts(nr[:], blr[:], -1.0, ALU.add)
            tt(bt1[:], blre[:], blre[:], ALU.mult)
            tt(bt2[:], v(1), v(1), ALU.mult)
            tt(den[:], bt1[:], bt2[:], ALU.add)
            kb.op('dve', lambda E: E.reciprocal(out=den[:], in_=den[:]), r=K_, w=K_)
            tt(bt1[:], nr[:], blre[:], ALU.mult)
            tt(bt2[:], bli[:], v(1), ALU.mult)
            tt(kr[:], bt1[:], bt2[:], ALU.add)
            tt(kr[:], kr[:], den[:], ALU.mult)
            tt(bt1[:], bli[:], blre[:], ALU.mult)
            tt(bt2[:], nr[:], v(1), ALU.mult)
            tt(ki[:], bt1[:], bt2[:], ALU.subtract)
            tt(ki[:], ki[:], den[:], ALU.mult)
            tt(bt1[:], brho[:], brho[:], ALU.mult)
            kb.op('dve', lambda E: E.reciprocal(out=bt1[:], in_=bt1[:]), r=K_, w=K_)
            tt(ir[:], blr[:], bt1[:], ALU.mult)
            tt(ii[:], bli[:], bt1[:], ALU.mult)
            ts(ii[:], ii[:], -1.0, ALU.mult)
            gr = T(shb); gi = T(shb); g2r = T(shb); g2i = T(shb); vr = T(shb); vi = T(shb)
            kb.cp('dve', gr[:], kr[:], r=K_, w=K_)
            kb.cp('dve', gi[:], ki[:], r=K_, w=K_)
            pm = cst[:, 1216:1218]
            for j in range(8):
                cmul(vr[:], vi[:], gr[:], gi[:], v(2), v(3), bt1[:], bt2[:])
                for e in range(2):
                    ts(BJtab[i][:, :, e, j, 0:64], vr[:], pm[:, e:e + 1], ALU.mult)
                    ts(BJtab[i][:, :, e, j, 64:128], vi[:], pm[:, e:e + 1], ALU.mult)
                if j < 7:
                    cmul(g2r[:], g2i[:], gr[:], gi[:], ir[:], ii[:], bt1[:], bt2[:])
                    gr, g2r = g2r, gr
                    gi, g2i = g2i, gi
            kb.op('pool', lambda E: E.memset(s5carry[i][:], 0.0), r=K_, w=K_)
            kb.dma('sp', BJ_d[i], BJtab[i][:], 'tabst', r=K_)
            kb.barrier()

    for i in even_ids:
        s5_tables(i)

    kb.act(sc[:], sc[:], AF.Silu, r=['sc'], w=['sc'])
    with ExitStack() as st:
        abuf = [kb.sb(f"abuf{i}", [128, 3 * D], F32, st) for i in range(2)]
        macc = kb.sb("macc", [128, 24], F32, st)
        n = 0
        for l in layers:
            for k in range(8):
                b = abuf[n % 2]
                kb.dma('sp', b[:], adaw[l, k * 128:(k + 1) * 128, :], f'ab{n % 2}', w=[f'abuf{n % 2}'])
                for j in range(24):
                    kb.mm(psC[:, j:j + 1], b[:, j * 128:(j + 1) * 128], sc[:, k:k + 1],
                          r=[f'abuf{n % 2}', 'sc'], w=['psC'])
                if k == 0:
                    kb.tt('dve', macc[:], psC[:, 0:24], adab_s[:, l * 24:(l + 1) * 24], ALU.add,
                          r=['psC', 'adab_s'], w=['macc'])
                else:
                    kb.tt('dve', macc[:], psC[:, 0:24], macc[:], ALU.add, r=['psC', 'macc'], w=['macc'])
                n += 1
            kb.cp('dve', modT[:, l, :], macc[:], r=['macc'], w=['modT'])
            kb.ts('dve', s1[:, l, :], modT[:, l, 8:16], 1.0, ALU.add, r=['modT'], w=['s1'])
            kb.tt('dve', s1[:, l, :], s1[:, l, :], normw_s[:, l * 8:(l + 1) * 8], ALU.mult,
                  r=['s1', 'normw_s'], w=['s1'])
        kb.barrier()

    wctr = [0]

    def load_w(src_ap):
        i = wctr[0] % 4
        wctr[0] += 1
        kb.dma('pool', wbuf[i][:], src_ap, f'w{i}', w=[f'wbuf{i}'])
        return wbuf[i], f'wbuf{i}'

    ictr = [0]

    def proj_chunk(src_ap, rhs_tile, rhs_key):
        wb, wk = load_w(src_ap)
        i = ictr[0] % 2
        ictr[0] += 1
        for k in range(8):
            rk_l = rhs_key if isinstance(rhs_key, list) else [rhs_key]
            kb.mm(psI[i][:], wb[:, k, :], rhs_tile[:, k, :], start=(k == 0), stop=(k == 7),
                  r=[wk] + rk_l, w=[f'psI{i}'])
        return psI[i], f'psI{i}'

    def rms_stats(src, src_key):
        for k in range(8):
            q = sqb[k % 2]
            kb.act(q[:], src[:, k, :], AF.Square, r=[src_key], w=[f'sqb{k % 2}'])
            kb.mm(psC[:, :], onesb[:], q[:], start=(k == 0), stop=(k == 7), r=[f'sqb{k % 2}', 'onesb'], w=['psC'])
        kb.act(rstd[:], psC[:], AF.Ln, scale=1.0 / D, bias=EPS, r=['psC'], w=['rstd'])
        kb.act(rstd[:], rstd[:], AF.Exp, scale=-0.5, r=['rstd'], w=['rstd'])

    def pre_norm(l):
        rms_stats(xres, 'xres')
        for k in range(8):
            t = tmpf[k % 2]
            kb.tt('dve', t[:], xres[:, k, :], rstd[:], ALU.mult, r=['xres', 'rstd'], w=[f'tmpf{k % 2}'])
            kb.act(hT[:, k, :], t[:], AF.Identity, scale=s1[:, l, k:k + 1], bias=modT[:, l, k:k + 1],
                   r=[f'tmpf{k % 2}', 's1', 'modT'], w=['hT'])

    def out_proj(l):
        for dm in range(8):
            ps, pk = proj_chunk(woutr[l, dm], ygT, ['ygT', 'ygT0', 'ygT1'])
            kb.stt(xres[:, dm, :], ps[:], modT[:, l, 16 + dm:17 + dm], xres[:, dm, :], ALU.mult, ALU.add,
                   r=[pk, 'modT', 'xres'], w=['xres'])


    PE_OF = [os.environ.get('POOLTO', 'dve'), os.environ.get('POOLTO1', 'dve')]

    def neumann(T, g):
        HN = T['HN']
        POOLE = PE_OF[g]
        Pf, Accf, Accb, PTf, Pw = T['Pf'], T['Accf'], T['Accb'], T['PTf'], T['Pw']
        pA, pB, pC = T['pA'], T['pB'], T['pC']
        kA, kB, kC = T['kA'], T['kB'], T['kC']
        K = lambda n: f'{n}_{g}'
        W = HN * 64
        f2 = lambda t: t[:].rearrange("p h c -> p (h c)")
        kb.tt(POOLE, Accf[:], Pf[:], bc(I64.unsqueeze(1), [64, HN, 64]), ALU.add, r=[K('Pf'), 'cst'], w=[K('Accf')])
        for h in range(HN):
            kb.tr(pC[0:64, h * 64:(h + 1) * 64], Pf[:, h, :], cst[0:64, 0:64], r=[K('Pf'), 'cst'], w=kC)
        kb.cp('act', f2(PTf), pC[0:64, 0:W], r=kC, w=[K('PTf')])
        cur, curk = Pf, K('Pf')
        oth, othk = Pw, K('Pw')
        for lev in range(5):
            last = (lev == 4)
            if not last and os.environ.get('NEU_TR', '1') == '1':
                for h in range(HN):
                    kb.mm(pA[0:64, h * 64:(h + 1) * 64], PTf[:, h, :], cur[:, h, :], r=[K('PTf'), curk], w=kA)
                kb.cp('act', f2(oth), pA[0:64, 0:W], r=kA, w=[othk])
                for h in range(HN):
                    kb.tr(pB[0:64, h * 64:(h + 1) * 64], oth[:, h, :], cst[0:64, 0:64], r=[othk, 'cst'], w=kB)
            else:
                if not last:
                    for h in range(HN):
                        kb.mm(pA[0:64, h * 64:(h + 1) * 64], PTf[:, h, :], cur[:, h, :], r=[K('PTf'), curk], w=kA)
                for h in range(HN):
                    kb.mm(pB[0:64, h * 64:(h + 1) * 64], cur[:, h, :], PTf[:, h, :], r=[K('PTf'), curk], w=kB)
                if not last:
                    kb.cp('act', f2(oth), pA[0:64, 0:W], r=kA, w=[othk])
            kb.cp('dve', f2(PTf), pB[0:64, 0:W], r=kB, w=[K('PTf')])
            cur, curk, oth, othk = oth, othk, cur, curk
            for h in range(HN):
                kb.mm(pC[0:64, h * 64:(h + 1) * 64], PTf[:, h, :], Accf[:, h, :], r=[K('PTf'), K('Accf')], w=kC)
            kb.tt('dve', f2(Accf), pC[0:64, 0:W], f2(Accf), ALU.add, r=kC + [K('Accf')], w=[K('Accf')])
        kb.cp('act', Accb[:], Accf[:], r=[K('Accf')], w=[K('Accb')])

    def psum_group(g, G=2):
        if G == 1:
            return dict(pA=psA[:, :], pB=psB[:, :], pC=psC[:, :], pT=psT[:, :], HN=8,
                        kA=['psA0', 'psA1'], kB=['psB0', 'psB1'], kC=['psC'], kT=['psT'])
        if g == 0:
            return dict(pA=psA[:, 0:512], pB=psA[:, 512:1024], pC=psC[:, :], pT=psT[:, :], HN=4,
                        kA=['psA0'], kB=['psA1'], kC=['psC'], kT=['psT'])
        return dict(pA=psB[:, 0:512], pB=psB[:, 512:1024], pC=psI[0][:, :], pT=psI[1][:, :].bitcast(BF16), HN=4,
                    kA=['psB0'], kB=['psB1'], kC=['psI0'], kT=['psI1'])

    def odd_layer(l, bi):
        i = l // 2
        with ExitStack() as st:
            acc = kb.sb("acc", [128, 8, TB], F32, st)
            qkn = kb.sb("qkn", [128, 16, TB], BF16, st)
            vb = kb.sb("vb", [128, 8, TB], BF16, st)
            ba = kb.sb("ba", [64, 8, 16], F32, st)
            beta = kb.sb("beta", [64, 8, 8], F32, st)
            aall = kb.sb("aall", [64, 8, 8], F32, st)
            eg = kb.sb("eg", [64, 8, 8], F32, st)
            eend = kb.sb("eend", [64, 8, 8], F32, st)
            cdb = kb.sb("cdb", [128, 8, 8], F32, st)
            TG = []
            GG = int(os.environ.get('GDN_G', '2'))
            HN = 8 // GG
            for g_ in range(GG):
                T_ = psum_group(g_, GG)
                for n_, shp, dt_ in [('R2', [64, HN, 64], F32), ('decT', [64, HN, 64], F32), ('dSb', [64, HN, 64], F32),
                                     ('dI', [64, HN, 64], F32), ('Pf', [64, HN, 64], F32), ('Accf', [64, HN, 64], F32),
                                     ('PTf', [64, HN, 64], F32), ('Pw', [64, HN, 64], F32), ('Accb', [64, HN, 64], BF16),
                                     ('intraT', [64, HN, 64], BF16), ('vk', [64, HN, 256], BF16), ('kend', [64, HN, 128], BF16),
                                     ('uu', [64, HN, 128], F32), ('ww', [64, HN, 128], BF16), ('wTb', [128, HN, 64], BF16),
                                     ('vnew', [64, HN, 128], BF16), ('o1', [64, HN, 128], F32), ('oo', [64, HN, 128], F32),
                                     ('osq', [64, HN, 128], F32), ('ss', [64, HN], F32), ('onb', [64, HN, 128], BF16)]:
                    T_[n_] = kb.sb(f"g{g_}{n_}", shp, dt_, st)
                TG.append(T_)
            cnew = kb.sb("cnew", [128, 24, 3], F32, st)

            pre_norm(l)
            for m in range(8):
                ps, pk = proj_chunk(winodd[i, 24 + m], hT, 'hT')
                kb.act(zs[:, m, :], ps[:], AF.Silu, r=[pk], w=['zs'])
            if dbg:
                kb.op('pool', lambda E: E.memset(zs[:], 1.0), r=['zs'], w=['zs'])
            cw = convw_s[i]
            cr = carry[i]
            for grp in range(3):
                gsl = slice(grp * 8, grp * 8 + 8)
                accall = [f'acc{mm_}' for mm_ in range(8)]
                for mm_ in range(8):
                    m = grp * 8 + mm_
                    ps, pk = proj_chunk(winodd[i, m], hT, 'hT')
                    am = f'acc{mm_}'
                    kb.act(acc[:, mm_, :], ps[:], AF.Copy, scale=cw[:, m, 3:4], r=[pk, f'convw{i}'], w=[am])
                    kb.cp('act', cnew[:, m, :], ps[:, TB - 3:TB], r=[pk], w=['cnew'])
                    for j in range(3):
                        sh = 3 - j
                        kb.stt(acc[:, mm_, sh:TB], ps[:, 0:TB - sh], cw[:, m, j:j + 1], acc[:, mm_, sh:TB],
                               ALU.mult, ALU.add, r=[pk, f'convw{i}', am], w=[am])
                for j in range(3):
                    n = 3 - j
                    tv = tmpf[0][:, 0:8 * n].rearrange("p (m n) -> p m n", n=n)
                    kb.tt('dve', tv, cr[:, gsl, j:3], bc(cw[:, gsl, j:j + 1], [128, 8, n]), ALU.mult,
                          r=[f'carry{i}', f'convw{i}'], w=['tmpf0'])
                    kb.tt('dve', acc[:, :, 0:n], acc[:, :, 0:n], tv, ALU.add, r=['tmpf0'] + accall, w=accall)
                kb.cp('dve', cr[:, gsl, :], cnew[:, gsl, :], r=['cnew'], w=[f'carry{i}'])
                kb.act(acc[:], acc[:], AF.Silu, r=accall, w=accall)
                if grp == 2:
                    kb.cp('pool', vb[:], acc[:], r=accall, w=['vb'])
                    continue
                kb.act(sqb[(grp * 8) % 2][:], acc[:, 0, :], AF.Square, r=['acc0'], w=[f'sqb{(grp * 8) % 2}'])
                for mm_ in range(8):
                    m = grp * 8 + mm_
                    q = sqb[m % 2]
                    if mm_ < 7:
                        kb.act(sqb[(m + 1) % 2][:], acc[:, mm_ + 1, :], AF.Square, r=[f'acc{mm_ + 1}'], w=[f'sqb{(m + 1) % 2}'])
                    kb.mm(psC[:], onesb[:], q[:], r=[f'sqb{m % 2}', 'onesb'], w=['psC'])
                    t = tmpf[m % 2]
                    kb.act(t[:], psC[:], AF.Ln, bias=EPS, r=['psC'], w=[f'tmpf{m % 2}'])
                    kb.act(t[:], t[:], AF.Exp, scale=-0.5, r=[f'tmpf{m % 2}'], w=[f'tmpf{m % 2}'])
                    kb.stt(qkn[:, m, :], acc[:, mm_, :], (128.0 ** -0.5) if m < 8 else 1.0, t[:], ALU.mult, ALU.mult,
                           r=[f'acc{mm_}', f'tmpf{m % 2}'], w=['qkn'])
            for j in range(NCH):
                for k in range(8):
                    kb.mm(psC[0:64, j * 16:(j + 1) * 16], hT[:, k, j * C:(j + 1) * C], wba_s[i][:, k, :],
                          start=(k == 0), stop=(k == 7), r=['hT', f'wba{i}'], w=['psC'])
            kb.cp('dve', ba[:], psC[0:64, 0:128].rearrange("p (j e) -> p j e", e=16), r=['psC'], w=['ba'])
            kb.act(beta[:], ba[:, :, 0:8], AF.Sigmoid, r=['ba'], w=['beta'])
            kb.tt('dve', aall[:], ba[:, :, 8:16], bc(dtb_s[i][0:64, :].unsqueeze(1), [64, 8, 8]), ALU.add,
                  r=['ba', f'dtb{i}'], w=['aall'])
            kb.act(aall[:], aall[:], AF.Exp, r=['aall'], w=['aall'])
            kb.act(aall[:], aall[:], AF.Ln, bias=1.0, r=['aall'], w=['aall'])
            kb.tt('dve', aall[:], aall[:], bc(negA[i][0:64, :].unsqueeze(1), [64, 8, 8]), ALU.mult,
                  r=['aall', f'negA{i}'], w=['aall'])
            a2 = aall[:].rearrange("p j h -> p (j h)")
            kb.mm(psI[0][0:64, 0:64], U, a2, r=['cst', 'aall'], w=['psI0'])
            kb.act(eg[:].rearrange("p j h -> p (j h)"), psI[0][0:64, 0:64], AF.Exp, r=['psI0'], w=['eg'])
            kb.mm(psI[1][0:64, 0:64], Lst, a2, r=['cst', 'aall'], w=['psI1'])
            kb.act(eend[:].rearrange("p j h -> p (j h)"), psI[1][0:64, 0:64], AF.Exp, r=['psI1'], w=['eend'])
            kb.mm(psI[0][:, 64:128], onesf[:], a2, r=['onesf', 'aall'], w=['psI0'])
            kb.act(cdb[:].rearrange("p j h -> p (j h)"), psI[0][:, 64:128], AF.Exp, r=['psI0'], w=['cdb'])

            S = Sst[i]
            Sbf = Sb[i]

            def gdn_stream(g):
                T = TG[g]
                HN = T['HN']
                POOLE = PE_OF[g]
                K = lambda n: f'{n}_{g}'
                pA, pB, pC, pT = T['pA'], T['pB'], T['pC'], T['pT']
                kA, kC, kT = T['kA'], T['kC'], T['kT']
                kBs = T['kB']
                W = HN * 64
                hs = slice(g * HN, (g + 1) * HN)
                f2 = lambda t: t[:].rearrange("p h c -> p (h c)")
                R2, decT, dSb, dI, Pf, intraT, Accb = (T[n] for n in ['R2', 'decT', 'dSb', 'dI', 'Pf', 'intraT', 'Accb'])
                vk, kend, uu, ww, wTb, vnew, o1, oo, osq, ss, onb = (T[n] for n in
                    ['vk', 'kend', 'uu', 'ww', 'wTb', 'vnew', 'o1', 'oo', 'osq', 'ss', 'onb'])
                for j in range(NCH):
                    cs = slice(j * C, (j + 1) * C)
                    be = beta[:, j, hs]
                    kb.tt('dve', R2[:], bc(U.unsqueeze(1), [64, HN, 64]), bc(aall[:, j, hs].unsqueeze(2), [64, HN, 64]),
                          ALU.mult, r=['cst', 'aall'], w=[K('R2')])
                    kb.mm(pC[0:64, 0:W], Lst, f2(R2), r=['cst', K('R2')], w=kC)
                    kb.act(f2(decT), pC[0:64, 0:W], AF.Exp, r=kC, w=[K('decT')])
                    kb.tt(POOLE, dSb[:], decT[:], bc(MsN.unsqueeze(1), [64, HN, 64]), ALU.mult, r=[K('decT'), 'cst'], w=[K('dSb')])
                    kb.tt(POOLE, dSb[:], dSb[:], bc(be.unsqueeze(2), [64, HN, 64]), ALU.mult, r=[K('dSb'), 'beta'], w=[K('dSb')])
                    kb.tt(POOLE, dI[:], decT[:], bc(Mi.unsqueeze(1), [64, HN, 64]), ALU.mult, r=[K('decT'), 'cst'], w=[K('dI')])
                    for h in range(HN):
                        H_ = g * HN + h
                        kb.mm(pA[0:64, h * 64:(h + 1) * 64], qkn[:, 8 + H_, cs], qkn[:, 8 + H_, cs], r=['qkn'], w=kA)
                    for h in range(HN):
                        H_ = g * HN + h
                        kb.mm(pA[0:64, W + h * 64:W + (h + 1) * 64], qkn[:, 8 + H_, cs], qkn[:, H_, cs], r=['qkn'], w=kA)
                    kb.tt('dve', f2(Pf), pA[0:64, 0:W], f2(dSb), ALU.mult, r=kA + [K('dSb')], w=[K('Pf')])
                    kb.tt('dve', f2(intraT), pA[0:64, W:2 * W], f2(dI), ALU.mult, r=kA + [K('dI')], w=[K('intraT')])
                    neumann(T, g)
                    for h in range(HN):
                        kb.tr(pT[0:64, h * 128:(h + 1) * 128], qkn[:, 8 + g * HN + h, cs], identb[:], r=['qkn', 'identb'], w=kT)
                    pT3 = pT[0:64, 0:HN * 128].rearrange("p (h d) -> p h d", d=128)
                    kb.tt('dve', vk[:, :, 128:256], pT3, bc(eg[:, j, hs].unsqueeze(2), [64, HN, 128]), ALU.mult,
                          r=kT + ['eg'], w=[K('vk1')])
                    kb.tt('dve', kend[:], pT3, bc(eend[:, j, hs].unsqueeze(2), [64, HN, 128]), ALU.mult,
                          r=kT + ['eend'], w=[K('kend')])
                    for h in range(HN):
                        kb.tr(pT[0:64, h * 128:(h + 1) * 128], vb[:, g * HN + h, cs], identb[:], r=['vb', 'identb'], w=kT)
                    kb.cp('act', vk[:, :, 0:128], pT3, r=kT, w=[K('vk0')])
                    for h in range(HN):
                        kb.mm(pA[0:64, h * 128:(h + 1) * 128], Accb[:, h, :], vk[:, h, 0:128], r=[K('Accb'), K('vk0')], w=kA)
                    for h in range(HN):
                        kb.mm(pB[0:64, h * 128:(h + 1) * 128], Accb[:, h, :], vk[:, h, 128:256], r=[K('Accb'), K('vk1')], w=kBs)
                    pA3 = pA[0:64, :].rearrange("p (h d) -> p h d", d=128)
                    pB3 = pB[0:64, :].rearrange("p (h d) -> p h d", d=128)
                    bet3 = bc(be.unsqueeze(2), [64, HN, 128])
                    kb.tt('dve', uu[:], pA3, bet3, ALU.mult, r=kA + ['beta'], w=[K('uu')])
                    kb.tt('dve', ww[:], pB3, bet3, ALU.mult, r=kBs + ['beta'], w=[K('ww')])
                    for h in range(HN):
                        kb.tr(pT[:, h * 64:(h + 1) * 64], ww[:, h, :], identb[0:64, 0:64], r=[K('ww'), 'identb'], w=kT)
                    kb.cp('act', f2(wTb), pT[:, 0:W], r=kT, w=[K('wTb')])
                    kS, kSb = f'S{i}_{g}', f'Sb{i}_{g}'
                    for h in range(HN):
                        kb.mm(pA[0:64, h * 128:(h + 1) * 128], wTb[:, h, :], Sbf[:, g * HN + h, :], r=[K('wTb'), kSb], w=kA)
                    kb.tt('dve', vnew[:], uu[:], pA3, ALU.subtract, r=[K('uu')] + kA, w=[K('vnew')])
                    for h in range(HN):
                        kb.mm(pB[0:64, h * 128:(h + 1) * 128], qkn[:, g * HN + h, cs], Sbf[:, g * HN + h, :], r=['qkn', kSb], w=kBs)
                    kb.tt('dve', o1[:], pB3, bc(eg[:, j, hs].unsqueeze(2), [64, HN, 128]), ALU.mult, r=kBs + ['eg'], w=[K('o1')])
                    for h in range(HN):
                        kb.mm(pA[0:64, h * 128:(h + 1) * 128], intraT[:, h, :], vnew[:, h, :], r=[K('intraT'), K('vnew')], w=kA)
                    kb.tt('dve', oo[:], pA3, o1[:], ALU.add, r=kA + [K('o1')], w=[K('oo')])
                    for h in range(HN):
                        kb.mm(pB[:, h * 128:(h + 1) * 128], kend[:, h, :], vnew[:, h, :], r=[K('kend'), K('vnew')], w=kBs)
                    kb.tt(POOLE, S[:, hs, :], S[:, hs, :], bc(cdb[:, j, hs].unsqueeze(2), [128, HN, 128]), ALU.mult,
                          r=[kS, 'cdb'], w=[kS])
                    kb.tt('dve', S[:, hs, :], S[:, hs, :], pB[:, :].rearrange("p (h d) -> p h d", d=128), ALU.add,
                          r=[kS] + kBs, w=[kS])
                    kb.cp('act', Sbf[:, hs, :], S[:, hs, :], r=[kS], w=[kSb])
                    kb.tt(POOLE, osq[:], oo[:], oo[:], ALU.mult, r=[K('oo')], w=[K('osq')])
                    kb.op('dve', lambda E: E.tensor_reduce(out=ss[:], in_=osq[:], axis=AX.X, op=ALU.add), r=[K('osq')], w=[K('ss')])
                    kb.act(ss[:], ss[:], AF.Ln, scale=1.0 / 128, bias=EPS, r=[K('ss')], w=[K('ss')])
                    kb.act(ss[:], ss[:], AF.Exp, scale=-0.5, r=[K('ss')], w=[K('ss')])
                    kb.tt(POOLE, osq[:], oo[:], bc(ss[:].unsqueeze(2), [64, HN, 128]), ALU.mult, r=[K('oo'), K('ss')], w=[K('osq')])
                    kb.tt(POOLE, onb[:], osq[:], bc(gnw_s[i][0:64, :].unsqueeze(1), [64, HN, 128]), ALU.mult,
                          r=[K('osq'), f'gnw{i}'], w=[K('onb')])
                    for h in range(HN):
                        kb.tr(pT[:, h * 64:(h + 1) * 64], onb[:, h, :], identb[0:64, 0:64], r=[K('onb'), 'identb'], w=kT)
                    kb.tt('dve', ygT[:, hs, cs], pT[:, 0:W].rearrange("p (h c) -> p h c", c=64), zs[:, hs, cs], ALU.mult,
                          r=kT + ['zs'], w=[f'ygT{g}'])

            kb.run_streams([(lambda g_=g_: gdn_stream(g_)) for g_ in range(GG)])
            if not dbg:
                out_proj(l)
        kb.barrier()

    RW_DS = float(np.exp(-0.5))
    LN_EPS = 1e-5 * 64

    def rwkv_phase(l, i, bi):
        POOLE = os.environ.get('POOLPREP', 'dve')
        P = rp[i]
        rk_ = f'rp{i}'
        with ExitStack() as st:
            gbuf = kb.sb("gbuf", [128, TB + 1], F32, st)
            tw = kb.sb("tw", [64, TB], BF16, st)
            xab = kb.sb("xab", [128, TB], BF16, st)
            sg = kb.sb("sg", [128, TB], BF16, st)
            gz = kb.sb("gz", [128, 4, TB], BF16, st)
            bonus = kb.sb("bonus", [128, 4, TB], BF16, st)
            vbT = kb.sb("vbT", [128, 4, TB], BF16, st)
            ops6 = {n: kb.sb("op_" + n, [128, 4, TB], BF16, st) for n in ['At', 'Qt', 'Kh', 'Bh', 'Kb', 'Bb']}
            GC = kb.sb("GC", [64, NCH, 8], F32, st)
            rt = [kb.sb(f"rt{k}", [128, TB], F32, st) for k in range(12)]
            rf, kf, vf, ldm, aam, kk, kp, bb_, lg, lgx, t1, t2 = rt
            rtk = [f'rt{k}' for k in range(12)]
            krf, kkf, kvf, kld, kaa, kkk, kkp, kbb, klg, klgx, kt1, kt2 = rtk
            fl = [t1, t2]
            flk = [kt1, kt2]
            RG = []
            RGN = int(os.environ.get('RWKV_G', '2'))
            HN = 8 // RGN
            for g_ in range(RGN):
                T_ = psum_group(g_, RGN)
                for n_, shp, dt_ in [('Pf', [64, HN, 64], F32), ('Accf', [64, HN, 64], F32), ('PTf', [64, HN, 64], F32),
                                     ('Pw', [64, HN, 64], F32), ('Accb', [64, HN, 64], BF16), ('Mav', [64, HN, 64], BF16),
                                     ('Mqk', [64, HN, 64], BF16), ('MqbN', [64, HN, 64], BF16), ('Vt', [64, HN * 64], BF16),
                                     ('Kbt', [64, HN * 64], BF16), ('BbtN', [64, HN * 64], BF16), ('RHSb', [64, HN * 64], BF16),
                                     ('Pb2', [64, HN * 64], BF16), ('tmpR', [64, HN * 64], F32), ('tmpO', [64, HN * 64], F32),
                                     ('oT', [64, HN, 64], F32), ('oc', [64, HN, 64], F32), ('osq', [64, HN, 64], F32),
                                     ('s8', [64, HN], F32), ('s8b', [64, HN], F32), ('onb', [64, HN * 64], BF16),
                                     ('t_o', [128, HN // 2, C], F32)]:
                    T_[n_] = kb.sb(f"r{g_}{n_}", shp, dt_, st)
                T_['opo'] = {n_: kb.sb(f"r{g_}opo_{n_}", [64, HN // 2, C], BF16, st) for n_ in ['At', 'Qt', 'Kh', 'Bh']}
                RG.append(T_)

            def shift(ps, pk, mf, out, outk):
                kb.cp('dve', gbuf[:, 0:1], tcarry[i][:, mf:mf + 1], r=[f'tcarry{i}'], w=['gbuf0'])
                kb.act(gbuf[:, 1:TB + 1], ps[:], AF.Copy, scale=P[:, mf:mf + 1], r=[pk, rk_], w=['gbuf'])
                kb.cp('pool', tcarry[i][:, mf:mf + 1], gbuf[:, TB:TB + 1], r=['gbuf'], w=[f'tcarry{i}'])
                kb.stt(out[:], ps[:], omm[i][:, mf:mf + 1], gbuf[:, 0:TB], ALU.mult, ALU.add,
                       r=[pk, f'omm{i}', 'gbuf', 'gbuf0'], w=[outk])

            for k2 in range(2):
                ps, pk = proj_chunk(winev[i, 16 + k2], hT, 'hT')
                shift(ps, pk, 12 + k2, fl[k2], flk[k2])
            kb.act(tw[:], fl[0][0:64, :], AF.Tanh, r=[flk[0]], w=['tw'])
            kb.cp('pool', xab[64:128, :], fl[0][64:128, :], r=[flk[0]], w=['xab'])
            kb.act(sg[:], fl[1][:], AF.Sigmoid, r=[flk[1]], w=['sg'])
            for m in range(4):
                mc = slice(m * 128, (m + 1) * 128)
                kb.mm(psC[:], wup_b[i][0:64, mc], tw[:], r=[f'wup{i}', 'tw'], w=['psC'])
                kb.act(ldm[:], psC[:], AF.Sigmoid, bias=P[:, 14 + m:15 + m], r=['psC', rk_], w=[kld])
                kb.ts('pool', ldm[:], ldm[:], -RW_DS, ALU.mult, r=[kld], w=[kld])
                kb.mm(psC[:], aup_b[i][64:128, mc], xab[64:128, :], r=[f'aup{i}', 'xab'], w=['psC'])
                kb.act(aam[:], psC[:], AF.Sigmoid, bias=P[:, 18 + m:19 + m], r=['psC', rk_], w=[kaa])
                kb.mm(psC[:], gup_b[i][:, mc], sg[:], r=[f'gup{i}', 'sg'], w=['psC'])
                kb.tt('dve', gz[:, m, :], psC[:], zs[:, 4 + m, :], ALU.mult, r=['psC', 'zs'], w=['gz'])
                ps, pk = proj_chunk(winev[i, 4 + m], hT, 'hT')
                shift(ps, pk, m, rf, krf)
                ps, pk = proj_chunk(winev[i, 8 + m], hT, 'hT')
                shift(ps, pk, 4 + m, kf, kkf)
                ps, pk = proj_chunk(winev[i, 12 + m], hT, 'hT')
                shift(ps, pk, 8 + m, vf, kvf)
                kb.act(sqb[0][:], kf[:], AF.Square, scale=P[:, 22 + m:23 + m], r=[kkf, rk_], w=['sqb0'])
                kb.mm(psC[:], bonesb[:], sqb[0][:], r=['bonesb', 'sqb0'], w=['psC'])
                kb.act(t1[:], psC[:], AF.Ln, bias=EPS, r=['psC'], w=[kt1])
                kb.act(t1[:], t1[:], AF.Exp, scale=-0.5, r=[kt1], w=[kt1])
                kb.stt(kk[:], kf[:], P[:, 22 + m:23 + m], t1[:], ALU.mult, ALU.mult, r=[kkf, rk_, kt1], w=[kkk])
                kb.ts('pool', t2[:], aam[:], -1.0, ALU.add, P[:, 26 + m:27 + m], ALU.mult, r=[kaa, rk_], w=[kt2])
                kb.stt(kp[:], t2[:], 1.0, kf[:], ALU.add, ALU.mult, r=[kt2, kkf], w=[kkp])
                kb.tt(POOLE, bb_[:], kk[:], aam[:], ALU.mult, r=[kkk, kaa], w=[kbb])
                kb.stt(sqb[1][:], rf[:], P[:, 30 + m:31 + m], kp[:], ALU.mult, ALU.mult, r=[krf, rk_, kkp], w=['sqb1'])
                kb.mm(psC[:], bonesb[:], sqb[1][:], r=['bonesb', 'sqb1'], w=['psC'])
                kb.tt('dve', bonus[:, m, :], psC[:], vf[:], ALU.mult, r=['psC', kvf], w=['bonus'])
                kb.cp('act', vbT[:, m, :], vf[:], r=[kvf], w=['vbT'])
                kb.op('dve', lambda E: E.tensor_tensor_scan(out=lg[:], data0=cmask, data1=ldm[:], initial=0.0,
                                                            op0=ALU.mult, op1=ALU.add), r=['cst', kld], w=[klg])
                kb.tt(POOLE, lgx[:], lg[:], ldm[:], ALU.subtract, r=[klg, kld], w=[klgx])
                lg3 = lg[:].rearrange("p (j c) -> p j c", c=C)
                kb.act(t1[:], lg[:], AF.Exp, r=[klg], w=[kt1])
                kb.tt('dve', ops6['Qt'][:, m, :], rf[:], t1[:], ALU.mult, r=[krf, kt1], w=['op_Qt'])
                kb.act(t1[:], lgx[:], AF.Exp, r=[klgx], w=[kt1])
                kb.tt('dve', ops6['At'][:, m, :], kk[:], t1[:], ALU.mult, r=[kkk, kt1], w=['op_At'])
                kb.act(t1[:], lg[:], AF.Exp, scale=-1.0, r=[klg], w=[kt1])
                kb.tt('dve', ops6['Kh'][:, m, :], kp[:], t1[:], ALU.mult, r=[kkp, kt1], w=['op_Kh'])
                kb.tt(POOLE, ops6['Bh'][:, m, :], bb_[:], t1[:], ALU.mult, r=[kbb, kt1], w=['op_Bh'])
                kb.tt('dve', t2[:].rearrange("p (j c) -> p j c", c=C), bc(lg3[:, :, C - 1:C], [128, NCH, C]), lg3,
                      ALU.subtract, r=[klg], w=[kt2])
                kb.act(t2[:], t2[:], AF.Exp, r=[kt2], w=[kt2])
                kb.tt('dve', ops6['Kb'][:, m, :], kp[:], t2[:], ALU.mult, r=[kkp, kt2], w=['op_Kb'])
                kb.tt(POOLE, ops6['Bb'][:, m, :], bb_[:], t2[:], ALU.mult, r=[kbb, kt2], w=['op_Bb'])
                for par in range(2):
                    kb.act(GC[:, :, 2 * m + par], lg3[par * 64:(par + 1) * 64, :, C - 1], AF.Exp, r=[klg], w=['GC'])

            H = Hst[i]
            Hbf = Hb[i]

            def rwkv_stream(g):
                T = RG[g]
                HN = T['HN']
                POOLE = PE_OF[g]
                K = lambda n: f'r{n}_{g}'
                pA, pB, pC, pT = T['pA'], T['pB'], T['pC'], T['pT']
                kA, kB, kC, kT = T['kA'], T['kB'], T['kC'], T['kT']
                W = HN * 64
                hs = slice(g * HN, (g + 1) * HN)
                NM = HN // 2
                ms = slice(NM * g, NM * g + NM)
                f2 = lambda t: t[:].rearrange("p h c -> p (h c)")
                m3 = lambda mk: bc(mk.unsqueeze(1), [64, HN, 64])
                p3 = lambda ap: ap.rearrange("p (h c) -> p h c", c=64)
                opo = T['opo']
                Pf, Accb, Mav, Mqk, MqbN = T['Pf'], T['Accb'], T['Mav'], T['Mqk'], T['MqbN']
                Vt, Kbt, BbtN, RHSb, Pb2, tmpR, tmpO = (T[n] for n in ['Vt', 'Kbt', 'BbtN', 'RHSb', 'Pb2', 'tmpR', 'tmpO'])
                oT, oc, osq, s8, s8b, onb, t_o = (T[n] for n in ['oT', 'oc', 'osq', 's8', 's8b', 'onb', 't_o'])
                kH, kHb = f'H{i}_{g}', f'Hb{i}_{g}'
                for j in range(NCH):
                    cs = slice(j * C, (j + 1) * C)
                    for n in ['At', 'Qt', 'Kh', 'Bh']:
                        kb.cp('dve', opo[n][:], ops6[n][64:128, ms, cs], r=['op_' + n], w=[K('opo_' + n)])

                    def X(n, hl):
                        h = g * HN + hl
                        return ops6[n][0:64, h // 2, cs] if h % 2 == 0 else opo[n][:, hl // 2, :]
                    xk = lambda *ns: [k for n in ns for k in ('op_' + n, K('opo_' + n))]
                    for h in range(HN):
                        kb.mm(pA[0:64, h * 64:(h + 1) * 64], X('Bh', h), X('At', h), r=xk('Bh', 'At'), w=kA)
                    for h in range(HN):
                        kb.mm(pA[0:64, W + h * 64:W + (h + 1) * 64], X('Kh', h), X('At', h), r=xk('Kh', 'At'), w=kA)
                    for h in range(HN):
                        kb.mm(pB[0:64, h * 64:(h + 1) * 64], X('Kh', h), X('Qt', h), r=xk('Kh', 'Qt'), w=kB)
                    for h in range(HN):
                        kb.mm(pB[0:64, W + h * 64:W + (h + 1) * 64], X('Bh', h), X('Qt', h), r=xk('Bh', 'Qt'), w=kB)
                    kb.tt('dve', Pf[:], p3(pA[0:64, 0:W]), m3(MsN), ALU.mult, r=kA + ['cst'], w=[f'Pf_{g}'])
                    kb.tt('dve', Mav[:], p3(pA[0:64, W:2 * W]), m3(Ms), ALU.mult, r=kA + ['cst'], w=[K('Mav')])
                    kb.tt('dve', Mqk[:], p3(pB[0:64, 0:W]), m3(Mi), ALU.mult, r=kB + ['cst'], w=[K('Mqk')])
                    kb.tt('dve', MqbN[:], p3(pB[0:64, W:2 * W]), m3(MiN), ALU.mult, r=kB + ['cst'], w=[K('MqbN')])
                    neumann(T, g)
                    kAcc = f'Accb_{g}'
                    for ml in range(NM):
                        kb.tr(pT[0:64, ml * 128:(ml + 1) * 128], vbT[:, NM * g + ml, cs], identb[:], r=['vbT', 'identb'], w=kT)
                    kb.cp('act', Vt[:], pT[0:64, 0:W], r=kT, w=[K('Vt')])
                    for h in range(HN):
                        kb.mm(pC[0:64, h * 64:(h + 1) * 64], X('At', h), Hbf[:, g * HN + h, :], r=xk('At') + [kHb], w=kC)
                    kb.cp('act', tmpR[:], pC[0:64, 0:W], r=kC, w=[K('tmpR')])
                    for h in range(HN):
                        kb.mm(pA[0:64, h * 64:(h + 1) * 64], Mav[:, h, :], Vt[:, h * 64:(h + 1) * 64], r=[K('Mav'), K('Vt')], w=kA)
                    kb.tt('dve', RHSb[:], pA[0:64, 0:W], tmpR[:], ALU.add, r=kA + [K('tmpR')], w=[K('RHSb')])
                    for h in range(HN):
                        kb.mm(pC[0:64, h * 64:(h + 1) * 64], Accb[:, h, :], RHSb[:, h * 64:(h + 1) * 64], r=[kAcc, K('RHSb')], w=kC)
                    kb.cp('act', Pb2[:], pC[0:64, 0:W], r=kC, w=[K('Pb2')])
                    for h in range(HN):
                        kb.mm(pB[0:64, h * 64:(h + 1) * 64], X('Qt', h), Hbf[:, g * HN + h, :], r=xk('Qt') + [kHb], w=kB)
                    kb.cp('act', tmpO[:], pB[0:64, 0:W], r=kB, w=[K('tmpO')])
                    for h in range(HN):
                        hsl = slice(h * 64, (h + 1) * 64)
                        kb.mm(pA[0:64, hsl], Mqk[:, h, :], Vt[:, hsl], start=True, stop=False, r=[K('Mqk'), K('Vt')], w=kA)
                        kb.mm(pA[0:64, hsl], MqbN[:, h, :], Pb2[:, hsl], start=False, stop=True, r=[K('MqbN'), K('Pb2')], w=kA)
                    kb.tt('dve', f2(oT), pA[0:64, 0:W], tmpO[:], ALU.add, r=kA + [K('tmpO')], w=[K('oT')])
                    for ml in range(NM):
                        kb.tr(pT[0:64, ml * 128:(ml + 1) * 128], ops6['Kb'][:, NM * g + ml, cs], identb[:], r=['op_Kb', 'identb'], w=kT)
                    kb.cp('act', Kbt[:], pT[0:64, 0:W], r=kT, w=[K('Kbt')])
                    for ml in range(NM):
                        kb.tr(pT[0:64, ml * 128:(ml + 1) * 128], ops6['Bb'][:, NM * g + ml, cs], identb[:], r=['op_Bb', 'identb'], w=kT)
                    kb.ts('dve', BbtN[:], pT[0:64, 0:W], -1.0, ALU.mult, r=kT, w=[K('BbtN')])
                    for h in range(HN):
                        hsl = slice(h * 64, (h + 1) * 64)
                        kb.mm(pC[0:64, hsl], Kbt[:, hsl], Vt[:, hsl], start=True, stop=False, r=[K('Kbt'), K('Vt')], w=kC)
                        kb.mm(pC[0:64, hsl], BbtN[:, hsl], Pb2[:, hsl], start=False, stop=True, r=[K('BbtN'), K('Pb2')], w=kC)
                    kb.tt(POOLE, H[:, hs, :], H[:, hs, :], bc(GC[:, j, hs].unsqueeze(2), [64, HN, 64]), ALU.mult, r=[kH, 'GC'], w=[kH])
                    kb.tt('dve', H[:, hs, :], H[:, hs, :], p3(pC[0:64, 0:W]), ALU.add, r=[kH] + kC, w=[kH])
                    kb.cp('act', Hbf[:, hs, :], H[:, hs, :], r=[kH], w=[kHb])
                    kb.op('dve', lambda E: E.tensor_reduce(out=s8[:], in_=oT[:], axis=AX.X, op=ALU.add), r=[K('oT')], w=[K('s8')])
                    kb.ts('dve', s8[:], s8[:], -1.0 / 64, ALU.mult, r=[K('s8')], w=[K('s8')])
                    kb.tt(POOLE, oc[:], oT[:], bc(s8[:].unsqueeze(2), [64, HN, 64]), ALU.add, r=[K('oT'), K('s8')], w=[K('oc')])
                    kb.tt(POOLE, osq[:], oc[:], oc[:], ALU.mult, r=[K('oc')], w=[K('osq')])
                    kb.op('dve', lambda E: E.tensor_reduce(out=s8b[:], in_=osq[:], axis=AX.X, op=ALU.add), r=[K('osq')], w=[K('s8b')])
                    kb.act(s8b[:], s8b[:], AF.Ln, scale=1.0 / 64, bias=LN_EPS, r=[K('s8b')], w=[K('s8b')])
                    kb.act(s8b[:], s8b[:], AF.Exp, scale=-0.5, r=[K('s8b')], w=[K('s8b')])
                    kb.tt(POOLE, oc[:], oc[:], bc(s8b[:].unsqueeze(2), [64, HN, 64]), ALU.mult, r=[K('oc'), K('s8b')], w=[K('oc')])
                    kb.tt(POOLE, f2(oc), f2(oc), lnw_s[i][:, g * W:(g + 1) * W], ALU.mult, r=[K('oc'), f'lnw{i}'], w=[K('oc')])
                    kb.tt(POOLE, onb[:], f2(oc), lnb_s[i][:, g * W:(g + 1) * W], ALU.add, r=[K('oc'), f'lnb{i}'], w=[K('onb')])
                    for ml in range(NM):
                        kb.tr(pT[:, ml * 64:(ml + 1) * 64], onb[:, ml * 128:(ml + 1) * 128], identb[0:64, 0:64],
                              r=[K('onb'), 'identb'], w=kT)
                    kb.tt('dve', t_o[:], pT[:, 0:NM * 64].rearrange("p (m c) -> p m c", c=C), bonus[:, ms, cs], ALU.add,
                          r=kT + ['bonus'], w=[K('t_o')])
                    kb.tt('dve', ygT[:, 4 + NM * g:4 + NM * g + NM, cs], t_o[:], gz[:, ms, cs], ALU.mult, r=[K('t_o'), 'gz'], w=[f'ygT{g}'])

            kb.run_streams([(lambda g_=g_: rwkv_stream(g_)) for g_ in range(RGN)])

    def s5_phase(l, i, bi):
        STOP = int(os.environ.get('S5STOP', '99'))
        S5PE = os.environ.get('S5POOL', 'dve')
        if STOP <= 0:
            kb.op('pool', lambda E: E.memset(ygT[:, 0:4, :], 0.0), w=['ygT'])
            return
        P = rp[i]
        rk_ = f'rp{i}'
        with ExitStack() as st:
            uT = kb.sb("uT", [128, 4, TB], F32, st)
            uTb = kb.sb("uTb", [128, 4, TB], BF16, st)
            uTb3 = kb.sb("uTb3", [128, 4, TB], BF16, st)
            yT = kb.sb("yT", [128, 4, TB], F32, st)
            ygb = kb.sb("ygb", [128, 4, TB], BF16, st)
            cs1f = kb.sb("cs1f", [128, 4, 8, 64], F32, st)
            cs1b = kb.sb("cs1b", [128, 8, 8, 64], BF16, st)
            cpb = kb.sb("cpb", [128, 8, 64], BF16, st)
            ea = kb.sb("ea", [128, 4, 64], F32, st)
            etm = kb.sb("etm", [128, 4, 64], F32, st)
            et = kb.sb("et", [128, 4, 64], F32, st)
            ch = kb.sb("ch", [128, 4, 64], F32, st)
            cN = kb.sb("cN", [128, 4, 64], F32, st)
            g1 = kb.sb("g1", [128, TB], F32, st)
            g2 = kb.sb("g2", [128, TB], F32, st)
            if STOP < 99 and 'c' not in os.environ.get('S5SKIP', ''):
                kb.op('pool', lambda E: E.memset(yT[:], 0.0), w=['yT'])
                kb.op('pool', lambda E: E.memset(ygb[:], 0.0), w=['ygb'])
                kb.op('pool', lambda E: E.memset(cs1b[:], 0.0), w=['cs1b'])
                kb.op('pool', lambda E: E.memset(cpb[:], 0.0), w=['cpb'])
                kb.op('pool', lambda E: E.memset(cs1f[:], 0.0), w=['cs1f'])
            for q in range(0 if 'd' in os.environ.get('S5SKIP', '') else 4):
                ps, pk = proj_chunk(winev[i, q], hT, 'hT')
                if 'e' not in os.environ.get('S5SKIP', ''):
                    kb.cp('act', uT[:, q, :], ps[:], r=[pk], w=['uT'])
                if 'f' not in os.environ.get('S5SKIP', ''):
                    kb.cp('dve', uTb[:, q, :], uT[:, q, :], r=['uT'], w=['uTb'])
                if 'a' not in os.environ.get('S5SKIP', ''):
                    kb.ts('dve', uTb3[64:128, q, :], uTb[64:128, q, :], cst[64:128, 1218:1219], ALU.mult,
                          r=['uTb', 'cst'], w=['uTb3'])
            banks = [psA[:, 0:512], psA[:, 512:1024], psB[:, 0:512], psB[:, 512:1024]]
            bkeys = ['psA0', 'psA1', 'psB0', 'psB1']
            TTv = [TT_d[k_][i].rearrange("p (q b e) n -> p q b e n", q=4, b=4) for k_ in range(4)]
            BJq = [kb.sb(f"BJq{k_}", [128, 2, 8, 128], BF16, st) for k_ in range(2)]
            CJloq = [kb.sb(f"CJloq{k_}", [128, 4, 9, 32], BF16, st) for k_ in range(2)]
            CJhiq = [kb.sb(f"CJhiq{k_}", [128, 4, 9, 64], BF16, st) for k_ in range(2)]
            TTq = [[kb.sb(f"TTq{k_}_{z_}", [128, 4, 64], F32, st) for k_ in range(4)] for z_ in range(2)]
            carv = s5carry[i][:].rearrange("p (q b e) -> p q b e", q=4, b=4)
            r8v = rho8[i][:].rearrange("p (q b e) -> p q b e", q=4, b=4)
            for q in range(4 if STOP > 1 else 0):
                qb = q % 2
                tk = f'tabq{qb}'
                kb.dma('sp', BJq[qb][:], BJ_d[i][:, q], f'tq{qb}', w=[tk])
                kb.dma('sp', CJloq[qb][:], CJlo_d[i][:, q], f'tq{qb}', w=[tk])
                kb.dma('sp', CJhiq[qb][:], CJhi_d[i][:, q], f'tq{qb}', w=[tk])
                for e in range(2):
                    tke = f'tte{e}'
                    for k_ in range(4):
                        kb.dma('sp', TTq[e][k_][:], TTv[k_][:, q, :, e, :], f'tt{e}', w=[tke])
                    T1v, T2v, T3v, T4v = TTq[e]
                    for b in range(4):
                        pb = slice(32 * b, 32 * b + 32) if b < 3 else slice(64, 128)
                        uv = (uTb if b < 3 else uTb3)[pb, q, :].rearrange("p (n j) -> p j n", j=8)
                        for j in range(8):
                            kb.mm(banks[b][:, j * 64:(j + 1) * 64], BJq[qb][pb, e, j, :], uv[:, j, :],
                                  r=[tk, 'uTb', 'uTb3'], w=[bkeys[b]])
                    if STOP <= 2:
                        continue
                    zA = psA[:, :].rearrange("p (b j n) -> p b j n", b=2, j=8)
                    zB = psB[:, :].rearrange("p (b j n) -> p b j n", b=2, j=8)
                    for (z, zk, bs) in ((zA, ['psA0', 'psA1'], slice(0, 2)), (zB, ['psB0', 'psB1'], slice(2, 4))):
                        kb.cp('dve', cs1f[:, bs, 0, :], z[:, :, 0, :], r=zk, w=['cs1f'])
                        for j in range(1, 8):
                            kb.tt('dve', cs1f[:, bs, j, :], z[:, :, j, :], cs1f[:, bs, j - 1, :], ALU.add,
                                  r=zk + ['cs1f'], w=['cs1f'])
                    cbv = cs1b[:].rearrange("p (b e) j n -> p b e j n", e=2)
                    kb.cp('act', cbv[:, :, e, :, :], cs1f[:], r=['cs1f'], w=['cs1b'])
                    if STOP <= 3:
                        continue
                    x = cs1f[:, :, 7, :]
                    kb.tt(S5PE, ea[:], x, T1v[:], ALU.mult, r=['cs1f', tke], w=['ea'])
                    kb.tt('dve', etm[0:64], cs1f[64:128, :, 7, :], T2v[64:128], ALU.mult, r=['cs1f', tke], w=['etm'])
                    kb.tt('dve', etm[64:128], cs1f[0:64, :, 7, :], T2v[0:64], ALU.mult, r=['cs1f', tke], w=['etm'])
                    kb.tt(S5PE, et[:], ea[:], etm[:], ALU.add, r=['ea', 'etm'], w=['et'])
                    for b in range(4):
                        kb.op('dve', lambda E, b=b: E.tensor_tensor_scan(
                            out=ch[:, b, :], data0=r8v[:, q, b, e:e + 1].to_broadcast([128, 64]), data1=et[:, b, :],
                            initial=carv[:, q, b, e:e + 1], op0=ALU.mult, op1=ALU.add), r=['et', f's5c{i}'], w=['ch'])
                    kb.tt(S5PE, ea[:], ch[:], T3v[:], ALU.mult, r=['ch', tke], w=['ea'])
                    kb.tt('dve', etm[0:64], ch[64:128], T4v[64:128], ALU.mult, r=['ch', tke], w=['etm'])
                    kb.tt('dve', etm[64:128], ch[0:64], T4v[0:64], ALU.mult, r=['ch', tke], w=['etm'])
                    kb.tt(S5PE, cN[:], ea[:], etm[:], ALU.add, r=['ea', 'etm'], w=['cN'])
                    cpv = cpb[:].rearrange("p (b e) n -> p b e n", e=2)
                    kb.cp('dve', cpv[:, :, e, 0:1], carv[:, q, :, e:e + 1], r=[f's5c{i}'], w=['cpb'])
                    kb.cp('act', cpv[:, :, e, 1:64], cN[:, :, 0:63], r=['cN'], w=['cpb'])
                    kb.cp('dve', carv[:, q, :, e:e + 1], cN[:, :, 63:64], r=['cN'], w=[f's5c{i}'])
                if STOP <= 4:
                    continue
                for j in range(8):
                    jc = slice(j * 64, (j + 1) * 64)
                    for b in range(2):
                        for e in range(2):
                            gl = 2 * b + e
                            o_ = psC[32 * b:32 * b + 32, jc]
                            kb.mm(o_, CJloq[qb][:, gl, j, :], cs1b[:, gl, j, :], start=(e == 0), stop=False,
                                  r=[tk, 'cs1b'], w=['psC'])
                            kb.mm(o_, CJloq[qb][:, gl, j + 1, :], cpb[:, gl, :], start=False, stop=(e == 1),
                                  r=[tk, 'cpb'], w=['psC'])
                    for gl in range(4, 8):
                        o_ = psC[64:128, jc]
                        kb.mm(o_, CJhiq[qb][:, gl - 4, j, :], cs1b[:, gl, j, :], start=(gl == 4), stop=False,
                              r=[tk, 'cs1b'], w=['psC'])
                        kb.mm(o_, CJhiq[qb][:, gl - 4, j + 1, :], cpb[:, gl, :], start=False, stop=(gl == 7),
                              r=[tk, 'cpb'], w=['psC'])
                yv = yT[:, q, :].rearrange("p (n j) -> p j n", j=8)
                uv32 = uT[:, q, :].rearrange("p (n j) -> p j n", j=8)
                kb.stt(yv, uv32, P[:, 34 + q:35 + q], psC[:, :].rearrange("p (j n) -> p j n", j=8), ALU.mult, ALU.add,
                       r=['uT', rk_, 'psC'], w=['yT'])
                xq = yT[:, q, :]
                kb.tt(S5PE, g1[:], xq, xq, ALU.mult, r=['yT'], w=['g1'])
                kb.ts(S5PE, g1[:], g1[:], 0.044715, ALU.mult, 1.0, ALU.add, r=['g1'], w=['g1'])
                kb.tt(S5PE, g1[:], g1[:], xq, ALU.mult, r=['g1', 'yT'], w=['g1'])
                kb.act(g2[:], g1[:], AF.Sigmoid, scale=1.5957691216057308, r=['g1'], w=['g2'])
                kb.tt('dve', yT[:, q, :], xq, g2[:], ALU.mult, r=['yT', 'g2'], w=['yT'])
                kb.cp('act', ygb[:, q, :], yT[:, q, :], r=['yT'], w=['ygb'])
            for qo in range(0 if 'b' in os.environ.get('S5SKIP', '') else 4):
                for k in range(4):
                    kb.mm(psC[:], gluw_b[i][:, k, qo * 128:(qo + 1) * 128], ygb[:, k, :], start=(k == 0), stop=(k == 3),
                          r=[f'gluw{i}', 'ygb'], w=['psC'])
                kb.act(g2[:], psC[:], AF.Sigmoid, bias=P[:, 38 + qo:39 + qo], r=['psC', rk_], w=['g2'])
                kb.tt('dve', g1[:], yT[:, qo, :], g2[:], ALU.mult, r=['yT', 'g2'], w=['g1'])
                kb.tt('dve', ygT[:, qo, :], g1[:], zs[:, qo, :], ALU.mult, r=['g1', 'zs'], w=['ygT'])

    def even_layer(l, bi):
        i = l // 2
        pre_norm(l)
        for m in range(8):
            ps, pk = proj_chunk(winev[i, 18 + m], hT, 'hT')
            kb.act(zs[:, m, :], ps[:], AF.Silu, r=[pk], w=['zs'])
        if dbg:
            kb.op('pool', lambda E: E.memset(zs[:], 1.0), r=['zs'], w=['zs'])
        s5_phase(l, i, bi)
        if not os.environ.get('RWSKIP'):
            rwkv_phase(l, i, bi)
        if not dbg:
            out_proj(l)
        kb.barrier()

    for bi in range(nblocks):
        ts_ = slice(bi * TB, (bi + 1) * TB)
        kb.dma('sp', xres[:], xT[:, ts_].rearrange("(k p) t -> p k t", p=128), 'xin', w=['xres'])
        for l in layers:
            if l % 2 == 1:
                odd_layer(l, bi)
            else:
                even_layer(l, bi)
        with ExitStack() as st:
            obuf = kb.sb("obuf", [128, 8, TB], F32, st)
            if final:
                rms_stats(xres, 'xres')
                for k in range(8):
                    kb.stt(obuf[:, k, :], xres[:, k, :], fnw_s[:, k:k + 1], rstd[:], ALU.mult, ALU.mult,
                           r=['xres', 'fnw_s', 'rstd'], w=['obuf'])
            elif dbg:
                kb.cp('dve', obuf[:], ygT[:], r=['ygT', 'ygT0', 'ygT1'], w=['obuf'])
            else:
                kb.cp('dve', obuf[:], xres[:], r=['xres'], w=['obuf'])
            kb.dma('sp', outT[:, ts_].rearrange("(k p) t -> p k t", p=128), obuf[:], 'xout', r=['obuf'])
            kb.barrier()
    kb.es.close()
    return nc, kb


def chunkify(W, cols):
    Wc = W[:, cols]
    n_m = Wc.shape[1] // 128
    return np.ascontiguousarray(Wc.reshape(8, 128, n_m, 128).transpose(2, 1, 0, 3))


def prep_shared(inp):
    sh = {}
    sh["normw"] = np.ascontiguousarray(inp["norm_w"].reshape(4, 8, 128).transpose(2, 0, 1).reshape(128, 32))
    sh["adaw"] = np.ascontiguousarray(inp["ada_w"])
    sh["adab"] = np.ascontiguousarray(inp["ada_b"].reshape(4, 24, 128).transpose(2, 0, 1).reshape(128, 96))
    sh["woutr"] = np.stack([chunkify(inp["w_out"][l], np.arange(1024)) for l in range(4)])
    sh["fnw"] = np.ascontiguousarray(inp["final_norm_w"].reshape(8, 128).T)
    sh["consts"] = make_consts()
    cols = np.concatenate([np.arange(0, 3072), np.arange(3088, 4112)])
    sh["winodd"] = np.stack([chunkify(inp["odd_w_in"][i], cols) for i in range(2)])
    sh["wba"] = np.ascontiguousarray(
        np.stack([inp["odd_w_in"][i][:, 3072:3088].reshape(8, 128, 16).transpose(1, 0, 2) for i in range(2)]))
    sh["convw"] = np.ascontiguousarray(
        np.stack([inp["gdn_conv_w"][i].reshape(4, 24, 128).transpose(2, 1, 0) for i in range(2)]))
    sh["alog"] = np.ascontiguousarray(np.broadcast_to(inp["gdn_a_log"][:, None, :], (2, 128, 8)))
    sh["dtb"] = np.ascontiguousarray(np.broadcast_to(inp["gdn_dt_bias"][:, None, :], (2, 128, 8)))
    sh["gnw"] = np.ascontiguousarray(np.broadcast_to(inp["gdn_norm_w"][:, None, :], (2, 128, 128)))
    sh["winev"] = np.stack([chunkify(inp["even_w_in"][i], np.arange(3328)) for i in range(2)])
    pc = lambda v, n: v.reshape(n, 128).T
    sh["rp"] = np.stack([np.concatenate([pc(inp["rwkv_mu"][i], 14), pc(inp["rwkv_w0"][i], 4), pc(inp["rwkv_a0"][i], 4),
                                         pc(inp["rwkv_k_k"][i], 4), pc(inp["rwkv_k_a"][i], 4), pc(inp["rwkv_r_k"][i], 4),
                                         pc(inp["s5_d"][i], 4), pc(inp["s5_glu_b"][i], 4)], axis=1) for i in range(2)])
    sh["wup"] = inp["rwkv_w_up"]
    sh["aup"] = inp["rwkv_a_up"]
    sh["gup"] = inp["rwkv_g_up"]
    sh["lnw"] = np.broadcast_to(inp["rwkv_ln_w"][:, None, :], (2, 64, 512))
    sh["lnb"] = np.broadcast_to(inp["rwkv_ln_b"][:, None, :], (2, 64, 512))
    s5a, s5b = [], []
    p_ = np.arange(128)
    for i in range(2):
        rep = lambda a: np.concatenate([a, a], axis=0)
        lre2 = rep(inp["s5_lambda_re"][i].T)
        lim2 = rep(inp["s5_lambda_im"][i].T)
        lst2 = np.broadcast_to(inp["s5_log_step"][i][None, :], (128, 32))
        crT = rep(inp["s5_c_re"][i].transpose(2, 0, 1)).reshape(128, 512)
        ciT = rep(inp["s5_c_im"][i].transpose(2, 0, 1)).reshape(128, 512)
        s5a.append(np.concatenate([lre2, lim2, lst2, crT, ciT], axis=1))
        g = 8 * np.arange(4)[None, :] + 2 * (p_ // 32)[:, None] + ((p_ % 32) // 16)[:, None]
        cp = (p_ % 16)[:, None]
        lreB = inp["s5_lambda_re"][i][g]
        limB = inp["s5_lambda_im"][i][g]
        breB = inp["s5_b_re"][i][g, :, cp]
        bimB = inp["s5_b_im"][i][g, :, cp]
        lstB = inp["s5_log_step"][i][g]
        s5b.append(np.concatenate([lreB.reshape(128, 256), limB.reshape(128, 256), breB.reshape(128, 256),
                                   bimB.reshape(128, 256), lstB], axis=1))
    sh["s5a"] = np.stack(s5a)
    sh["s5b"] = np.stack(s5b)
    sh["gluw"] = np.stack([inp["s5_glu_w"][i].reshape(4, 128, 512).transpose(1, 0, 2) for i in range(2)])
    return {k: np.ascontiguousarray(np.asarray(v, np.float32)) for k, v in sh.items()}


_PLAN = [([0, 1, 2, 3], True)]


def kernel(**inputs):
    inp = {k: np.asarray(v) for k, v in inputs.items()}
    nb = inp["x"].shape[0]
    sh = prep_shared(inp)
    xT = [np.ascontiguousarray(inp["x"][b].T) for b in range(nb)]
    cTs = [np.ascontiguousarray(inp["c"][b].reshape(8, 128).T) for b in range(nb)]
    for layers, final in _PLAN:
        nc, _ = build(layers, final)
        in_maps = [dict(sh, xT=xT[b], cT=cTs[b]) for b in range(nb)]
        res = run_bass_kernel_spmd(nc, in_maps, core_ids=list(range(nb)))
        xT = [np.asarray(res.results[b]["outT"]) for b in range(nb)]
    return np.stack([x.T for x in xT]).astype(np.float32)
```

```python
import os
import numpy as np
from contextlib import ExitStack
import concourse.bass as bass
import concourse.mybir as mybir
from concourse.bass_utils import run_bass_kernel_spmd

F32 = mybir.dt.float32
BF16 = mybir.dt.bfloat16
AF = mybir.ActivationFunctionType
ALU = mybir.AluOpType
AX = mybir.AxisListType

D = 1024
L = 4096
TB = 512
NB = L // TB
C = 64
NCH = TB // C
EPS = 1e-6


class KB:
    def __init__(self):
        self.nc = bass.Bass("TRN2", target_bir_lowering=False)
        self.es = ExitStack()
        nc = self.nc
        self.eng = {'pe': nc.tensor, 'dve': nc.vector, 'act': nc.scalar, 'pool': nc.gpsimd, 'sp': nc.sync}
        self.sem = {e: self.es.enter_context(nc.semaphore("sem_" + e)) for e in self.eng}
        self.cnt = {e: 0 for e in self.eng}
        self.clock = {e: {} for e in self.eng}
        self.lastw = {}
        self.readers = {}
        self.dsem = {}
        self.nins = 0

    def sb(self, name, shape, dt, stack=None):
        self.minrem = min(getattr(self, 'minrem', 1 << 30), self.nc.sbuf_bytes_remaining)
        if stack is not None:
            self.uid = getattr(self, 'uid', 0) + 1
            name = f"{name}_u{self.uid}"
        return (stack or self.es).enter_context(self.nc.sbuf_tensor(name, shape, dt))

    def ps(self, name, shape, dt, stack=None):
        return (stack or self.es).enter_context(self.nc.psum_tensor(name, shape, dt))

    def _sync(self, e, reads, writes):
        need = {}

        def add(ev):
            if ev is None:
                return
            k, h, v = ev
            if k not in need or need[k][1] < v:
                need[k] = (h, v)

        own = 'sem_' + e
        for r in reads:
            add(self.lastw.get(r))
        raw_own = need.get(own)
        for w in writes:
            add(self.lastw.get(w))
            for ev in self.readers.get(w, {}).values():
                add(ev)
        if own in need:
            if raw_own is None:
                del need[own]
            else:
                need[own] = raw_own
        ck = self.clock[e]
        for k, (h, v) in need.items():
            if e == 'pe' and k == 'sem_pe':
                continue
            if ck.get(k, 0) >= v:
                continue
            self.eng[e].wait_ge(h, v)
            ck[k] = v

    def _mark(self, ev, reads, writes):
        for r in reads:
            self.readers.setdefault(r, {})[ev[0]] = ev
        for w in writes:
            self.lastw[w] = ev
            self.readers[w] = {}

    def op(self, e, fn, r=(), w=()):
        reads, writes = r, list(w) + [k for k in r if k.startswith('ps')]
        if getattr(self, '_yp', None) is not None:
            tl = self._tl
            last = getattr(tl, 'last', None)
            if e == 'pe' and last is not None and last != 'pe' and not getattr(self, '_hold', False):
                self._yp()
            tl.last = e
        self._sync(e, reads, writes)
        ins = fn(self.eng[e])
        self.cnt[e] += 1
        self.nins += 1
        ins.then_inc(self.sem[e], 1)
        self._mark(('sem_' + e, self.sem[e], self.cnt[e]), reads, writes)
        return ins

    def dma(self, q, out, in_, slot, r=(), w=()):
        reads, writes = r, w
        self._sync(q, reads, writes)
        if slot not in self.dsem:
            self.dsem[slot] = [self.es.enter_context(self.nc.semaphore("d_" + slot)), 0]
        s = self.dsem[slot]
        s[1] += 16
        self.eng[q].dma_start(out=out, in_=in_).then_inc(s[0], 16)
        self.nins += 1
        self._mark(('d_' + slot, s[0], s[1]), reads, writes)

    def run_streams(self, bodies):
        import threading
        n = len(bodies)
        sems = [threading.Semaphore(0) for _ in range(n)]
        alive = [True] * n
        done = threading.Semaphore(0)
        errs = []
        tl = threading.local()

        def nxt(i):
            for d in range(1, n + 1):
                k = (i + d) % n
                if alive[k]:
                    return k
            return None

        def yp():
            i = tl.idx
            k = nxt(i)
            if k is None or k == i:
                return
            sems[k].release()
            sems[i].acquire()

        def runner(i):
            tl.idx = i
            sems[i].acquire()
            try:
                bodies[i]()
            except BaseException as e:
                errs.append(e)
            alive[i] = False
            k = nxt(i)
            if k is None:
                done.release()
            else:
                sems[k].release()

        self._yp = yp
        self._tl = tl
        ths = [threading.Thread(target=runner, args=(i,)) for i in range(n)]
        for t in ths:
            t.start()
        sems[0].release()
        done.acquire()
        for t in ths:
            t.join()
        self._yp = None
        if errs:
            raise errs[0]

    def barrier(self):
        for e in self.eng:
            for e2 in self.eng:
                if e2 != e and self.cnt[e2] > self.clock[e].get('sem_' + e2, 0):
                    self.eng[e].wait_ge(self.sem[e2], self.cnt[e2])
                    self.clock[e]['sem_' + e2] = self.cnt[e2]
            for k, (h, v) in self.dsem.items():
                if v > self.clock[e].get('d_' + k, 0):
                    self.eng[e].wait_ge(h, v)
                    self.clock[e]['d_' + k] = v
        self.lastw = {}
        self.readers = {}

    def mm(self, out, lhsT, rhs, start=True, stop=True, r=(), w=()):
        self._hold = not stop
        return self.op('pe', lambda E: E.matmul(out, lhsT=lhsT, rhs=rhs, start=start, stop=stop), r, w)

    def tr(self, out, in_, ident, r=(), w=()):
        return self.op('pe', lambda E: E.transpose(out, in_, ident), r, w)

    def act(self, out, in_, func, scale=None, bias=None, r=(), w=(), e='act'):
        kw = {}
        if scale is not None:
            kw['scale'] = scale
        if bias is not None:
            kw['bias'] = bias
        return self.op('act', lambda E: E.activation(out=out, in_=in_, func=func, **kw), r, w)

    def tt(self, e, out, in0, in1, op, r=(), w=()):
        return self.op(e, lambda E: E.tensor_tensor(out=out, in0=in0, in1=in1, op=op), r, w)

    def ts(self, e, out, in0, s1, op0, s2=None, op1=None, r=(), w=()):
        if op1 is None:
            return self.op(e, lambda E: E.tensor_scalar(out=out, in0=in0, scalar1=s1, scalar2=None, op0=op0), r, w)
        return self.op(e, lambda E: E.tensor_scalar(out=out, in0=in0, scalar1=s1, scalar2=s2, op0=op0, op1=op1), r, w)

    def stt(self, out, in0, scalar, in1, op0, op1, r=(), w=()):
        return self.op('dve', lambda E: E.scalar_tensor_tensor(out=out, in0=in0, scalar=scalar, in1=in1, op0=op0, op1=op1), r, w)

    def cp(self, e, out, in_, r=(), w=()):
        if e == 'act':
            return self.op('act', lambda E: E.copy(out=out, in_=in_), r, w)
        return self.op(e, lambda E: E.tensor_copy(out=out, in_=in_), r, w)


def bc(ap, shape):
    return ap.to_broadcast(shape)


CB = {'ident': (0, 128), 'U': (128, 64), 'Lst': (192, 64), 'MsN': (256, 64), 'Mi': (320, 64), 'I64': (384, 64),
      'Ms': (448, 64), 'MiN': (512, 64), 'bones': (576, 128), 'cmask': (704, 512), 'pmask': (1216, 2)}
NCONST = 1219


def make_consts():
    i = np.arange(64)
    blob = np.zeros((128, NCONST), np.float32)
    blob[:, 0:128] = np.eye(128, dtype=np.float32)
    m = {}
    m['U'] = (i[:, None] <= i[None, :]).astype(np.float32)
    m['Lst'] = (i[:, None] > i[None, :]).astype(np.float32)
    m['MsN'] = -(i[:, None] < i[None, :]).astype(np.float32)
    m['Mi'] = (i[:, None] <= i[None, :]).astype(np.float32)
    m['I64'] = np.eye(64, dtype=np.float32)
    m['Ms'] = -m['MsN']
    m['MiN'] = -m['Mi']
    for k, v in m.items():
        blob[0:64, CB[k][0]:CB[k][0] + 64] = v
    bo = np.zeros((128, 128), np.float32)
    bo[0:64, 0:64] = 1.0
    bo[64:128, 64:128] = 1.0
    blob[:, 576:704] = bo
    cm = np.ones(512, np.float32)
    cm[0::64] = 0.0
    blob[:, 704:1216] = cm[None, :]
    p = np.arange(128)
    blob[:, 1216] = ((p % 32) // 16 == 0)
    blob[:, 1217] = ((p % 32) // 16 == 1)
    blob[:, 1218] = (p >= 96)
    return blob


def build(layers, final, nblocks=NB, dbg=None):
    kb = KB()
    nc = kb.nc
    n_odd = 2
    dr = {}

    def din(name, shape, dt=F32):
        dr[name] = nc.dram_tensor(name, list(shape), dt, kind="ExternalInput").ap()
        return dr[name]

    xT = din("xT", [D, L])
    cT = din("cT", [128, 8])
    normw = din("normw", [128, 32])
    adaw = din("adaw", [4, D, 3 * D])
    adab = din("adab", [128, 96])
    woutr = din("woutr", [4, 8, 128, 8, 128])
    fnw_d = din("fnw", [128, 8])
    consts_d = din("consts", [128, NCONST])
    winodd = din("winodd", [2, 32, 128, 8, 128])
    wba_d = din("wba", [2, 128, 8, 16])
    convw_d = din("convw", [2, 128, 24, 4])
    alog_d = din("alog", [2, 128, 8])
    dtb_d = din("dtb", [2, 128, 8])
    gnw_d = din("gnw", [2, 128, 128])
    winev = din("winev", [2, 26, 128, 8, 128])
    rp_d = din("rp", [2, 128, 42])
    wup_d = din("wup", [2, 64, 512])
    aup_d = din("aup", [2, 64, 512])
    gup_d = din("gup", [2, 128, 512])
    lnw_d = din("lnw", [2, 64, 512])
    lnb_d = din("lnb", [2, 64, 512])
    gluw_d = din("gluw", [2, 128, 4, 512])
    s5a_d = din("s5a", [2, 128, 96 + 1024])
    s5b_d = din("s5b", [2, 128, 1028])
    outT = nc.dram_tensor("outT", [D, L], F32, kind="ExternalOutput").ap()
    odd_ids = sorted({l // 2 for l in layers if l % 2 == 1})
    even_ids = sorted({l // 2 for l in layers if l % 2 == 0})

    cst = kb.sb("cst", [128, NCONST], F32)
    identb = kb.sb("identb", [128, 128], BF16)
    onesb = kb.sb("onesb", [128, 128], BF16)
    onesf = kb.sb("onesf", [64, 128], F32)
    xres = kb.sb("xres", [128, 8, TB], F32)
    hT = kb.sb("hT", [128, 8, TB], BF16)
    rstd = kb.sb("rstd", [128, TB], F32)
    sqb = [kb.sb(f"sqb{i}", [128, TB], BF16) for i in range(2)]
    tmpf = [kb.sb(f"tmpf{i}", [128, TB], F32) for i in range(2)]
    wbuf = [kb.sb(f"wbuf{i}", [128, 8, 128], BF16) for i in range(4)]
    ygT = kb.sb("ygT", [128, 8, TB], BF16)
    zs = kb.sb("zs", [128, 8, TB], BF16)
    modT = kb.sb("modT", [128, 4, 24], F32)
    s1 = kb.sb("s1", [128, 4, 8], F32)
    normw_s = kb.sb("normw_s", [128, 32], F32)
    adab_s = kb.sb("adab_s", [128, 96], F32)
    fnw_s = kb.sb("fnw_s", [128, 8], F32)
    sc = kb.sb("sc", [128, 8], F32)
    Sst = [kb.sb(f"Sst{i}", [128, 8, 128], F32) if i in odd_ids else None for i in range(n_odd)]
    Sb = [kb.sb(f"Sb{i}", [128, 8, 128], BF16) if i in odd_ids else None for i in range(n_odd)]
    carry = [kb.sb(f"carry{i}", [128, 24, 3], F32) if i in odd_ids else None for i in range(n_odd)]
    wba_s = [kb.sb(f"wba_s{i}", [128, 8, 16], BF16) if i in odd_ids else None for i in range(n_odd)]
    convw_s = [kb.sb(f"convw_s{i}", [128, 24, 4], F32) if i in odd_ids else None for i in range(n_odd)]
    negA = [kb.sb(f"negA{i}", [128, 8], F32) if i in odd_ids else None for i in range(n_odd)]
    dtb_s = [kb.sb(f"dtb_s{i}", [128, 8], F32) if i in odd_ids else None for i in range(n_odd)]
    gnw_s = [kb.sb(f"gnw_s{i}", [128, 128], F32) if i in odd_ids else None for i in range(n_odd)]

    def per_even(name, shape, dt):
        return [kb.sb(f"{name}{i}", shape, dt) if i in even_ids else None for i in range(2)]
    Hst = per_even("Hst", [64, 8, 64], F32)
    Hb = per_even("Hb", [64, 8, 64], BF16)
    tcarry = per_even("tcarry", [128, 14], F32)
    rp = per_even("rp_s", [128, 42], F32)
    omm = per_even("omm", [128, 14], F32)
    wup_b = per_even("wup_b", [64, 512], BF16)
    aup_b = per_even("aup_b", [128, 512], BF16)
    gup_b = per_even("gup_b", [128, 512], BF16)
    lnw_s = per_even("lnw_s", [64, 512], F32)
    lnb_s = per_even("lnb_s", [64, 512], F32)
    gluw_b = per_even("gluw_b", [128, 4, 512], BF16)
    bonesb = kb.sb("bonesb", [128, 128], BF16)
    def scratch(name, shape, dt):
        return [nc.dram_tensor(f"{name}{i}", list(shape), dt, kind="Internal").ap() if i in even_ids else None
                for i in range(2)]
    BJ_d = scratch("BJ_d", [128, 4, 2, 8, 128], BF16)
    CJlo_d = scratch("CJlo_d", [128, 4, 4, 9, 32], BF16)
    CJhi_d = scratch("CJhi_d", [128, 4, 4, 9, 64], BF16)
    TT_d = [scratch(f"TT{k}_d", [128, 32, 64], F32) for k in range(4)]
    rho8 = per_even("rho8", [128, 32], F32)
    s5carry = per_even("s5carry", [128, 32], F32)

    psI = [kb.ps(f"psI{i}", [128, 512], F32) for i in range(2)]
    psA = kb.ps("psA", [128, 1024], F32)
    psB = kb.ps("psB", [128, 1024], F32)
    psC = kb.ps("psC", [128, 512], F32)
    psT = kb.ps("psT", [128, 1024], BF16)

    def cv(k, rows=64):
        return cst[0:rows, CB[k][0]:CB[k][0] + CB[k][1]]
    U, Lst, MsN, Mi, I64, Ms, MiN = (cv(k) for k in ['U', 'Lst', 'MsN', 'Mi', 'I64', 'Ms', 'MiN'])
    cmask = cv('cmask', 128)

    kb.dma('sp', cst[:], consts_d[:, :], 'par', w=['cst'])
    kb.dma('sp', normw_s[:], normw[:, :], 'par', w=['normw_s'])
    kb.dma('sp', adab_s[:], adab[:, :], 'par', w=['adab_s'])
    kb.dma('sp', fnw_s[:], fnw_d[:, :], 'par', w=['fnw_s'])
    kb.dma('sp', sc[:], cT[:, :], 'par', w=['sc'])
    for i in odd_ids:
        kb.dma('pool', wba_s[i][:], wba_d[i], 'parp', w=[f'wba{i}'])
        kb.dma('sp', convw_s[i][:], convw_d[i], 'par', w=[f'convw{i}'])
        kb.dma('sp', negA[i][:], alog_d[i], 'par', w=[f'negA{i}'])
        kb.dma('sp', dtb_s[i][:], dtb_d[i], 'par', w=[f'dtb{i}'])
        kb.dma('sp', gnw_s[i][:], gnw_d[i], 'par', w=[f'gnw{i}'])
    for i in even_ids:
        kb.dma('sp', rp[i][:], rp_d[i], 'par', w=[f'rp{i}'])
        kb.dma('sp', lnw_s[i][:], lnw_d[i], 'par', w=[f'lnw{i}'])
        kb.dma('sp', lnb_s[i][:], lnb_d[i], 'par', w=[f'lnb{i}'])
        kb.dma('pool', wup_b[i][:], wup_d[i], 'parp', w=[f'wup{i}'])
        kb.dma('pool', aup_b[i][64:128, :], aup_d[i], 'parp', w=[f'aup{i}'])
        kb.dma('pool', gup_b[i][:], gup_d[i], 'parp', w=[f'gup{i}'])
        kb.dma('pool', gluw_b[i][:], gluw_d[i], 'parp', w=[f'gluw{i}'])
    kb.barrier()
    kb.cp('dve', identb[:], cst[:, 0:128], r=['cst'], w=['identb'])
    kb.cp('dve', bonesb[:], cst[:, 576:704], r=['cst'], w=['bonesb'])
    for i in even_ids:
        kb.ts('dve', omm[i][:], rp[i][:, 0:14], -1.0, ALU.mult, 1.0, ALU.add, r=[f'rp{i}'], w=[f'omm{i}'])
        kb.op('pool', lambda E, i=i: E.memset(Hst[i][:], 0.0), w=[f'H{i}_0', f'H{i}_1'])
        kb.op('pool', lambda E, i=i: E.memset(Hb[i][:], 0.0), w=[f'Hb{i}_0', f'Hb{i}_1'])
        kb.op('pool', lambda E, i=i: E.memset(tcarry[i][:], 0.0), w=[f'tcarry{i}'])
    kb.op('dve', lambda E: E.memset(onesb[:], 1.0), w=['onesb'])
    kb.op('dve', lambda E: E.memset(onesf[:], 1.0), w=['onesf'])
    for i in odd_ids:
        kb.act(negA[i][:], negA[i][:], AF.Exp, r=[f'negA{i}'], w=[f'negA{i}'])
        kb.ts('dve', negA[i][:], negA[i][:], -1.0, ALU.mult, r=[f'negA{i}'], w=[f'negA{i}'])
        kb.op('pool', lambda E, i=i: E.memset(Sst[i][:], 0.0), w=[f'S{i}_0', f'S{i}_1'])
        kb.op('pool', lambda E, i=i: E.memset(Sb[i][:], 0.0), w=[f'Sb{i}_0', f'Sb{i}_1'])
        kb.op('pool', lambda E, i=i: E.memset(carry[i][:], 0.0), w=[f'carry{i}'])

    def s5_tables(i):
        with ExitStack() as st:
            K_ = ['s5tab']
            sa = kb.sb("s5a_s", [128, 96 + 1024], F32, st)
            kb.dma('sp', sa[:], s5a_d[i], 's5a', w=K_)
            cnt = [0]
            CJlo = {i: kb.sb("CJlo_t", [128, 4, 4, 9, 32], BF16, st)}
            CJhi = {i: kb.sb("CJhi_t", [128, 4, 4, 9, 64], BF16, st)}
            T1 = {i: kb.sb("T1_t", [128, 32, 64], F32, st)}
            T2 = {i: kb.sb("T2_t", [128, 32, 64], F32, st)}
            T3 = {i: kb.sb("T3_t", [128, 32, 64], F32, st)}
            T4 = {i: kb.sb("T4_t", [128, 32, 64], F32, st)}

            cur = [st]

            def T(shape):
                cnt[0] += 1
                return kb.sb(f"s5t{cnt[0]}", shape, F32, cur[0])

            def tt(o, a, b, op, e='dve'):
                kb.tt(e, o, a, b, op, r=K_, w=K_)

            def ts(o, a, s1, op0, s2=None, op1=None):
                kb.ts('dve', o, a, s1, op0, s2, op1, r=K_, w=K_)

            def cmul(orr, oi, ar, ai, br, bi, t1, t2):
                tt(t1, ar, br, ALU.mult)
                tt(t2, ai, bi, ALU.mult)
                tt(orr, t1, t2, ALU.subtract)
                tt(t1, ar, bi, ALU.mult)
                tt(t2, ai, br, ALU.mult)
                tt(oi, t1, t2, ALU.add)

            def lam_bar(lre_in, lim_in, lst_in, shape):
                lre = T(shape); stp = T(shape); ar = T(shape); ai = T(shape)
                cr = T(shape); si = T(shape); t1 = T(shape); t2 = T(shape); rho = T(shape)
                ts(lre[:], lre_in, -1e-4, ALU.min)
                kb.act(stp[:], lst_in, AF.Exp, r=K_, w=K_)
                tt(ar[:], lre[:], stp[:], ALU.mult)
                tt(ai[:], lim_in, stp[:], ALU.mult)
                kb.act(rho[:], ar[:], AF.Exp, r=K_, w=K_)
                kb.act(si[:], ai[:], AF.Sin, scale=1.0 / 16, r=K_, w=K_)
                ts(t1[:], ai[:], 1.0 / 16, ALU.mult, float(np.pi / 2), ALU.add)
                kb.act(cr[:], t1[:], AF.Sin, r=K_, w=K_)
                for _ in range(4):
                    tt(t1[:], cr[:], cr[:], ALU.mult)
                    tt(t2[:], si[:], si[:], ALU.mult)
                    tt(ar[:], cr[:], si[:], ALU.mult)
                    tt(cr[:], t1[:], t2[:], ALU.subtract)
                    ts(si[:], ar[:], 2.0, ALU.mult)
                lr = T(shape); li = T(shape)
                tt(lr[:], cr[:], rho[:], ALU.mult)
                tt(li[:], si[:], rho[:], ALU.mult)
                return lr, li, rho, cr, si, lre

            sh = [128, 32]
            lr, li, rho, ur, ui, _ = lam_bar(sa[:, 0:32], sa[:, 32:64], sa[:, 64:96], sh)
            t1 = T(sh); t2 = T(sh)
            tt(t1[:], rho[:], rho[:], ALU.mult)
            tt(t2[:], t1[:], t1[:], ALU.mult)
            tt(rho8[i][:], t2[:], t2[:], ALU.mult)
            Lr = T([128, 32, 9]); Li = T([128, 32, 9])
            kb.op('dve', lambda E: E.memset(Lr[:, :, 0], 1.0), r=K_, w=K_)
            kb.op('dve', lambda E: E.memset(Li[:, :, 0], 0.0), r=K_, w=K_)
            for j in range(8):
                cmul(Lr[:, :, j + 1], Li[:, :, j + 1], Lr[:, :, j], Li[:, :, j], lr[:], li[:], t1[:], t2[:])
            kb.op('pool', lambda E: E.memset(CJlo[i][:], 0.0), r=K_, w=K_)
            kb.op('pool', lambda E: E.memset(CJhi[i][:], 0.0), r=K_, w=K_)
            Cr = sa[:, 96:96 + 512].rearrange("p (g c) -> p g c", c=16)
            Ci = sa[:, 96 + 512:96 + 1024].rearrange("p (g c) -> p g c", c=16)
            d1 = T([128, 32, 16]); d2 = T([128, 32, 16]); dr = T([128, 32, 16]); di = T([128, 32, 16])
            for j in range(9):
                lrj = bc(Lr[:, :, j:j + 1], [128, 32, 16])
                lij = bc(Li[:, :, j:j + 1], [128, 32, 16])
                tt(d1[:], Cr, lrj, ALU.mult)
                tt(d2[:], Ci, lij, ALU.mult)
                tt(dr[:], d1[:], d2[:], ALU.subtract)
                tt(d1[:], Cr, lij, ALU.mult)
                tt(d2[:], Ci, lrj, ALU.mult)
                tt(di[:], d1[:], d2[:], ALU.add)
                ts(di[:], di[:], -1.0, ALU.mult)
                drv = dr[:].rearrange("p (q gl) c -> p q gl c", gl=8)
                div = di[:].rearrange("p (q gl) c -> p q gl c", gl=8)
                for gl in range(8):
                    if gl < 4:
                        dst = lambda ps_: CJlo[i][ps_, :, gl, j, (gl % 2) * 16:(gl % 2) * 16 + 16]
                    else:
                        dst = lambda ps_: CJhi[i][ps_, :, gl - 4, j, (gl - 4) * 16:(gl - 4) * 16 + 16]
                    kb.cp('dve', dst(slice(0, 64)), drv[0:64, :, gl, :], r=K_, w=K_)
                    kb.cp('dve', dst(slice(64, 128)), div[64:128, :, gl, :], r=K_, w=K_)
            er = T(sh); ei = T(sh)
            kb.cp('dve', er[:], ur[:], r=K_, w=K_)
            kb.cp('dve', ei[:], ui[:], r=K_, w=K_)
            for _ in range(3):
                tt(t1[:], er[:], er[:], ALU.mult)
                tt(t2[:], ei[:], ei[:], ALU.mult)
                tt(ei[:], er[:], ei[:], ALU.mult)
                tt(er[:], t1[:], t2[:], ALU.subtract)
                ts(ei[:], ei[:], 2.0, ALU.mult)
            Pr = T3[i]
            Pi = T([128, 32, 64])
            kb.cp('dve', Pr[:, :, 0], er[:], r=K_, w=K_)
            kb.cp('dve', Pi[:, :, 0], ei[:], r=K_, w=K_)
            p1 = T([128, 32, 32]); p2 = T([128, 32, 32])
            m = 1
            while m < 64:
                br_ = bc(Pr[:, :, m - 1:m], [128, 32, m])
                bi_ = bc(Pi[:, :, m - 1:m], [128, 32, m])
                cmul(Pr[:, :, m:2 * m], Pi[:, :, m:2 * m], Pr[:, :, 0:m], Pi[:, :, 0:m], br_, bi_,
                     p1[:, :, 0:m], p2[:, :, 0:m])
                m *= 2
            l7r = bc(Lr[:, :, 7:8], [128, 32, 64])
            l7i = bc(Li[:, :, 7:8], [128, 32, 64])
            tt(T1[i][:], Pr[:], l7r, ALU.mult)
            tt(T2[i][:], Pi[:], l7i, ALU.mult)
            tt(T1[i][:], T1[i][:], T2[i][:], ALU.add)
            tt(T2[i][:], Pr[:], l7i, ALU.mult)
            tt(T4[i][:], Pi[:], l7r, ALU.mult)
            tt(T2[i][:], T2[i][:], T4[i][:], ALU.subtract)
            ts(T2[i][64:128], T2[i][64:128], -1.0, ALU.mult)
            kb.cp('dve', T4[i][0:64], Pi[0:64], r=K_, w=K_)
            ts(T4[i][64:128], Pi[64:128], -1.0, ALU.mult)
            kb.dma('sp', CJlo_d[i], CJlo[i][:], 'tabst', r=K_)
            kb.dma('sp', CJhi_d[i], CJhi[i][:], 'tabst', r=K_)
            for k_, t_ in enumerate((T1, T2, T3, T4)):
                kb.dma('sp', TT_d[k_][i], t_[i][:], 'tabst', r=K_)
            kb.barrier()
        with ExitStack() as st:
            cnt[0] += 1000
            cur[0] = st
            BJtab = {i: kb.sb("BJ_t", [128, 4, 2, 8, 128], BF16, st)}
            sbb = kb.sb("s5b_s", [128, 1028], F32, st)
            kb.dma('sp', sbb[:], s5b_d[i], 's5b', w=K_)
            shb = [128, 4, 64]
            v = lambda a: sbb[:, a * 256:(a + 1) * 256].rearrange("p (q n) -> p q n", n=64)
            lstb = bc(sbb[:, 1024:1028].unsqueeze(2), shb)
            blr, bli, brho, _, _, blre = lam_bar(v(0), v(1), lstb, shb)
            bt1 = T(shb); bt2 = T(shb); den = T(shb); kr = T(shb); ki = T(shb); ir = T(shb); ii = T(shb)
            nr = T(shb)
            ts(nr[:], blr[:], -1.0, ALU.add)
            tt(bt1[:], blre[:], blre[:], ALU.mult)
            tt(bt2[:], v(1), v(1), ALU.mult)
            tt(den[:], bt1[:], bt2[:], ALU.add)
            kb.op('dve', lambda E: E.reciprocal(out=den[:], in_=den[:]), r=K_, w=K_)
            tt(bt1[:], nr[:], blre[:], ALU.mult)
            tt(bt2[:], bli[:], v(1), ALU.mult)
            tt(kr[:], bt1[:], bt2[:], ALU.add)
            tt(kr[:], kr[:], den[:], ALU.mult)
            tt(bt1[:], bli[:], blre[:], ALU.mult)
            tt(bt2[:], nr[:], v(1), ALU.mult)
            tt(ki[:], bt1[:], bt2[:], ALU.subtract)
            tt(ki[:], ki[:], den[:], ALU.mult)
            tt(bt1[:], brho[:], brho[:], ALU.mult)
            kb.op('dve', lambda E: E.reciprocal(out=bt1[:], in_=bt1[:]), r=K_, w=K_)
            tt(ir[:], blr[:], bt1[:], ALU.mult)
            tt(ii[:], bli[:], bt1[:], ALU.mult)
            ts(ii[:], ii[:], -1.0, ALU.mult)
            gr = T(shb); gi = T(shb); g2r = T(shb); g2i = T(shb); vr = T(shb); vi = T(shb)
            kb.cp('dve', gr[:], kr[:], r=K_, w=K_)
            kb.cp('dve', gi[:], ki[:], r=K_, w=K_)
            pm = cst[:, 1216:1218]
            for j in range(8):
                cmul(vr[:], vi[:], gr[:], gi[:], v(2), v(3), bt1[:], bt2[:])
                for e in range(2):
                    ts(BJtab[i][:, :, e, j, 0:64], vr[:], pm[:, e:e + 1], ALU.mult)
                    ts(BJtab[i][:, :, e, j, 64:128], vi[:], pm[:, e:e + 1], ALU.mult)
                if j < 7:
                    cmul(g2r[:], g2i[:], gr[:], gi[:], ir[:], ii[:], bt1[:], bt2[:])
                    gr, g2r = g2r, gr
                    gi, g2i = g2i, gi
            kb.op('pool', lambda E: E.memset(s5carry[i][:], 0.0), r=K_, w=K_)
            kb.dma('sp', BJ_d[i], BJtab[i][:], 'tabst', r=K_)
            kb.barrier()

    for i in even_ids:
        s5_tables(i)

    kb.act(sc[:], sc[:], AF.Silu, r=['sc'], w=['sc'])
    with ExitStack() as st:
        abuf = [kb.sb(f"abuf{i}", [128, 3 * D], F32, st) for i in range(2)]
        macc = kb.sb("macc", [128, 24], F32, st)
        n = 0
        for l in layers:
            for k in range(8):
                b = abuf[n % 2]
                kb.dma('sp', b[:], adaw[l, k * 128:(k + 1) * 128, :], f'ab{n % 2}', w=[f'abuf{n % 2}'])
                for j in range(24):
                    kb.mm(psC[:, j:j + 1], b[:, j * 128:(j + 1) * 128], sc[:, k:k + 1],
                          r=[f'abuf{n % 2}', 'sc'], w=['psC'])
                if k == 0:
                    kb.tt('dve', macc[:], psC[:, 0:24], adab_s[:, l * 24:(l + 1) * 24], ALU.add,
                          r=['psC', 'adab_s'], w=['macc'])
                else:
                    kb.tt('dve', macc[:], psC[:, 0:24], macc[:], ALU.add, r=['psC', 'macc'], w=['macc'])
                n += 1
            kb.cp('dve', modT[:, l, :], macc[:], r=['macc'], w=['modT'])
            kb.ts('dve', s1[:, l, :], modT[:, l, 8:16], 1.0, ALU.add, r=['modT'], w=['s1'])
            kb.tt('dve', s1[:, l, :], s1[:, l, :], normw_s[:, l * 8:(l + 1) * 8], ALU.mult,
                  r=['s1', 'normw_s'], w=['s1'])
        kb.barrier()

    wctr = [0]

    def load_w(src_ap):
        i = wctr[0] % 4
        wctr[0] += 1
        kb.dma('pool', wbuf[i][:], src_ap, f'w{i}', w=[f'wbuf{i}'])
        return wbuf[i], f'wbuf{i}'

    ictr = [0]

    def proj_chunk(src_ap, rhs_tile, rhs_key):
        wb, wk = load_w(src_ap)
        i = ictr[0] % 2
        ictr[0] += 1
        for k in range(8):
            rk_l = rhs_key if isinstance(rhs_key, list) else [rhs_key]
            kb.mm(psI[i][:], wb[:, k, :], rhs_tile[:, k, :], start=(k == 0), stop=(k == 7),
                  r=[wk] + rk_l, w=[f'psI{i}'])
        return psI[i], f'psI{i}'

    def rms_stats(src, src_key):
        for k in range(8):
            q = sqb[k % 2]
            kb.act(q[:], src[:, k, :], AF.Square, r=[src_key], w=[f'sqb{k % 2}'])
            kb.mm(psC[:, :], onesb[:], q[:], start=(k == 0), stop=(k == 7), r=[f'sqb{k % 2}', 'onesb'], w=['psC'])
        kb.act(rstd[:], psC[:], AF.Ln, scale=1.0 / D, bias=EPS, r=['psC'], w=['rstd'])
        kb.act(rstd[:], rstd[:], AF.Exp, scale=-0.5, r=['rstd'], w=['rstd'])

    def pre_norm(l):
        rms_stats(xres, 'xres')
        for k in range(8):
            t = tmpf[k % 2]
            kb.tt('dve', t[:], xres[:, k, :], rstd[:], ALU.mult, r=['xres', 'rstd'], w=[f'tmpf{k % 2}'])
            kb.act(hT[:, k, :], t[:], AF.Identity, scale=s1[:, l, k:k + 1], bias=modT[:, l, k:k + 1],
                   r=[f'tmpf{k % 2}', 's1', 'modT'], w=['hT'])

    def out_proj(l):
        for dm in range(8):
            ps, pk = proj_chunk(woutr[l, dm], ygT, ['ygT', 'ygT0', 'ygT1'])
            kb.stt(xres[:, dm, :], ps[:], modT[:, l, 16 + dm:17 + dm], xres[:, dm, :], ALU.mult, ALU.add,
                   r=[pk, 'modT', 'xres'], w=['xres'])


    PE_OF = [os.environ.get('POOLTO', 'dve'), os.environ.get('POOLTO1', 'dve')]

    def neumann(T, g):
        HN = T['HN']
        POOLE = PE_OF[g]
        Pf, Accf, Accb, PTf, Pw = T['Pf'], T['Accf'], T['Accb'], T['PTf'], T['Pw']
        pA, pB, pC = T['pA'], T['pB'], T['pC']
        kA, kB, kC = T['kA'], T['kB'], T['kC']
        K = lambda n: f'{n}_{g}'
        W = HN * 64
        f2 = lambda t: t[:].rearrange("p h c -> p (h c)")
        kb.tt(POOLE, Accf[:], Pf[:], bc(I64.unsqueeze(1), [64, HN, 64]), ALU.add, r=[K('Pf'), 'cst'], w=[K('Accf')])
        for h in range(HN):
            kb.tr(pC[0:64, h * 64:(h + 1) * 64], Pf[:, h, :], cst[0:64, 0:64], r=[K('Pf'), 'cst'], w=kC)
        kb.cp('act', f2(PTf), pC[0:64, 0:W], r=kC, w=[K('PTf')])
        cur, curk = Pf, K('Pf')
        oth, othk = Pw, K('Pw')
        for lev in range(5):
            last = (lev == 4)
            if not last and os.environ.get('NEU_TR', '1') == '1':
                for h in range(HN):
                    kb.mm(pA[0:64, h * 64:(h + 1) * 64], PTf[:, h, :], cur[:, h, :], r=[K('PTf'), curk], w=kA)
                kb.cp('act', f2(oth), pA[0:64, 0:W], r=kA, w=[othk])
                for h in range(HN):
                    kb.tr(pB[0:64, h * 64:(h + 1) * 64], oth[:, h, :], cst[0:64, 0:64], r=[othk, 'cst'], w=kB)
            else:
                if not last:
                    for h in range(HN):
                        kb.mm(pA[0:64, h * 64:(h + 1) * 64], PTf[:, h, :], cur[:, h, :], r=[K('PTf'), curk], w=kA)
                for h in range(HN):
                    kb.mm(pB[0:64, h * 64:(h + 1) * 64], cur[:, h, :], PTf[:, h, :], r=[K('PTf'), curk], w=kB)
                if not last:
                    kb.cp('act', f2(oth), pA[0:64, 0:W], r=kA, w=[othk])
            kb.cp('dve', f2(PTf), pB[0:64, 0:W], r=kB, w=[K('PTf')])
            cur, curk, oth, othk = oth, othk, cur, curk
            for h in range(HN):
                kb.mm(pC[0:64, h * 64:(h + 1) * 64], PTf[:, h, :], Accf[:, h, :], r=[K('PTf'), K('Accf')], w=kC)
            kb.tt('dve', f2(Accf), pC[0:64, 0:W], f2(Accf), ALU.add, r=kC + [K('Accf')], w=[K('Accf')])
        kb.cp('act', Accb[:], Accf[:], r=[K('Accf')], w=[K('Accb')])

    def psum_group(g, G=2):
        if G == 1:
            return dict(pA=psA[:, :], pB=psB[:, :], pC=psC[:, :], pT=psT[:, :], HN=8,
                        kA=['psA0', 'psA1'], kB=['psB0', 'psB1'], kC=['psC'], kT=['psT'])
        if g == 0:
            return dict(pA=psA[:, 0:512], pB=psA[:, 512:1024], pC=psC[:, :], pT=psT[:, :], HN=4,
                        kA=['psA0'], kB=['psA1'], kC=['psC'], kT=['psT'])
        return dict(pA=psB[:, 0:512], pB=psB[:, 512:1024], pC=psI[0][:, :], pT=psI[1][:, :].bitcast(BF16), HN=4,
                    kA=['psB0'], kB=['psB1'], kC=['psI0'], kT=['psI1'])

    def odd_layer(l, bi):
        i = l // 2
        with ExitStack() as st:
            acc = kb.sb("acc", [128, 8, TB], F32, st)
            qkn = kb.sb("qkn", [128, 16, TB], BF16, st)
            vb = kb.sb("vb", [128, 8, TB], BF16, st)
            ba = kb.sb("ba", [64, 8, 16], F32, st)
            beta = kb.sb("beta", [64, 8, 8], F32, st)
            aall = kb.sb("aall", [64, 8, 8], F32, st)
            eg = kb.sb("eg", [64, 8, 8], F32, st)
            eend = kb.sb("eend", [64, 8, 8], F32, st)
            cdb = kb.sb("cdb", [128, 8, 8], F32, st)
            TG = []
            GG = int(os.environ.get('GDN_G', '2'))
            HN = 8 // GG
            for g_ in range(GG):
                T_ = psum_group(g_, GG)
                for n_, shp, dt_ in [('R2', [64, HN, 64], F32), ('decT', [64, HN, 64], F32), ('dSb', [64, HN, 64], F32),
                                     ('dI', [64, HN, 64], F32), ('Pf', [64, HN, 64], F32), ('Accf', [64, HN, 64], F32),
                                     ('PTf', [64, HN, 64], F32), ('Pw', [64, HN, 64], F32), ('Accb', [64, HN, 64], BF16),
                                     ('intraT', [64, HN, 64], BF16), ('vk', [64, HN, 256], BF16), ('kend', [64, HN, 128], BF16),
                                     ('uu', [64, HN, 128], F32), ('ww', [64, HN, 128], BF16), ('wTb', [128, HN, 64], BF16),
                                     ('vnew', [64, HN, 128], BF16), ('o1', [64, HN, 128], F32), ('oo', [64, HN, 128], F32),
                                     ('osq', [64, HN, 128], F32), ('ss', [64, HN], F32), ('onb', [64, HN, 128], BF16)]:
                    T_[n_] = kb.sb(f"g{g_}{n_}", shp, dt_, st)
                TG.append(T_)
            cnew = kb.sb("cnew", [128, 24, 3], F32, st)

            pre_norm(l)
            for m in range(8):
                ps, pk = proj_chunk(winodd[i, 24 + m], hT, 'hT')
                kb.act(zs[:, m, :], ps[:], AF.Silu, r=[pk], w=['zs'])
            if dbg:
                kb.op('pool', lambda E: E.memset(zs[:], 1.0), r=['zs'], w=['zs'])
            cw = convw_s[i]
            cr = carry[i]
            for grp in range(3):
                gsl = slice(grp * 8, grp * 8 + 8)
                accall = [f'acc{mm_}' for mm_ in range(8)]
                for mm_ in range(8):
                    m = grp * 8 + mm_
                    ps, pk = proj_chunk(winodd[i, m], hT, 'hT')
                    am = f'acc{mm_}'
                    kb.act(acc[:, mm_, :], ps[:], AF.Copy, scale=cw[:, m, 3:4], r=[pk, f'convw{i}'], w=[am])
                    kb.cp('act', cnew[:, m, :], ps[:, TB - 3:TB], r=[pk], w=['cnew'])
                    for j in range(3):
                        sh = 3 - j
                        kb.stt(acc[:, mm_, sh:TB], ps[:, 0:TB - sh], cw[:, m, j:j + 1], acc[:, mm_, sh:TB],
                               ALU.mult, ALU.add, r=[pk, f'convw{i}', am], w=[am])
                for j in range(3):
                    n = 3 - j
                    tv = tmpf[0][:, 0:8 * n].rearrange("p (m n) -> p m n", n=n)
                    kb.tt('dve', tv, cr[:, gsl, j:3], bc(cw[:, gsl, j:j + 1], [128, 8, n]), ALU.mult,
                          r=[f'carry{i}', f'convw{i}'], w=['tmpf0'])
                    kb.tt('dve', acc[:, :, 0:n], acc[:, :, 0:n], tv, ALU.add, r=['tmpf0'] + accall, w=accall)
                kb.cp('dve', cr[:, gsl, :], cnew[:, gsl, :], r=['cnew'], w=[f'carry{i}'])
                kb.act(acc[:], acc[:], AF.Silu, r=accall, w=accall)
                if grp == 2:
                    kb.cp('pool', vb[:], acc[:], r=accall, w=['vb'])
                    continue
                kb.act(sqb[(grp * 8) % 2][:], acc[:, 0, :], AF.Square, r=['acc0'], w=[f'sqb{(grp * 8) % 2}'])
                for mm_ in range(8):
                    m = grp * 8 + mm_
                    q = sqb[m % 2]
                    if mm_ < 7:
                        kb.act(sqb[(m + 1) % 2][:], acc[:, mm_ + 1, :], AF.Square, r=[f'acc{mm_ + 1}'], w=[f'sqb{(m + 1) % 2}'])
                    kb.mm(psC[:], onesb[:], q[:], r=[f'sqb{m % 2}', 'onesb'], w=['psC'])
                    t = tmpf[m % 2]
                    kb.act(t[:], psC[:], AF.Ln, bias=EPS, r=['psC'], w=[f'tmpf{m % 2}'])
                    kb.act(t[:], t[:], AF.Exp, scale=-0.5, r=[f'tmpf{m % 2}'], w=[f'tmpf{m % 2}'])
                    kb.stt(qkn[:, m, :], acc[:, mm_, :], (128.0 ** -0.5) if m < 8 else 1.0, t[:], ALU.mult, ALU.mult,
                           r=[f'acc{mm_}', f'tmpf{m % 2}'], w=['qkn'])
            for j in range(NCH):
                for k in range(8):
                    kb.mm(psC[0:64, j * 16:(j + 1) * 16], hT[:, k, j * C:(j + 1) * C], wba_s[i][:, k, :],
                          start=(k == 0), stop=(k == 7), r=['hT', f'wba{i}'], w=['psC'])
            kb.cp('dve', ba[:], psC[0:64, 0:128].rearrange("p (j e) -> p j e", e=16), r=['psC'], w=['ba'])
            kb.act(beta[:], ba[:, :, 0:8], AF.Sigmoid, r=['ba'], w=['beta'])
            kb.tt('dve', aall[:], ba[:, :, 8:16], bc(dtb_s[i][0:64, :].unsqueeze(1), [64, 8, 8]), ALU.add,
                  r=['ba', f'dtb{i}'], w=['aall'])
            kb.act(aall[:], aall[:], AF.Exp, r=['aall'], w=['aall'])
            kb.act(aall[:], aall[:], AF.Ln, bias=1.0, r=['aall'], w=['aall'])
            kb.tt('dve', aall[:], aall[:], bc(negA[i][0:64, :].unsqueeze(1), [64, 8, 8]), ALU.mult,
                  r=['aall', f'negA{i}'], w=['aall'])
            a2 = aall[:].rearrange("p j h -> p (j h)")
            kb.mm(psI[0][0:64, 0:64], U, a2, r=['cst', 'aall'], w=['psI0'])
            kb.act(eg[:].rearrange("p j h -> p (j h)"), psI[0][0:64, 0:64], AF.Exp, r=['psI0'], w=['eg'])
            kb.mm(psI[1][0:64, 0:64], Lst, a2, r=['cst', 'aall'], w=['psI1'])
            kb.act(eend[:].rearrange("p j h -> p (j h)"), psI[1][0:64, 0:64], AF.Exp, r=['psI1'], w=['eend'])
            kb.mm(psI[0][:, 64:128], onesf[:], a2, r=['onesf', 'aall'], w=['psI0'])
            kb.act(cdb[:].rearrange("p j h -> p (j h)"), psI[0][:, 64:128], AF.Exp, r=['psI0'], w=['cdb'])

            S = Sst[i]
            Sbf = Sb[i]

            def gdn_stream(g):
                T = TG[g]
                HN = T['HN']
                POOLE = PE_OF[g]
                K = lambda n: f'{n}_{g}'
                pA, pB, pC, pT = T['pA'], T['pB'], T['pC'], T['pT']
                kA, kC, kT = T['kA'], T['kC'], T['kT']
                kBs = T['kB']
                W = HN * 64
                hs = slice(g * HN, (g + 1) * HN)
                f2 = lambda t: t[:].rearrange("p h c -> p (h c)")
                R2, decT, dSb, dI, Pf, intraT, Accb = (T[n] for n in ['R2', 'decT', 'dSb', 'dI', 'Pf', 'intraT', 'Accb'])
                vk, kend, uu, ww, wTb, vnew, o1, oo, osq, ss, onb = (T[n] for n in
                    ['vk', 'kend', 'uu', 'ww', 'wTb', 'vnew', 'o1', 'oo', 'osq', 'ss', 'onb'])
                for j in range(NCH):
                    cs = slice(j * C, (j + 1) * C)
                    be = beta[:, j, hs]
                    kb.tt('dve', R2[:], bc(U.unsqueeze(1), [64, HN, 64]), bc(aall[:, j, hs].unsqueeze(2), [64, HN, 64]),
                          ALU.mult, r=['cst', 'aall'], w=[K('R2')])
                    kb.mm(pC[0:64, 0:W], Lst, f2(R2), r=['cst', K('R2')], w=kC)
                    kb.act(f2(decT), pC[0:64, 0:W], AF.Exp, r=kC, w=[K('decT')])
                    kb.tt(POOLE, dSb[:], decT[:], bc(MsN.unsqueeze(1), [64, HN, 64]), ALU.mult, r=[K('decT'), 'cst'], w=[K('dSb')])
                    kb.tt(POOLE, dSb[:], dSb[:], bc(be.unsqueeze(2), [64, HN, 64]), ALU.mult, r=[K('dSb'), 'beta'], w=[K('dSb')])
                    kb.tt(POOLE, dI[:], decT[:], bc(Mi.unsqueeze(1), [64, HN, 64]), ALU.mult, r=[K('decT'), 'cst'], w=[K('dI')])
                    for h in range(HN):
                        H_ = g * HN + h
                        kb.mm(pA[0:64, h * 64:(h + 1) * 64], qkn[:, 8 + H_, cs], qkn[:, 8 + H_, cs], r=['qkn'], w=kA)
                    for h in range(HN):
                        H_ = g * HN + h
                        kb.mm(pA[0:64, W + h * 64:W + (h + 1) * 64], qkn[:, 8 + H_, cs], qkn[:, H_, cs], r=['qkn'], w=kA)
                    kb.tt('dve', f2(Pf), pA[0:64, 0:W], f2(dSb), ALU.mult, r=kA + [K('dSb')], w=[K('Pf')])
                    kb.tt('dve', f2(intraT), pA[0:64, W:2 * W], f2(dI), ALU.mult, r=kA + [K('dI')], w=[K('intraT')])
                    neumann(T, g)
                    for h in range(HN):
                        kb.tr(pT[0:64, h * 128:(h + 1) * 128], qkn[:, 8 + g * HN + h, cs], identb[:], r=['qkn', 'identb'], w=kT)
                    pT3 = pT[0:64, 0:HN * 128].rearrange("p (h d) -> p h d", d=128)
                    kb.tt('dve', vk[:, :, 128:256], pT3, bc(eg[:, j, hs].unsqueeze(2), [64, HN, 128]), ALU.mult,
                          r=kT + ['eg'], w=[K('vk1')])
                    kb.tt('dve', kend[:], pT3, bc(eend[:, j, hs].unsqueeze(2), [64, HN, 128]), ALU.mult,
                          r=kT + ['eend'], w=[K('kend')])
                    for h in range(HN):
                        kb.tr(pT[0:64, h * 128:(h + 1) * 128], vb[:, g * HN + h, cs], identb[:], r=['vb', 'identb'], w=kT)
                    kb.cp('act', vk[:, :, 0:128], pT3, r=kT, w=[K('vk0')])
                    for h in range(HN):
                        kb.mm(pA[0:64, h * 128:(h + 1) * 128], Accb[:, h, :], vk[:, h, 0:128], r=[K('Accb'), K('vk0')], w=kA)
                    for h in range(HN):
                        kb.mm(pB[0:64, h * 128:(h + 1) * 128], Accb[:, h, :], vk[:, h, 128:256], r=[K('Accb'), K('vk1')], w=kBs)
                    pA3 = pA[0:64, :].rearrange("p (h d) -> p h d", d=128)
                    pB3 = pB[0:64, :].rearrange("p (h d) -> p h d", d=128)
                    bet3 = bc(be.unsqueeze(2), [64, HN, 128])
                    kb.tt('dve', uu[:], pA3, bet3, ALU.mult, r=kA + ['beta'], w=[K('uu')])
                    kb.tt('dve', ww[:], pB3, bet3, ALU.mult, r=kBs + ['beta'], w=[K('ww')])
                    for h in range(HN):
                        kb.tr(pT[:, h * 64:(h + 1) * 64], ww[:, h, :], identb[0:64, 0:64], r=[K('ww'), 'identb'], w=kT)
                    kb.cp('act', f2(wTb), pT[:, 0:W], r=kT, w=[K('wTb')])
                    kS, kSb = f'S{i}_{g}', f'Sb{i}_{g}'
                    for h in range(HN):
                        kb.mm(pA[0:64, h * 128:(h + 1) * 128], wTb[:, h, :], Sbf[:, g * HN + h, :], r=[K('wTb'), kSb], w=kA)
                    kb.tt('dve', vnew[:], uu[:], pA3, ALU.subtract, r=[K('uu')] + kA, w=[K('vnew')])
                    for h in range(HN):
                        kb.mm(pB[0:64, h * 128:(h + 1) * 128], qkn[:, g * HN + h, cs], Sbf[:, g * HN + h, :], r=['qkn', kSb], w=kBs)
                    kb.tt('dve', o1[:], pB3, bc(eg[:, j, hs].unsqueeze(2), [64, HN, 128]), ALU.mult, r=kBs + ['eg'], w=[K('o1')])
                    for h in range(HN):
                        kb.mm(pA[0:64, h * 128:(h + 1) * 128], intraT[:, h, :], vnew[:, h, :], r=[K('intraT'), K('vnew')], w=kA)
                    kb.tt('dve', oo[:], pA3, o1[:], ALU.add, r=kA + [K('o1')], w=[K('oo')])
                    for h in range(HN):
                        kb.mm(pB[:, h * 128:(h + 1) * 128], kend[:, h, :], vnew[:, h, :], r=[K('kend'), K('vnew')], w=kBs)
                    kb.tt(POOLE, S[:, hs, :], S[:, hs, :], bc(cdb[:, j, hs].unsqueeze(2), [128, HN, 128]), ALU.mult,
                          r=[kS, 'cdb'], w=[kS])
                    kb.tt('dve', S[:, hs, :], S[:, hs, :], pB[:, :].rearrange("p (h d) -> p h d", d=128), ALU.add,
                          r=[kS] + kBs, w=[kS])
                    kb.cp('act', Sbf[:, hs, :], S[:, hs, :], r=[kS], w=[kSb])
                    kb.tt(POOLE, osq[:], oo[:], oo[:], ALU.mult, r=[K('oo')], w=[K('osq')])
                    kb.op('dve', lambda E: E.tensor_reduce(out=ss[:], in_=osq[:], axis=AX.X, op=ALU.add), r=[K('osq')], w=[K('ss')])
                    kb.act(ss[:], ss[:], AF.Ln, scale=1.0 / 128, bias=EPS, r=[K('ss')], w=[K('ss')])
                    kb.act(ss[:], ss[:], AF.Exp, scale=-0.5, r=[K('ss')], w=[K('ss')])
                    kb.tt(POOLE, osq[:], oo[:], bc(ss[:].unsqueeze(2), [64, HN, 128]), ALU.mult, r=[K('oo'), K('ss')], w=[K('osq')])
                    kb.tt(POOLE, onb[:], osq[:], bc(gnw_s[i][0:64, :].unsqueeze(1), [64, HN, 128]), ALU.mult,
                          r=[K('osq'), f'gnw{i}'], w=[K('onb')])
                    for h in range(HN):
                        kb.tr(pT[:, h * 64:(h + 1) * 64], onb[:, h, :], identb[0:64, 0:64], r=[K('onb'), 'identb'], w=kT)
                    kb.tt('dve', ygT[:, hs, cs], pT[:, 0:W].rearrange("p (h c) -> p h c", c=64), zs[:, hs, cs], ALU.mult,
                          r=kT + ['zs'], w=[f'ygT{g}'])

            kb.run_streams([(lambda g_=g_: gdn_stream(g_)) for g_ in range(GG)])
            if not dbg:
                out_proj(l)
        kb.barrier()

    RW_DS = float(np.exp(-0.5))
    LN_EPS = 1e-5 * 64

    def rwkv_phase(l, i, bi):
        POOLE = os.environ.get('POOLPREP', 'dve')
        P = rp[i]
        rk_ = f'rp{i}'
        with ExitStack() as st:
            gbuf = kb.sb("gbuf", [128, TB + 1], F32, st)
            tw = kb.sb("tw", [64, TB], BF16, st)
            xab = kb.sb("xab", [128, TB], BF16, st)
            sg = kb.sb("sg", [128, TB], BF16, st)
            gz = kb.sb("gz", [128, 4, TB], BF16, st)
            bonus = kb.sb("bonus", [128, 4, TB], BF16, st)
            vbT = kb.sb("vbT", [128, 4, TB], BF16, st)
            ops6 = {n: kb.sb("op_" + n, [128, 4, TB], BF16, st) for n in ['At', 'Qt', 'Kh', 'Bh', 'Kb', 'Bb']}
            GC = kb.sb("GC", [64, NCH, 8], F32, st)
            rt = [kb.sb(f"rt{k}", [128, TB], F32, st) for k in range(12)]
            rf, kf, vf, ldm, aam, kk, kp, bb_, lg, lgx, t1, t2 = rt
            rtk = [f'rt{k}' for k in range(12)]
            krf, kkf, kvf, kld, kaa, kkk, kkp, kbb, klg, klgx, kt1, kt2 = rtk
            fl = [t1, t2]
            flk = [kt1, kt2]
            RG = []
            RGN = int(os.environ.get('RWKV_G', '2'))
            HN = 8 // RGN
            for g_ in range(RGN):
                T_ = psum_group(g_, RGN)
                for n_, shp, dt_ in [('Pf', [64, HN, 64], F32), ('Accf', [64, HN, 64], F32), ('PTf', [64, HN, 64], F32),
                                     ('Pw', [64, HN, 64], F32), ('Accb', [64, HN, 64], BF16), ('Mav', [64, HN, 64], BF16),
                                     ('Mqk', [64, HN, 64], BF16), ('MqbN', [64, HN, 64], BF16), ('Vt', [64, HN * 64], BF16),
                                     ('Kbt', [64, HN * 64], BF16), ('BbtN', [64, HN * 64], BF16), ('RHSb', [64, HN * 64], BF16),
                                     ('Pb2', [64, HN * 64], BF16), ('tmpR', [64, HN * 64], F32), ('tmpO', [64, HN * 64], F32),
                                     ('oT', [64, HN, 64], F32), ('oc', [64, HN, 64], F32), ('osq', [64, HN, 64], F32),
                                     ('s8', [64, HN], F32), ('s8b', [64, HN], F32), ('onb', [64, HN * 64], BF16),
                                     ('t_o', [128, HN // 2, C], F32)]:
                    T_[n_] = kb.sb(f"r{g_}{n_}", shp, dt_, st)
                T_['opo'] = {n_: kb.sb(f"r{g_}opo_{n_}", [64, HN // 2, C], BF16, st) for n_ in ['At', 'Qt', 'Kh', 'Bh']}
                RG.append(T_)

            def shift(ps, pk, mf, out, outk):
                kb.cp('dve', gbuf[:, 0:1], tcarry[i][:, mf:mf + 1], r=[f'tcarry{i}'], w=['gbuf0'])
                kb.act(gbuf[:, 1:TB + 1], ps[:], AF.Copy, scale=P[:, mf:mf + 1], r=[pk, rk_], w=['gbuf'])
                kb.cp('pool', tcarry[i][:, mf:mf + 1], gbuf[:, TB:TB + 1], r=['gbuf'], w=[f'tcarry{i}'])
                kb.stt(out[:], ps[:], omm[i][:, mf:mf + 1], gbuf[:, 0:TB], ALU.mult, ALU.add,
                       r=[pk, f'omm{i}', 'gbuf', 'gbuf0'], w=[outk])

            for k2 in range(2):
                ps, pk = proj_chunk(winev[i, 16 + k2], hT, 'hT')
                shift(ps, pk, 12 + k2, fl[k2], flk[k2])
            kb.act(tw[:], fl[0][0:64, :], AF.Tanh, r=[flk[0]], w=['tw'])
            kb.cp('pool', xab[64:128, :], fl[0][64:128, :], r=[flk[0]], w=['xab'])
            kb.act(sg[:], fl[1][:], AF.Sigmoid, r=[flk[1]], w=['sg'])
            for m in range(4):
                mc = slice(m * 128, (m + 1) * 128)
                kb.mm(psC[:], wup_b[i][0:64, mc], tw[:], r=[f'wup{i}', 'tw'], w=['psC'])
                kb.act(ldm[:], psC[:], AF.Sigmoid, bias=P[:, 14 + m:15 + m], r=['psC', rk_], w=[kld])
                kb.ts('pool', ldm[:], ldm[:], -RW_DS, ALU.mult, r=[kld], w=[kld])
                kb.mm(psC[:], aup_b[i][64:128, mc], xab[64:128, :], r=[f'aup{i}', 'xab'], w=['psC'])
                kb.act(aam[:], psC[:], AF.Sigmoid, bias=P[:, 18 + m:19 + m], r=['psC', rk_], w=[kaa])
                kb.mm(psC[:], gup_b[i][:, mc], sg[:], r=[f'gup{i}', 'sg'], w=['psC'])
                kb.tt('dve', gz[:, m, :], psC[:], zs[:, 4 + m, :], ALU.mult, r=['psC', 'zs'], w=['gz'])
                ps, pk = proj_chunk(winev[i, 4 + m], hT, 'hT')
                shift(ps, pk, m, rf, krf)
                ps, pk = proj_chunk(winev[i, 8 + m], hT, 'hT')
                shift(ps, pk, 4 + m, kf, kkf)
                ps, pk = proj_chunk(winev[i, 12 + m], hT, 'hT')
                shift(ps, pk, 8 + m, vf, kvf)
                kb.act(sqb[0][:], kf[:], AF.Square, scale=P[:, 22 + m:23 + m], r=[kkf, rk_], w=['sqb0'])
                kb.mm(psC[:], bonesb[:], sqb[0][:], r=['bonesb', 'sqb0'], w=['psC'])
                kb.act(t1[:], psC[:], AF.Ln, bias=EPS, r=['psC'], w=[kt1])
                kb.act(t1[:], t1[:], AF.Exp, scale=-0.5, r=[kt1], w=[kt1])
                kb.stt(kk[:], kf[:], P[:, 22 + m:23 + m], t1[:], ALU.mult, ALU.mult, r=[kkf, rk_, kt1], w=[kkk])
                kb.ts('pool', t2[:], aam[:], -1.0, ALU.add, P[:, 26 + m:27 + m], ALU.mult, r=[kaa, rk_], w=[kt2])
                kb.stt(kp[:], t2[:], 1.0, kf[:], ALU.add, ALU.mult, r=[kt2, kkf], w=[kkp])
                kb.tt(POOLE, bb_[:], kk[:], aam[:], ALU.mult, r=[kkk, kaa], w=[kbb])
                kb.stt(sqb[1][:], rf[:], P[:, 30 + m:31 + m], kp[:], ALU.mult, ALU.mult, r=[krf, rk_, kkp], w=['sqb1'])
                kb.mm(psC[:], bonesb[:], sqb[1][:], r=['bonesb', 'sqb1'], w=['psC'])
                kb.tt('dve', bonus[:, m, :], psC[:], vf[:], ALU.mult, r=['psC', kvf], w=['bonus'])
                kb.cp('act', vbT[:, m, :], vf[:], r=[kvf], w=['vbT'])
                kb.op('dve', lambda E: E.tensor_tensor_scan(out=lg[:], data0=cmask, data1=ldm[:], initial=0.0,
                                                            op0=ALU.mult, op1=ALU.add), r=['cst', kld], w=[klg])
                kb.tt(POOLE, lgx[:], lg[:], ldm[:], ALU.subtract, r=[klg, kld], w=[klgx])
                lg3 = lg[:].rearrange("p (j c) -> p j c", c=C)
                kb.act(t1[:], lg[:], AF.Exp, r=[klg], w=[kt1])
                kb.tt('dve', ops6['Qt'][:, m, :], rf[:], t1[:], ALU.mult, r=[krf, kt1], w=['op_Qt'])
                kb.act(t1[:], lgx[:], AF.Exp, r=[klgx], w=[kt1])
                kb.tt('dve', ops6['At'][:, m, :], kk[:], t1[:], ALU.mult, r=[kkk, kt1], w=['op_At'])
                kb.act(t1[:], lg[:], AF.Exp, scale=-1.0, r=[klg], w=[kt1])
                kb.tt('dve', ops6['Kh'][:, m, :], kp[:], t1[:], ALU.mult, r=[kkp, kt1], w=['op_Kh'])
                kb.tt(POOLE, ops6['Bh'][:, m, :], bb_[:], t1[:], ALU.mult, r=[kbb, kt1], w=['op_Bh'])
                kb.tt('dve', t2[:].rearrange("p (j c) -> p j c", c=C), bc(lg3[:, :, C - 1:C], [128, NCH, C]), lg3,
                      ALU.subtract, r=[klg], w=[kt2])
                kb.act(t2[:], t2[:], AF.Exp, r=[kt2], w=[kt2])
                kb.tt('dve', ops6['Kb'][:, m, :], kp[:], t2[:], ALU.mult, r=[kkp, kt2], w=['op_Kb'])
                kb.tt(POOLE, ops6['Bb'][:, m, :], bb_[:], t2[:], ALU.mult, r=[kbb, kt2], w=['op_Bb'])
                for par in range(2):
                    kb.act(GC[:, :, 2 * m + par], lg3[par * 64:(par + 1) * 64, :, C - 1], AF.Exp, r=[klg], w=['GC'])

            H = Hst[i]
            Hbf = Hb[i]

            def rwkv_stream(g):
                T = RG[g]
                HN = T['HN']
                POOLE = PE_OF[g]
                K = lambda n: f'r{n}_{g}'
                pA, pB, pC, pT = T['pA'], T['pB'], T['pC'], T['pT']
                kA, kB, kC, kT = T['kA'], T['kB'], T['kC'], T['kT']
                W = HN * 64
                hs = slice(g * HN, (g + 1) * HN)
                NM = HN // 2
                ms = slice(NM * g, NM * g + NM)
                f2 = lambda t: t[:].rearrange("p h c -> p (h c)")
                m3 = lambda mk: bc(mk.unsqueeze(1), [64, HN, 64])
                p3 = lambda ap: ap.rearrange("p (h c) -> p h c", c=64)
                opo = T['opo']
                Pf, Accb, Mav, Mqk, MqbN = T['Pf'], T['Accb'], T['Mav'], T['Mqk'], T['MqbN']
                Vt, Kbt, BbtN, RHSb, Pb2, tmpR, tmpO = (T[n] for n in ['Vt', 'Kbt', 'BbtN', 'RHSb', 'Pb2', 'tmpR', 'tmpO'])
                oT, oc, osq, s8, s8b, onb, t_o = (T[n] for n in ['oT', 'oc', 'osq', 's8', 's8b', 'onb', 't_o'])
                kH, kHb = f'H{i}_{g}', f'Hb{i}_{g}'
                for j in range(NCH):
                    cs = slice(j * C, (j + 1) * C)
                    for n in ['At', 'Qt', 'Kh', 'Bh']:
                        kb.cp('dve', opo[n][:], ops6[n][64:128, ms, cs], r=['op_' + n], w=[K('opo_' + n)])

                    def X(n, hl):
                        h = g * HN + hl
                        return ops6[n][0:64, h // 2, cs] if h % 2 == 0 else opo[n][:, hl // 2, :]
                    xk = lambda *ns: [k for n in ns for k in ('op_' + n, K('opo_' + n))]
                    for h in range(HN):
                        kb.mm(pA[0:64, h * 64:(h + 1) * 64], X('Bh', h), X('At', h), r=xk('Bh', 'At'), w=kA)
                    for h in range(HN):
                        kb.mm(pA[0:64, W + h * 64:W + (h + 1) * 64], X('Kh', h), X('At', h), r=xk('Kh', 'At'), w=kA)
                    for h in range(HN):
                        kb.mm(pB[0:64, h * 64:(h + 1) * 64], X('Kh', h), X('Qt', h), r=xk('Kh', 'Qt'), w=kB)
                    for h in range(HN):
                        kb.mm(pB[0:64, W + h * 64:W + (h + 1) * 64], X('Bh', h), X('Qt', h), r=xk('Bh', 'Qt'), w=kB)
                    kb.tt('dve', Pf[:], p3(pA[0:64, 0:W]), m3(MsN), ALU.mult, r=kA + ['cst'], w=[f'Pf_{g}'])
                    kb.tt('dve', Mav[:], p3(pA[0:64, W:2 * W]), m3(Ms), ALU.mult, r=kA + ['cst'], w=[K('Mav')])
                    kb.tt('dve', Mqk[:], p3(pB[0:64, 0:W]), m3(Mi), ALU.mult, r=kB + ['cst'], w=[K('Mqk')])
                    kb.tt('dve', MqbN[:], p3(pB[0:64, W:2 * W]), m3(MiN), ALU.mult, r=kB + ['cst'], w=[K('MqbN')])
                    neumann(T, g)
                    kAcc = f'Accb_{g}'
                    for ml in range(NM):
                        kb.tr(pT[0:64, ml * 128:(ml + 1) * 128], vbT[:, NM * g + ml, cs], identb[:], r=['vbT', 'identb'], w=kT)
                    kb.cp('act', Vt[:], pT[0:64, 0:W], r=kT, w=[K('Vt')])
                    for h in range(HN):
                        kb.mm(pC[0:64, h * 64:(h + 1) * 64], X('At', h), Hbf[:, g * HN + h, :], r=xk('At') + [kHb], w=kC)
                    kb.cp('act', tmpR[:], pC[0:64, 0:W], r=kC, w=[K('tmpR')])
                    for h in range(HN):
                        kb.mm(pA[0:64, h * 64:(h + 1) * 64], Mav[:, h, :], Vt[:, h * 64:(h + 1) * 64], r=[K('Mav'), K('Vt')], w=kA)
                    kb.tt('dve', RHSb[:], pA[0:64, 0:W], tmpR[:], ALU.add, r=kA + [K('tmpR')], w=[K('RHSb')])
                    for h in range(HN):
                        kb.mm(pC[0:64, h * 64:(h + 1) * 64], Accb[:, h, :], RHSb[:, h * 64:(h + 1) * 64], r=[kAcc, K('RHSb')], w=kC)
                    kb.cp('act', Pb2[:], pC[0:64, 0:W], r=kC, w=[K('Pb2')])
                    for h in range(HN):
                        kb.mm(pB[0:64, h * 64:(h + 1) * 64], X('Qt', h), Hbf[:, g * HN + h, :], r=xk('Qt') + [kHb], w=kB)
                    kb.cp('act', tmpO[:], pB[0:64, 0:W], r=kB, w=[K('tmpO')])
                    for h in range(HN):
                        hsl = slice(h * 64, (h + 1) * 64)
                        kb.mm(pA[0:64, hsl], Mqk[:, h, :], Vt[:, hsl], start=True, stop=False, r=[K('Mqk'), K('Vt')], w=kA)
                        kb.mm(pA[0:64, hsl], MqbN[:, h, :], Pb2[:, hsl], start=False, stop=True, r=[K('MqbN'), K('Pb2')], w=kA)
                    kb.tt('dve', f2(oT), pA[0:64, 0:W], tmpO[:], ALU.add, r=kA + [K('tmpO')], w=[K('oT')])
                    for ml in range(NM):
                        kb.tr(pT[0:64, ml * 128:(ml + 1) * 128], ops6['Kb'][:, NM * g + ml, cs], identb[:], r=['op_Kb', 'identb'], w=kT)
                    kb.cp('act', Kbt[:], pT[0:64, 0:W], r=kT, w=[K('Kbt')])
                    for ml in range(NM):
                        kb.tr(pT[0:64, ml * 128:(ml + 1) * 128], ops6['Bb'][:, NM * g + ml, cs], identb[:], r=['op_Bb', 'identb'], w=kT)
                    kb.ts('dve', BbtN[:], pT[0:64, 0:W], -1.0, ALU.mult, r=kT, w=[K('BbtN')])
                    for h in range(HN):
                        hsl = slice(h * 64, (h + 1) * 64)
                        kb.mm(pC[0:64, hsl], Kbt[:, hsl], Vt[:, hsl], start=True, stop=False, r=[K('Kbt'), K('Vt')], w=kC)
                        kb.mm(pC[0:64, hsl], BbtN[:, hsl], Pb2[:, hsl], start=False, stop=True, r=[K('BbtN'), K('Pb2')], w=kC)
                    kb.tt(POOLE, H[:, hs, :], H[:, hs, :], bc(GC[:, j, hs].unsqueeze(2), [64, HN, 64]), ALU.mult, r=[kH, 'GC'], w=[kH])
                    kb.tt('dve', H[:, hs, :], H[:, hs, :], p3(pC[0:64, 0:W]), ALU.add, r=[kH] + kC, w=[kH])
                    kb.cp('act', Hbf[:, hs, :], H[:, hs, :], r=[kH], w=[kHb])
                    kb.op('dve', lambda E: E.tensor_reduce(out=s8[:], in_=oT[:], axis=AX.X, op=ALU.add), r=[K('oT')], w=[K('s8')])
                    kb.ts('dve', s8[:], s8[:], -1.0 / 64, ALU.mult, r=[K('s8')], w=[K('s8')])
                    kb.tt(POOLE, oc[:], oT[:], bc(s8[:].unsqueeze(2), [64, HN, 64]), ALU.add, r=[K('oT'), K('s8')], w=[K('oc')])
                    kb.tt(POOLE, osq[:], oc[:], oc[:], ALU.mult, r=[K('oc')], w=[K('osq')])
                    kb.op('dve', lambda E: E.tensor_reduce(out=s8b[:], in_=osq[:], axis=AX.X, op=ALU.add), r=[K('osq')], w=[K('s8b')])
                    kb.act(s8b[:], s8b[:], AF.Ln, scale=1.0 / 64, bias=LN_EPS, r=[K('s8b')], w=[K('s8b')])
                    kb.act(s8b[:], s8b[:], AF.Exp, scale=-0.5, r=[K('s8b')], w=[K('s8b')])
                    kb.tt(POOLE, oc[:], oc[:], bc(s8b[:].unsqueeze(2), [64, HN, 64]), ALU.mult, r=[K('oc'), K('s8b')], w=[K('oc')])
                    kb.tt(POOLE, f2(oc), f2(oc), lnw_s[i][:, g * W:(g + 1) * W], ALU.mult, r=[K('oc'), f'lnw{i}'], w=[K('oc')])
                    kb.tt(POOLE, onb[:], f2(oc), lnb_s[i][:, g * W:(g + 1) * W], ALU.add, r=[K('oc'), f'lnb{i}'], w=[K('onb')])
                    for ml in range(NM):
                        kb.tr(pT[:, ml * 64:(ml + 1) * 64], onb[:, ml * 128:(ml + 1) * 128], identb[0:64, 0:64],
                              r=[K('onb'), 'identb'], w=kT)
                    kb.tt('dve', t_o[:], pT[:, 0:NM * 64].rearrange("p (m c) -> p m c", c=C), bonus[:, ms, cs], ALU.add,
                          r=kT + ['bonus'], w=[K('t_o')])
                    kb.tt('dve', ygT[:, 4 + NM * g:4 + NM * g + NM, cs], t_o[:], gz[:, ms, cs], ALU.mult, r=[K('t_o'), 'gz'], w=[f'ygT{g}'])

            kb.run_streams([(lambda g_=g_: rwkv_stream(g_)) for g_ in range(RGN)])

    def s5_phase(l, i, bi):
        STOP = int(os.environ.get('S5STOP', '99'))
        S5PE = os.environ.get('S5POOL', 'dve')
        if STOP <= 0:
            kb.op('pool', lambda E: E.memset(ygT[:, 0:4, :], 0.0), w=['ygT'])
            return
        P = rp[i]
        rk_ = f'rp{i}'
        with ExitStack() as st:
            uT = kb.sb("uT", [128, 4, TB], F32, st)
            uTb = kb.sb("uTb", [128, 4, TB], BF16, st)
            uTb3 = kb.sb("uTb3", [128, 4, TB], BF16, st)
            yT = kb.sb("yT", [128, 4, TB], F32, st)
            ygb = kb.sb("ygb", [128, 4, TB], BF16, st)
            cs1f = kb.sb("cs1f", [128, 4, 8, 64], F32, st)
            cs1b = kb.sb("cs1b", [128, 8, 8, 64], BF16, st)
            cpb = kb.sb("cpb", [128, 8, 64], BF16, st)
            ea = kb.sb("ea", [128, 4, 64], F32, st)
            etm = kb.sb("etm", [128, 4, 64], F32, st)
            et = kb.sb("et", [128, 4, 64], F32, st)
            ch = kb.sb("ch", [128, 4, 64], F32, st)
            cN = kb.sb("cN", [128, 4, 64], F32, st)
            g1 = kb.sb("g1", [128, TB], F32, st)
            g2 = kb.sb("g2", [128, TB], F32, st)
            if STOP < 99 and 'c' not in os.environ.get('S5SKIP', ''):
                kb.op('pool', lambda E: E.memset(yT[:], 0.0), w=['yT'])
                kb.op('pool', lambda E: E.memset(ygb[:], 0.0), w=['ygb'])
                kb.op('pool', lambda E: E.memset(cs1b[:], 0.0), w=['cs1b'])
                kb.op('pool', lambda E: E.memset(cpb[:], 0.0), w=['cpb'])
                kb.op('pool', lambda E: E.memset(cs1f[:], 0.0), w=['cs1f'])
            for q in range(0 if 'd' in os.environ.get('S5SKIP', '') else 4):
                ps, pk = proj_chunk(winev[i, q], hT, 'hT')
                if 'e' not in os.environ.get('S5SKIP', ''):
                    kb.cp('act', uT[:, q, :], ps[:], r=[pk], w=['uT'])
                if 'f' not in os.environ.get('S5SKIP', ''):
                    kb.cp('dve', uTb[:, q, :], uT[:, q, :], r=['uT'], w=['uTb'])
                if 'a' not in os.environ.get('S5SKIP', ''):
                    kb.ts('dve', uTb3[64:128, q, :], uTb[64:128, q, :], cst[64:128, 1218:1219], ALU.mult,
                          r=['uTb', 'cst'], w=['uTb3'])
            banks = [psA[:, 0:512], psA[:, 512:1024], psB[:, 0:512], psB[:, 512:1024]]
            bkeys = ['psA0', 'psA1', 'psB0', 'psB1']
            TTv = [TT_d[k_][i].rearrange("p (q b e) n -> p q b e n", q=4, b=4) for k_ in range(4)]
            BJq = [kb.sb(f"BJq{k_}", [128, 2, 8, 128], BF16, st) for k_ in range(2)]
            CJloq = [kb.sb(f"CJloq{k_}", [128, 4, 9, 32], BF16, st) for k_ in range(2)]
            CJhiq = [kb.sb(f"CJhiq{k_}", [128, 4, 9, 64], BF16, st) for k_ in range(2)]
            TTq = [[kb.sb(f"TTq{k_}_{z_}", [128, 4, 64], F32, st) for k_ in range(4)] for z_ in range(2)]
            carv = s5carry[i][:].rearrange("p (q b e) -> p q b e", q=4, b=4)
            r8v = rho8[i][:].rearrange("p (q b e) -> p q b e", q=4, b=4)
            for q in range(4 if STOP > 1 else 0):
                qb = q % 2
                tk = f'tabq{qb}'
                kb.dma('sp', BJq[qb][:], BJ_d[i][:, q], f'tq{qb}', w=[tk])
                kb.dma('sp', CJloq[qb][:], CJlo_d[i][:, q], f'tq{qb}', w=[tk])
                kb.dma('sp', CJhiq[qb][:], CJhi_d[i][:, q], f'tq{qb}', w=[tk])
                for e in range(2):
                    tke = f'tte{e}'
                    for k_ in range(4):
                        kb.dma('sp', TTq[e][k_][:], TTv[k_][:, q, :, e, :], f'tt{e}', w=[tke])
                    T1v, T2v, T3v, T4v = TTq[e]
                    for b in range(4):
                        pb = slice(32 * b, 32 * b + 32) if b < 3 else slice(64, 128)
                        uv = (uTb if b < 3 else uTb3)[pb, q, :].rearrange("p (n j) -> p j n", j=8)
                        for j in range(8):
                            kb.mm(banks[b][:, j * 64:(j + 1) * 64], BJq[qb][pb, e, j, :], uv[:, j, :],
                                  r=[tk, 'uTb', 'uTb3'], w=[bkeys[b]])
                    if STOP <= 2:
                        continue
                    zA = psA[:, :].rearrange("p (b j n) -> p b j n", b=2, j=8)
                    zB = psB[:, :].rearrange("p (b j n) -> p b j n", b=2, j=8)
                    for (z, zk, bs) in ((zA, ['psA0', 'psA1'], slice(0, 2)), (zB, ['psB0', 'psB1'], slice(2, 4))):
                        kb.cp('dve', cs1f[:, bs, 0, :], z[:, :, 0, :], r=zk, w=['cs1f'])
                        for j in range(1, 8):
                            kb.tt('dve', cs1f[:, bs, j, :], z[:, :, j, :], cs1f[:, bs, j - 1, :], ALU.add,
                                  r=zk + ['cs1f'], w=['cs1f'])
                    cbv = cs1b[:].rearrange("p (b e) j n -> p b e j n", e=2)
                    kb.cp('act', cbv[:, :, e, :, :], cs1f[:], r=['cs1f'], w=['cs1b'])
                    if STOP <= 3:
                        continue
                    x = cs1f[:, :, 7, :]
                    kb.tt(S5PE, ea[:], x, T1v[:], ALU.mult, r=['cs1f', tke], w=['ea'])
                    kb.tt('dve', etm[0:64], cs1f[64:128, :, 7, :], T2v[64:128], ALU.mult, r=['cs1f', tke], w=['etm'])
                    kb.tt('dve', etm[64:128], cs1f[0:64, :, 7, :], T2v[0:64], ALU.mult, r=['cs1f', tke], w=['etm'])
                    kb.tt(S5PE, et[:], ea[:], etm[:], ALU.add, r=['ea', 'etm'], w=['et'])
                    for b in range(4):
                        kb.op('dve', lambda E, b=b: E.tensor_tensor_scan(
                            out=ch[:, b, :], data0=r8v[:, q, b, e:e + 1].to_broadcast([128, 64]), data1=et[:, b, :],
                            initial=carv[:, q, b, e:e + 1], op0=ALU.mult, op1=ALU.add), r=['et', f's5c{i}'], w=['ch'])
                    kb.tt(S5PE, ea[:], ch[:], T3v[:], ALU.mult, r=['ch', tke], w=['ea'])
                    kb.tt('dve', etm[0:64], ch[64:128], T4v[64:128], ALU.mult, r=['ch', tke], w=['etm'])
                    kb.tt('dve', etm[64:128], ch[0:64], T4v[0:64], ALU.mult, r=['ch', tke], w=['etm'])
                    kb.tt(S5PE, cN[:], ea[:], etm[:], ALU.add, r=['ea', 'etm'], w=['cN'])
                    cpv = cpb[:].rearrange("p (b e) n -> p b e n", e=2)
                    kb.cp('dve', cpv[:, :, e, 0:1], carv[:, q, :, e:e + 1], r=[f's5c{i}'], w=['cpb'])
                    kb.cp('act', cpv[:, :, e, 1:64], cN[:, :, 0:63], r=['cN'], w=['cpb'])
                    kb.cp('dve', carv[:, q, :, e:e + 1], cN[:, :, 63:64], r=['cN'], w=[f's5c{i}'])
                if STOP <= 4:
                    continue
                for j in range(8):
                    jc = slice(j * 64, (j + 1) * 64)
                    for b in range(2):
                        for e in range(2):
                            gl = 2 * b + e
                            o_ = psC[32 * b:32 * b + 32, jc]
                            kb.mm(o_, CJloq[qb][:, gl, j, :], cs1b[:, gl, j, :], start=(e == 0), stop=False,
                                  r=[tk, 'cs1b'], w=['psC'])
                            kb.mm(o_, CJloq[qb][:, gl, j + 1, :], cpb[:, gl, :], start=False, stop=(e == 1),
                                  r=[tk, 'cpb'], w=['psC'])
                    for gl in range(4, 8):
                        o_ = psC[64:128, jc]
                        kb.mm(o_, CJhiq[qb][:, gl - 4, j, :], cs1b[:, gl, j, :], start=(gl == 4), stop=False,
                              r=[tk, 'cs1b'], w=['psC'])
                        kb.mm(o_, CJhiq[qb][:, gl - 4, j + 1, :], cpb[:, gl, :], start=False, stop=(gl == 7),
                              r=[tk, 'cpb'], w=['psC'])
                yv = yT[:, q, :].rearrange("p (n j) -> p j n", j=8)
                uv32 = uT[:, q, :].rearrange("p (n j) -> p j n", j=8)
                kb.stt(yv, uv32, P[:, 34 + q:35 + q], psC[:, :].rearrange("p (j n) -> p j n", j=8), ALU.mult, ALU.add,
                       r=['uT', rk_, 'psC'], w=['yT'])
                xq = yT[:, q, :]
                kb.tt(S5PE, g1[:], xq, xq, ALU.mult, r=['yT'], w=['g1'])
                kb.ts(S5PE, g1[:], g1[:], 0.044715, ALU.mult, 1.0, ALU.add, r=['g1'], w=['g1'])
                kb.tt(S5PE, g1[:], g1[:], xq, ALU.mult, r=['g1', 'yT'], w=['g1'])
                kb.act(g2[:], g1[:], AF.Sigmoid, scale=1.5957691216057308, r=['g1'], w=['g2'])
                kb.tt('dve', yT[:, q, :], xq, g2[:], ALU.mult, r=['yT', 'g2'], w=['yT'])
                kb.cp('act', ygb[:, q, :], yT[:, q, :], r=['yT'], w=['ygb'])
            for qo in range(0 if 'b' in os.environ.get('S5SKIP', '') else 4):
                for k in range(4):
                    kb.mm(psC[:], gluw_b[i][:, k, qo * 128:(qo + 1) * 128], ygb[:, k, :], start=(k == 0), stop=(k == 3),
                          r=[f'gluw{i}', 'ygb'], w=['psC'])
                kb.act(g2[:], psC[:], AF.Sigmoid, bias=P[:, 38 + qo:39 + qo], r=['psC', rk_], w=['g2'])
                kb.tt('dve', g1[:], yT[:, qo, :], g2[:], ALU.mult, r=['yT', 'g2'], w=['g1'])
                kb.tt('dve', ygT[:, qo, :], g1[:], zs[:, qo, :], ALU.mult, r=['g1', 'zs'], w=['ygT'])

    def even_layer(l, bi):
        i = l // 2
        pre_norm(l)
        for m in range(8):
            ps, pk = proj_chunk(winev[i, 18 + m], hT, 'hT')
            kb.act(zs[:, m, :], ps[:], AF.Silu, r=[pk], w=['zs'])
        if dbg:
            kb.op('pool', lambda E: E.memset(zs[:], 1.0), r=['zs'], w=['zs'])
        s5_phase(l, i, bi)
        if not os.environ.get('RWSKIP'):
            rwkv_phase(l, i, bi)
        if not dbg:
            out_proj(l)
        kb.barrier()

    for bi in range(nblocks):
        ts_ = slice(bi * TB, (bi + 1) * TB)
        kb.dma('sp', xres[:], xT[:, ts_].rearrange("(k p) t -> p k t", p=128), 'xin', w=['xres'])
        for l in layers:
            if l % 2 == 1:
                odd_layer(l, bi)
            else:
                even_layer(l, bi)
        with ExitStack() as st:
            obuf = kb.sb("obuf", [128, 8, TB], F32, st)
            if final:
                rms_stats(xres, 'xres')
                for k in range(8):
                    kb.stt(obuf[:, k, :], xres[:, k, :], fnw_s[:, k:k + 1], rstd[:], ALU.mult, ALU.mult,
                           r=['xres', 'fnw_s', 'rstd'], w=['obuf'])
            elif dbg:
                kb.cp('dve', obuf[:], ygT[:], r=['ygT', 'ygT0', 'ygT1'], w=['obuf'])
            else:
                kb.cp('dve', obuf[:], xres[:], r=['xres'], w=['obuf'])
            kb.dma('sp', outT[:, ts_].rearrange("(k p) t -> p k t", p=128), obuf[:], 'xout', r=['obuf'])
            kb.barrier()
    kb.es.close()
    return nc, kb


def chunkify(W, cols):
    Wc = W[:, cols]
    n_m = Wc.shape[1] // 128
    return np.ascontiguousarray(Wc.reshape(8, 128, n_m, 128).transpose(2, 1, 0, 3))


def prep_shared(inp):
    sh = {}
    sh["normw"] = np.ascontiguousarray(inp["norm_w"].reshape(4, 8, 128).transpose(2, 0, 1).reshape(128, 32))
    sh["adaw"] = np.ascontiguousarray(inp["ada_w"])
    sh["adab"] = np.ascontiguousarray(inp["ada_b"].reshape(4, 24, 128).transpose(2, 0, 1).reshape(128, 96))
    sh["woutr"] = np.stack([chunkify(inp["w_out"][l], np.arange(1024)) for l in range(4)])
    sh["fnw"] = np.ascontiguousarray(inp["final_norm_w"].reshape(8, 128).T)
    sh["consts"] = make_consts()
    cols = np.concatenate([np.arange(0, 3072), np.arange(3088, 4112)])
    sh["winodd"] = np.stack([chunkify(inp["odd_w_in"][i], cols) for i in range(2)])
    sh["wba"] = np.ascontiguousarray(
        np.stack([inp["odd_w_in"][i][:, 3072:3088].reshape(8, 128, 16).transpose(1, 0, 2) for i in range(2)]))
    sh["convw"] = np.ascontiguousarray(
        np.stack([inp["gdn_conv_w"][i].reshape(4, 24, 128).transpose(2, 1, 0) for i in range(2)]))
    sh["alog"] = np.ascontiguousarray(np.broadcast_to(inp["gdn_a_log"][:, None, :], (2, 128, 8)))
    sh["dtb"] = np.ascontiguousarray(np.broadcast_to(inp["gdn_dt_bias"][:, None, :], (2, 128, 8)))
    sh["gnw"] = np.ascontiguousarray(np.broadcast_to(inp["gdn_norm_w"][:, None, :], (2, 128, 128)))
    sh["winev"] = np.stack([chunkify(inp["even_w_in"][i], np.arange(3328)) for i in range(2)])
    pc = lambda v, n: v.reshape(n, 128).T
    sh["rp"] = np.stack([np.concatenate([pc(inp["rwkv_mu"][i], 14), pc(inp["rwkv_w0"][i], 4), pc(inp["rwkv_a0"][i], 4),
                                         pc(inp["rwkv_k_k"][i], 4), pc(inp["rwkv_k_a"][i], 4), pc(inp["rwkv_r_k"][i], 4),
                                         pc(inp["s5_d"][i], 4), pc(inp["s5_glu_b"][i], 4)], axis=1) for i in range(2)])
    sh["wup"] = inp["rwkv_w_up"]
    sh["aup"] = inp["rwkv_a_up"]
    sh["gup"] = inp["rwkv_g_up"]
    sh["lnw"] = np.broadcast_to(inp["rwkv_ln_w"][:, None, :], (2, 64, 512))
    sh["lnb"] = np.broadcast_to(inp["rwkv_ln_b"][:, None, :], (2, 64, 512))
    s5a, s5b = [], []
    p_ = np.arange(128)
    for i in range(2):
        rep = lambda a: np.concatenate([a, a], axis=0)
        lre2 = rep(inp["s5_lambda_re"][i].T)
        lim2 = rep(inp["s5_lambda_im"][i].T)
        lst2 = np.broadcast_to(inp["s5_log_step"][i][None, :], (128, 32))
        crT = rep(inp["s5_c_re"][i].transpose(2, 0, 1)).reshape(128, 512)
        ciT = rep(inp["s5_c_im"][i].transpose(2, 0, 1)).reshape(128, 512)
        s5a.append(np.concatenate([lre2, lim2, lst2, crT, ciT], axis=1))
        g = 8 * np.arange(4)[None, :] + 2 * (p_ // 32)[:, None] + ((p_ % 32) // 16)[:, None]
        cp = (p_ % 16)[:, None]
        lreB = inp["s5_lambda_re"][i][g]
        limB = inp["s5_lambda_im"][i][g]
        breB = inp["s5_b_re"][i][g, :, cp]
        bimB = inp["s5_b_im"][i][g, :, cp]
        lstB = inp["s5_log_step"][i][g]
        s5b.append(np.concatenate([lreB.reshape(128, 256), limB.reshape(128, 256), breB.reshape(128, 256),
                                   bimB.reshape(128, 256), lstB], axis=1))
    sh["s5a"] = np.stack(s5a)
    sh["s5b"] = np.stack(s5b)
    sh["gluw"] = np.stack([inp["s5_glu_w"][i].reshape(4, 128, 512).transpose(1, 0, 2) for i in range(2)])
    return {k: np.ascontiguousarray(np.asarray(v, np.float32)) for k, v in sh.items()}


_PLAN = [([0, 1, 2, 3], True)]


def kernel(**inputs):
    inp = {k: np.asarray(v) for k, v in inputs.items()}
    nb = inp["x"].shape[0]
    sh = prep_shared(inp)
    xT = [np.ascontiguousarray(inp["x"][b].T) for b in range(nb)]
    cTs = [np.ascontiguousarray(inp["c"][b].reshape(8, 128).T) for b in range(nb)]
    for layers, final in _PLAN:
        nc, _ = build(layers, final)
        in_maps = [dict(sh, xT=xT[b], cT=cTs[b]) for b in range(nb)]
        res = run_bass_kernel_spmd(nc, in_maps, core_ids=list(range(nb)))
        xT = [np.asarray(res.results[b]["outT"]) for b in range(nb)]
    return np.stack([x.T for x in xT]).astype(np.float32)
```
